# Optimizing a Trainium2 kernel written in Bass

```python
import jax, jax.numpy as jnp
from jax import lax
import numpy as np

D_MODEL = 1024
BATCH = 2
SEQ = 16384
DEPTH = 1
DEC_BATCH = 4
DEC_SEQ = 8192
PAST_LEN = 128

D_MIX = D_MODEL
RET_HEADS = 4
RET_HEAD_DIM = 128
RET_WIDTH = RET_HEADS * RET_HEAD_DIM
MLA_HEADS = 4
MLA_NOPE_DIM = 128
MLA_ROPE_DIM = 64
MLA_QK_DIM = MLA_NOPE_DIM + MLA_ROPE_DIM
MLA_V_DIM = 128
MLA_WIDTH = MLA_HEADS * MLA_V_DIM
Q_LORA = 384
KV_LORA = 256
D_FF = 4 * D_MODEL
D_IN = 4 * RET_WIDTH + Q_LORA + KV_LORA + MLA_ROPE_DIM
IN_SPLITS = (RET_WIDTH, 2 * RET_WIDTH, 3 * RET_WIDTH, 4 * RET_WIDTH,
             4 * RET_WIDTH + Q_LORA, 4 * RET_WIDTH + Q_LORA + KV_LORA)
CHUNK = 128
Q_BLOCK = 128
ROPE_BASE = 10000.0
LN_EPS = 1e-5
RMS_EPS = 1e-6
DEEPNORM_ALPHA = float((2 * DEPTH) ** 0.25)
DEEPNORM_BETA = float((8 * DEPTH) ** -0.25)
N_MOD = 6
MLA_SCALE = float(MLA_QK_DIM ** -0.5)

kernel_name = "hymba_retention_mla_deepnorm_adaln_encoder"


def layer_norm_plain(x):
    xf = x.astype(jnp.float32)
    mu = jnp.mean(xf, -1, keepdims=True)
    var = jnp.mean(jnp.square(xf - mu), -1, keepdims=True)
    return ((xf - mu) * lax.rsqrt(var + LN_EPS)).astype(x.dtype)


def layer_norm_affine(x, w, b):
    xf = x.astype(jnp.float32)
    mu = jnp.mean(xf, -1, keepdims=True)
    var = jnp.mean(jnp.square(xf - mu), -1, keepdims=True)
    return ((xf - mu) * lax.rsqrt(var + LN_EPS) * w + b).astype(x.dtype)


def rms_norm(x, w):
    xf = x.astype(jnp.float32)
    y = xf * lax.rsqrt(jnp.mean(xf * xf, -1, keepdims=True) + RMS_EPS)
    return (y * w).astype(x.dtype)


def apply_rotary(x):
    S, d = x.shape[1], x.shape[-1]
    inv = ROPE_BASE ** (-jnp.arange(0, d, 2, dtype=jnp.float32) / d)
    ang = jnp.arange(S, dtype=jnp.float32)[:, None] * inv[None, :]
    cos = jnp.cos(ang)[:, None, :]
    sin = jnp.sin(ang)[:, None, :]
    xf = x.astype(jnp.float32)
    x1, x2 = xf[..., : d // 2], xf[..., d // 2:]
    return jnp.concatenate([x1 * cos - x2 * sin, x1 * sin + x2 * cos], -1).astype(x.dtype)


def retention_chunkwise(q, k, v, log_gamma, include_diag):
    B, S, H, dk = q.shape
    dv = v.shape[-1]
    n = S // CHUNK
    idx = jnp.arange(CHUNK, dtype=jnp.float32)
    rel = idx[:, None] - idx[None, :]
    mask = (rel >= 0) if include_diag else (rel > 0)
    decay_in = jnp.where(mask[None], jnp.exp(log_gamma[:, None, None] * jnp.maximum(rel, 0.0)[None]), 0.0)
    decay_q = jnp.exp(log_gamma[:, None] * (idx[None, :] + 1.0))
    decay_k = jnp.exp(log_gamma[:, None] * (CHUNK - 1.0 - idx[None, :]))
    decay_chunk = jnp.exp(log_gamma * CHUNK)

    def chunks(t):
        return t.astype(jnp.float32).reshape(B, n, CHUNK, H, t.shape[-1]).transpose(1, 0, 3, 2, 4)

    def step(state, inp):
        qi, ki, vi = inp
        s = jnp.einsum('bhid,bhjd->bhij', qi, ki) * decay_in[None]
        inner = jnp.einsum('bhij,bhje->bhie', s, vi)
        cross = jnp.einsum('bhid,bhde->bhie', qi, state) * decay_q[None, :, :, None]
        new_state = state * decay_chunk[None, :, None, None] + jnp.einsum(
            'bhjd,bhje->bhde', ki * decay_k[None, :, :, None], vi)
        return new_state, inner + cross

    state0 = jnp.zeros((B, H, dk, dv), jnp.float32)
    _, out = lax.scan(step, state0, (chunks(q), chunks(k), chunks(v)))
    return out.transpose(1, 0, 3, 2, 4).reshape(B, S, H, dv)


def bidirectional_retention(q, k, v, log_gamma_f, log_gamma_b):
    fwd = retention_chunkwise(q, k, v, log_gamma_f, True)
    bwd = retention_chunkwise(q[:, ::-1], k[:, ::-1], v[:, ::-1], log_gamma_b, False)[:, ::-1]
    return fwd + bwd


def head_group_norm(y, w):
    B, S, H, dv = y.shape
    mu = jnp.mean(y, -1, keepdims=True)
    var = jnp.mean(jnp.square(y - mu), -1, keepdims=True)
    return ((y - mu) * lax.rsqrt(var + LN_EPS)).reshape(B, S, H * dv) * w


def latent_attention(c_q, c_kv, k_rope, q_norm_w, w_uq, kv_norm_w, w_ukv):
    B, S, _ = c_q.shape
    q = (rms_norm(c_q, q_norm_w) @ w_uq).reshape(B, S, MLA_HEADS, MLA_QK_DIM)
    q = jnp.concatenate([q[..., :MLA_NOPE_DIM], apply_rotary(q[..., MLA_NOPE_DIM:])], -1) * MLA_SCALE
    kv = (rms_norm(c_kv, kv_norm_w) @ w_ukv).reshape(B, S, MLA_HEADS, MLA_NOPE_DIM + MLA_V_DIM)
    k_nope, v = kv[..., :MLA_NOPE_DIM], kv[..., MLA_NOPE_DIM:]
    k_pe = jnp.broadcast_to(apply_rotary(k_rope[:, :, None, :]), (B, S, MLA_HEADS, MLA_ROPE_DIM))
    k = jnp.concatenate([k_nope, k_pe], -1)
    nq = S // Q_BLOCK
    qb = q.reshape(B, nq, Q_BLOCK, MLA_HEADS, MLA_QK_DIM).transpose(1, 0, 2, 3, 4)

    def attend(q_blk):
        s = jnp.einsum('bqhd,bkhd->bhqk', q_blk, k).astype(jnp.float32)
        p = jax.nn.softmax(s, axis=-1).astype(v.dtype)
        return jnp.einsum('bhqk,bkhd->bqhd', p, v)

    o = lax.map(attend, qb)
    return o.transpose(1, 0, 2, 3, 4).reshape(B, S, MLA_WIDTH)


def encoder_layer(x, c, w_ada, b_ada, w_in, ret_decay_f, ret_decay_b, ret_gn_w,
                  q_norm_w, w_uq, kv_norm_w, w_ukv, w_o, ln1_w, ln1_b,
                  w_up, w_down, ln2_w, ln2_b):
    B, S, _ = x.shape
    mod = (jax.nn.silu(c) @ w_ada + b_ada)[:, None, :]
    sh1, sc1, g1, sh2, sc2, g2 = jnp.split(mod, N_MOD, axis=-1)

    h = layer_norm_plain(x) * (1.0 + sc1) + sh1
    r_q, r_k, r_v, r_g, c_q, c_kv, k_rope = jnp.split(h @ w_in, IN_SPLITS, axis=-1)
    heads = lambda t: t.reshape(B, S, RET_HEADS, RET_HEAD_DIM)
    rq = apply_rotary(heads(r_q))
    rk = apply_rotary(heads(r_k)) * (RET_HEAD_DIM ** -0.5)
    lg_f = jax.nn.log_sigmoid(ret_decay_f.astype(jnp.float32))
    lg_b = jax.nn.log_sigmoid(ret_decay_b.astype(jnp.float32))
    ret = bidirectional_retention(rq, rk, heads(r_v), lg_f, lg_b)
    ret = (jax.nn.silu(r_g) * head_group_norm(ret, ret_gn_w)).astype(x.dtype)
    att = latent_attention(c_q, c_kv, k_rope, q_norm_w, w_uq, kv_norm_w, w_ukv)
    mix = jnp.concatenate([ret, att], -1) @ w_o
    x = layer_norm_affine(DEEPNORM_ALPHA * x + g1 * mix, ln1_w, ln1_b)

    h = layer_norm_plain(x) * (1.0 + sc2) + sh2
    ff = jnp.square(jax.nn.relu(h @ w_up)) @ w_down
    return layer_norm_affine(DEEPNORM_ALPHA * x + g2 * ff, ln2_w, ln2_b)


def encoder_trunk(x, c, w_ada, b_ada, w_in, ret_decay_f, ret_decay_b, ret_gn_w,
                  q_norm_w, w_uq, kv_norm_w, w_ukv, w_o, ln1_w, ln1_b,
                  w_up, w_down, ln2_w, ln2_b):
    for l in range(DEPTH):
        x = encoder_layer(x, c, w_ada[l], b_ada[l], w_in[l], ret_decay_f[l], ret_decay_b[l],
                          ret_gn_w[l], q_norm_w[l], w_uq[l], kv_norm_w[l], w_ukv[l], w_o[l],
                          ln1_w[l], ln1_b[l], w_up[l], w_down[l], ln2_w[l], ln2_b[l])
    return x


def setup_inputs(seed: int = 0) -> dict:
    key = jax.random.key(seed)
    ks = jax.random.split(key, 24)
    nrm = lambda k, shape, s: jax.random.normal(k, shape, jnp.float32) * s
    L = DEPTH
    hidx = jnp.arange(RET_HEADS, dtype=jnp.float32)
    dec_f = jnp.log(2.0 ** (5.0 + hidx) - 1.0)
    dec_b = jnp.log(2.0 ** (5.5 + hidx) - 1.0)
    return {
        "x_prompt": nrm(ks[0], (BATCH, SEQ, D_MODEL), 1.0),
        "x_sample": nrm(ks[1], (DEC_BATCH, DEC_SEQ, D_MODEL), 1.0),
        "c_prompt": nrm(ks[2], (BATCH, D_MODEL), 1.0),
        "c_sample": nrm(ks[3], (DEC_BATCH, D_MODEL), 1.0),
        "w_ada": nrm(ks[4], (L, D_MODEL, N_MOD * D_MODEL), D_MODEL ** -0.5),
        "b_ada": nrm(ks[5], (L, N_MOD * D_MODEL), 0.01),
        "w_in": nrm(ks[6], (L, D_MODEL, D_IN), D_MODEL ** -0.5),
        "ret_decay_f": dec_f + nrm(ks[7], (L, RET_HEADS), 0.01),
        "ret_decay_b": dec_b + nrm(ks[8], (L, RET_HEADS), 0.01),
        "ret_gn_w": 1.0 + nrm(ks[9], (L, RET_WIDTH), 0.01),
        "q_norm_w": 1.0 + nrm(ks[10], (L, Q_LORA), 0.01),
        "w_uq": nrm(ks[11], (L, Q_LORA, MLA_HEADS * MLA_QK_DIM), Q_LORA ** -0.5),
        "kv_norm_w": 1.0 + nrm(ks[12], (L, KV_LORA), 0.01),
        "w_ukv": nrm(ks[13], (L, KV_LORA, MLA_HEADS * (MLA_NOPE_DIM + MLA_V_DIM)), KV_LORA ** -0.5),
        "w_o": nrm(ks[14], (L, D_MIX, D_MODEL), (D_MIX ** -0.5) * DEEPNORM_BETA),
        "ln1_w": 1.0 + nrm(ks[15], (L, D_MODEL), 0.01),
        "ln1_b": nrm(ks[16], (L, D_MODEL), 0.01),
        "w_up": nrm(ks[17], (L, D_MODEL, D_FF), D_MODEL ** -0.5),
        "w_down": nrm(ks[18], (L, D_FF, D_MODEL), (D_FF ** -0.5) * DEEPNORM_BETA),
        "ln2_w": 1.0 + nrm(ks[19], (L, D_MODEL), 0.01),
        "ln2_b": nrm(ks[20], (L, D_MODEL), 0.01),
    }


def reference(x_prompt, x_sample, c_prompt, c_sample, w_ada, b_ada, w_in, ret_decay_f,
              ret_decay_b, ret_gn_w, q_norm_w, w_uq, kv_norm_w, w_ukv, w_o, ln1_w, ln1_b,
              w_up, w_down, ln2_w, ln2_b):
    y_prompt = encoder_trunk(x_prompt, c_prompt, w_ada, b_ada, w_in, ret_decay_f, ret_decay_b,
                             ret_gn_w, q_norm_w, w_uq, kv_norm_w, w_ukv, w_o, ln1_w, ln1_b,
                             w_up, w_down, ln2_w, ln2_b)
    y_sample = encoder_trunk(x_sample, c_sample, w_ada, b_ada, w_in, ret_decay_f, ret_decay_b,
                             ret_gn_w, q_norm_w, w_uq, kv_norm_w, w_ukv, w_o, ln1_w, ln1_b,
                             w_up, w_down, ln2_w, ln2_b)
    return (y_prompt, y_sample)
```

```python
import math
import numpy as np
import concourse.bass as bass
import concourse.mybir as mybir
from concourse.bass_utils import run_bass_kernel_spmd

F32 = mybir.dt.float32
BF16 = mybir.dt.bfloat16
AF = mybir.ActivationFunctionType
ALU = mybir.AluOpType

D = 1024
DIN = 2752
DFF = 4096
ALPHA = float(2.0 ** 0.25)
MLA_SCALE = float(192 ** -0.5)
NCORES = 8


class Prog:
    SEM_LIMIT = 24000

    def __init__(self, nc):
        self.nc = nc
        self.ops = []
        self.last_w = {}
        self.readers = {}
        self.out_dma_ops = []
        self.barriers = []

    def add(self, eng, fn, r=(), w=(), dma=False, key=None, out=False, nosync=False):
        i = len(self.ops)
        if dma and key is None:
            key = w[0]
        ps = [t for t in r if isinstance(t, tuple) and t[0] == 'pb']
        r = [t for t in r if not (isinstance(t, tuple) and t[0] == 'pb')]
        w = list(w) + [t for t in ps if t not in w]
        deps = set()
        for t in r:
            j = self.last_w.get(t)
            if j is not None:
                deps.add((j, 'raw'))
        for t in w:
            j = self.last_w.get(t)
            if j is not None:
                deps.add((j, 'waw'))
            for k in self.readers.get(t, ()):
                deps.add((k, 'war'))
        for t in r:
            self.readers.setdefault(t, []).append(i)
        for t in w:
            self.last_w[t] = i
            self.readers[t] = []
        self.ops.append(dict(eng=eng, fn=fn, deps=deps, dma=dma, key=key, bar=len(self.barriers), nosync=nosync))
        if out:
            self.out_dma_ops.append(i)
        return i

    def barrier(self):
        self.barriers.append(len(self.ops))

    def emit(self):
        nc = self.nc
        ops = self.ops
        bar_sets = []
        for b in self.barriers:
            last = {}
            for i in range(b):
                op = ops[i]
                k = ('dma', op['key']) if op['dma'] else ('eng', op['eng'])
                last[k] = i
            bar_sets.append(set(last.values()))
        for i, op in enumerate(ops):
            need = set()
            for (j, kind) in op['deps']:
                pj = ops[j]
                if j == i:
                    continue
                if not pj['dma'] and not op['dma'] and pj['eng'] == op['eng']:
                    if op['eng'] == 'pe':
                        continue
                    if kind != 'raw' or op['nosync']:
                        continue
                need.add(j)
            if op['bar'] > 0:
                for j in bar_sets[op['bar'] - 1]:
                    pj = ops[j]
                    if not pj['dma'] and pj['eng'] == op['eng']:
                        continue
                    need.add(j)
            op['need'] = need
        has_dep = [False] * len(ops)
        for op in ops:
            for j in op['need']:
                has_dep[j] = True
        for j in self.out_dma_ops:
            has_dep[j] = True
        cnt = {}
        for i, op in enumerate(ops):
            if op['dma']:
                k = ('dma', op['key'])
            elif has_dep[i]:
                k = ('eng', op['eng'])
            else:
                op['sig'] = None
                continue
            cnt[k] = cnt.get(k, 0) + 1
            op['sig'] = (k, cnt[k])
        sems = {}
        cms = []
        for k, n in cnt.items():
            inc = 16 if k[0] == 'dma' else 1
            per = self.SEM_LIMIT // inc
            ns = (n + per - 1) // per
            lst = []
            for s in range(ns):
                cm = nc.semaphore("s%d_%d" % (len(sems), s))
                lst.append(cm.__enter__())
                cms.append(cm)
            sems[k] = (lst, inc, per)
        self.n_sems = len(cms)

        def semval(sig):
            k, n = sig
            lst, inc, per = sems[k]
            return lst[(n - 1) // per], ((n - 1) % per + 1) * inc

        engs = {}
        for i, op in enumerate(ops):
            engs.setdefault(op['eng'], []).append(i)

        def run(ename, e):
            waited = {}
            for i in engs.get(ename, []):
                op = ops[i]
                tgt = {}
                for j in op['need']:
                    k, n = ops[j]['sig']
                    if waited.get(k, 0) >= n:
                        continue
                    if tgt.get(k, 0) < n:
                        tgt[k] = n
                for k, n in tgt.items():
                    s, v = semval((k, n))
                    e.wait_ge(s, v)
                    waited[k] = n
                ins = op['fn'](e)
                if op['sig'] is not None:
                    s, v = semval(op['sig'])
                    ins.then_inc(s, 16 if op['dma'] else 1)
            if ename == 'sp':
                fin = {}
                for j in self.out_dma_ops:
                    k, n = ops[j]['sig']
                    fin[k] = max(fin.get(k, 0), n)
                for k, n in fin.items():
                    if waited.get(k, 0) >= n:
                        continue
                    s, v = semval((k, n))
                    e.wait_ge(s, v)

        with nc.Block() as block:
            @block.tensor
            def _(e):
                run('pe', e)

            @block.scalar
            def _(e):
                run('act', e)

            @block.vector
            def _(e):
                run('dve', e)

            @block.gpsimd
            def _(e):
                run('pool', e)

            @block.sync
            def _(e):
                run('sp', e)
        for cm in cms:
            cm.__exit__(None, None, None)


class Arena:
    def __init__(self, nc, words):
        self.t = nc.alloc_sbuf_tensor("arena", [128, words], F32)
        self.words = words
        self.top = 0

    def mark(self):
        return self.top

    def release(self, m):
        self.top = m

    def alloc(self, shape, dtype=F32):
        n = 1
        for s in shape[1:]:
            n *= s
        nw = n if dtype == F32 else (n + 1) // 2
        nw = (nw + 7) // 8 * 8
        assert self.top + nw <= self.words, ("SBUF arena overflow", self.top, nw, self.words)
        v = self.t[:, self.top:self.top + nw]
        self.top += nw
        if dtype != F32:
            v = v.bitcast(dtype)
        v = v[:, 0:n]
        if len(shape) == 3:
            v = v.rearrange("p (a b) -> p a b", a=shape[1])
        elif len(shape) == 4:
            v = v.rearrange("p (a b c) -> p a b c", a=shape[1], b=shape[2])
        elif len(shape) == 5:
            v = v.rearrange("p (a b c d) -> p a b c d", a=shape[1], b=shape[2], c=shape[3])
        if shape[0] < 128:
            v = v[0:shape[0]]
        return v


def build(L, NSLOT, debug=False):
    NJ = len(NSLOT)
    NCH = L // 128
    SS = [ns * L for ns in NSLOT]
    nc = bass.Bass("TRN2", target_bir_lowering=False)
    P = Prog(nc)

    def din(name, shape, dt=F32):
        return nc.dram_tensor(name, list(shape), dt, kind="ExternalInput").ap()

    def dscr(name, shape, dt=BF16):
        return nc.dram_tensor(name, list(shape), dt, kind=("ExternalOutput" if debug else "Internal")).ap()

    x_in = [din("x%d" % j, [SS[j], D]) for j in range(NJ)]
    tb_in = [din("tb%d" % j, [SS[j], 256]) for j in range(NJ)]
    tm_in = [din("tm%d" % j, [64, 2, SS[j]]) for j in range(NJ)]
    cT_in = din("cT", [128, 8, NJ])
    mk_in = din("mk", [128, NJ, 2, 128])
    ex_in = din("ex", [128, NJ, 2, 128])
    cst_in = din("cst", [128, 4 * 128 + 8])
    wada_in = din("w_ada", [D, 6 * D])
    bfm_in = din("b_ada_fm", [128, 48])
    brow_in = din("b_ada_row", [1, 6 * D])
    win_in = din("w_in", [D, DIN])
    wuq_in = din("w_uq", [384, 768])
    wukv_in = din("w_ukv", [256, 1024])
    wo_in = din("w_o", [D, D])
    wup_in = din("w_up", [D, DFF])
    wdn_in = din("w_down", [DFF, D])
    dec_in = din("dec", [1, 8])
    gnw_in = din("gnw", [1, 512])
    qnw_in = din("qnw_fm", [128, 3])
    kvnw_in = din("kvnw_fm", [128, 2])
    ln_in = din("ln", [4, D])
    y_out = [nc.dram_tensor("y%d" % j, [L, D], F32, kind="ExternalOutput").ap() for j in range(NJ)]

    KTd = [dscr("KTd%d" % j, [4, 128, SS[j]]) for j in range(NJ)]
    KRd = [dscr("KRd%d" % j, [64, SS[j]]) for j in range(NJ)]
    VVd = [dscr("VVd%d" % j, [4, 128, SS[j] // 128, 128]) for j in range(NJ)]
    QTd = [dscr("QTd%d" % j, [4, 128, L]) for j in range(NJ)]
    QRd = [dscr("QRd%d" % j, [4, 64, L]) for j in range(NJ)]
    RTd = [dscr("RTd%d" % j, [4, 128, L]) for j in range(NJ)]
    ATd = [dscr("ATd%d" % j, [4, 128, L]) for j in range(NJ)]
    X1d = [dscr("X1d%d" % j, [L, D], F32) for j in range(NJ)]

    AR = Arena(nc, 49100)
    PB = [nc.alloc_psum_tensor("pb%d" % i, [128, 512], F32) for i in range(8)]

    def mm(out, lhsT, rhs, start, stop, r, w):
        P.add('pe', lambda e: e.matmul(out, lhsT=lhsT, rhs=rhs, start=start, stop=stop), r=r, w=w)

    def act(out, in_, func, r, w, bias=None, scale=None):
        kw = {}
        if bias is not None:
            kw['bias'] = bias
        if scale is not None:
            kw['scale'] = scale
        P.add('act', lambda e: e.activation(out=out, in_=in_, func=func, **kw), r=r, w=w)

    def tt(eng, out, in0, in1, op, r, w, nosync=False):
        P.add(eng, lambda e: e.tensor_tensor(out=out, in0=in0, in1=in1, op=op), r=r, w=w, nosync=nosync)

    def ts(eng, out, in0, s1, s2, op0, op1, r, w):
        if s2 is None:
            P.add(eng, lambda e: e.tensor_scalar(out=out, in0=in0, scalar1=s1, scalar2=None, op0=op0), r=r, w=w)
        else:
            P.add(eng, lambda e: e.tensor_scalar(out=out, in0=in0, scalar1=s1, scalar2=s2, op0=op0, op1=op1), r=r, w=w)

    def stt(eng, out, in0, scalar, in1, op0, op1, r, w):
        P.add(eng, lambda e: e.scalar_tensor_tensor(out=out, in0=in0, scalar=scalar, in1=in1, op0=op0, op1=op1),
              r=r, w=w)

    def cp(eng, out, in_, r, w):
        if eng == 'act':
            P.add('act', lambda e: e.copy(out=out, in_=in_), r=r, w=w)
        else:
            P.add(eng, lambda e: e.tensor_copy(out=out, in_=in_), r=r, w=w)

    def dma(q, out, in_, r, w, key=None, outp=False):
        P.add(q, lambda e: e.dma_start(out=out, in_=in_), r=r, w=w, dma=True, key=key, out=outp)

    def recip(out, in_, r, w):
        P.add('dve', lambda e: e.reciprocal(out=out, in_=in_), r=r, w=w)

    def bnstats(out, in_, r, w):
        P.add('dve', lambda e: e.bn_stats(out=out, in_=in_), r=r, w=w)

    def bnaggr(out, in_, r, w):
        P.add('dve', lambda e: e.bn_aggr(out=out, in_=in_), r=r, w=w)

    def actmul(out, in_, m, r, w):
        P.add('act', lambda e: e.mul(out=out, in_=in_, mul=m), r=r, w=w)

    def memset(eng, ap, v, w):
        P.add(eng, lambda e: e.memset(ap, v), w=w)

    ident_bf = AR.alloc([128, 128], BF16)
    ident_f = AR.alloc([128, 128], F32)
    ones_bf = AR.alloc([128, 128], BF16)
    ones_f = AR.alloc([128, 128], F32)
    cst = AR.alloc([128, 4 * 128 + 8], F32)
    dma('sp', cst, cst_in, r=[], w=['cst'])
    memset('pool', ident_f, 1.0, ['ident_f'])
    P.add('pool', lambda e: e.affine_select(out=ident_f, in_=ident_f, pattern=[[-1, 128]], compare_op=ALU.is_equal,
                                            fill=0.0, base=0, channel_multiplier=1), r=['ident_f'], w=['ident_f'])
    cp('pool', ident_bf, ident_f, ['ident_f'], ['ident_bf'])
    memset('pool', ones_f, 1.0, ['ones_f'])
    memset('pool', ones_bf, 1.0, ['ones_bf'])

    def tr(out, in_, r, w):
        k = in_.shape[0]
        P.add('pe', lambda e: e.transpose(out=out, in_=in_, identity=ident_bf[0:k, 0:k]), r=list(r) + ['ident_bf'], w=w)

    RF = cst[:, 0:128]
    RB = cst[:, 128:256]
    MF = cst[:, 256:384]
    MB = cst[:, 384:512]
    c_127mj = cst[:, 512:513]
    c_j = cst[:, 513:514]
    c_128 = cst[:, 514:515]

    dec = AR.alloc([128, 8], F32)
    dma('sp', dec, dec_in.partition_broadcast(128).rearrange("p a b -> p (a b)"), r=[], w=['dec'])
    lg = AR.alloc([128, 8], F32)
    act(lg, dec, AF.Exp, ['dec'], ['lg'], scale=-1.0)
    ts('dve', lg, lg, 1.0, None, ALU.add, None, ['lg'], ['lg'])
    act(lg, lg, AF.Ln, ['lg'], ['lg'])
    ts('dve', lg, lg, -1.0, None, ALU.mult, None, ['lg'], ['lg'])
    qnw = AR.alloc([128, 3], F32)
    kvnw = AR.alloc([128, 2], F32)
    dma('sp', qnw, qnw_in, r=[], w=['qnw'])
    dma('sp', kvnw, kvnw_in, r=[], w=['kvnw'])
    bfm = AR.alloc([128, 48], F32)
    dma('sp', bfm, bfm_in, r=[], w=['bfm'])
    cT = AR.alloc([128, 8, NJ], F32)
    dma('sp', cT, cT_in, r=[], w=['cT'])
    siluc = AR.alloc([128, 8, NJ], F32)
    act(siluc, cT, AF.Silu, ['cT'], ['siluc'])
    modfm = AR.alloc([128, 48, NJ], F32)
    scp = AR.alloc([128, 2, 8, NJ], F32)

    persist_mark = AR.mark()

    def ln_apply(src, dst, tag, rtok, wtok, eps=1e-5, eng='dve'):
        st, mv, rs = LNS[tag]
        for c in range(2):
            P.add('dve', (lambda e, c=c: e.bn_stats(out=st[:, c, :], in_=src[:, c * 512:(c + 1) * 512])),
                  r=rtok, w=[('st', tag, c)])
        P.add('dve', lambda e: e.bn_aggr(out=mv, in_=st), r=[('st', tag, 0), ('st', tag, 1)], w=[('mv', tag)])
        ts('dve', rs, mv[:, 1:2], eps, None, ALU.add, None, [('mv', tag)], [('rs', tag)])
        P.add('dve', lambda e: e.reciprocal(out=rs, in_=rs), r=[('rs', tag)], w=[('rs', tag)])
        P.add('act', lambda e: e.sqrt(out=rs, in_=rs), r=[('rs', tag)], w=[('rs', tag)])
        ts(eng, dst, src, mv[:, 0:1], rs[:, 0:1], ALU.subtract, ALU.mult, list(rtok) + [('mv', tag), ('rs', tag)], wtok)

    LNS = {}

    def ln_scratch(tag):
        LNS[tag] = (AR.alloc([128, 2, 6], F32), AR.alloc([128, 2], F32), AR.alloc([128, 1], F32))

    def rsqrt_small(x, tag, mul, eps):
        ts('dve', x, x, mul, eps, ALU.mult, ALU.add, [tag], [tag])
        P.add('dve', lambda e: e.reciprocal(out=x, in_=x), r=[tag], w=[tag])
        P.add('act', lambda e: e.sqrt(out=x, in_=x), r=[tag], w=[tag])

    for j in range(NJ):
        S = SS[j]
        NS = NSLOT[j]
        NT = S // 128
        xj, tbj, tmj = x_in[j], tb_in[j], tm_in[j]
        P.barrier()
        AR.release(persist_mark)
        gB = AR.alloc([128, 2, 1024], F32)
        job_mark = AR.mark()
        Win = AR.alloc([128, 8, DIN], BF16)
        Wkrr = AR.alloc([128, 8, 64], BF16)
        brow = AR.alloc([1, DIN], BF16)
        bcfm = AR.alloc([128, 5], F32)
        bkr = AR.alloc([64, 2], F32)
        coef = AR.alloc([128, 2, 4, 128], F32)
        Sf = AR.alloc([128, 4, 128], F32)
        Tb = AR.alloc([128, 4, 128], F32)
        Sbf = AR.alloc([128, 4, 128], BF16)
        Tsave = AR.alloc([128, NCH, 4, 128], BF16)
        DcT = AR.alloc([128, 4, 128], F32)
        dqfB = AR.alloc([128, 4, 128], F32)
        dqbB = AR.alloc([128, 4, 128], F32)
        dkk = AR.alloc([128, 8], F32)
        dcc = AR.alloc([128, 8], F32)
        tmpa = AR.alloc([128, 128], F32)
        tmpb = AR.alloc([128, 128], F32)
        rowi1 = AR.alloc([128, 128], F32)
        row128mi = AR.alloc([128, 128], F32)
        Wukv = AR.alloc([128, 2, 2, 4, 128], BF16)
        Wuq = AR.alloc([128, 3, 4, 192], BF16)
        Wuqr = AR.alloc([128, 3, 4, 64], BF16)
        prep_mark = AR.mark()
        mkt = AR.alloc([128, 2, 128], F32)
        ext = AR.alloc([128, 2, 128], F32)
        st_ukv = AR.alloc([128, 2, 1024], F32)
        st_uq = AR.alloc([128, 3, 768], F32)
        silucB = AR.alloc([128, 8, 128], F32)
        browada = AR.alloc([1, 2, 1024], F32)
        stkr = AR.alloc([128, 8, 64], F32)
        wst = [AR.alloc([128, 8, 512], F32) for _ in range(2)]
        tt('dve', rowi1, RF, RB, ALU.subtract, ['cst'], ['rowi1'])
        ts('dve', rowi1, rowi1, c_j, 1.0, ALU.add, ALU.add, ['rowi1', 'cst'], ['rowi1'])
        ts('dve', row128mi, rowi1, -1.0, 129.0, ALU.mult, ALU.add, ['rowi1'], ['row128mi'])
        for h in range(4):
            act(tmpa, RF, AF.Exp, ['cst', 'lg'], ['tmpa'], scale=lg[:, h:h + 1])
            tt('dve', tmpa, tmpa, MF, ALU.mult, ['tmpa', 'cst'], ['tmpa'])
            act(tmpb, RB, AF.Exp, ['cst', 'lg'], ['tmpb'], scale=lg[:, 4 + h:5 + h])
            tt('dve', tmpb, tmpb, MB, ALU.mult, ['tmpb', 'cst'], ['tmpb'])
            tt('dve', DcT[:, h, :], tmpa, tmpb, ALU.add, ['tmpa', 'tmpb'], ['DcT'])
            act(dqfB[:, h, :], rowi1, AF.Exp, ['rowi1', 'lg'], ['dqfB'], scale=lg[:, h:h + 1])
            act(dqbB[:, h, :], row128mi, AF.Exp, ['row128mi', 'lg'], ['dqbB'], scale=lg[:, 4 + h:5 + h])
            act(dkk[:, h:h + 1], c_127mj, AF.Exp, ['cst', 'lg'], ['dkk'], scale=lg[:, h:h + 1])
            act(dkk[:, 4 + h:5 + h], c_j, AF.Exp, ['cst', 'lg'], ['dkk'], scale=lg[:, 4 + h:5 + h])
            act(dcc[:, h:h + 1], c_128, AF.Exp, ['cst', 'lg'], ['dcc'], scale=lg[:, h:h + 1])
            act(dcc[:, 4 + h:5 + h], c_128, AF.Exp, ['cst', 'lg'], ['dcc'], scale=lg[:, 4 + h:5 + h])

        dma('sp', st_ukv, wukv_in.rearrange("(j p) n -> p j n", p=128), r=[], w=['st_ukv'])
        dma('sp', st_uq, wuq_in.rearrange("(j p) n -> p j n", p=128), r=[], w=['st_uq'])
        for cj in range(2):
            v = st_ukv[:, cj, :].rearrange("p (h t d) -> p t h d", h=4, t=2)
            ts('dve', Wukv[:, cj], v, kvnw[:, cj:cj + 1], None, ALU.mult, None, ['st_ukv', 'kvnw'], ['Wukv'])
        for cj in range(3):
            v = st_uq[:, cj, :].rearrange("p (h d) -> p h d", h=4)
            ts('dve', Wuq[:, cj], v, qnw[:, cj:cj + 1], MLA_SCALE, ALU.mult, ALU.mult, ['st_uq', 'qnw'], ['Wuq'])
            ts('dve', Wuqr[:, cj, :, 0:32], v[:, :, 160:192], qnw[:, cj:cj + 1], -MLA_SCALE, ALU.mult, ALU.mult,
               ['st_uq', 'qnw'], ['Wuqr'])
            ts('dve', Wuqr[:, cj, :, 32:64], v[:, :, 128:160], qnw[:, cj:cj + 1], MLA_SCALE, ALU.mult, ALU.mult,
               ['st_uq', 'qnw'], ['Wuqr'])

        for k in range(8):
            cp('dve', silucB[:, k, :], siluc[:, k, j:j + 1].broadcast_to([128, 128]), ['siluc'], ['silucB'])
        dma('sp', browada[:, 0, :], brow_in[:, 2048:3072], r=[], w=['browada'])
        dma('sp', browada[:, 1, :], brow_in[:, 5120:6144], r=[], w=['browada'])
        gi = 0
        for g in range(12):
            b = gi % 2
            gi += 1
            dma('sp', wst[b], wada_in[:, g * 512:(g + 1) * 512].rearrange("(k p) n -> p k n", p=128),
                r=[], w=[('wst', b)])
            if g in (4, 5, 10, 11):
                which = 0 if g < 6 else 1
                half = g % 2 if g < 6 else (g - 10)
                for k in range(8):
                    mm(PB[0][:, :], silucB[:, k, :], wst[b][:, k, :], k == 0, False, ['silucB', ('wst', b)], [('pb', 0)])
                mm(PB[0][:, :], ones_f[0:1, :], browada[0:1, which, half * 512:(half + 1) * 512], False, True,
                   ['ones_f', 'browada'], [('pb', 0)])
                cp('act', gB[:, which, half * 512:(half + 1) * 512], PB[0][:, :], [('pb', 0)], ['gB'])
            else:
                for c in range(4):
                    ch = g * 4 + c
                    for k in range(8):
                        mm(PB[1][:, 0:1], wst[b][:, k, c * 128:(c + 1) * 128], siluc[:, k, j:j + 1], k == 0, k == 7,
                           [('wst', b), 'siluc'], [('pb', 1)])
                    ts('dve', modfm[:, ch, j:j + 1], PB[1][:, 0:1], bfm[:, ch:ch + 1], None, ALU.add, None,
                       [('pb', 1), 'bfm'], ['modfm'])
        for k in range(8):
            ts('dve', scp[:, 0, k, j:j + 1], modfm[:, 8 + k, j:j + 1], 1.0, None, ALU.add, None, ['modfm'], ['scp'])
            ts('dve', scp[:, 1, k, j:j + 1], modfm[:, 32 + k, j:j + 1], 1.0, None, ALU.add, None, ['modfm'], ['scp'])

        groups = [(0, 512), (512, 512), (1024, 512), (1536, 512), (2048, 512), (2560, 192)]
        for (c0, cw) in groups:
            b = gi % 2
            gi += 1
            dma('sp', wst[b][:, :, 0:cw], win_in[:, c0:c0 + cw].rearrange("(k p) n -> p k n", p=128),
                r=[], w=[('wst', b)])
            for k in range(8):
                mm(PB[0][0:1, 0:cw], modfm[:, k, j:j + 1], wst[b][:, k, 0:cw], k == 0, k == 7,
                   ['modfm', ('wst', b)], [('pb', 0)])
            if c0 == 512:
                actmul(brow[0:1, c0:c0 + cw], PB[0][0:1, 0:cw], float(128 ** -0.5), [('pb', 0)], ['brow'])
            else:
                cp('act', brow[0:1, c0:c0 + cw], PB[0][0:1, 0:cw], [('pb', 0)], ['brow'])
            if c0 == 2048:
                for c in range(4):
                    for k in range(8):
                        mm(PB[1][:, 0:1], wst[b][:, k, c * 128:(c + 1) * 128], modfm[:, k, j:j + 1], k == 0, k == 7,
                           [('wst', b), 'modfm'], [('pb', 1)])
                    cp('dve', bcfm[:, c:c + 1], PB[1][:, 0:1], [('pb', 1)], ['bcfm'])
            if c0 == 2560:
                for k in range(8):
                    mm(PB[1][:, 0:1], wst[b][:, k, 0:128], modfm[:, k, j:j + 1], k == 0, k == 7,
                       [('wst', b), 'modfm'], [('pb', 1)])
                cp('dve', bcfm[:, 4:5], PB[1][:, 0:1], [('pb', 1)], ['bcfm'])
                ts('dve', stkr[:, :, 0:32], wst[b][:, :, 160:192], -1.0, None, ALU.mult, None, [('wst', b)], ['stkr'])
                cp('dve', stkr[:, :, 32:64], wst[b][:, :, 128:160], [('wst', b)], ['stkr'])
                for k in range(8):
                    mm(PB[1][0:64, 0:1], wst[b][:, k, 128:192], modfm[:, k, j:j + 1], k == 0, k == 7,
                       [('wst', b), 'modfm'], [('pb', 1)])
                cp('dve', bkr[:, 0:1], PB[1][0:64, 0:1], [('pb', 1)], ['bkr'])
                for k in range(8):
                    mm(PB[1][0:64, 0:1], stkr[:, k, :], modfm[:, k, j:j + 1], k == 0, k == 7,
                       ['stkr', 'modfm'], [('pb', 1)])
                cp('dve', bkr[:, 1:2], PB[1][0:64, 0:1], [('pb', 1)], ['bkr'])
                for k in range(8):
                    ts('dve', Wkrr[:, k, :], stkr[:, k, :], scp[:, 0, k, j:j + 1], None, ALU.mult, None,
                       ['stkr', 'scp'], ['Wkrr'])
            for k in range(8):
                if c0 == 512:
                    ts('dve', Win[:, k, c0:c0 + cw], wst[b][:, k, 0:cw], scp[:, 0, k, j:j + 1], float(128 ** -0.5),
                       ALU.mult, ALU.mult, [('wst', b), 'scp'], ['Win'])
                else:
                    ts('dve', Win[:, k, c0:c0 + cw], wst[b][:, k, 0:cw], scp[:, 0, k, j:j + 1], None, ALU.mult, None,
                       [('wst', b), 'scp'], ['Win'])
        dma('sp', mkt, mk_in[:, j], r=[], w=['mkt'])
        dma('sp', ext, ex_in[:, j], r=[], w=['ext'])
        for dr in range(2):
            for h in range(4):
                act(coef[:, dr, h, :], ext[:, dr, :], AF.Exp, ['ext', 'lg'], ['coef'], scale=lg[:, 4 * dr + h:4 * dr + h + 1])
                tt('dve', coef[:, dr, h, :], coef[:, dr, h, :], mkt[:, dr, :], ALU.mult, ['coef', 'mkt'], ['coef'])
        memset('pool', Sf, 0.0, ['Sf'])
        memset('pool', Tb, 0.0, ['Tb'])
        P.barrier()
        AR.release(prep_mark)
        ab_mark = AR.mark()

        xs = [AR.alloc([128, D], F32) for _ in range(2)]
        tbt = [AR.alloc([128, 256], F32) for _ in range(2)]
        tmt = [AR.alloc([64, 2, 128], F32) for _ in range(2)]
        xn = AR.alloc([128, D], BF16)
        xT = AR.alloc([128, 8, 128], BF16)
        rA = AR.alloc([128, 4, 128], F32)
        rBt = AR.alloc([128, 4, 128], F32)
        rstd_t = AR.alloc([128, 1], F32)
        rB = AR.alloc([128, 128], F32)
        aonly_mark = AR.mark()
        kr1 = AR.alloc([64, 128], F32)
        kr2 = AR.alloc([64, 128], F32)
        pT = PB[0][:, :].bitcast(BF16).rearrange("p (a b) -> p a b", a=8)[:, :, 0:128]

        def load(t, b):
            dma('sp', xs[b], xj[t * 128:(t + 1) * 128, :], r=[], w=[('xs', b)])
            dma('sp', tbt[b], tbj[t * 128:(t + 1) * 128, :], r=[], w=[('tbt', b)])
            dma('sp', tmt[b], tmj[:, :, t * 128:(t + 1) * 128], r=[], w=[('tmt', b)])

        def proj_tok(pb, tok, c0, cw):
            for k in range(8):
                mm(pb[:, 0:cw], xT[:, k, :], Win[:, k, c0:c0 + cw], k == 0, False, ['xT', 'Win'], [tok])
            mm(pb[:, 0:cw], ones_bf[0:1, :], brow[0:1, c0:c0 + cw], False, True, ['ones_bf', 'brow'], [tok])

        def rotary_tok(pb, tok, b, dst, dtok):
            v = pb[:, :].rearrange("p (h d) -> p h d", h=4)
            cosB = tbt[b][:, 0:128].unsqueeze(1).broadcast_to([128, 4, 128])
            sinB = tbt[b][:, 128:256].unsqueeze(1).broadcast_to([128, 4, 128])
            tt('dve', rA, v, cosB, ALU.mult, [tok, ('tbt', b)], ['rA'])
            tt('dve', rBt, v, sinB, ALU.mult, [tok, ('tbt', b)], ['rBt'])
            tt('pool', dst[:, :, 0:64], rA[:, :, 0:64], rBt[:, :, 64:128], ALU.subtract, ['rA', 'rBt'], [dtok])
            tt('pool', dst[:, :, 64:128], rBt[:, :, 0:64], rA[:, :, 64:128], ALU.add, ['rA', 'rBt'], [dtok])

        Kp2 = [AR.alloc([128, 4, 128], BF16) for _ in range(2)]
        Kwf2 = [AR.alloc([128, 4, 128], BF16) for _ in range(2)]
        Kwb2 = [AR.alloc([128, 4, 128], BF16) for _ in range(2)]
        Vr2 = [AR.alloc([128, 512], BF16) for _ in range(2)]
        sqv2 = [AR.alloc([128, 2, 128], BF16) for _ in range(2)]
        ckv2 = [AR.alloc([128, 2, 128], BF16) for _ in range(2)]
        KTb2 = [AR.alloc([128, 4, 512], BF16) for _ in range(2)]
        KRb2 = [AR.alloc([64, 512], BF16) for _ in range(2)]
        VMb2 = [AR.alloc([128, 4, 4, 128], BF16) for _ in range(2)]
        ckt = [AR.alloc([128, 2, 4], F32) for _ in range(2)]
        dkfB = dkk[:, 0:4].unsqueeze(2).broadcast_to([128, 4, 128])
        dkbB = dkk[:, 4:8].unsqueeze(2).broadcast_to([128, 4, 128])

        xnA = [xn, AR.alloc([128, D], BF16)]
        xTA = [xT, AR.alloc([128, 8, 128], BF16)]
        ln_scratch('a0')
        ln_scratch('a1')

        def lnA(t):
            b = t % 2
            ln_apply(xs[b], xnA[b], 'a%d' % b, [('xs', b)], [('xnA', b)])

        def trA(t):
            b = t % 2
            for k in range(8):
                tr(pT[:, k, :], xnA[b][:, k * 128:(k + 1) * 128], [('xnA', b)], [('pb', 0)])
            cp('act', xTA[b], pT, [('pb', 0)], [('xTA', b)])

        def projA(t):
            b = t % 2
            xTb = xTA[b]
            xtok = ('xTA', b)
            for (pb_, c0) in ((1, 512), (2, 1024)):
                for k in range(8):
                    mm(PB[pb_][:, :], xTb[:, k, :], Win[:, k, c0:c0 + 512], k == 0, False, [xtok, 'Win'], [('pb', pb_)])
                mm(PB[pb_][:, :], ones_bf[0:1, :], brow[0:1, c0:c0 + 512], False, True, ['ones_bf', 'brow'], [('pb', pb_)])
            for cj in range(2):
                for k in range(8):
                    mm(PB[3][:, cj * 128:(cj + 1) * 128], Win[:, k, 2432 + cj * 128:2432 + (cj + 1) * 128], xTb[:, k, :],
                       k == 0, k == 7, ['Win', xtok], [('pb', 3)])
            for k in range(8):
                mm(PB[3][0:64, 256:384], Win[:, k, 2688:2752], xTb[:, k, :], k == 0, k == 7, ['Win', xtok], [('pb', 3)])
            for k in range(8):
                mm(PB[3][0:64, 384:512], Wkrr[:, k, :], xTb[:, k, :], k == 0, k == 7, ['Wkrr', xtok], [('pb', 3)])

        def firstA(t):
            b = t % 2
            s = t // NCH
            q4 = t % 4
            bp = (t // 4) % 2
            rotary_tok(PB[1], ('pb', 1), b, Kp2[b], ('Kp2', b))
            cp('act', Vr2[b], PB[2][:, :], [('pb', 2)], [('Vr2', b)])
            ckvp = PB[3][:, 0:256].rearrange("p (a b) -> p a b", a=2)
            for cj in range(2):
                act(sqv2[b][:, cj, :], ckvp[:, cj, :], AF.Square, [('pb', 3), 'bcfm'], [('sqv2', b)], bias=bcfm[:, 3 + cj:4 + cj])
                act(ckv2[b][:, cj, :], ckvp[:, cj, :], AF.Identity, [('pb', 3), 'bcfm'], [('ckv2', b)], bias=bcfm[:, 3 + cj:4 + cj])
            stt('dve', kr1, PB[3][0:64, 256:384], bkr[:, 0:1], tmt[b][:, 0, :], ALU.add, ALU.mult,
                [('pb', 3), 'bkr', ('tmt', b)], ['kr1'])
            stt('dve', kr2, PB[3][0:64, 384:512], bkr[:, 1:2], tmt[b][:, 1, :], ALU.add, ALU.mult,
                [('pb', 3), 'bkr', ('tmt', b)], ['kr2'])
            tt('pool', KRb2[bp][:, q4 * 128:(q4 + 1) * 128], kr1, kr2, ALU.add, ['kr1', 'kr2'], [('KRb2', bp)])
            if s > 0:
                tt('dve', ckt[b], dkk.rearrange("p (a h) -> p a h", a=2), coef[:, :, :, t], ALU.mult, ['dkk', 'coef'], [('ckt', b)])
                tt('pool', Kwf2[b], Kp2[b], ckt[b][:, 0, :].unsqueeze(2).broadcast_to([128, 4, 128]), ALU.mult,
                   [('Kp2', b), ('ckt', b)], [('Kwf2', b)])
                tt('pool', Kwb2[b], Kp2[b], ckt[b][:, 1, :].unsqueeze(2).broadcast_to([128, 4, 128]), ALU.mult,
                   [('Kp2', b), ('ckt', b)], [('Kwb2', b)])
            else:
                tt('pool', Kwb2[b], Kp2[b], dkbB, ALU.mult, [('Kp2', b), 'dkk'], [('Kwb2', b)])

        def secondA(t):
            b = t % 2
            s = t // NCH
            n = t % NCH
            q4 = t % 4
            bp = (t // 4) % 2
            for cj in range(2):
                mm(PB[4][:, 0:1], sqv2[b][:, cj, :], ones_bf[:, 0:1], cj == 0, cj == 1, [('sqv2', b), 'ones_bf'], [('pb', 4)])
            for cj in range(2):
                mm(PB[4][:, 128:256], ones_bf, sqv2[b][:, cj, :], cj == 0, cj == 1, ['ones_bf', ('sqv2', b)], [('pb', 4)])
            cp('dve', rstd_t, PB[4][:, 0:1], [('pb', 4)], ['rstd_t'])
            cp('dve', rB, PB[4][:, 128:256], [('pb', 4)], ['rB'])
            rsqrt_small(rstd_t, 'rstd_t', 1.0 / 256.0, 1e-6)
            rsqrt_small(rB, 'rB', 1.0 / 256.0, 1e-6)
            Pst = [PB[6][:, :].rearrange("p (h e) -> p h e", h=4), PB[7][:, :].rearrange("p (h e) -> p h e", h=4)]
            if s > 0:
                first_t = (t == NT - 1)
                last_t = (t == NCH)
                for h in range(4):
                    mm(Pst[1][:, h, :], Kwb2[b][:, h, :], Vr2[b][:, h * 128:(h + 1) * 128], first_t and h == 0, last_t,
                       [('Kwb2', b), ('Vr2', b)], [('pb', 7)])
                for h in range(4):
                    mm(Pst[0][:, h, :], Kwf2[b][:, h, :], Vr2[b][:, h * 128:(h + 1) * 128], first_t and h == 0, last_t,
                       [('Kwf2', b), ('Vr2', b)], [('pb', 6)])
            else:
                for h in range(4):
                    mm(Pst[1][:, h, :], Kwb2[b][:, h, :], Vr2[b][:, h * 128:(h + 1) * 128], True, True,
                       [('Kwb2', b), ('Vr2', b)], [('pb', 7)])
            knp = PB[5][:, :].rearrange("p (h d) -> p h d", h=4)
            for h in range(4):
                for cj in range(2):
                    mm(knp[:, h, :], Wukv[:, cj, 0, h, :], ckv2[b][:, cj, :], cj == 0, cj == 1, ['Wukv', ('ckv2', b)], [('pb', 5)])
            for cj in range(2):
                mm(PB[4][:, :], ckv2[b][:, cj, :], Wukv[:, cj, 1].rearrange("p h d -> p (h d)"), cj == 0, cj == 1,
                   [('ckv2', b), 'Wukv'], [('pb', 4)])
            if s > 0:
                if t == NCH:
                    cp('dve', Sf, Pst[0], [('pb', 6)], ['Sf'])
                    cp('dve', Tb, Pst[1], [('pb', 7)], ['Tb'])
            else:
                cp('act', Tsave[:, n], Tb, ['Tb'], ['Tsave'])
                for h in range(4):
                    stt('dve', Tb[:, h, :], Tb[:, h, :], dcc[:, 4 + h:5 + h], Pst[1][:, h, :], ALU.mult, ALU.add,
                        [('pb', 7), 'dcc', 'Tb'], ['Tb'])
            tt('dve', KTb2[bp][:, :, q4 * 128:(q4 + 1) * 128], knp, rB.unsqueeze(1).broadcast_to([128, 4, 128]), ALU.mult,
               [('pb', 5), 'rB'], [('KTb2', bp)])
            act(VMb2[bp][:, :, q4, :], PB[4][:, :].rearrange("p (h d) -> p h d", h=4), AF.Identity, [('pb', 4), 'rstd_t'],
                [('VMb2', bp)], scale=rstd_t[:, 0:1])
            if q4 == 0:
                t0 = t
                dma('sp', KTd[j][:, :, t0 * 128:(t0 + 4) * 128].rearrange("h p n -> p h n"), KTb2[bp], r=[('KTb2', bp)], w=[('KTd', j)])
                dma('sp', KRd[j][:, t0 * 128:(t0 + 4) * 128], KRb2[bp], r=[('KRb2', bp)], w=[('KRd', j)])
                dma('sp', VVd[j][:, :, t0:t0 + 4, :].rearrange("h p t e -> p h t e"), VMb2[bp], r=[('VMb2', bp)], w=[('VVd', j)])

        load(NT - 1, (NT - 1) % 2)
        load(NT - 2, (NT - 2) % 2)
        lnA(NT - 1)
        trA(NT - 1)
        projA(NT - 1)
        lnA(NT - 2)
        firstA(NT - 1)
        trA(NT - 2)
        for t in range(NT - 1, -1, -1):
            if t - 1 >= 0:
                projA(t - 1)
                if t - 2 >= 0:
                    load(t - 2, t % 2)
                    lnA(t - 2)
                firstA(t - 1)
                if t - 2 >= 0:
                    trA(t - 2)
            secondA(t)

        P.barrier()
        AR.release(aonly_mark)
        QT3 = AR.alloc([128, 3, 4, 128], BF16)
        KT = AR.alloc([128, 4, 128], BF16)
        AcT = AR.alloc([128, 4, 128], BF16)
        gnB = AR.alloc([128, 512], F32)
        dma('sp', gnB, gnw_in.partition_broadcast(128).rearrange("p a b -> p (a b)"), r=[], w=['gnB'])
        st4 = AR.alloc([128, 4, 6], F32)
        mv4 = AR.alloc([128, 4, 2], F32)
        rs4 = AR.alloc([128, 4], F32)
        y1 = AR.alloc([128, 4, 128], F32)
        retb = AR.alloc([128, 512], BF16)
        RTb = AR.alloc([128, 4, 512], BF16)
        QNb = AR.alloc([128, 4, 512], BF16)
        QRb = AR.alloc([64, 4, 512], BF16)
        qr1 = AR.alloc([64, 4, 128], F32)
        qr2 = AR.alloc([64, 4, 128], F32)
        xnB = [xn, AR.alloc([128, D], BF16)]
        xTB = [xT, AR.alloc([128, 8, 128], BF16)]
        ln_scratch('b0')
        ln_scratch('b1')
        tmt3 = [AR.alloc([64, 2, 128], F32) for _ in range(3)]
        Qp2 = [AR.alloc([128, 4, 128], BF16) for _ in range(2)]
        KpB = [AR.alloc([128, 4, 128], BF16) for _ in range(2)]
        VrB = [AR.alloc([128, 512], BF16) for _ in range(2)]
        gate2 = [AR.alloc([128, 512], BF16) for _ in range(2)]
        KwfB = [AR.alloc([128, 4, 128], BF16) for _ in range(2)]
        sqB = [AR.alloc([128, 3, 128], BF16) for _ in range(2)]
        cqB = [AR.alloc([128, 3, 128], BF16) for _ in range(2)]
        dkfB2 = dkk[:, 0:4].unsqueeze(2).broadcast_to([128, 4, 128])

        def loadB(c):
            b = c % 2
            dma('sp', xs[b], xj[c * 128:(c + 1) * 128, :], r=[], w=[('xs', b)])
            dma('sp', tbt[b], tbj[c * 128:(c + 1) * 128, :], r=[], w=[('tbt', b)])
            dma('sp', tmt3[c % 3], tmj[:, :, c * 128:(c + 1) * 128], r=[], w=[('tmt3', c % 3)])

        def lnB_(c):
            b = c % 2
            ln_apply(xs[b], xnB[b], 'b%d' % b, [('xs', b)], [('xnB', b)])

        def trB(c):
            b = c % 2
            for k in range(8):
                tr(pT[:, k, :], xnB[b][:, k * 128:(k + 1) * 128], [('xnB', b)], [('pb', 0)])
            cp('act', xTB[b], pT, [('pb', 0)], [('xTB', b)])

        def projB(c):
            b = c % 2
            xTb = xTB[b]
            xtok = ('xTB', b)
            for (pb_, c0) in ((1, 0), (2, 512), (3, 1024), (4, 1536)):
                for k in range(8):
                    mm(PB[pb_][:, :], xTb[:, k, :], Win[:, k, c0:c0 + 512], k == 0, False, [xtok, 'Win'], [('pb', pb_)])
                mm(PB[pb_][:, :], ones_bf[0:1, :], brow[0:1, c0:c0 + 512], False, True, ['ones_bf', 'brow'], [('pb', pb_)])
            for cj in range(3):
                for k in range(8):
                    mm(PB[5][:, cj * 128:(cj + 1) * 128], Win[:, k, 2048 + cj * 128:2048 + (cj + 1) * 128], xTb[:, k, :],
                       k == 0, k == 7, ['Win', xtok], [('pb', 5)])

        def evacB(c):
            b = c % 2
            rotary_tok(PB[1], ('pb', 1), b, Qp2[b], ('Qp2', b))
            rotary_tok(PB[2], ('pb', 2), b, KpB[b], ('KpB', b))
            cp('act', VrB[b], PB[3][:, :], [('pb', 3)], [('VrB', b)])
            for cj in range(3):
                act(sqB[b][:, cj, :], PB[5][:, cj * 128:(cj + 1) * 128], AF.Square, [('pb', 5), 'bcfm'], [('sqB', b)],
                    bias=bcfm[:, cj:cj + 1])
                act(cqB[b][:, cj, :], PB[5][:, cj * 128:(cj + 1) * 128], AF.Identity, [('pb', 5), 'bcfm'], [('cqB', b)],
                    bias=bcfm[:, cj:cj + 1])
            act(gate2[b], PB[4][:, :], AF.Silu, [('pb', 4)], [('gate2', b)])
            tt('pool', gate2[b], gate2[b], gnB, ALU.mult, [('gate2', b), 'gnB'], [('gate2', b)])
            tt('pool', KwfB[b], KpB[b], dkfB2, ALU.mult, [('KpB', b), 'dkk'], [('KwfB', b)])

        def secondB(c):
            b = c % 2
            q4 = c % 4
            tm_ = tmt3[c % 3]
            tmtok = ('tmt3', c % 3)
            pQK = PB[0][:, :].bitcast(BF16).rearrange("p (a b) -> p a b", a=8)[:, :, 0:128]
            for h in range(4):
                tr(pQK[:, h, :], Qp2[b][:, h, :], [('Qp2', b)], [('pb', 0)])
                tr(pQK[:, 4 + h, :], KpB[b][:, h, :], [('KpB', b)], [('pb', 0)])
            cp('act', QT3[:, 0], pQK[:, 0:4, :], [('pb', 0)], ['QT0'])
            cp('act', KT, pQK[:, 4:8, :], [('pb', 0)], ['KT'])
            tt('dve', QT3[:, 1], pQK[:, 0:4, :], dqfB, ALU.mult, [('pb', 0), 'dqfB'], ['QT1'])
            tt('dve', QT3[:, 2], pQK[:, 0:4, :], dqbB, ALU.mult, [('pb', 0), 'dqbB'], ['QT2'])
            for cj in range(3):
                mm(PB[5][:, 128:256], ones_bf, sqB[b][:, cj, :], cj == 0, cj == 2, ['ones_bf', ('sqB', b)], [('pb', 5)])
            Ap = PB[6][:, :].rearrange("p (h d) -> p h d", h=4)
            for h in range(4):
                mm(Ap[:, h, :], KT[:, h, :], QT3[:, 0, h, :], True, True, ['KT', 'QT0'], [('pb', 6)])
            cp('dve', rB, PB[5][:, 128:256], [('pb', 5)], ['rB'])
            rsqrt_small(rB, 'rB', 1.0 / 384.0, 1e-6)
            tt('dve', AcT, Ap, DcT, ALU.mult, [('pb', 6), 'DcT'], ['AcT'])
            cp('act', Sbf, Sf, ['Sf'], ['Sbf'])
            qnp = PB[5][:, :].rearrange("p (h d) -> p h d", h=4)
            for h in range(4):
                for cj in range(3):
                    mm(qnp[:, h, :], Wuq[:, cj, h, 0:128], cqB[b][:, cj, :], cj == 0, cj == 2, ['Wuq', ('cqB', b)], [('pb', 5)])
            qrp = PB[1][0:64, :].rearrange("p (h d) -> p h d", h=4)
            qrrp = PB[2][0:64, :].rearrange("p (h d) -> p h d", h=4)
            for h in range(4):
                for cj in range(3):
                    mm(qrp[:, h, :], Wuq[:, cj, h, 128:192], cqB[b][:, cj, :], cj == 0, cj == 2, ['Wuq', ('cqB', b)], [('pb', 1)])
                for cj in range(3):
                    mm(qrrp[:, h, :], Wuqr[:, cj, h, :], cqB[b][:, cj, :], cj == 0, cj == 2, ['Wuqr', ('cqB', b)], [('pb', 2)])
            Op = PB[7][:, :].rearrange("p (h d) -> p h d", h=4)
            for h in range(4):
                mm(Op[:, h, :], AcT[:, h, :], VrB[b][:, h * 128:(h + 1) * 128], True, False, ['AcT', ('VrB', b)], [('pb', 7)])
                mm(Op[:, h, :], QT3[:, 1, h, :], Sbf[:, h, :], False, False, ['QT1', 'Sbf'], [('pb', 7)])
                mm(Op[:, h, :], QT3[:, 2, h, :], Tsave[:, c, h, :], False, True, ['QT2', 'Tsave'], [('pb', 7)])
            Pf = PB[6][:, :].rearrange("p (h d) -> p h d", h=4)
            for h in range(4):
                mm(Pf[:, h, :], KwfB[b][:, h, :], VrB[b][:, h * 128:(h + 1) * 128], True, True, [('KwfB', b), ('VrB', b)], [('pb', 6)])
            tt('dve', QNb[:, :, q4 * 128:(q4 + 1) * 128], qnp, rB.unsqueeze(1).broadcast_to([128, 4, 128]), ALU.mult,
               [('pb', 5), 'rB'], ['QNb'])
            tt('dve', qr1, qrp, tm_[:, 0, :].unsqueeze(1).broadcast_to([64, 4, 128]), ALU.mult, [('pb', 1), tmtok], ['qr1'])
            tt('dve', qr2, qrrp, tm_[:, 1, :].unsqueeze(1).broadcast_to([64, 4, 128]), ALU.mult, [('pb', 2), tmtok], ['qr2'])
            tt('pool', qr1, qr1, qr2, ALU.add, ['qr1', 'qr2'], ['qr1'])
            tt('pool', QRb[:, :, q4 * 128:(q4 + 1) * 128], qr1, rB[0:64, :].unsqueeze(1).broadcast_to([64, 4, 128]),
               ALU.mult, ['qr1', 'rB'], ['QRb'])
            for h in range(4):
                stt('dve', Sf[:, h, :], Sf[:, h, :], dcc[:, h:h + 1], Pf[:, h, :], ALU.mult, ALU.add,
                    [('pb', 6), 'dcc', 'Sf', 'Sbf'], ['Sf'])
            for h in range(4):
                bnstats(st4[:, h, :], Op[:, h, :], [('pb', 7)], ['st4'])
            for h in range(4):
                bnaggr(mv4[:, h, :], st4[:, h, :], ['st4'], ['mv4'])
            cp('dve', rs4, mv4[:, :, 1], ['mv4'], ['rs4'])
            rsqrt_small(rs4, 'rs4', 1.0, 1e-5)
            tt('dve', y1, Op, mv4[:, :, 0:1].broadcast_to([128, 4, 128]), ALU.subtract, [('pb', 7), 'mv4'], ['y1'])
            tt('pool', y1, y1, rs4.unsqueeze(2).broadcast_to([128, 4, 128]), ALU.mult, ['y1', 'rs4'], ['y1'])
            tt('pool', retb, y1.rearrange("p h d -> p (h d)"), gate2[b], ALU.mult, ['y1', ('gate2', b)], ['retb'])
            pR = PB[0][:, :].bitcast(BF16).rearrange("p (a b) -> p a b", a=8)[:, :, 0:128]
            for h in range(4):
                tr(pR[:, h, :], retb[:, h * 128:(h + 1) * 128], ['retb'], [('pb', 0)])
            cp('act', RTb[:, :, q4 * 128:(q4 + 1) * 128], pR[:, 0:4, :], [('pb', 0)], ['RTb'])
            if q4 == 3:
                t0 = c - 3
                dma('sp', RTd[j][:, :, t0 * 128:(t0 + 4) * 128].rearrange("h p n -> p h n"), RTb, r=['RTb'], w=[('RTd', j)])
                dma('sp', QTd[j][:, :, t0 * 128:(t0 + 4) * 128].rearrange("h p n -> p h n"), QNb, r=['QNb'], w=[('QTd', j)])
                dma('sp', QRd[j][:, :, t0 * 128:(t0 + 4) * 128].rearrange("h p n -> p h n"), QRb, r=['QRb'], w=[('QRd', j)])

        loadB(0)
        loadB(1)
        lnB_(0)
        trB(0)
        projB(0)
        lnB_(1)
        evacB(0)
        trB(1)
        for c in range(NCH):
            if c + 1 < NCH:
                projB(c + 1)
                if c + 2 < NCH:
                    loadB(c + 2)
                    lnB_(c + 2)
                evacB(c + 1)
                if c + 2 < NCH:
                    trB(c + 2)
            secondB(c)

        P.barrier()
        AR.release(job_mark)
        KTh = AR.alloc([128, S], BF16)
        KRh = AR.alloc([128, S], BF16)
        Vh = AR.alloc([128, NT, 128], BF16)
        QTh = [AR.alloc([128, L], BF16) for _ in range(2)]
        QRh = [AR.alloc([128, L], BF16) for _ in range(2)]
        memset('pool', KRh, 0.0, ['KRh'])
        memset('pool', QRh[0], 0.0, [('QRh', 0)])
        memset('pool', QRh[1], 0.0, [('QRh', 1)])
        NPT = 4
        PT = [AR.alloc([128, 512], BF16) for _ in range(NPT)]
        dacc = [AR.alloc([128, 512], F32) for _ in range(6)]
        rden = AR.alloc([128, 512], F32)
        ato = AR.alloc([128, 512], BF16)
        dma('sp', KRh[0:64, :], KRd[j], r=[('KRd', j)], w=['KRh'])
        NQB = L // 512
        NH2 = NT // 2

        def load_head(h, hf):
            hp = h % 2
            dma('sp', KTh[:, hf * NH2 * 128:(hf + 1) * NH2 * 128], KTd[j][h][:, hf * NH2 * 128:(hf + 1) * NH2 * 128],
                r=[('KTd', j)], w=[('KTh', hf)])
            dma('sp', Vh[:, hf * NH2:(hf + 1) * NH2, :], VVd[j][h][:, hf * NH2:(hf + 1) * NH2, :],
                r=[('VVd', j)], w=[('Vh', hf)])
            if hf == 0:
                dma('sp', QTh[hp], QTd[j][h], r=[('QTd', j)], w=[('QTh', hp)])
                dma('sp', QRh[hp][0:64, :], QRd[j][h], r=[('QRd', j)], w=[('QRh', hp)])

        items = [(h, qb, kt) for h in range(4) for qb in range(NQB) for kt in range(NT)]

        def qk(item, gi_):
            h, qb, kt = item
            hp = h % 2
            if qb == 0 and kt == 0:
                load_head(h, 0)
                if h == 0:
                    load_head(h, 1)
            qs = slice(qb * 512, (qb + 1) * 512)
            sb = gi_ % NPT
            hf = kt // NH2
            mm(PB[sb][:, :], KTh[:, kt * 128:(kt + 1) * 128], QTh[hp][:, qs], True, False,
               [('KTh', hf), ('QTh', hp)], [('pb', sb)])
            mm(PB[sb][:, :], KRh[:, kt * 128:(kt + 1) * 128], QRh[hp][:, qs], False, True,
               ['KRh', ('QRh', hp)], [('pb', sb)])
            act(PT[sb], PB[sb][:, :], AF.Exp, [('pb', sb)], [('PT', sb)])

        def pv(item, gi_):
            h, qb, kt = item
            qs = slice(qb * 512, (qb + 1) * 512)
            blk = h * NQB + qb
            ob = 4 + (blk % 2)
            db = 6 + (blk % 2)
            sb = gi_ % NPT
            hf = kt // NH2
            mm(PB[ob][:, :], Vh[:, kt, :], PT[sb], kt == 0, kt == NT - 1, [('Vh', hf), ('PT', sb)], [('pb', ob)])
            if kt % 4 == 3:
                ai, eng_ = 2, 'pool'
            else:
                ai, eng_ = (kt - kt // 4) % 2, 'dve'
            da = dacc[(blk % 2) * 3 + ai]
            dtok = ('dacc', blk % 2, ai)
            if kt in (0, 1, 3):
                cp(eng_, da, PT[sb], [('PT', sb)], [dtok])
            else:
                tt(eng_, da, da, PT[sb], ALU.add, [dtok, ('PT', sb)], [dtok], nosync=(eng_ == 'dve'))
            if kt == NT - 1:
                for ai2 in range(3):
                    mm(PB[db][:, :], ones_f, dacc[(blk % 2) * 3 + ai2], ai2 == 0, ai2 == 2,
                       ['ones_f', ('dacc', blk % 2, ai2)], [('pb', db)])
                recip(rden, PB[db][:, :], [('pb', db)], ['rden'])
                tt('dve', ato, PB[ob][:, :], rden, ALU.mult, [('pb', ob), 'rden'], ['ato'])
                dma('sp', ATd[j][h][:, qs], ato, r=['ato'], w=[('ATd', j)])
                if qb == NQB - 1 and h < 3:
                    load_head(h + 1, 1)

        LA = 2
        for i_ in range(min(LA, len(items))):
            qk(items[i_], i_)
        for i_ in range(len(items)):
            if i_ + LA < len(items):
                qk(items[i_ + LA], i_ + LA)
            pv(items[i_], i_)

        P.barrier()
        AR.release(job_mark)
        lnB = AR.alloc([128, 2, 1024], F32)
        dma('sp', lnB, ln_in[0:2, :].partition_broadcast(128), r=[], w=['lnB'])
        Wo = AR.alloc([128, 8, 1024], BF16)
        d_mark = AR.mark()
        wst = [AR.alloc([128, 8, 512], F32) for _ in range(2)]
        gi = 0
        for g in range(2):
            b = gi % 2
            gi += 1
            dma('sp', wst[b], wo_in[:, g * 512:(g + 1) * 512].rearrange("(k p) n -> p k n", p=128), r=[], w=[('wst', b)])
            for k in range(8):
                tt('dve', Wo[:, k, g * 512:(g + 1) * 512], wst[b][:, k, :], gB[:, 0, g * 512:(g + 1) * 512], ALU.mult,
                   [('wst', b), 'gB'], ['Wo'])
        cat = AR.alloc([128, 8, 512], BF16)
        xo = [AR.alloc([128, D], F32) for _ in range(2)]
        yv2 = [AR.alloc([128, D], F32) for _ in range(2)]
        x1 = [AR.alloc([128, D], F32) for _ in range(2)]
        ln_scratch('d1')
        for m in range(L // 512):
            ms = slice(m * 512, (m + 1) * 512)
            dma('sp', cat[:, 0:4, :], RTd[j][:, :, ms].rearrange("h p n -> p h n"), r=[('RTd', j)], w=[('cat', 0)])
            dma('sp', cat[:, 4:8, :], ATd[j][:, :, ms].rearrange("h p n -> p h n"), r=[('ATd', j)], w=[('cat', 1)])
            for u in range(4):
                tix = m * 4 + u
                b = tix % 2
                dma('sp', xo[b], xj[tix * 128:(tix + 1) * 128, :], r=[], w=[('xo', b)])
                for half in range(2):
                    for k in range(8):
                        mm(PB[half][:, :], cat[:, k, u * 128:(u + 1) * 128], Wo[:, k, half * 512:(half + 1) * 512],
                           k == 0, k == 7, [('cat', 0), ('cat', 1), 'Wo'], [('pb', half)])
                    stt('dve', yv2[b][:, half * 512:(half + 1) * 512], xo[b][:, half * 512:(half + 1) * 512], ALPHA,
                        PB[half][:, :], ALU.mult, ALU.add, [('xo', b), ('pb', half)], [('yv', b)])
                ln_apply(yv2[b], yv2[b], 'd1', [('yv', b)], [('yv', b)])
                tt('pool', yv2[b], yv2[b], lnB[:, 0, :], ALU.mult, [('yv', b), 'lnB'], [('yv', b)])
                tt('pool', x1[b], yv2[b], lnB[:, 1, :], ALU.add, [('yv', b), 'lnB'], [('x1', b)])
                dma('sp', X1d[j][tix * 128:(tix + 1) * 128, :], x1[b], r=[('x1', b)], w=[('X1d', j)])

        P.barrier()
        AR.release(job_mark)
        lnB = AR.alloc([128, 2, 1024], F32)
        dma('sp', lnB, ln_in[2:4, :].partition_broadcast(128), r=[], w=['lnB'])
        Wup = AR.alloc([128, 8, DFF], BF16)
        Wdn = AR.alloc([128, 32, D], BF16)
        bup = AR.alloc([128, 32], F32)
        e_mark = AR.mark()
        wst = [AR.alloc([128, 8, 512], F32) for _ in range(2)]
        gi = 0
        for g in range(8):
            b = gi % 2
            gi += 1
            dma('sp', wst[b], wup_in[:, g * 512:(g + 1) * 512].rearrange("(k p) n -> p k n", p=128), r=[], w=[('wst', b)])
            for c in range(4):
                for k in range(8):
                    mm(PB[1][:, 0:1], wst[b][:, k, c * 128:(c + 1) * 128], modfm[:, 24 + k, j:j + 1], k == 0, k == 7,
                       [('wst', b), 'modfm'], [('pb', 1)])
                cp('dve', bup[:, g * 4 + c:g * 4 + c + 1], PB[1][:, 0:1], [('pb', 1)], ['bup'])
            for k in range(8):
                ts('dve', Wup[:, k, g * 512:(g + 1) * 512], wst[b][:, k, :], scp[:, 1, k, j:j + 1], None, ALU.mult, None,
                   [('wst', b), 'scp'], ['Wup'])
        for g in range(8):
            b = gi % 2
            gi += 1
            fg, half = g // 2, g % 2
            dma('sp', wst[b], wdn_in[fg * 1024:(fg + 1) * 1024, half * 512:(half + 1) * 512].rearrange("(k p) n -> p k n", p=128),
                r=[], w=[('wst', b)])
            for k in range(8):
                tt('dve', Wdn[:, fg * 8 + k, half * 512:(half + 1) * 512], wst[b][:, k, :], gB[:, 1, half * 512:(half + 1) * 512],
                   ALU.mult, [('wst', b), 'gB'], ['Wdn'])
        P.barrier()
        AR.release(e_mark)
        TBK = 256
        NU = TBK // 128
        NB = L // TBK
        x1t = [[AR.alloc([128, D], F32) for _ in range(NU)] for _ in range(2)]
        xn2 = AR.alloc([128, D], BF16)
        h2T = [AR.alloc([128, 8, TBK], BF16) for _ in range(2)]
        u2 = [AR.alloc([128, 8, TBK], BF16) for _ in range(2)]
        rt = [AR.alloc([128, TBK], F32) for _ in range(2)]
        oo = [gB[:, 0, :], gB[:, 1, :]]
        ln_scratch('d2a')
        ln_scratch('d2b')
        pT2 = PB[6][:, :].bitcast(BF16).rearrange("p (a b) -> p a b", a=8)[:, :, 0:128]

        def front2(m):
            mp = m % 2
            for u in range(NU):
                tix = m * NU + u
                dma('sp', x1t[mp][u], X1d[j][tix * 128:(tix + 1) * 128, :], r=[('X1d', j)], w=[('x1t', mp, u)])
                ln_apply(x1t[mp][u], xn2, 'd2a', [('x1t', mp, u)], ['xn2'])
                for k in range(8):
                    tr(pT2[:, k, :], xn2[:, k * 128:(k + 1) * 128], ['xn2'], [('pb', 6)])
                cp('act', h2T[mp][:, :, u * 128:(u + 1) * 128], pT2, [('pb', 6)], [('h2T', mp)])

        def up(m, g):
            mp = m % 2
            ub = (m * 4 + g) % 2
            for fl in range(8):
                f = g * 8 + fl
                pb = 4 + (f % 2)
                for k in range(8):
                    mm(PB[pb][:, 0:TBK], Wup[:, k, f * 128:(f + 1) * 128], h2T[mp][:, k, :], k == 0, k == 7,
                       ['Wup', ('h2T', mp)], [('pb', pb)])
                rb = f % 2
                act(rt[rb], PB[pb][:, 0:TBK], AF.Relu, [('pb', pb), 'bup'], [('rt', rb)], bias=bup[:, f:f + 1])
                tt('pool' if f % 2 else 'dve', u2[ub][:, fl, :], rt[rb], rt[rb], ALU.mult, [('rt', rb)], [('u2', ub)])

        def down(m, g):
            ub = (m * 4 + g) % 2
            for u in range(NU):
                for half in range(2):
                    pbd = u * 2 + half
                    for fl in range(8):
                        f = g * 8 + fl
                        mm(PB[pbd][:, :], u2[ub][:, fl, u * 128:(u + 1) * 128], Wdn[:, f, half * 512:(half + 1) * 512],
                           f == 0, f == 31, [('u2', ub), 'Wdn'], [('pb', pbd)])

        def tail(m):
            mp = m % 2
            for u in range(NU):
                tix = m * NU + u
                ob = tix % 2
                for half in range(2):
                    pbd = u * 2 + half
                    stt('dve', oo[ob][:, half * 512:(half + 1) * 512], x1t[mp][u][:, half * 512:(half + 1) * 512], ALPHA,
                        PB[pbd][:, :], ALU.mult, ALU.add, [('x1t', mp, u), ('pb', pbd)], [('oo', ob)])
                ln_apply(oo[ob], oo[ob], 'd2b', [('oo', ob)], [('oo', ob)])
                tt('pool', oo[ob], oo[ob], lnB[:, 0, :], ALU.mult, [('oo', ob), 'lnB'], [('oo', ob)])
                tt('pool', oo[ob], oo[ob], lnB[:, 1, :], ALU.add, [('oo', ob), 'lnB'], [('oo', ob)])
                dma('sp', y_out[j][tix * 128:(tix + 1) * 128, :], oo[ob], r=[('oo', ob)], w=[('yout', j, ob)], outp=True)

        G = [(m, g) for m in range(NB) for g in range(4)]
        front2(0)
        up(0, 0)
        for i_, (m, g) in enumerate(G):
            if i_ + 1 < len(G):
                m2, g2 = G[i_ + 1]
                if g2 == 0:
                    front2(m2)
                up(m2, g2)
            down(m, g)
            if g == 3:
                tail(m)

    P.emit()
    return nc, P


def _rope_tables(pos, d):
    inv = (10000.0 ** (-np.arange(0, d, 2, dtype=np.float32) / np.float32(d))).astype(np.float32)
    ang = pos.astype(np.float32)[:, None] * inv[None, :]
    return np.cos(ang).astype(np.float32), np.sin(ang).astype(np.float32)


def host_inputs(L, NSLOT, core_jobs, xs, cs, w):
    NJ = len(NSLOT)
    NCH = L // 128
    p = np.arange(128, dtype=np.float32)
    i = np.arange(128, dtype=np.float32)
    cst = np.zeros((128, 4 * 128 + 8), np.float32)
    cst[:, 0:128] = np.maximum(i[None, :] - p[:, None], 0)
    cst[:, 128:256] = np.maximum(p[:, None] - i[None, :], 0)
    cst[:, 256:384] = (p[:, None] <= i[None, :])
    cst[:, 384:512] = (p[:, None] > i[None, :])
    cst[:, 512] = 127 - p
    cst[:, 513] = p
    cst[:, 514] = 128.0
    shared = dict(
        cst=cst,
        w_ada=np.ascontiguousarray(w['w_ada'][0]), b_ada_fm=np.ascontiguousarray(w['b_ada'][0].reshape(48, 128).T),
        b_ada_row=np.ascontiguousarray(w['b_ada'][0].reshape(1, -1)),
        w_in=np.ascontiguousarray(w['w_in'][0]), w_uq=np.ascontiguousarray(w['w_uq'][0]),
        w_ukv=np.ascontiguousarray(w['w_ukv'][0]), w_o=np.ascontiguousarray(w['w_o'][0]),
        w_up=np.ascontiguousarray(w['w_up'][0]), w_down=np.ascontiguousarray(w['w_down'][0]),
        dec=np.concatenate([w['ret_decay_f'][0], w['ret_decay_b'][0]]).reshape(1, 8).astype(np.float32),
        gnw=np.ascontiguousarray(w['ret_gn_w'][0].reshape(1, 512)),
        qnw_fm=np.ascontiguousarray(w['q_norm_w'][0].reshape(3, 128).T),
        kvnw_fm=np.ascontiguousarray(w['kv_norm_w'][0].reshape(2, 128).T),
        ln=np.ascontiguousarray(np.stack([w['ln1_w'][0], w['ln1_b'][0], w['ln2_w'][0], w['ln2_b'][0]])),
    )
    in_maps = []
    for core, jobs in enumerate(core_jobs):
        m = dict(shared)
        cT = np.zeros((128, 8, NJ), np.float32)
        mk = np.zeros((128, NJ, 2, 128), np.float32)
        ex = np.zeros((128, NJ, 2, 128), np.float32)
        for j, (bi, own) in enumerate(jobs):
            NS = NSLOT[j]
            S = NS * L
            pos = (np.arange(S) + own * L) % S
            m["x%d" % j] = np.ascontiguousarray(np.roll(xs[j][bi], -own * L, axis=0))
            cR, sR = _rope_tables(pos, 128)
            m["tb%d" % j] = np.ascontiguousarray(np.concatenate([cR, cR, sR, sR], axis=1))
            cM, sM = _rope_tables(pos, 64)
            tm = np.stack([np.concatenate([cM, cM], 1).T, np.concatenate([sM, sM], 1).T], axis=1)
            m["tm%d" % j] = np.ascontiguousarray(tm.astype(np.float32))
            cT[:, :, j] = cs[j][bi].reshape(8, 128).T
            for s in range(1, NS):
                before = (own + s) >= NS
                for n in range(NCH):
                    t = s * NCH + n
                    ex[:, j, 0, t] = (NS - s - 1) * L + (L - 128 - 128 * n)
                    ex[:, j, 1, t] = (s - 1) * L + 128 * n
                    mk[:, j, 0, t] = 1.0 if before else 0.0
                    mk[:, j, 1, t] = 0.0 if before else 1.0
        m["cT"] = cT
        m["mk"] = mk
        m["ex"] = ex
        in_maps.append(m)
    return in_maps


_CACHE = {}


def kernel(x_prompt, x_sample, c_prompt, c_sample, **w):
    L = 4096
    NSLOT = [4, 2]
    x_prompt = np.asarray(x_prompt, np.float32)
    x_sample = np.asarray(x_sample, np.float32)
    c_prompt = np.asarray(c_prompt, np.float32)
    c_sample = np.asarray(c_sample, np.float32)
    w = {k: np.asarray(v, np.float32) for k, v in w.items()}
    core_jobs = [[(c // 4, c % 4), (c // 2, c % 2)] for c in range(NCORES)]
    in_maps = host_inputs(L, NSLOT, core_jobs, [x_prompt, x_sample], [c_prompt, c_sample], w)
    if 'nc' not in _CACHE:
        _CACHE['nc'] = build(L, NSLOT)[0]
    res = run_bass_kernel_spmd(_CACHE['nc'], in_maps, core_ids=list(range(NCORES)))
    yp = np.empty_like(x_prompt)
    ysm = np.empty_like(x_sample)
    for c in range(NCORES):
        r = res.results[c]
        yp[c // 4, (c % 4) * L:(c % 4 + 1) * L] = r["y0"]
        ysm[c // 2, (c % 2) * L:(c % 2 + 1) * L] = r["y1"]
    return (yp, ysm)
```

```python
import math
import numpy as np
import concourse.bass as bass
import concourse.mybir as mybir
from concourse.bass_utils import run_bass_kernel_spmd

F32 = mybir.dt.float32
BF16 = mybir.dt.bfloat16
AF = mybir.ActivationFunctionType
ALU = mybir.AluOpType

D = 1024
DIN = 2752
DFF = 4096
ALPHA = float(2.0 ** 0.25)
MLA_SCALE = float(192 ** -0.5)
NCORES = 8


class Prog:
    SEM_LIMIT = 24000

    def __init__(self, nc):
        self.nc = nc
        self.ops = []
        self.last_w = {}
        self.readers = {}
        self.out_dma_ops = []
        self.barriers = []

    def add(self, eng, fn, r=(), w=(), dma=False, key=None, out=False, nosync=False):
        i = len(self.ops)
        if dma and key is None:
            key = w[0]
        ps = [t for t in r if isinstance(t, tuple) and t[0] == 'pb']
        r = [t for t in r if not (isinstance(t, tuple) and t[0] == 'pb')]
        w = list(w) + [t for t in ps if t not in w]
        deps = set()
        for t in r:
            j = self.last_w.get(t)
            if j is not None:
                deps.add((j, 'raw'))
        for t in w:
            j = self.last_w.get(t)
            if j is not None:
                deps.add((j, 'waw'))
            for k in self.readers.get(t, ()):
                deps.add((k, 'war'))
        for t in r:
            self.readers.setdefault(t, []).append(i)
        for t in w:
            self.last_w[t] = i
            self.readers[t] = []
        self.ops.append(dict(eng=eng, fn=fn, deps=deps, dma=dma, key=key, bar=len(self.barriers), nosync=nosync))
        if out:
            self.out_dma_ops.append(i)
        return i

    def barrier(self):
        self.barriers.append(len(self.ops))

    def emit(self):
        nc = self.nc
        ops = self.ops
        bar_sets = []
        for b in self.barriers:
            last = {}
            for i in range(b):
                op = ops[i]
                k = ('dma', op['key']) if op['dma'] else ('eng', op['eng'])
                last[k] = i
            bar_sets.append(set(last.values()))
        for i, op in enumerate(ops):
            need = set()
            for (j, kind) in op['deps']:
                pj = ops[j]
                if j == i:
                    continue
                if not pj['dma'] and not op['dma'] and pj['eng'] == op['eng']:
                    if op['eng'] == 'pe':
                        continue
                    if kind != 'raw' or op['nosync']:
                        continue
                need.add(j)
            if op['bar'] > 0:
                for j in bar_sets[op['bar'] - 1]:
                    pj = ops[j]
                    if not pj['dma'] and pj['eng'] == op['eng']:
                        continue
                    need.add(j)
            op['need'] = need
        has_dep = [False] * len(ops)
        for op in ops:
            for j in op['need']:
                has_dep[j] = True
        for j in self.out_dma_ops:
            has_dep[j] = True
        cnt = {}
        for i, op in enumerate(ops):
            if op['dma']:
                k = ('dma', op['key'])
            elif has_dep[i]:
                k = ('eng', op['eng'])
            else:
                op['sig'] = None
                continue
            cnt[k] = cnt.get(k, 0) + 1
            op['sig'] = (k, cnt[k])
        sems = {}
        cms = []
        for k, n in cnt.items():
            inc = 16 if k[0] == 'dma' else 1
            per = self.SEM_LIMIT // inc
            ns = (n + per - 1) // per
            lst = []
            for s in range(ns):
                cm = nc.semaphore("s%d_%d" % (len(sems), s))
                lst.append(cm.__enter__())
                cms.append(cm)
            sems[k] = (lst, inc, per)
        self.n_sems = len(cms)

        def semval(sig):
            k, n = sig
            lst, inc, per = sems[k]
            return lst[(n - 1) // per], ((n - 1) % per + 1) * inc

        engs = {}
        for i, op in enumerate(ops):
            engs.setdefault(op['eng'], []).append(i)

        def run(ename, e):
            waited = {}
            for i in engs.get(ename, []):
                op = ops[i]
                tgt = {}
                for j in op['need']:
                    k, n = ops[j]['sig']
                    if waited.get(k, 0) >= n:
                        continue
                    if tgt.get(k, 0) < n:
                        tgt[k] = n
                for k, n in tgt.items():
                    s, v = semval((k, n))
                    e.wait_ge(s, v)
                    waited[k] = n
                ins = op['fn'](e)
                if op['sig'] is not None:
                    s, v = semval(op['sig'])
                    ins.then_inc(s, 16 if op['dma'] else 1)
            if ename == 'sp':
                fin = {}
                for j in self.out_dma_ops:
                    k, n = ops[j]['sig']
                    fin[k] = max(fin.get(k, 0), n)
                for k, n in fin.items():
                    if waited.get(k, 0) >= n:
                        continue
                    s, v = semval((k, n))
                    e.wait_ge(s, v)

        with nc.Block() as block:
            @block.tensor
            def _(e):
                run('pe', e)

            @block.scalar
            def _(e):
                run('act', e)

            @block.vector
            def _(e):
                run('dve', e)

            @block.gpsimd
            def _(e):
                run('pool', e)

            @block.sync
            def _(e):
                run('sp', e)
        for cm in cms:
            cm.__exit__(None, None, None)


class Arena:
    def __init__(self, nc, words):
        self.t = nc.alloc_sbuf_tensor("arena", [128, words], F32)
        self.words = words
        self.top = 0

    def mark(self):
        return self.top

    def release(self, m):
        self.top = m

    def alloc(self, shape, dtype=F32):
        n = 1
        for s in shape[1:]:
            n *= s
        nw = n if dtype == F32 else (n + 1) // 2
        nw = (nw + 7) // 8 * 8
        assert self.top + nw <= self.words, ("SBUF arena overflow", self.top, nw, self.words)
        v = self.t[:, self.top:self.top + nw]
        self.top += nw
        if dtype != F32:
            v = v.bitcast(dtype)
        v = v[:, 0:n]
        if len(shape) == 3:
            v = v.rearrange("p (a b) -> p a b", a=shape[1])
        elif len(shape) == 4:
            v = v.rearrange("p (a b c) -> p a b c", a=shape[1], b=shape[2])
        elif len(shape) == 5:
            v = v.rearrange("p (a b c d) -> p a b c d", a=shape[1], b=shape[2], c=shape[3])
        if shape[0] < 128:
            v = v[0:shape[0]]
        return v


def build(L, NSLOT, debug=False):
    NJ = len(NSLOT)
    NCH = L // 128
    SS = [ns * L for ns in NSLOT]
    nc = bass.Bass("TRN2", target_bir_lowering=False)
    P = Prog(nc)

    def din(name, shape, dt=F32):
        return nc.dram_tensor(name, list(shape), dt, kind="ExternalInput").ap()

    def dscr(name, shape, dt=BF16):
        return nc.dram_tensor(name, list(shape), dt, kind=("ExternalOutput" if debug else "Internal")).ap()

    x_in = [din("x%d" % j, [SS[j], D]) for j in range(NJ)]
    tb_in = [din("tb%d" % j, [SS[j], 256]) for j in range(NJ)]
    tm_in = [din("tm%d" % j, [64, 2, SS[j]]) for j in range(NJ)]
    cT_in = din("cT", [128, 8, NJ])
    mk_in = din("mk", [128, NJ, 2, 128])
    ex_in = din("ex", [128, NJ, 2, 128])
    cst_in = din("cst", [128, 4 * 128 + 8])
    wada_in = din("w_ada", [D, 6 * D])
    bfm_in = din("b_ada_fm", [128, 48])
    brow_in = din("b_ada_row", [1, 6 * D])
    win_in = din("w_in", [D, DIN])
    wuq_in = din("w_uq", [384, 768])
    wukv_in = din("w_ukv", [256, 1024])
    wo_in = din("w_o", [D, D])
    wup_in = din("w_up", [D, DFF])
    wdn_in = din("w_down", [DFF, D])
    dec_in = din("dec", [1, 8])
    gnw_in = din("gnw", [1, 512])
    qnw_in = din("qnw_fm", [128, 3])
    kvnw_in = din("kvnw_fm", [128, 2])
    ln_in = din("ln", [4, D])
    y_out = [nc.dram_tensor("y%d" % j, [L, D], F32, kind="ExternalOutput").ap() for j in range(NJ)]

    KTd = [dscr("KTd%d" % j, [4, 128, SS[j]]) for j in range(NJ)]
    KRd = [dscr("KRd%d" % j, [64, SS[j]]) for j in range(NJ)]
    VVd = [dscr("VVd%d" % j, [4, 128, SS[j] // 128, 128]) for j in range(NJ)]
    QTd = [dscr("QTd%d" % j, [4, 128, L]) for j in range(NJ)]
    QRd = [dscr("QRd%d" % j, [4, 64, L]) for j in range(NJ)]
    RTd = [dscr("RTd%d" % j, [4, 128, L]) for j in range(NJ)]
    ATd = [dscr("ATd%d" % j, [4, 128, L]) for j in range(NJ)]
    X1d = [dscr("X1d%d" % j, [L, D], F32) for j in range(NJ)]

    AR = Arena(nc, 49100)
    PB = [nc.alloc_psum_tensor("pb%d" % i, [128, 512], F32) for i in range(8)]

    def mm(out, lhsT, rhs, start, stop, r, w):
        P.add('pe', lambda e: e.matmul(out, lhsT=lhsT, rhs=rhs, start=start, stop=stop), r=r, w=w)

    def act(out, in_, func, r, w, bias=None, scale=None):
        kw = {}
        if bias is not None:
            kw['bias'] = bias
        if scale is not None:
            kw['scale'] = scale
        P.add('act', lambda e: e.activation(out=out, in_=in_, func=func, **kw), r=r, w=w)

    def tt(eng, out, in0, in1, op, r, w, nosync=False):
        P.add(eng, lambda e: e.tensor_tensor(out=out, in0=in0, in1=in1, op=op), r=r, w=w, nosync=nosync)

    def ts(eng, out, in0, s1, s2, op0, op1, r, w):
        if s2 is None:
            P.add(eng, lambda e: e.tensor_scalar(out=out, in0=in0, scalar1=s1, scalar2=None, op0=op0), r=r, w=w)
        else:
            P.add(eng, lambda e: e.tensor_scalar(out=out, in0=in0, scalar1=s1, scalar2=s2, op0=op0, op1=op1), r=r, w=w)

    def stt(eng, out, in0, scalar, in1, op0, op1, r, w):
        P.add(eng, lambda e: e.scalar_tensor_tensor(out=out, in0=in0, scalar=scalar, in1=in1, op0=op0, op1=op1),
              r=r, w=w)

    def cp(eng, out, in_, r, w):
        if eng == 'act':
            P.add('act', lambda e: e.copy(out=out, in_=in_), r=r, w=w)
        else:
            P.add(eng, lambda e: e.tensor_copy(out=out, in_=in_), r=r, w=w)

    def dma(q, out, in_, r, w, key=None, outp=False):
        P.add(q, lambda e: e.dma_start(out=out, in_=in_), r=r, w=w, dma=True, key=key, out=outp)

    def recip(out, in_, r, w):
        P.add('dve', lambda e: e.reciprocal(out=out, in_=in_), r=r, w=w)

    def bnstats(out, in_, r, w):
        P.add('dve', lambda e: e.bn_stats(out=out, in_=in_), r=r, w=w)

    def bnaggr(out, in_, r, w):
        P.add('dve', lambda e: e.bn_aggr(out=out, in_=in_), r=r, w=w)

    def actmul(out, in_, m, r, w):
        P.add('act', lambda e: e.mul(out=out, in_=in_, mul=m), r=r, w=w)

    def memset(eng, ap, v, w):
        P.add(eng, lambda e: e.memset(ap, v), w=w)

    ident_bf = AR.alloc([128, 128], BF16)
    ident_f = AR.alloc([128, 128], F32)
    ones_bf = AR.alloc([128, 128], BF16)
    ones_f = AR.alloc([128, 128], F32)
    cst = AR.alloc([128, 4 * 128 + 8], F32)
    dma('sp', cst, cst_in, r=[], w=['cst'])
    memset('pool', ident_f, 1.0, ['ident_f'])
    P.add('pool', lambda e: e.affine_select(out=ident_f, in_=ident_f, pattern=[[-1, 128]], compare_op=ALU.is_equal,
                                            fill=0.0, base=0, channel_multiplier=1), r=['ident_f'], w=['ident_f'])
    cp('pool', ident_bf, ident_f, ['ident_f'], ['ident_bf'])
    memset('pool', ones_f, 1.0, ['ones_f'])
    memset('pool', ones_bf, 1.0, ['ones_bf'])

    def tr(out, in_, r, w):
        k = in_.shape[0]
        P.add('pe', lambda e: e.transpose(out=out, in_=in_, identity=ident_bf[0:k, 0:k]), r=list(r) + ['ident_bf'], w=w)

    RF = cst[:, 0:128]
    RB = cst[:, 128:256]
    MF = cst[:, 256:384]
    MB = cst[:, 384:512]
    c_127mj = cst[:, 512:513]
    c_j = cst[:, 513:514]
    c_128 = cst[:, 514:515]

    dec = AR.alloc([128, 8], F32)
    dma('sp', dec, dec_in.partition_broadcast(128).rearrange("p a b -> p (a b)"), r=[], w=['dec'])
    lg = AR.alloc([128, 8], F32)
    act(lg, dec, AF.Exp, ['dec'], ['lg'], scale=-1.0)
    ts('dve', lg, lg, 1.0, None, ALU.add, None, ['lg'], ['lg'])
    act(lg, lg, AF.Ln, ['lg'], ['lg'])
    ts('dve', lg, lg, -1.0, None, ALU.mult, None, ['lg'], ['lg'])
    qnw = AR.alloc([128, 3], F32)
    kvnw = AR.alloc([128, 2], F32)
    dma('sp', qnw, qnw_in, r=[], w=['qnw'])
    dma('sp', kvnw, kvnw_in, r=[], w=['kvnw'])
    bfm = AR.alloc([128, 48], F32)
    dma('sp', bfm, bfm_in, r=[], w=['bfm'])
    cT = AR.alloc([128, 8, NJ], F32)
    dma('sp', cT, cT_in, r=[], w=['cT'])
    siluc = AR.alloc([128, 8, NJ], F32)
    act(siluc, cT, AF.Silu, ['cT'], ['siluc'])
    modfm = AR.alloc([128, 48, NJ], F32)
    scp = AR.alloc([128, 2, 8, NJ], F32)

    persist_mark = AR.mark()

    def ln_apply(src, dst, tag, rtok, wtok, eps=1e-5, eng='dve'):
        st, mv, rs = LNS[tag]
        for c in range(2):
            P.add('dve', (lambda e, c=c: e.bn_stats(out=st[:, c, :], in_=src[:, c * 512:(c + 1) * 512])),
                  r=rtok, w=[('st', tag, c)])
        P.add('dve', lambda e: e.bn_aggr(out=mv, in_=st), r=[('st', tag, 0), ('st', tag, 1)], w=[('mv', tag)])
        ts('dve', rs, mv[:, 1:2], eps, None, ALU.add, None, [('mv', tag)], [('rs', tag)])
        P.add('dve', lambda e: e.reciprocal(out=rs, in_=rs), r=[('rs', tag)], w=[('rs', tag)])
        P.add('act', lambda e: e.sqrt(out=rs, in_=rs), r=[('rs', tag)], w=[('rs', tag)])
        ts(eng, dst, src, mv[:, 0:1], rs[:, 0:1], ALU.subtract, ALU.mult, list(rtok) + [('mv', tag), ('rs', tag)], wtok)

    LNS = {}

    def ln_scratch(tag):
        LNS[tag] = (AR.alloc([128, 2, 6], F32), AR.alloc([128, 2], F32), AR.alloc([128, 1], F32))

    def rsqrt_small(x, tag, mul, eps):
        ts('dve', x, x, mul, eps, ALU.mult, ALU.add, [tag], [tag])
        P.add('dve', lambda e: e.reciprocal(out=x, in_=x), r=[tag], w=[tag])
        P.add('act', lambda e: e.sqrt(out=x, in_=x), r=[tag], w=[tag])

    for j in range(NJ):
        S = SS[j]
        NS = NSLOT[j]
        NT = S // 128
        xj, tbj, tmj = x_in[j], tb_in[j], tm_in[j]
        P.barrier()
        AR.release(persist_mark)
        gB = AR.alloc([128, 2, 1024], F32)
        job_mark = AR.mark()
        Win = AR.alloc([128, 8, DIN], BF16)
        Wkrr = AR.alloc([128, 8, 64], BF16)
        brow = AR.alloc([1, DIN], BF16)
        bcfm = AR.alloc([128, 5], F32)
        bkr = AR.alloc([64, 2], F32)
        coef = AR.alloc([128, 2, 4, 128], F32)
        Sf = AR.alloc([128, 4, 128], F32)
        Tb = AR.alloc([128, 4, 128], F32)
        Sbf = AR.alloc([128, 4, 128], BF16)
        Tsave = AR.alloc([128, NCH, 4, 128], BF16)
        DcT = AR.alloc([128, 4, 128], F32)
        dqfB = AR.alloc([128, 4, 128], F32)
        dqbB = AR.alloc([128, 4, 128], F32)
        dkk = AR.alloc([128, 8], F32)
        dcc = AR.alloc([128, 8], F32)
        tmpa = AR.alloc([128, 128], F32)
        tmpb = AR.alloc([128, 128], F32)
        rowi1 = AR.alloc([128, 128], F32)
        row128mi = AR.alloc([128, 128], F32)
        Wukv = AR.alloc([128, 2, 2, 4, 128], BF16)
        Wuq = AR.alloc([128, 3, 4, 192], BF16)
        Wuqr = AR.alloc([128, 3, 4, 64], BF16)
        prep_mark = AR.mark()
        mkt = AR.alloc([128, 2, 128], F32)
        ext = AR.alloc([128, 2, 128], F32)
        st_ukv = AR.alloc([128, 2, 1024], F32)
        st_uq = AR.alloc([128, 3, 768], F32)
        silucB = AR.alloc([128, 8, 128], F32)
        browada = AR.alloc([1, 2, 1024], F32)
        stkr = AR.alloc([128, 8, 64], F32)
        wst = [AR.alloc([128, 8, 512], F32) for _ in range(2)]
        tt('dve', rowi1, RF, RB, ALU.subtract, ['cst'], ['rowi1'])
        ts('dve', rowi1, rowi1, c_j, 1.0, ALU.add, ALU.add, ['rowi1', 'cst'], ['rowi1'])
        ts('dve', row128mi, rowi1, -1.0, 129.0, ALU.mult, ALU.add, ['rowi1'], ['row128mi'])
        for h in range(4):
            act(tmpa, RF, AF.Exp, ['cst', 'lg'], ['tmpa'], scale=lg[:, h:h + 1])
            tt('dve', tmpa, tmpa, MF, ALU.mult, ['tmpa', 'cst'], ['tmpa'])
            act(tmpb, RB, AF.Exp, ['cst', 'lg'], ['tmpb'], scale=lg[:, 4 + h:5 + h])
            tt('dve', tmpb, tmpb, MB, ALU.mult, ['tmpb', 'cst'], ['tmpb'])
            tt('dve', DcT[:, h, :], tmpa, tmpb, ALU.add, ['tmpa', 'tmpb'], ['DcT'])
            act(dqfB[:, h, :], rowi1, AF.Exp, ['rowi1', 'lg'], ['dqfB'], scale=lg[:, h:h + 1])
            act(dqbB[:, h, :], row128mi, AF.Exp, ['row128mi', 'lg'], ['dqbB'], scale=lg[:, 4 + h:5 + h])
            act(dkk[:, h:h + 1], c_127mj, AF.Exp, ['cst', 'lg'], ['dkk'], scale=lg[:, h:h + 1])
            act(dkk[:, 4 + h:5 + h], c_j, AF.Exp, ['cst', 'lg'], ['dkk'], scale=lg[:, 4 + h:5 + h])
            act(dcc[:, h:h + 1], c_128, AF.Exp, ['cst', 'lg'], ['dcc'], scale=lg[:, h:h + 1])
            act(dcc[:, 4 + h:5 + h], c_128, AF.Exp, ['cst', 'lg'], ['dcc'], scale=lg[:, 4 + h:5 + h])

        dma('sp', st_ukv, wukv_in.rearrange("(j p) n -> p j n", p=128), r=[], w=['st_ukv'])
        dma('sp', st_uq, wuq_in.rearrange("(j p) n -> p j n", p=128), r=[], w=['st_uq'])
        for cj in range(2):
            v = st_ukv[:, cj, :].rearrange("p (h t d) -> p t h d", h=4, t=2)
            ts('dve', Wukv[:, cj], v, kvnw[:, cj:cj + 1], None, ALU.mult, None, ['st_ukv', 'kvnw'], ['Wukv'])
        for cj in range(3):
            v = st_uq[:, cj, :].rearrange("p (h d) -> p h d", h=4)
            ts('dve', Wuq[:, cj], v, qnw[:, cj:cj + 1], MLA_SCALE, ALU.mult, ALU.mult, ['st_uq', 'qnw'], ['Wuq'])
            ts('dve', Wuqr[:, cj, :, 0:32], v[:, :, 160:192], qnw[:, cj:cj + 1], -MLA_SCALE, ALU.mult, ALU.mult,
               ['st_uq', 'qnw'], ['Wuqr'])
            ts('dve', Wuqr[:, cj, :, 32:64], v[:, :, 128:160], qnw[:, cj:cj + 1], MLA_SCALE, ALU.mult, ALU.mult,
               ['st_uq', 'qnw'], ['Wuqr'])

        for k in range(8):
            cp('dve', silucB[:, k, :], siluc[:, k, j:j + 1].broadcast_to([128, 128]), ['siluc'], ['silucB'])
        dma('sp', browada[:, 0, :], brow_in[:, 2048:3072], r=[], w=['browada'])
        dma('sp', browada[:, 1, :], brow_in[:, 5120:6144], r=[], w=['browada'])
        gi = 0
        for g in range(12):
            b = gi % 2
            gi += 1
            dma('sp', wst[b], wada_in[:, g * 512:(g + 1) * 512].rearrange("(k p) n -> p k n", p=128),
                r=[], w=[('wst', b)])
            if g in (4, 5, 10, 11):
                which = 0 if g < 6 else 1
                half = g % 2 if g < 6 else (g - 10)
                for k in range(8):
                    mm(PB[0][:, :], silucB[:, k, :], wst[b][:, k, :], k == 0, False, ['silucB', ('wst', b)], [('pb', 0)])
                mm(PB[0][:, :], ones_f[0:1, :], browada[0:1, which, half * 512:(half + 1) * 512], False, True,
                   ['ones_f', 'browada'], [('pb', 0)])
                cp('act', gB[:, which, half * 512:(half + 1) * 512], PB[0][:, :], [('pb', 0)], ['gB'])
            else:
                for c in range(4):
                    ch = g * 4 + c
                    for k in range(8):
                        mm(PB[1][:, 0:1], wst[b][:, k, c * 128:(c + 1) * 128], siluc[:, k, j:j + 1], k == 0, k == 7,
                           [('wst', b), 'siluc'], [('pb', 1)])
                    ts('dve', modfm[:, ch, j:j + 1], PB[1][:, 0:1], bfm[:, ch:ch + 1], None, ALU.add, None,
                       [('pb', 1), 'bfm'], ['modfm'])
        for k in range(8):
            ts('dve', scp[:, 0, k, j:j + 1], modfm[:, 8 + k, j:j + 1], 1.0, None, ALU.add, None, ['modfm'], ['scp'])
            ts('dve', scp[:, 1, k, j:j + 1], modfm[:, 32 + k, j:j + 1], 1.0, None, ALU.add, None, ['modfm'], ['scp'])

        groups = [(0, 512), (512, 512), (1024, 512), (1536, 512), (2048, 512), (2560, 192)]
        for (c0, cw) in groups:
            b = gi % 2
            gi += 1
            dma('sp', wst[b][:, :, 0:cw], win_in[:, c0:c0 + cw].rearrange("(k p) n -> p k n", p=128),
                r=[], w=[('wst', b)])
            for k in range(8):
                mm(PB[0][0:1, 0:cw], modfm[:, k, j:j + 1], wst[b][:, k, 0:cw], k == 0, k == 7,
                   ['modfm', ('wst', b)], [('pb', 0)])
            if c0 == 512:
                actmul(brow[0:1, c0:c0 + cw], PB[0][0:1, 0:cw], float(128 ** -0.5), [('pb', 0)], ['brow'])
            else:
                cp('act', brow[0:1, c0:c0 + cw], PB[0][0:1, 0:cw], [('pb', 0)], ['brow'])
            if c0 == 2048:
                for c in range(4):
                    for k in range(8):
                        mm(PB[1][:, 0:1], wst[b][:, k, c * 128:(c + 1) * 128], modfm[:, k, j:j + 1], k == 0, k == 7,
                           [('wst', b), 'modfm'], [('pb', 1)])
                    cp('dve', bcfm[:, c:c + 1], PB[1][:, 0:1], [('pb', 1)], ['bcfm'])
            if c0 == 2560:
                for k in range(8):
                    mm(PB[1][:, 0:1], wst[b][:, k, 0:128], modfm[:, k, j:j + 1], k == 0, k == 7,
                       [('wst', b), 'modfm'], [('pb', 1)])
                cp('dve', bcfm[:, 4:5], PB[1][:, 0:1], [('pb', 1)], ['bcfm'])
                ts('dve', stkr[:, :, 0:32], wst[b][:, :, 160:192], -1.0, None, ALU.mult, None, [('wst', b)], ['stkr'])
                cp('dve', stkr[:, :, 32:64], wst[b][:, :, 128:160], [('wst', b)], ['stkr'])
                for k in range(8):
                    mm(PB[1][0:64, 0:1], wst[b][:, k, 128:192], modfm[:, k, j:j + 1], k == 0, k == 7,
                       [('wst', b), 'modfm'], [('pb', 1)])
                cp('dve', bkr[:, 0:1], PB[1][0:64, 0:1], [('pb', 1)], ['bkr'])
                for k in range(8):
                    mm(PB[1][0:64, 0:1], stkr[:, k, :], modfm[:, k, j:j + 1], k == 0, k == 7,
                       ['stkr', 'modfm'], [('pb', 1)])
                cp('dve', bkr[:, 1:2], PB[1][0:64, 0:1], [('pb', 1)], ['bkr'])
                for k in range(8):
                    ts('dve', Wkrr[:, k, :], stkr[:, k, :], scp[:, 0, k, j:j + 1], None, ALU.mult, None,
                       ['stkr', 'scp'], ['Wkrr'])
            for k in range(8):
                if c0 == 512:
                    ts('dve', Win[:, k, c0:c0 + cw], wst[b][:, k, 0:cw], scp[:, 0, k, j:j + 1], float(128 ** -0.5),
                       ALU.mult, ALU.mult, [('wst', b), 'scp'], ['Win'])
                else:
                    ts('dve', Win[:, k, c0:c0 + cw], wst[b][:, k, 0:cw], scp[:, 0, k, j:j + 1], None, ALU.mult, None,
                       [('wst', b), 'scp'], ['Win'])
        dma('sp', mkt, mk_in[:, j], r=[], w=['mkt'])
        dma('sp', ext, ex_in[:, j], r=[], w=['ext'])
        for dr in range(2):
            for h in range(4):
                act(coef[:, dr, h, :], ext[:, dr, :], AF.Exp, ['ext', 'lg'], ['coef'], scale=lg[:, 4 * dr + h:4 * dr + h + 1])
                tt('dve', coef[:, dr, h, :], coef[:, dr, h, :], mkt[:, dr, :], ALU.mult, ['coef', 'mkt'], ['coef'])
        memset('pool', Sf, 0.0, ['Sf'])
        memset('pool', Tb, 0.0, ['Tb'])
        P.barrier()
        AR.release(prep_mark)
        ab_mark = AR.mark()

        xs = [AR.alloc([128, D], F32) for _ in range(2)]
        tbt = [AR.alloc([128, 256], F32) for _ in range(2)]
        tmt = [AR.alloc([64, 2, 128], F32) for _ in range(2)]
        xn = AR.alloc([128, D], BF16)
        xT = AR.alloc([128, 8, 128], BF16)
        rA = AR.alloc([128, 4, 128], F32)
        rBt = AR.alloc([128, 4, 128], F32)
        rstd_t = AR.alloc([128, 1], F32)
        rB = AR.alloc([128, 128], F32)
        aonly_mark = AR.mark()
        kr1 = AR.alloc([64, 128], F32)
        kr2 = AR.alloc([64, 128], F32)
        pT = PB[0][:, :].bitcast(BF16).rearrange("p (a b) -> p a b", a=8)[:, :, 0:128]

        def load(t, b):
            dma('sp', xs[b], xj[t * 128:(t + 1) * 128, :], r=[], w=[('xs', b)])
            dma('sp', tbt[b], tbj[t * 128:(t + 1) * 128, :], r=[], w=[('tbt', b)])
            dma('sp', tmt[b], tmj[:, :, t * 128:(t + 1) * 128], r=[], w=[('tmt', b)])

        def proj_tok(pb, tok, c0, cw):
            for k in range(8):
                mm(pb[:, 0:cw], xT[:, k, :], Win[:, k, c0:c0 + cw], k == 0, False, ['xT', 'Win'], [tok])
            mm(pb[:, 0:cw], ones_bf[0:1, :], brow[0:1, c0:c0 + cw], False, True, ['ones_bf', 'brow'], [tok])

        def rotary_tok(pb, tok, b, dst, dtok):
            v = pb[:, :].rearrange("p (h d) -> p h d", h=4)
            cosB = tbt[b][:, 0:128].unsqueeze(1).broadcast_to([128, 4, 128])
            sinB = tbt[b][:, 128:256].unsqueeze(1).broadcast_to([128, 4, 128])
            tt('dve', rA, v, cosB, ALU.mult, [tok, ('tbt', b)], ['rA'])
            tt('dve', rBt, v, sinB, ALU.mult, [tok, ('tbt', b)], ['rBt'])
            tt('pool', dst[:, :, 0:64], rA[:, :, 0:64], rBt[:, :, 64:128], ALU.subtract, ['rA', 'rBt'], [dtok])
            tt('pool', dst[:, :, 64:128], rBt[:, :, 0:64], rA[:, :, 64:128], ALU.add, ['rA', 'rBt'], [dtok])

        Kp2 = [AR.alloc([128, 4, 128], BF16) for _ in range(2)]
        Kwf2 = [AR.alloc([128, 4, 128], BF16) for _ in range(2)]
        Kwb2 = [AR.alloc([128, 4, 128], BF16) for _ in range(2)]
        Vr2 = [AR.alloc([128, 512], BF16) for _ in range(2)]
        sqv2 = [AR.alloc([128, 2, 128], BF16) for _ in range(2)]
        ckv2 = [AR.alloc([128, 2, 128], BF16) for _ in range(2)]
        KTb2 = [AR.alloc([128, 4, 512], BF16) for _ in range(2)]
        KRb2 = [AR.alloc([64, 512], BF16) for _ in range(2)]
        VMb2 = [AR.alloc([128, 4, 4, 128], BF16) for _ in range(2)]
        ckt = [AR.alloc([128, 2, 4], F32) for _ in range(2)]
        dkfB = dkk[:, 0:4].unsqueeze(2).broadcast_to([128, 4, 128])
        dkbB = dkk[:, 4:8].unsqueeze(2).broadcast_to([128, 4, 128])

        xnA = [xn, AR.alloc([128, D], BF16)]
        xTA = [xT, AR.alloc([128, 8, 128], BF16)]
        ln_scratch('a0')
        ln_scratch('a1')

        def lnA(t):
            b = t % 2
            ln_apply(xs[b], xnA[b], 'a%d' % b, [('xs', b)], [('xnA', b)])

        def trA(t):
            b = t % 2
            for k in range(8):
                tr(pT[:, k, :], xnA[b][:, k * 128:(k + 1) * 128], [('xnA', b)], [('pb', 0)])
            cp('act', xTA[b], pT, [('pb', 0)], [('xTA', b)])

        def projA(t):
            b = t % 2
            xTb = xTA[b]
            xtok = ('xTA', b)
            for (pb_, c0) in ((1, 512), (2, 1024)):
                for k in range(8):
                    mm(PB[pb_][:, :], xTb[:, k, :], Win[:, k, c0:c0 + 512], k == 0, False, [xtok, 'Win'], [('pb', pb_)])
                mm(PB[pb_][:, :], ones_bf[0:1, :], brow[0:1, c0:c0 + 512], False, True, ['ones_bf', 'brow'], [('pb', pb_)])
            for cj in range(2):
                for k in range(8):
                    mm(PB[3][:, cj * 128:(cj + 1) * 128], Win[:, k, 2432 + cj * 128:2432 + (cj + 1) * 128], xTb[:, k, :],
                       k == 0, k == 7, ['Win', xtok], [('pb', 3)])
            for k in range(8):
                mm(PB[3][0:64, 256:384], Win[:, k, 2688:2752], xTb[:, k, :], k == 0, k == 7, ['Win', xtok], [('pb', 3)])
            for k in range(8):
                mm(PB[3][0:64, 384:512], Wkrr[:, k, :], xTb[:, k, :], k == 0, k == 7, ['Wkrr', xtok], [('pb', 3)])

        def firstA(t):
            b = t % 2
            s = t // NCH
            q4 = t % 4
            bp = (t // 4) % 2
            rotary_tok(PB[1], ('pb', 1), b, Kp2[b], ('Kp2', b))
            cp('act', Vr2[b], PB[2][:, :], [('pb', 2)], [('Vr2', b)])
            ckvp = PB[3][:, 0:256].rearrange("p (a b) -> p a b", a=2)
            for cj in range(2):
                act(sqv2[b][:, cj, :], ckvp[:, cj, :], AF.Square, [('pb', 3), 'bcfm'], [('sqv2', b)], bias=bcfm[:, 3 + cj:4 + cj])
                act(ckv2[b][:, cj, :], ckvp[:, cj, :], AF.Identity, [('pb', 3), 'bcfm'], [('ckv2', b)], bias=bcfm[:, 3 + cj:4 + cj])
            stt('dve', kr1, PB[3][0:64, 256:384], bkr[:, 0:1], tmt[b][:, 0, :], ALU.add, ALU.mult,
                [('pb', 3), 'bkr', ('tmt', b)], ['kr1'])
            stt('dve', kr2, PB[3][0:64, 384:512], bkr[:, 1:2], tmt[b][:, 1, :], ALU.add, ALU.mult,
                [('pb', 3), 'bkr', ('tmt', b)], ['kr2'])
            tt('pool', KRb2[bp][:, q4 * 128:(q4 + 1) * 128], kr1, kr2, ALU.add, ['kr1', 'kr2'], [('KRb2', bp)])
            if s > 0:
                tt('dve', ckt[b], dkk.rearrange("p (a h) -> p a h", a=2), coef[:, :, :, t], ALU.mult, ['dkk', 'coef'], [('ckt', b)])
                tt('pool', Kwf2[b], Kp2[b], ckt[b][:, 0, :].unsqueeze(2).broadcast_to([128, 4, 128]), ALU.mult,
                   [('Kp2', b), ('ckt', b)], [('Kwf2', b)])
                tt('pool', Kwb2[b], Kp2[b], ckt[b][:, 1, :].unsqueeze(2).broadcast_to([128, 4, 128]), ALU.mult,
                   [('Kp2', b), ('ckt', b)], [('Kwb2', b)])
            else:
                tt('pool', Kwb2[b], Kp2[b], dkbB, ALU.mult, [('Kp2', b), 'dkk'], [('Kwb2', b)])

        def secondA(t):
            b = t % 2
            s = t // NCH
            n = t % NCH
            q4 = t % 4
            bp = (t // 4) % 2
            for cj in range(2):
                mm(PB[4][:, 0:1], sqv2[b][:, cj, :], ones_bf[:, 0:1], cj == 0, cj == 1, [('sqv2', b), 'ones_bf'], [('pb', 4)])
            for cj in range(2):
                mm(PB[4][:, 128:256], ones_bf, sqv2[b][:, cj, :], cj == 0, cj == 1, ['ones_bf', ('sqv2', b)], [('pb', 4)])
            cp('dve', rstd_t, PB[4][:, 0:1], [('pb', 4)], ['rstd_t'])
            cp('dve', rB, PB[4][:, 128:256], [('pb', 4)], ['rB'])
            rsqrt_small(rstd_t, 'rstd_t', 1.0 / 256.0, 1e-6)
            rsqrt_small(rB, 'rB', 1.0 / 256.0, 1e-6)
            Pst = [PB[6][:, :].rearrange("p (h e) -> p h e", h=4), PB[7][:, :].rearrange("p (h e) -> p h e", h=4)]
            if s > 0:
                first_t = (t == NT - 1)
                last_t = (t == NCH)
                for h in range(4):
                    mm(Pst[1][:, h, :], Kwb2[b][:, h, :], Vr2[b][:, h * 128:(h + 1) * 128], first_t and h == 0, last_t,
                       [('Kwb2', b), ('Vr2', b)], [('pb', 7)])
                for h in range(4):
                    mm(Pst[0][:, h, :], Kwf2[b][:, h, :], Vr2[b][:, h * 128:(h + 1) * 128], first_t and h == 0, last_t,
                       [('Kwf2', b), ('Vr2', b)], [('pb', 6)])
            else:
                for h in range(4):
                    mm(Pst[1][:, h, :], Kwb2[b][:, h, :], Vr2[b][:, h * 128:(h + 1) * 128], True, True,
                       [('Kwb2', b), ('Vr2', b)], [('pb', 7)])
            knp = PB[5][:, :].rearrange("p (h d) -> p h d", h=4)
            for h in range(4):
                for cj in range(2):
                    mm(knp[:, h, :], Wukv[:, cj, 0, h, :], ckv2[b][:, cj, :], cj == 0, cj == 1, ['Wukv', ('ckv2', b)], [('pb', 5)])
            for cj in range(2):
                mm(PB[4][:, :], ckv2[b][:, cj, :], Wukv[:, cj, 1].rearrange("p h d -> p (h d)"), cj == 0, cj == 1,
                   [('ckv2', b), 'Wukv'], [('pb', 4)])
            if s > 0:
                if t == NCH:
                    cp('dve', Sf, Pst[0], [('pb', 6)], ['Sf'])
                    cp('dve', Tb, Pst[1], [('pb', 7)], ['Tb'])
            else:
                cp('act', Tsave[:, n], Tb, ['Tb'], ['Tsave'])
                for h in range(4):
                    stt('dve', Tb[:, h, :], Tb[:, h, :], dcc[:, 4 + h:5 + h], Pst[1][:, h, :], ALU.mult, ALU.add,
                        [('pb', 7), 'dcc', 'Tb'], ['Tb'])
            tt('dve', KTb2[bp][:, :, q4 * 128:(q4 + 1) * 128], knp, rB.unsqueeze(1).broadcast_to([128, 4, 128]), ALU.mult,
               [('pb', 5), 'rB'], [('KTb2', bp)])
            act(VMb2[bp][:, :, q4, :], PB[4][:, :].rearrange("p (h d) -> p h d", h=4), AF.Identity, [('pb', 4), 'rstd_t'],
                [('VMb2', bp)], scale=rstd_t[:, 0:1])
            if q4 == 0:
                t0 = t
                dma('sp', KTd[j][:, :, t0 * 128:(t0 + 4) * 128].rearrange("h p n -> p h n"), KTb2[bp], r=[('KTb2', bp)], w=[('KTd', j)])
                dma('sp', KRd[j][:, t0 * 128:(t0 + 4) * 128], KRb2[bp], r=[('KRb2', bp)], w=[('KRd', j)])
                dma('sp', VVd[j][:, :, t0:t0 + 4, :].rearrange("h p t e -> p h t e"), VMb2[bp], r=[('VMb2', bp)], w=[('VVd', j)])

        load(NT - 1, (NT - 1) % 2)
        load(NT - 2, (NT - 2) % 2)
        lnA(NT - 1)
        trA(NT - 1)
        projA(NT - 1)
        lnA(NT - 2)
        firstA(NT - 1)
        trA(NT - 2)
        for t in range(NT - 1, -1, -1):
            if t - 1 >= 0:
                projA(t - 1)
                if t - 2 >= 0:
                    load(t - 2, t % 2)
                    lnA(t - 2)
                firstA(t - 1)
                if t - 2 >= 0:
                    trA(t - 2)
            secondA(t)

        P.barrier()
        AR.release(aonly_mark)
        QT3 = AR.alloc([128, 3, 4, 128], BF16)
        KT = AR.alloc([128, 4, 128], BF16)
        AcT = AR.alloc([128, 4, 128], BF16)
        gnB = AR.alloc([128, 512], F32)
        dma('sp', gnB, gnw_in.partition_broadcast(128).rearrange("p a b -> p (a b)"), r=[], w=['gnB'])
        st4 = AR.alloc([128, 4, 6], F32)
        mv4 = AR.alloc([128, 4, 2], F32)
        rs4 = AR.alloc([128, 4], F32)
        y1 = AR.alloc([128, 4, 128], F32)
        retb = AR.alloc([128, 512], BF16)
        RTb = AR.alloc([128, 4, 512], BF16)
        QNb = AR.alloc([128, 4, 512], BF16)
        QRb = AR.alloc([64, 4, 512], BF16)
        qr1 = AR.alloc([64, 4, 128], F32)
        qr2 = AR.alloc([64, 4, 128], F32)
        xnB = [xn, AR.alloc([128, D], BF16)]
        xTB = [xT, AR.alloc([128, 8, 128], BF16)]
        ln_scratch('b0')
        ln_scratch('b1')
        tmt3 = [AR.alloc([64, 2, 128], F32) for _ in range(3)]
        Qp2 = [AR.alloc([128, 4, 128], BF16) for _ in range(2)]
        KpB = [AR.alloc([128, 4, 128], BF16) for _ in range(2)]
        VrB = [AR.alloc([128, 512], BF16) for _ in range(2)]
        gate2 = [AR.alloc([128, 512], BF16) for _ in range(2)]
        KwfB = [AR.alloc([128, 4, 128], BF16) for _ in range(2)]
        sqB = [AR.alloc([128, 3, 128], BF16) for _ in range(2)]
        cqB = [AR.alloc([128, 3, 128], BF16) for _ in range(2)]
        dkfB2 = dkk[:, 0:4].unsqueeze(2).broadcast_to([128, 4, 128])

        def loadB(c):
            b = c % 2
            dma('sp', xs[b], xj[c * 128:(c + 1) * 128, :], r=[], w=[('xs', b)])
            dma('sp', tbt[b], tbj[c * 128:(c + 1) * 128, :], r=[], w=[('tbt', b)])
            dma('sp', tmt3[c % 3], tmj[:, :, c * 128:(c + 1) * 128], r=[], w=[('tmt3', c % 3)])

        def lnB_(c):
            b = c % 2
            ln_apply(xs[b], xnB[b], 'b%d' % b, [('xs', b)], [('xnB', b)])

        def trB(c):
            b = c % 2
            for k in range(8):
                tr(pT[:, k, :], xnB[b][:, k * 128:(k + 1) * 128], [('xnB', b)], [('pb', 0)])
            cp('act', xTB[b], pT, [('pb', 0)], [('xTB', b)])

        def projB(c):
            b = c % 2
            xTb = xTB[b]
            xtok = ('xTB', b)
            for (pb_, c0) in ((1, 0), (2, 512), (3, 1024), (4, 1536)):
                for k in range(8):
                    mm(PB[pb_][:, :], xTb[:, k, :], Win[:, k, c0:c0 + 512], k == 0, False, [xtok, 'Win'], [('pb', pb_)])
                mm(PB[pb_][:, :], ones_bf[0:1, :], brow[0:1, c0:c0 + 512], False, True, ['ones_bf', 'brow'], [('pb', pb_)])
            for cj in range(3):
                for k in range(8):
                    mm(PB[5][:, cj * 128:(cj + 1) * 128], Win[:, k, 2048 + cj * 128:2048 + (cj + 1) * 128], xTb[:, k, :],
                       k == 0, k == 7, ['Win', xtok], [('pb', 5)])

        def evacB(c):
            b = c % 2
            rotary_tok(PB[1], ('pb', 1), b, Qp2[b], ('Qp2', b))
            rotary_tok(PB[2], ('pb', 2), b, KpB[b], ('KpB', b))
            cp('act', VrB[b], PB[3][:, :], [('pb', 3)], [('VrB', b)])
            for cj in range(3):
                act(sqB[b][:, cj, :], PB[5][:, cj * 128:(cj + 1) * 128], AF.Square, [('pb', 5), 'bcfm'], [('sqB', b)],
                    bias=bcfm[:, cj:cj + 1])
                act(cqB[b][:, cj, :], PB[5][:, cj * 128:(cj + 1) * 128], AF.Identity, [('pb', 5), 'bcfm'], [('cqB', b)],
                    bias=bcfm[:, cj:cj + 1])
            act(gate2[b], PB[4][:, :], AF.Silu, [('pb', 4)], [('gate2', b)])
            tt('pool', gate2[b], gate2[b], gnB, ALU.mult, [('gate2', b), 'gnB'], [('gate2', b)])
            tt('pool', KwfB[b], KpB[b], dkfB2, ALU.mult, [('KpB', b), 'dkk'], [('KwfB', b)])

        def secondB(c):
            b = c % 2
            q4 = c % 4
            tm_ = tmt3[c % 3]
            tmtok = ('tmt3', c % 3)
            pQK = PB[0][:, :].bitcast(BF16).rearrange("p (a b) -> p a b", a=8)[:, :, 0:128]
            for h in range(4):
                tr(pQK[:, h, :], Qp2[b][:, h, :], [('Qp2', b)], [('pb', 0)])
                tr(pQK[:, 4 + h, :], KpB[b][:, h, :], [('KpB', b)], [('pb', 0)])
            cp('act', QT3[:, 0], pQK[:, 0:4, :], [('pb', 0)], ['QT0'])
            cp('act', KT, pQK[:, 4:8, :], [('pb', 0)], ['KT'])
            tt('dve', QT3[:, 1], pQK[:, 0:4, :], dqfB, ALU.mult, [('pb', 0), 'dqfB'], ['QT1'])
            tt('dve', QT3[:, 2], pQK[:, 0:4, :], dqbB, ALU.mult, [('pb', 0), 'dqbB'], ['QT2'])
            for cj in range(3):
                mm(PB[5][:, 128:256], ones_bf, sqB[b][:, cj, :], cj == 0, cj == 2, ['ones_bf', ('sqB', b)], [('pb', 5)])
            Ap = PB[6][:, :].rearrange("p (h d) -> p h d", h=4)
            for h in range(4):
                mm(Ap[:, h, :], KT[:, h, :], QT3[:, 0, h, :], True, True, ['KT', 'QT0'], [('pb', 6)])
            cp('dve', rB, PB[5][:, 128:256], [('pb', 5)], ['rB'])
            rsqrt_small(rB, 'rB', 1.0 / 384.0, 1e-6)
            tt('dve', AcT, Ap, DcT, ALU.mult, [('pb', 6), 'DcT'], ['AcT'])
            cp('act', Sbf, Sf, ['Sf'], ['Sbf'])
            qnp = PB[5][:, :].rearrange("p (h d) -> p h d", h=4)
            for h in range(4):
                for cj in range(3):
                    mm(qnp[:, h, :], Wuq[:, cj, h, 0:128], cqB[b][:, cj, :], cj == 0, cj == 2, ['Wuq', ('cqB', b)], [('pb', 5)])
            qrp = PB[1][0:64, :].rearrange("p (h d) -> p h d", h=4)
            qrrp = PB[2][0:64, :].rearrange("p (h d) -> p h d", h=4)
            for h in range(4):
                for cj in range(3):
                    mm(qrp[:, h, :], Wuq[:, cj, h, 128:192], cqB[b][:, cj, :], cj == 0, cj == 2, ['Wuq', ('cqB', b)], [('pb', 1)])
                for cj in range(3):
                    mm(qrrp[:, h, :], Wuqr[:, cj, h, :], cqB[b][:, cj, :], cj == 0, cj == 2, ['Wuqr', ('cqB', b)], [('pb', 2)])
            Op = PB[7][:, :].rearrange("p (h d) -> p h d", h=4)
            for h in range(4):
                mm(Op[:, h, :], AcT[:, h, :], VrB[b][:, h * 128:(h + 1) * 128], True, False, ['AcT', ('VrB', b)], [('pb', 7)])
                mm(Op[:, h, :], QT3[:, 1, h, :], Sbf[:, h, :], False, False, ['QT1', 'Sbf'], [('pb', 7)])
                mm(Op[:, h, :], QT3[:, 2, h, :], Tsave[:, c, h, :], False, True, ['QT2', 'Tsave'], [('pb', 7)])
            Pf = PB[6][:, :].rearrange("p (h d) -> p h d", h=4)
            for h in range(4):
                mm(Pf[:, h, :], KwfB[b][:, h, :], VrB[b][:, h * 128:(h + 1) * 128], True, True, [('KwfB', b), ('VrB', b)], [('pb', 6)])
            tt('dve', QNb[:, :, q4 * 128:(q4 + 1) * 128], qnp, rB.unsqueeze(1).broadcast_to([128, 4, 128]), ALU.mult,
               [('pb', 5), 'rB'], ['QNb'])
            tt('dve', qr1, qrp, tm_[:, 0, :].unsqueeze(1).broadcast_to([64, 4, 128]), ALU.mult, [('pb', 1), tmtok], ['qr1'])
            tt('dve', qr2, qrrp, tm_[:, 1, :].unsqueeze(1).broadcast_to([64, 4, 128]), ALU.mult, [('pb', 2), tmtok], ['qr2'])
            tt('pool', qr1, qr1, qr2, ALU.add, ['qr1', 'qr2'], ['qr1'])
            tt('pool', QRb[:, :, q4 * 128:(q4 + 1) * 128], qr1, rB[0:64, :].unsqueeze(1).broadcast_to([64, 4, 128]),
               ALU.mult, ['qr1', 'rB'], ['QRb'])
            for h in range(4):
                stt('dve', Sf[:, h, :], Sf[:, h, :], dcc[:, h:h + 1], Pf[:, h, :], ALU.mult, ALU.add,
                    [('pb', 6), 'dcc', 'Sf', 'Sbf'], ['Sf'])
            for h in range(4):
                bnstats(st4[:, h, :], Op[:, h, :], [('pb', 7)], ['st4'])
            for h in range(4):
                bnaggr(mv4[:, h, :], st4[:, h, :], ['st4'], ['mv4'])
            cp('dve', rs4, mv4[:, :, 1], ['mv4'], ['rs4'])
            rsqrt_small(rs4, 'rs4', 1.0, 1e-5)
            tt('dve', y1, Op, mv4[:, :, 0:1].broadcast_to([128, 4, 128]), ALU.subtract, [('pb', 7), 'mv4'], ['y1'])
            tt('pool', y1, y1, rs4.unsqueeze(2).broadcast_to([128, 4, 128]), ALU.mult, ['y1', 'rs4'], ['y1'])
            tt('pool', retb, y1.rearrange("p h d -> p (h d)"), gate2[b], ALU.mult, ['y1', ('gate2', b)], ['retb'])
            pR = PB[0][:, :].bitcast(BF16).rearrange("p (a b) -> p a b", a=8)[:, :, 0:128]
            for h in range(4):
                tr(pR[:, h, :], retb[:, h * 128:(h + 1) * 128], ['retb'], [('pb', 0)])
            cp('act', RTb[:, :, q4 * 128:(q4 + 1) * 128], pR[:, 0:4, :], [('pb', 0)], ['RTb'])
            if q4 == 3:
                t0 = c - 3
                dma('sp', RTd[j][:, :, t0 * 128:(t0 + 4) * 128].rearrange("h p n -> p h n"), RTb, r=['RTb'], w=[('RTd', j)])
                dma('sp', QTd[j][:, :, t0 * 128:(t0 + 4) * 128].rearrange("h p n -> p h n"), QNb, r=['QNb'], w=[('QTd', j)])
                dma('sp', QRd[j][:, :, t0 * 128:(t0 + 4) * 128].rearrange("h p n -> p h n"), QRb, r=['QRb'], w=[('QRd', j)])

        loadB(0)
        loadB(1)
        lnB_(0)
        trB(0)
        projB(0)
        lnB_(1)
        evacB(0)
        trB(1)
        for c in range(NCH):
            if c + 1 < NCH:
                projB(c + 1)
                if c + 2 < NCH:
                    loadB(c + 2)
                    lnB_(c + 2)
                evacB(c + 1)
                if c + 2 < NCH:
                    trB(c + 2)
            secondB(c)

        P.barrier()
        AR.release(job_mark)
        KTh = AR.alloc([128, S], BF16)
        KRh = AR.alloc([128, S], BF16)
        Vh = AR.alloc([128, NT, 128], BF16)
        QTh = [AR.alloc([128, L], BF16) for _ in range(2)]
        QRh = [AR.alloc([128, L], BF16) for _ in range(2)]
        memset('pool', KRh, 0.0, ['KRh'])
        memset('pool', QRh[0], 0.0, [('QRh', 0)])
        memset('pool', QRh[1], 0.0, [('QRh', 1)])
        NPT = 4
        PT = [AR.alloc([128, 512], BF16) for _ in range(NPT)]
        dacc = [AR.alloc([128, 512], F32) for _ in range(6)]
        rden = AR.alloc([128, 512], F32)
        ato = AR.alloc([128, 512], BF16)
        dma('sp', KRh[0:64, :], KRd[j], r=[('KRd', j)], w=['KRh'])
        NQB = L // 512
        NH2 = NT // 2

        def load_head(h, hf):
            hp = h % 2
            dma('sp', KTh[:, hf * NH2 * 128:(hf + 1) * NH2 * 128], KTd[j][h][:, hf * NH2 * 128:(hf + 1) * NH2 * 128],
                r=[('KTd', j)], w=[('KTh', hf)])
            dma('sp', Vh[:, hf * NH2:(hf + 1) * NH2, :], VVd[j][h][:, hf * NH2:(hf + 1) * NH2, :],
                r=[('VVd', j)], w=[('Vh', hf)])
            if hf == 0:
                dma('sp', QTh[hp], QTd[j][h], r=[('QTd', j)], w=[('QTh', hp)])
                dma('sp', QRh[hp][0:64, :], QRd[j][h], r=[('QRd', j)], w=[('QRh', hp)])

        items = [(h, qb, kt) for h in range(4) for qb in range(NQB) for kt in range(NT)]

        def qk(item, gi_):
            h, qb, kt = item
            hp = h % 2
            if qb == 0 and kt == 0:
                load_head(h, 0)
                if h == 0:
                    load_head(h, 1)
            qs = slice(qb * 512, (qb + 1) * 512)
            sb = gi_ % NPT
            hf = kt // NH2
            mm(PB[sb][:, :], KTh[:, kt * 128:(kt + 1) * 128], QTh[hp][:, qs], True, False,
               [('KTh', hf), ('QTh', hp)], [('pb', sb)])
            mm(PB[sb][:, :], KRh[:, kt * 128:(kt + 1) * 128], QRh[hp][:, qs], False, True,
               ['KRh', ('QRh', hp)], [('pb', sb)])
            act(PT[sb], PB[sb][:, :], AF.Exp, [('pb', sb)], [('PT', sb)])

        def pv(item, gi_):
            h, qb, kt = item
            qs = slice(qb * 512, (qb + 1) * 512)
            blk = h * NQB + qb
            ob = 4 + (blk % 2)
            db = 6 + (blk % 2)
            sb = gi_ % NPT
            hf = kt // NH2
            mm(PB[ob][:, :], Vh[:, kt, :], PT[sb], kt == 0, kt == NT - 1, [('Vh', hf), ('PT', sb)], [('pb', ob)])
            if kt % 4 == 3:
                ai, eng_ = 2, 'pool'
            else:
                ai, eng_ = (kt - kt // 4) % 2, 'dve'
            da = dacc[(blk % 2) * 3 + ai]
            dtok = ('dacc', blk % 2, ai)
            if kt in (0, 1, 3):
                cp(eng_, da, PT[sb], [('PT', sb)], [dtok])
            else:
                tt(eng_, da, da, PT[sb], ALU.add, [dtok, ('PT', sb)], [dtok], nosync=(eng_ == 'dve'))
            if kt == NT - 1:
                for ai2 in range(3):
                    mm(PB[db][:, :], ones_f, dacc[(blk % 2) * 3 + ai2], ai2 == 0, ai2 == 2,
                       ['ones_f', ('dacc', blk % 2, ai2)], [('pb', db)])
                recip(rden, PB[db][:, :], [('pb', db)], ['rden'])
                tt('dve', ato, PB[ob][:, :], rden, ALU.mult, [('pb', ob), 'rden'], ['ato'])
                dma('sp', ATd[j][h][:, qs], ato, r=['ato'], w=[('ATd', j)])
                if qb == NQB - 1 and h < 3:
                    load_head(h + 1, 1)

        LA = 2
        for i_ in range(min(LA, len(items))):
            qk(items[i_], i_)
        for i_ in range(len(items)):
            if i_ + LA < len(items):
                qk(items[i_ + LA], i_ + LA)
            pv(items[i_], i_)

        P.barrier()
        AR.release(job_mark)
        lnB = AR.alloc([128, 2, 1024], F32)
        dma('sp', lnB, ln_in[0:2, :].partition_broadcast(128), r=[], w=['lnB'])
        Wo = AR.alloc([128, 8, 1024], BF16)
        d_mark = AR.mark()
        wst = [AR.alloc([128, 8, 512], F32) for _ in range(2)]
        gi = 0
        for g in range(2):
            b = gi % 2
            gi += 1
            dma('sp', wst[b], wo_in[:, g * 512:(g + 1) * 512].rearrange("(k p) n -> p k n", p=128), r=[], w=[('wst', b)])
            for k in range(8):
                tt('dve', Wo[:, k, g * 512:(g + 1) * 512], wst[b][:, k, :], gB[:, 0, g * 512:(g + 1) * 512], ALU.mult,
                   [('wst', b), 'gB'], ['Wo'])
        cat2 = [AR.alloc([128, 8, 512], BF16) for _ in range(2)]
        xo = [AR.alloc([128, D], F32) for _ in range(2)]
        yv2 = [AR.alloc([128, D], F32) for _ in range(2)]
        x1 = [AR.alloc([128, D], F32) for _ in range(2)]
        ln_scratch('d1')
        NT1 = L // 128

        def load_d1(tix):
            m, u = tix // 4, tix % 4
            if u == 0:
                ms = slice(m * 512, (m + 1) * 512)
                cb = m % 2
                dma('sp', cat2[cb][:, 0:4, :], RTd[j][:, :, ms].rearrange("h p n -> p h n"), r=[('RTd', j)], w=[('cat', cb, 0)])
                dma('sp', cat2[cb][:, 4:8, :], ATd[j][:, :, ms].rearrange("h p n -> p h n"), r=[('ATd', j)], w=[('cat', cb, 1)])
            dma('sp', xo[tix % 2], xj[tix * 128:(tix + 1) * 128, :], r=[], w=[('xo', tix % 2)])

        load_d1(0)
        for tix in range(NT1):
            m, u = tix // 4, tix % 4
            b = tix % 2
            cb = m % 2
            if tix + 1 < NT1:
                load_d1(tix + 1)
            for half in range(2):
                for k in range(8):
                    mm(PB[half][:, :], cat2[cb][:, k, u * 128:(u + 1) * 128], Wo[:, k, half * 512:(half + 1) * 512],
                       k == 0, k == 7, [('cat', cb, 0), ('cat', cb, 1), 'Wo'], [('pb', half)])
                stt('dve', yv2[b][:, half * 512:(half + 1) * 512], xo[b][:, half * 512:(half + 1) * 512], ALPHA,
                    PB[half][:, :], ALU.mult, ALU.add, [('xo', b), ('pb', half)], [('yv', b)])
            ln_apply(yv2[b], yv2[b], 'd1', [('yv', b)], [('yv', b)])
            tt('pool', yv2[b], yv2[b], lnB[:, 0, :], ALU.mult, [('yv', b), 'lnB'], [('yv', b)])
            tt('pool', x1[b], yv2[b], lnB[:, 1, :], ALU.add, [('yv', b), 'lnB'], [('x1', b)])
            dma('sp', X1d[j][tix * 128:(tix + 1) * 128, :], x1[b], r=[('x1', b)], w=[('X1d', j)])

        P.barrier()
        AR.release(job_mark)
        lnB = AR.alloc([128, 2, 1024], F32)
        dma('sp', lnB, ln_in[2:4, :].partition_broadcast(128), r=[], w=['lnB'])
        Wup = AR.alloc([128, 8, DFF], BF16)
        Wdn = AR.alloc([128, 32, D], BF16)
        bup = AR.alloc([128, 32], F32)
        e_mark = AR.mark()
        wst = [AR.alloc([128, 8, 512], F32) for _ in range(2)]
        gi = 0
        for g in range(8):
            b = gi % 2
            gi += 1
            dma('sp', wst[b], wup_in[:, g * 512:(g + 1) * 512].rearrange("(k p) n -> p k n", p=128), r=[], w=[('wst', b)])
            for c in range(4):
                for k in range(8):
                    mm(PB[1][:, 0:1], wst[b][:, k, c * 128:(c + 1) * 128], modfm[:, 24 + k, j:j + 1], k == 0, k == 7,
                       [('wst', b), 'modfm'], [('pb', 1)])
                cp('dve', bup[:, g * 4 + c:g * 4 + c + 1], PB[1][:, 0:1], [('pb', 1)], ['bup'])
            for k in range(8):
                ts('dve', Wup[:, k, g * 512:(g + 1) * 512], wst[b][:, k, :], scp[:, 1, k, j:j + 1], None, ALU.mult, None,
                   [('wst', b), 'scp'], ['Wup'])
        for g in range(8):
            b = gi % 2
            gi += 1
            fg, half = g // 2, g % 2
            dma('sp', wst[b], wdn_in[fg * 1024:(fg + 1) * 1024, half * 512:(half + 1) * 512].rearrange("(k p) n -> p k n", p=128),
                r=[], w=[('wst', b)])
            for k in range(8):
                tt('dve', Wdn[:, fg * 8 + k, half * 512:(half + 1) * 512], wst[b][:, k, :], gB[:, 1, half * 512:(half + 1) * 512],
                   ALU.mult, [('wst', b), 'gB'], ['Wdn'])
        P.barrier()
        AR.release(e_mark)
        TBK = 256
        NU = TBK // 128
        NB = L // TBK
        x1t = [[AR.alloc([128, D], F32) for _ in range(NU)] for _ in range(2)]
        xn2 = AR.alloc([128, D], BF16)
        h2T = [AR.alloc([128, 8, TBK], BF16) for _ in range(2)]
        u2 = [AR.alloc([128, 8, TBK], BF16) for _ in range(2)]
        rt = [AR.alloc([128, TBK], F32) for _ in range(2)]
        oo = [gB[:, 0, :], gB[:, 1, :]]
        ln_scratch('d2a')
        ln_scratch('d2b')
        pT2 = PB[6][:, :].bitcast(BF16).rearrange("p (a b) -> p a b", a=8)[:, :, 0:128]

        def front2(m):
            mp = m % 2
            for u in range(NU):
                tix = m * NU + u
                dma('sp', x1t[mp][u], X1d[j][tix * 128:(tix + 1) * 128, :], r=[('X1d', j)], w=[('x1t', mp, u)])
                ln_apply(x1t[mp][u], xn2, 'd2a', [('x1t', mp, u)], ['xn2'])
                for k in range(8):
                    tr(pT2[:, k, :], xn2[:, k * 128:(k + 1) * 128], ['xn2'], [('pb', 6)])
                cp('act', h2T[mp][:, :, u * 128:(u + 1) * 128], pT2, [('pb', 6)], [('h2T', mp)])

        def up(m, g):
            mp = m % 2
            ub = (m * 4 + g) % 2
            for fl in range(8):
                f = g * 8 + fl
                pb = 4 + (f % 2)
                for k in range(8):
                    mm(PB[pb][:, 0:TBK], Wup[:, k, f * 128:(f + 1) * 128], h2T[mp][:, k, :], k == 0, k == 7,
                       ['Wup', ('h2T', mp)], [('pb', pb)])
                rb = f % 2
                act(rt[rb], PB[pb][:, 0:TBK], AF.Relu, [('pb', pb), 'bup'], [('rt', rb)], bias=bup[:, f:f + 1])
                tt('pool' if f % 2 else 'dve', u2[ub][:, fl, :], rt[rb], rt[rb], ALU.mult, [('rt', rb)], [('u2', ub)])

        def down(m, g):
            ub = (m * 4 + g) % 2
            for u in range(NU):
                for half in range(2):
                    pbd = u * 2 + half
                    for fl in range(8):
                        f = g * 8 + fl
                        mm(PB[pbd][:, :], u2[ub][:, fl, u * 128:(u + 1) * 128], Wdn[:, f, half * 512:(half + 1) * 512],
                           f == 0, f == 31, [('u2', ub), 'Wdn'], [('pb', pbd)])

        def tail(m):
            mp = m % 2
            for u in range(NU):
                tix = m * NU + u
                ob = tix % 2
                for half in range(2):
                    pbd = u * 2 + half
                    stt('dve', oo[ob][:, half * 512:(half + 1) * 512], x1t[mp][u][:, half * 512:(half + 1) * 512], ALPHA,
                        PB[pbd][:, :], ALU.mult, ALU.add, [('x1t', mp, u), ('pb', pbd)], [('oo', ob)])
                ln_apply(oo[ob], oo[ob], 'd2b', [('oo', ob)], [('oo', ob)])
                tt('pool', oo[ob], oo[ob], lnB[:, 0, :], ALU.mult, [('oo', ob), 'lnB'], [('oo', ob)])
                tt('pool', oo[ob], oo[ob], lnB[:, 1, :], ALU.add, [('oo', ob), 'lnB'], [('oo', ob)])
                dma('sp', y_out[j][tix * 128:(tix + 1) * 128, :], oo[ob], r=[('oo', ob)], w=[('yout', j, ob)], outp=True)

        G = [(m, g) for m in range(NB) for g in range(4)]
        front2(0)
        up(0, 0)
        for i_, (m, g) in enumerate(G):
            if i_ + 1 < len(G):
                m2, g2 = G[i_ + 1]
                if g2 == 0:
                    front2(m2)
                up(m2, g2)
            down(m, g)
            if g == 3:
                tail(m)

    P.emit()
    return nc, P


def _rope_tables(pos, d):
    inv = (10000.0 ** (-np.arange(0, d, 2, dtype=np.float32) / np.float32(d))).astype(np.float32)
    ang = pos.astype(np.float32)[:, None] * inv[None, :]
    return np.cos(ang).astype(np.float32), np.sin(ang).astype(np.float32)


def host_inputs(L, NSLOT, core_jobs, xs, cs, w):
    NJ = len(NSLOT)
    NCH = L // 128
    p = np.arange(128, dtype=np.float32)
    i = np.arange(128, dtype=np.float32)
    cst = np.zeros((128, 4 * 128 + 8), np.float32)
    cst[:, 0:128] = np.maximum(i[None, :] - p[:, None], 0)
    cst[:, 128:256] = np.maximum(p[:, None] - i[None, :], 0)
    cst[:, 256:384] = (p[:, None] <= i[None, :])
    cst[:, 384:512] = (p[:, None] > i[None, :])
    cst[:, 512] = 127 - p
    cst[:, 513] = p
    cst[:, 514] = 128.0
    shared = dict(
        cst=cst,
        w_ada=np.ascontiguousarray(w['w_ada'][0]), b_ada_fm=np.ascontiguousarray(w['b_ada'][0].reshape(48, 128).T),
        b_ada_row=np.ascontiguousarray(w['b_ada'][0].reshape(1, -1)),
        w_in=np.ascontiguousarray(w['w_in'][0]), w_uq=np.ascontiguousarray(w['w_uq'][0]),
        w_ukv=np.ascontiguousarray(w['w_ukv'][0]), w_o=np.ascontiguousarray(w['w_o'][0]),
        w_up=np.ascontiguousarray(w['w_up'][0]), w_down=np.ascontiguousarray(w['w_down'][0]),
        dec=np.concatenate([w['ret_decay_f'][0], w['ret_decay_b'][0]]).reshape(1, 8).astype(np.float32),
        gnw=np.ascontiguousarray(w['ret_gn_w'][0].reshape(1, 512)),
        qnw_fm=np.ascontiguousarray(w['q_norm_w'][0].reshape(3, 128).T),
        kvnw_fm=np.ascontiguousarray(w['kv_norm_w'][0].reshape(2, 128).T),
        ln=np.ascontiguousarray(np.stack([w['ln1_w'][0], w['ln1_b'][0], w['ln2_w'][0], w['ln2_b'][0]])),
    )
    in_maps = []
    for core, jobs in enumerate(core_jobs):
        m = dict(shared)
        cT = np.zeros((128, 8, NJ), np.float32)
        mk = np.zeros((128, NJ, 2, 128), np.float32)
        ex = np.zeros((128, NJ, 2, 128), np.float32)
        for j, (bi, own) in enumerate(jobs):
            NS = NSLOT[j]
            S = NS * L
            pos = (np.arange(S) + own * L) % S
            m["x%d" % j] = np.ascontiguousarray(np.roll(xs[j][bi], -own * L, axis=0))
            cR, sR = _rope_tables(pos, 128)
            m["tb%d" % j] = np.ascontiguousarray(np.concatenate([cR, cR, sR, sR], axis=1))
            cM, sM = _rope_tables(pos, 64)
            tm = np.stack([np.concatenate([cM, cM], 1).T, np.concatenate([sM, sM], 1).T], axis=1)
            m["tm%d" % j] = np.ascontiguousarray(tm.astype(np.float32))
            cT[:, :, j] = cs[j][bi].reshape(8, 128).T
            for s in range(1, NS):
                before = (own + s) >= NS
                for n in range(NCH):
                    t = s * NCH + n
                    ex[:, j, 0, t] = (NS - s - 1) * L + (L - 128 - 128 * n)
                    ex[:, j, 1, t] = (s - 1) * L + 128 * n
                    mk[:, j, 0, t] = 1.0 if before else 0.0
                    mk[:, j, 1, t] = 0.0 if before else 1.0
        m["cT"] = cT
        m["mk"] = mk
        m["ex"] = ex
        in_maps.append(m)
    return in_maps


_CACHE = {}


def kernel(x_prompt, x_sample, c_prompt, c_sample, **w):
    L = 4096
    NSLOT = [4, 2]
    x_prompt = np.asarray(x_prompt, np.float32)
    x_sample = np.asarray(x_sample, np.float32)
    c_prompt = np.asarray(c_prompt, np.float32)
    c_sample = np.asarray(c_sample, np.float32)
    w = {k: np.asarray(v, np.float32) for k, v in w.items()}
    core_jobs = [[(c // 4, c % 4), (c // 2, c % 2)] for c in range(NCORES)]
    in_maps = host_inputs(L, NSLOT, core_jobs, [x_prompt, x_sample], [c_prompt, c_sample], w)
    if 'nc' not in _CACHE:
        _CACHE['nc'] = build(L, NSLOT)[0]
    res = run_bass_kernel_spmd(_CACHE['nc'], in_maps, core_ids=list(range(NCORES)))
    yp = np.empty_like(x_prompt)
    ysm = np.empty_like(x_sample)
    for c in range(NCORES):
        r = res.results[c]
        yp[c // 4, (c % 4) * L:(c % 4 + 1) * L] = r["y0"]
        ysm[c // 2, (c % 2) * L:(c % 2 + 1) * L] = r["y1"]
    return (yp, ysm)
```

```python
import math
import numpy as np
import concourse.bass as bass
import concourse.mybir as mybir
from concourse.bass_utils import run_bass_kernel_spmd

F32 = mybir.dt.float32
BF16 = mybir.dt.bfloat16
AF = mybir.ActivationFunctionType
ALU = mybir.AluOpType

D = 1024
DIN = 2752
DFF = 4096
ALPHA = float(2.0 ** 0.25)
MLA_SCALE = float(192 ** -0.5)
NCORES = 8


class Prog:
    SEM_LIMIT = 24000

    def __init__(self, nc):
        self.nc = nc
        self.ops = []
        self.last_w = {}
        self.readers = {}
        self.out_dma_ops = []
        self.barriers = []

    def add(self, eng, fn, r=(), w=(), dma=False, key=None, out=False, nosync=False):
        i = len(self.ops)
        if dma and key is None:
            key = w[0]
        ps = [t for t in r if isinstance(t, tuple) and t[0] == 'pb']
        r = [t for t in r if not (isinstance(t, tuple) and t[0] == 'pb')]
        w = list(w) + [t for t in ps if t not in w]
        deps = set()
        for t in r:
            j = self.last_w.get(t)
            if j is not None:
                deps.add((j, 'raw'))
        for t in w:
            j = self.last_w.get(t)
            if j is not None:
                deps.add((j, 'waw'))
            for k in self.readers.get(t, ()):
                deps.add((k, 'war'))
        for t in r:
            self.readers.setdefault(t, []).append(i)
        for t in w:
            self.last_w[t] = i
            self.readers[t] = []
        self.ops.append(dict(eng=eng, fn=fn, deps=deps, dma=dma, key=key, bar=len(self.barriers), nosync=nosync))
        if out:
            self.out_dma_ops.append(i)
        return i

    def barrier(self):
        self.barriers.append(len(self.ops))

    def emit(self):
        nc = self.nc
        ops = self.ops
        bar_sets = []
        for b in self.barriers:
            last = {}
            for i in range(b):
                op = ops[i]
                k = ('dma', op['key']) if op['dma'] else ('eng', op['eng'])
                last[k] = i
            bar_sets.append(set(last.values()))
        for i, op in enumerate(ops):
            need = set()
            for (j, kind) in op['deps']:
                pj = ops[j]
                if j == i:
                    continue
                if not pj['dma'] and not op['dma'] and pj['eng'] == op['eng']:
                    if op['eng'] == 'pe':
                        continue
                    if kind != 'raw' or op['nosync']:
                        continue
                need.add(j)
            if op['bar'] > 0:
                for j in bar_sets[op['bar'] - 1]:
                    pj = ops[j]
                    if not pj['dma'] and pj['eng'] == op['eng']:
                        continue
                    need.add(j)
            op['need'] = need
        has_dep = [False] * len(ops)
        for op in ops:
            for j in op['need']:
                has_dep[j] = True
        for j in self.out_dma_ops:
            has_dep[j] = True
        cnt = {}
        for i, op in enumerate(ops):
            if op['dma']:
                k = ('dma', op['key'])
            elif has_dep[i]:
                k = ('eng', op['eng'])
            else:
                op['sig'] = None
                continue
            cnt[k] = cnt.get(k, 0) + 1
            op['sig'] = (k, cnt[k])
        sems = {}
        cms = []
        for k, n in cnt.items():
            inc = 16 if k[0] == 'dma' else 1
            per = self.SEM_LIMIT // inc
            ns = (n + per - 1) // per
            lst = []
            for s in range(ns):
                cm = nc.semaphore("s%d_%d" % (len(sems), s))
                lst.append(cm.__enter__())
                cms.append(cm)
            sems[k] = (lst, inc, per)
        self.n_sems = len(cms)

        def semval(sig):
            k, n = sig
            lst, inc, per = sems[k]
            return lst[(n - 1) // per], ((n - 1) % per + 1) * inc

        engs = {}
        for i, op in enumerate(ops):
            engs.setdefault(op['eng'], []).append(i)

        def run(ename, e):
            waited = {}
            for i in engs.get(ename, []):
                op = ops[i]
                tgt = {}
                for j in op['need']:
                    k, n = ops[j]['sig']
                    if waited.get(k, 0) >= n:
                        continue
                    if tgt.get(k, 0) < n:
                        tgt[k] = n
                for k, n in tgt.items():
                    s, v = semval((k, n))
                    e.wait_ge(s, v)
                    waited[k] = n
                ins = op['fn'](e)
                if op['sig'] is not None:
                    s, v = semval(op['sig'])
                    ins.then_inc(s, 16 if op['dma'] else 1)
            if ename == 'sp':
                fin = {}
                for j in self.out_dma_ops:
                    k, n = ops[j]['sig']
                    fin[k] = max(fin.get(k, 0), n)
                for k, n in fin.items():
                    if waited.get(k, 0) >= n:
                        continue
                    s, v = semval((k, n))
                    e.wait_ge(s, v)

        with nc.Block() as block:
            @block.tensor
            def _(e):
                run('pe', e)

            @block.scalar
            def _(e):
                run('act', e)

            @block.vector
            def _(e):
                run('dve', e)

            @block.gpsimd
            def _(e):
                run('pool', e)

            @block.sync
            def _(e):
                run('sp', e)
        for cm in cms:
            cm.__exit__(None, None, None)


class Arena:
    def __init__(self, nc, words):
        self.t = nc.alloc_sbuf_tensor("arena", [128, words], F32)
        self.words = words
        self.top = 0

    def mark(self):
        return self.top

    def release(self, m):
        self.top = m

    def alloc(self, shape, dtype=F32):
        n = 1
        for s in shape[1:]:
            n *= s
        nw = n if dtype == F32 else (n + 1) // 2
        nw = (nw + 7) // 8 * 8
        assert self.top + nw <= self.words, ("SBUF arena overflow", self.top, nw, self.words)
        v = self.t[:, self.top:self.top + nw]
        self.top += nw
        if dtype != F32:
            v = v.bitcast(dtype)
        v = v[:, 0:n]
        if len(shape) == 3:
            v = v.rearrange("p (a b) -> p a b", a=shape[1])
        elif len(shape) == 4:
            v = v.rearrange("p (a b c) -> p a b c", a=shape[1], b=shape[2])
        elif len(shape) == 5:
            v = v.rearrange("p (a b c d) -> p a b c d", a=shape[1], b=shape[2], c=shape[3])
        if shape[0] < 128:
            v = v[0:shape[0]]
        return v


def build(L, NSLOT, debug=False):
    NJ = len(NSLOT)
    NCH = L // 128
    SS = [ns * L for ns in NSLOT]
    nc = bass.Bass("TRN2", target_bir_lowering=False)
    P = Prog(nc)

    def din(name, shape, dt=F32):
        return nc.dram_tensor(name, list(shape), dt, kind="ExternalInput").ap()

    def dscr(name, shape, dt=BF16):
        return nc.dram_tensor(name, list(shape), dt, kind=("ExternalOutput" if debug else "Internal")).ap()

    x_in = [din("x%d" % j, [SS[j], D]) for j in range(NJ)]
    tb_in = [din("tb%d" % j, [SS[j], 256]) for j in range(NJ)]
    tm_in = [din("tm%d" % j, [64, 2, SS[j]]) for j in range(NJ)]
    cT_in = din("cT", [128, 8, NJ])
    mk_in = din("mk", [128, NJ, 2, 128])
    ex_in = din("ex", [128, NJ, 2, 128])
    cst_in = din("cst", [128, 4 * 128 + 8])
    wada_in = din("w_ada", [D, 6 * D])
    bfm_in = din("b_ada_fm", [128, 48])
    brow_in = din("b_ada_row", [1, 6 * D])
    win_in = din("w_in", [D, DIN])
    wuq_in = din("w_uq", [384, 768])
    wukv_in = din("w_ukv", [256, 1024])
    wo_in = din("w_o", [D, D])
    wup_in = din("w_up", [D, DFF])
    wdn_in = din("w_down", [DFF, D])
    dec_in = din("dec", [1, 8])
    gnw_in = din("gnw", [1, 512])
    qnw_in = din("qnw_fm", [128, 3])
    kvnw_in = din("kvnw_fm", [128, 2])
    ln_in = din("ln", [4, D])
    y_out = [nc.dram_tensor("y%d" % j, [L, D], F32, kind="ExternalOutput").ap() for j in range(NJ)]

    KTd = [dscr("KTd%d" % j, [4, 128, SS[j]]) for j in range(NJ)]
    KRd = [dscr("KRd%d" % j, [64, SS[j]]) for j in range(NJ)]
    VVd = [dscr("VVd%d" % j, [4, 128, SS[j] // 128, 128]) for j in range(NJ)]
    QTd = [dscr("QTd%d" % j, [4, 128, L]) for j in range(NJ)]
    QRd = [dscr("QRd%d" % j, [4, 64, L]) for j in range(NJ)]
    RTd = [dscr("RTd%d" % j, [4, 128, L]) for j in range(NJ)]
    ATd = [dscr("ATd%d" % j, [4, 128, L]) for j in range(NJ)]
    X1d = [dscr("X1d%d" % j, [L, D], F32) for j in range(NJ)]

    AR = Arena(nc, 49100)
    PB = [nc.alloc_psum_tensor("pb%d" % i, [128, 512], F32) for i in range(8)]

    def mm(out, lhsT, rhs, start, stop, r, w):
        P.add('pe', lambda e: e.matmul(out, lhsT=lhsT, rhs=rhs, start=start, stop=stop), r=r, w=w)

    def act(out, in_, func, r, w, bias=None, scale=None):
        kw = {}
        if bias is not None:
            kw['bias'] = bias
        if scale is not None:
            kw['scale'] = scale
        P.add('act', lambda e: e.activation(out=out, in_=in_, func=func, **kw), r=r, w=w)

    def tt(eng, out, in0, in1, op, r, w, nosync=False):
        P.add(eng, lambda e: e.tensor_tensor(out=out, in0=in0, in1=in1, op=op), r=r, w=w, nosync=nosync)

    def ts(eng, out, in0, s1, s2, op0, op1, r, w):
        if s2 is None:
            P.add(eng, lambda e: e.tensor_scalar(out=out, in0=in0, scalar1=s1, scalar2=None, op0=op0), r=r, w=w)
        else:
            P.add(eng, lambda e: e.tensor_scalar(out=out, in0=in0, scalar1=s1, scalar2=s2, op0=op0, op1=op1), r=r, w=w)

    def stt(eng, out, in0, scalar, in1, op0, op1, r, w):
        P.add(eng, lambda e: e.scalar_tensor_tensor(out=out, in0=in0, scalar=scalar, in1=in1, op0=op0, op1=op1),
              r=r, w=w)

    def cp(eng, out, in_, r, w):
        if eng == 'act':
            P.add('act', lambda e: e.copy(out=out, in_=in_), r=r, w=w)
        else:
            P.add(eng, lambda e: e.tensor_copy(out=out, in_=in_), r=r, w=w)

    def dma(q, out, in_, r, w, key=None, outp=False):
        P.add(q, lambda e: e.dma_start(out=out, in_=in_), r=r, w=w, dma=True, key=key, out=outp)

    def recip(out, in_, r, w):
        P.add('dve', lambda e: e.reciprocal(out=out, in_=in_), r=r, w=w)

    def bnstats(out, in_, r, w):
        P.add('dve', lambda e: e.bn_stats(out=out, in_=in_), r=r, w=w)

    def bnaggr(out, in_, r, w):
        P.add('dve', lambda e: e.bn_aggr(out=out, in_=in_), r=r, w=w)

    def actmul(out, in_, m, r, w):
        P.add('act', lambda e: e.mul(out=out, in_=in_, mul=m), r=r, w=w)

    def memset(eng, ap, v, w):
        P.add(eng, lambda e: e.memset(ap, v), w=w)

    ident_bf = AR.alloc([128, 128], BF16)
    ident_f = AR.alloc([128, 128], F32)
    ones_bf = AR.alloc([128, 128], BF16)
    ones_f = AR.alloc([128, 128], F32)
    cst = AR.alloc([128, 4 * 128 + 8], F32)
    dma('sp', cst, cst_in, r=[], w=['cst'])
    memset('pool', ident_f, 1.0, ['ident_f'])
    P.add('pool', lambda e: e.affine_select(out=ident_f, in_=ident_f, pattern=[[-1, 128]], compare_op=ALU.is_equal,
                                            fill=0.0, base=0, channel_multiplier=1), r=['ident_f'], w=['ident_f'])
    cp('pool', ident_bf, ident_f, ['ident_f'], ['ident_bf'])
    memset('pool', ones_f, 1.0, ['ones_f'])
    memset('pool', ones_bf, 1.0, ['ones_bf'])

    def tr(out, in_, r, w):
        k = in_.shape[0]
        P.add('pe', lambda e: e.transpose(out=out, in_=in_, identity=ident_bf[0:k, 0:k]), r=list(r) + ['ident_bf'], w=w)

    RF = cst[:, 0:128]
    RB = cst[:, 128:256]
    MF = cst[:, 256:384]
    MB = cst[:, 384:512]
    c_127mj = cst[:, 512:513]
    c_j = cst[:, 513:514]
    c_128 = cst[:, 514:515]

    dec = AR.alloc([128, 8], F32)
    dma('sp', dec, dec_in.partition_broadcast(128).rearrange("p a b -> p (a b)"), r=[], w=['dec'])
    lg = AR.alloc([128, 8], F32)
    act(lg, dec, AF.Exp, ['dec'], ['lg'], scale=-1.0)
    ts('dve', lg, lg, 1.0, None, ALU.add, None, ['lg'], ['lg'])
    act(lg, lg, AF.Ln, ['lg'], ['lg'])
    ts('dve', lg, lg, -1.0, None, ALU.mult, None, ['lg'], ['lg'])
    qnw = AR.alloc([128, 3], F32)
    kvnw = AR.alloc([128, 2], F32)
    dma('sp', qnw, qnw_in, r=[], w=['qnw'])
    dma('sp', kvnw, kvnw_in, r=[], w=['kvnw'])
    bfm = AR.alloc([128, 48], F32)
    dma('sp', bfm, bfm_in, r=[], w=['bfm'])
    cT = AR.alloc([128, 8, NJ], F32)
    dma('sp', cT, cT_in, r=[], w=['cT'])
    siluc = AR.alloc([128, 8, NJ], F32)
    act(siluc, cT, AF.Silu, ['cT'], ['siluc'])
    modfm = AR.alloc([128, 48, NJ], F32)
    scp = AR.alloc([128, 2, 8, NJ], F32)

    persist_mark = AR.mark()

    def ln_apply(src, dst, tag, rtok, wtok, eps=1e-5, eng='dve'):
        st, mv, rs = LNS[tag]
        for c in range(2):
            P.add('dve', (lambda e, c=c: e.bn_stats(out=st[:, c, :], in_=src[:, c * 512:(c + 1) * 512])),
                  r=rtok, w=[('st', tag, c)])
        P.add('dve', lambda e: e.bn_aggr(out=mv, in_=st), r=[('st', tag, 0), ('st', tag, 1)], w=[('mv', tag)])
        ts('dve', rs, mv[:, 1:2], eps, None, ALU.add, None, [('mv', tag)], [('rs', tag)])
        P.add('dve', lambda e: e.reciprocal(out=rs, in_=rs), r=[('rs', tag)], w=[('rs', tag)])
        P.add('act', lambda e: e.sqrt(out=rs, in_=rs), r=[('rs', tag)], w=[('rs', tag)])
        ts(eng, dst, src, mv[:, 0:1], rs[:, 0:1], ALU.subtract, ALU.mult, list(rtok) + [('mv', tag), ('rs', tag)], wtok)

    LNS = {}

    def ln_scratch(tag):
        LNS[tag] = (AR.alloc([128, 2, 6], F32), AR.alloc([128, 2], F32), AR.alloc([128, 1], F32))

    def rsqrt_small(x, tag, mul, eps):
        ts('dve', x, x, mul, eps, ALU.mult, ALU.add, [tag], [tag])
        P.add('dve', lambda e: e.reciprocal(out=x, in_=x), r=[tag], w=[tag])
        P.add('act', lambda e: e.sqrt(out=x, in_=x), r=[tag], w=[tag])

    for j in range(NJ):
        S = SS[j]
        NS = NSLOT[j]
        NT = S // 128
        xj, tbj, tmj = x_in[j], tb_in[j], tm_in[j]
        P.barrier()
        AR.release(persist_mark)
        gB = AR.alloc([128, 2, 1024], F32)
        job_mark = AR.mark()
        Win = AR.alloc([128, 8, DIN], BF16)
        Wkrr = AR.alloc([128, 8, 64], BF16)
        brow = AR.alloc([1, DIN], BF16)
        bcfm = AR.alloc([128, 5], F32)
        bkr = AR.alloc([64, 2], F32)
        coef = AR.alloc([128, 2, 4, 128], F32)
        Sf = AR.alloc([128, 4, 128], F32)
        Tb = AR.alloc([128, 4, 128], F32)
        Sbf = AR.alloc([128, 4, 128], BF16)
        Tsave = AR.alloc([128, NCH, 4, 128], BF16)
        DcT = AR.alloc([128, 4, 128], F32)
        dqfB = AR.alloc([128, 4, 128], F32)
        dqbB = AR.alloc([128, 4, 128], F32)
        dkk = AR.alloc([128, 8], F32)
        dcc = AR.alloc([128, 8], F32)
        tmpa = AR.alloc([128, 128], F32)
        tmpb = AR.alloc([128, 128], F32)
        rowi1 = AR.alloc([128, 128], F32)
        row128mi = AR.alloc([128, 128], F32)
        Wukv = AR.alloc([128, 2, 2, 4, 128], BF16)
        Wuq = AR.alloc([128, 3, 4, 192], BF16)
        Wuqr = AR.alloc([128, 3, 4, 64], BF16)
        prep_mark = AR.mark()
        mkt = AR.alloc([128, 2, 128], F32)
        ext = AR.alloc([128, 2, 128], F32)
        st_ukv = AR.alloc([128, 2, 1024], F32)
        st_uq = AR.alloc([128, 3, 768], F32)
        silucB = AR.alloc([128, 8, 128], F32)
        browada = AR.alloc([1, 2, 1024], F32)
        stkr = AR.alloc([128, 8, 64], F32)
        wst = [AR.alloc([128, 8, 512], F32) for _ in range(2)]
        tt('dve', rowi1, RF, RB, ALU.subtract, ['cst'], ['rowi1'])
        ts('dve', rowi1, rowi1, c_j, 1.0, ALU.add, ALU.add, ['rowi1', 'cst'], ['rowi1'])
        ts('dve', row128mi, rowi1, -1.0, 129.0, ALU.mult, ALU.add, ['rowi1'], ['row128mi'])
        for h in range(4):
            act(tmpa, RF, AF.Exp, ['cst', 'lg'], ['tmpa'], scale=lg[:, h:h + 1])
            tt('dve', tmpa, tmpa, MF, ALU.mult, ['tmpa', 'cst'], ['tmpa'])
            act(tmpb, RB, AF.Exp, ['cst', 'lg'], ['tmpb'], scale=lg[:, 4 + h:5 + h])
            tt('dve', tmpb, tmpb, MB, ALU.mult, ['tmpb', 'cst'], ['tmpb'])
            tt('dve', DcT[:, h, :], tmpa, tmpb, ALU.add, ['tmpa', 'tmpb'], ['DcT'])
            act(dqfB[:, h, :], rowi1, AF.Exp, ['rowi1', 'lg'], ['dqfB'], scale=lg[:, h:h + 1])
            act(dqbB[:, h, :], row128mi, AF.Exp, ['row128mi', 'lg'], ['dqbB'], scale=lg[:, 4 + h:5 + h])
            act(dkk[:, h:h + 1], c_127mj, AF.Exp, ['cst', 'lg'], ['dkk'], scale=lg[:, h:h + 1])
            act(dkk[:, 4 + h:5 + h], c_j, AF.Exp, ['cst', 'lg'], ['dkk'], scale=lg[:, 4 + h:5 + h])
            act(dcc[:, h:h + 1], c_128, AF.Exp, ['cst', 'lg'], ['dcc'], scale=lg[:, h:h + 1])
            act(dcc[:, 4 + h:5 + h], c_128, AF.Exp, ['cst', 'lg'], ['dcc'], scale=lg[:, 4 + h:5 + h])

        dma('sp', st_ukv, wukv_in.rearrange("(j p) n -> p j n", p=128), r=[], w=['st_ukv'])
        dma('sp', st_uq, wuq_in.rearrange("(j p) n -> p j n", p=128), r=[], w=['st_uq'])
        for cj in range(2):
            v = st_ukv[:, cj, :].rearrange("p (h t d) -> p t h d", h=4, t=2)
            ts('dve', Wukv[:, cj], v, kvnw[:, cj:cj + 1], None, ALU.mult, None, ['st_ukv', 'kvnw'], ['Wukv'])
        for cj in range(3):
            v = st_uq[:, cj, :].rearrange("p (h d) -> p h d", h=4)
            ts('dve', Wuq[:, cj], v, qnw[:, cj:cj + 1], MLA_SCALE, ALU.mult, ALU.mult, ['st_uq', 'qnw'], ['Wuq'])
            ts('dve', Wuqr[:, cj, :, 0:32], v[:, :, 160:192], qnw[:, cj:cj + 1], -MLA_SCALE, ALU.mult, ALU.mult,
               ['st_uq', 'qnw'], ['Wuqr'])
            ts('dve', Wuqr[:, cj, :, 32:64], v[:, :, 128:160], qnw[:, cj:cj + 1], MLA_SCALE, ALU.mult, ALU.mult,
               ['st_uq', 'qnw'], ['Wuqr'])

        for k in range(8):
            cp('dve', silucB[:, k, :], siluc[:, k, j:j + 1].broadcast_to([128, 128]), ['siluc'], ['silucB'])
        dma('sp', browada[:, 0, :], brow_in[:, 2048:3072], r=[], w=['browada'])
        dma('sp', browada[:, 1, :], brow_in[:, 5120:6144], r=[], w=['browada'])
        gi = 0
        for g in range(12):
            b = gi % 2
            gi += 1
            dma('sp', wst[b], wada_in[:, g * 512:(g + 1) * 512].rearrange("(k p) n -> p k n", p=128),
                r=[], w=[('wst', b)])
            if g in (4, 5, 10, 11):
                which = 0 if g < 6 else 1
                half = g % 2 if g < 6 else (g - 10)
                for k in range(8):
                    mm(PB[0][:, :], silucB[:, k, :], wst[b][:, k, :], k == 0, False, ['silucB', ('wst', b)], [('pb', 0)])
                mm(PB[0][:, :], ones_f[0:1, :], browada[0:1, which, half * 512:(half + 1) * 512], False, True,
                   ['ones_f', 'browada'], [('pb', 0)])
                cp('act', gB[:, which, half * 512:(half + 1) * 512], PB[0][:, :], [('pb', 0)], ['gB'])
            else:
                for c in range(4):
                    ch = g * 4 + c
                    for k in range(8):
                        mm(PB[1][:, 0:1], wst[b][:, k, c * 128:(c + 1) * 128], siluc[:, k, j:j + 1], k == 0, k == 7,
                           [('wst', b), 'siluc'], [('pb', 1)])
                    ts('dve', modfm[:, ch, j:j + 1], PB[1][:, 0:1], bfm[:, ch:ch + 1], None, ALU.add, None,
                       [('pb', 1), 'bfm'], ['modfm'])
        for k in range(8):
            ts('dve', scp[:, 0, k, j:j + 1], modfm[:, 8 + k, j:j + 1], 1.0, None, ALU.add, None, ['modfm'], ['scp'])
            ts('dve', scp[:, 1, k, j:j + 1], modfm[:, 32 + k, j:j + 1], 1.0, None, ALU.add, None, ['modfm'], ['scp'])

        groups = [(0, 512), (512, 512), (1024, 512), (1536, 512), (2048, 512), (2560, 192)]
        for (c0, cw) in groups:
            b = gi % 2
            gi += 1
            dma('sp', wst[b][:, :, 0:cw], win_in[:, c0:c0 + cw].rearrange("(k p) n -> p k n", p=128),
                r=[], w=[('wst', b)])
            for k in range(8):
                mm(PB[0][0:1, 0:cw], modfm[:, k, j:j + 1], wst[b][:, k, 0:cw], k == 0, k == 7,
                   ['modfm', ('wst', b)], [('pb', 0)])
            if c0 == 512:
                actmul(brow[0:1, c0:c0 + cw], PB[0][0:1, 0:cw], float(128 ** -0.5), [('pb', 0)], ['brow'])
            else:
                cp('act', brow[0:1, c0:c0 + cw], PB[0][0:1, 0:cw], [('pb', 0)], ['brow'])
            if c0 == 2048:
                for c in range(4):
                    for k in range(8):
                        mm(PB[1][:, 0:1], wst[b][:, k, c * 128:(c + 1) * 128], modfm[:, k, j:j + 1], k == 0, k == 7,
                           [('wst', b), 'modfm'], [('pb', 1)])
                    cp('dve', bcfm[:, c:c + 1], PB[1][:, 0:1], [('pb', 1)], ['bcfm'])
            if c0 == 2560:
                for k in range(8):
                    mm(PB[1][:, 0:1], wst[b][:, k, 0:128], modfm[:, k, j:j + 1], k == 0, k == 7,
                       [('wst', b), 'modfm'], [('pb', 1)])
                cp('dve', bcfm[:, 4:5], PB[1][:, 0:1], [('pb', 1)], ['bcfm'])
                ts('dve', stkr[:, :, 0:32], wst[b][:, :, 160:192], -1.0, None, ALU.mult, None, [('wst', b)], ['stkr'])
                cp('dve', stkr[:, :, 32:64], wst[b][:, :, 128:160], [('wst', b)], ['stkr'])
                for k in range(8):
                    mm(PB[1][0:64, 0:1], wst[b][:, k, 128:192], modfm[:, k, j:j + 1], k == 0, k == 7,
                       [('wst', b), 'modfm'], [('pb', 1)])
                cp('dve', bkr[:, 0:1], PB[1][0:64, 0:1], [('pb', 1)], ['bkr'])
                for k in range(8):
                    mm(PB[1][0:64, 0:1], stkr[:, k, :], modfm[:, k, j:j + 1], k == 0, k == 7,
                       ['stkr', 'modfm'], [('pb', 1)])
                cp('dve', bkr[:, 1:2], PB[1][0:64, 0:1], [('pb', 1)], ['bkr'])
                for k in range(8):
                    ts('dve', Wkrr[:, k, :], stkr[:, k, :], scp[:, 0, k, j:j + 1], None, ALU.mult, None,
                       ['stkr', 'scp'], ['Wkrr'])
            for k in range(8):
                if c0 == 512:
                    ts('dve', Win[:, k, c0:c0 + cw], wst[b][:, k, 0:cw], scp[:, 0, k, j:j + 1], float(128 ** -0.5),
                       ALU.mult, ALU.mult, [('wst', b), 'scp'], ['Win'])
                else:
                    ts('dve', Win[:, k, c0:c0 + cw], wst[b][:, k, 0:cw], scp[:, 0, k, j:j + 1], None, ALU.mult, None,
                       [('wst', b), 'scp'], ['Win'])
        dma('sp', mkt, mk_in[:, j], r=[], w=['mkt'])
        dma('sp', ext, ex_in[:, j], r=[], w=['ext'])
        for dr in range(2):
            for h in range(4):
                act(coef[:, dr, h, :], ext[:, dr, :], AF.Exp, ['ext', 'lg'], ['coef'], scale=lg[:, 4 * dr + h:4 * dr + h + 1])
                tt('dve', coef[:, dr, h, :], coef[:, dr, h, :], mkt[:, dr, :], ALU.mult, ['coef', 'mkt'], ['coef'])
        memset('pool', Sf, 0.0, ['Sf'])
        memset('pool', Tb, 0.0, ['Tb'])
        P.barrier()
        AR.release(prep_mark)
        ab_mark = AR.mark()

        xs = [AR.alloc([128, D], F32) for _ in range(2)]
        tbt = [AR.alloc([128, 256], F32) for _ in range(2)]
        tmt = [AR.alloc([64, 2, 128], F32) for _ in range(2)]
        xn = AR.alloc([128, D], BF16)
        xT = AR.alloc([128, 8, 128], BF16)
        rA = AR.alloc([128, 4, 128], F32)
        rBt = AR.alloc([128, 4, 128], F32)
        rstd_t = AR.alloc([128, 1], F32)
        rB = AR.alloc([128, 128], F32)
        aonly_mark = AR.mark()
        kr1 = AR.alloc([64, 128], F32)
        kr2 = AR.alloc([64, 128], F32)
        pT = PB[0][:, :].bitcast(BF16).rearrange("p (a b) -> p a b", a=8)[:, :, 0:128]

        def load(t, b):
            dma('sp', xs[b], xj[t * 128:(t + 1) * 128, :], r=[], w=[('xs', b)])
            dma('sp', tbt[b], tbj[t * 128:(t + 1) * 128, :], r=[], w=[('tbt', b)])
            dma('sp', tmt[b], tmj[:, :, t * 128:(t + 1) * 128], r=[], w=[('tmt', b)])

        def proj_tok(pb, tok, c0, cw):
            for k in range(8):
                mm(pb[:, 0:cw], xT[:, k, :], Win[:, k, c0:c0 + cw], k == 0, False, ['xT', 'Win'], [tok])
            mm(pb[:, 0:cw], ones_bf[0:1, :], brow[0:1, c0:c0 + cw], False, True, ['ones_bf', 'brow'], [tok])

        def rotary_tok(pb, tok, b, dst, dtok):
            v = pb[:, :].rearrange("p (h d) -> p h d", h=4)
            cosB = tbt[b][:, 0:128].unsqueeze(1).broadcast_to([128, 4, 128])
            sinB = tbt[b][:, 128:256].unsqueeze(1).broadcast_to([128, 4, 128])
            tt('dve', rA, v, cosB, ALU.mult, [tok, ('tbt', b)], ['rA'])
            tt('dve', rBt, v, sinB, ALU.mult, [tok, ('tbt', b)], ['rBt'])
            tt('pool', dst[:, :, 0:64], rA[:, :, 0:64], rBt[:, :, 64:128], ALU.subtract, ['rA', 'rBt'], [dtok])
            tt('pool', dst[:, :, 64:128], rBt[:, :, 0:64], rA[:, :, 64:128], ALU.add, ['rA', 'rBt'], [dtok])

        Kp2 = [AR.alloc([128, 4, 128], BF16) for _ in range(2)]
        Kwf2 = [AR.alloc([128, 4, 128], BF16) for _ in range(2)]
        Kwb2 = [AR.alloc([128, 4, 128], BF16) for _ in range(2)]
        Vr2 = [AR.alloc([128, 512], BF16) for _ in range(2)]
        sqv2 = [AR.alloc([128, 2, 128], BF16) for _ in range(2)]
        ckv2 = [AR.alloc([128, 2, 128], BF16) for _ in range(2)]
        KTb2 = [AR.alloc([128, 4, 512], BF16) for _ in range(2)]
        KRb2 = [AR.alloc([64, 512], BF16) for _ in range(2)]
        VMb2 = [AR.alloc([128, 4, 4, 128], BF16) for _ in range(2)]
        ckt = [AR.alloc([128, 2, 4], F32) for _ in range(2)]
        dkfB = dkk[:, 0:4].unsqueeze(2).broadcast_to([128, 4, 128])
        dkbB = dkk[:, 4:8].unsqueeze(2).broadcast_to([128, 4, 128])

        xnA = [xn, AR.alloc([128, D], BF16)]
        xTA = [xT, AR.alloc([128, 8, 128], BF16)]
        ln_scratch('a0')
        ln_scratch('a1')

        def lnA(t):
            b = t % 2
            ln_apply(xs[b], xnA[b], 'a%d' % b, [('xs', b)], [('xnA', b)])

        def trA(t):
            b = t % 2
            for k in range(8):
                tr(pT[:, k, :], xnA[b][:, k * 128:(k + 1) * 128], [('xnA', b)], [('pb', 0)])
            cp('act', xTA[b], pT, [('pb', 0)], [('xTA', b)])

        def projA(t):
            b = t % 2
            xTb = xTA[b]
            xtok = ('xTA', b)
            for (pb_, c0) in ((1, 512), (2, 1024)):
                for k in range(8):
                    mm(PB[pb_][:, :], xTb[:, k, :], Win[:, k, c0:c0 + 512], k == 0, False, [xtok, 'Win'], [('pb', pb_)])
                mm(PB[pb_][:, :], ones_bf[0:1, :], brow[0:1, c0:c0 + 512], False, True, ['ones_bf', 'brow'], [('pb', pb_)])
            for cj in range(2):
                for k in range(8):
                    mm(PB[3][:, cj * 128:(cj + 1) * 128], Win[:, k, 2432 + cj * 128:2432 + (cj + 1) * 128], xTb[:, k, :],
                       k == 0, k == 7, ['Win', xtok], [('pb', 3)])
            for k in range(8):
                mm(PB[3][0:64, 256:384], Win[:, k, 2688:2752], xTb[:, k, :], k == 0, k == 7, ['Win', xtok], [('pb', 3)])
            for k in range(8):
                mm(PB[3][0:64, 384:512], Wkrr[:, k, :], xTb[:, k, :], k == 0, k == 7, ['Wkrr', xtok], [('pb', 3)])

        def firstA(t):
            b = t % 2
            s = t // NCH
            q4 = t % 4
            bp = (t // 4) % 2
            rotary_tok(PB[1], ('pb', 1), b, Kp2[b], ('Kp2', b))
            cp('act', Vr2[b], PB[2][:, :], [('pb', 2)], [('Vr2', b)])
            ckvp = PB[3][:, 0:256].rearrange("p (a b) -> p a b", a=2)
            for cj in range(2):
                act(sqv2[b][:, cj, :], ckvp[:, cj, :], AF.Square, [('pb', 3), 'bcfm'], [('sqv2', b)], bias=bcfm[:, 3 + cj:4 + cj])
                act(ckv2[b][:, cj, :], ckvp[:, cj, :], AF.Identity, [('pb', 3), 'bcfm'], [('ckv2', b)], bias=bcfm[:, 3 + cj:4 + cj])
            stt('dve', kr1, PB[3][0:64, 256:384], bkr[:, 0:1], tmt[b][:, 0, :], ALU.add, ALU.mult,
                [('pb', 3), 'bkr', ('tmt', b)], ['kr1'])
            stt('dve', kr2, PB[3][0:64, 384:512], bkr[:, 1:2], tmt[b][:, 1, :], ALU.add, ALU.mult,
                [('pb', 3), 'bkr', ('tmt', b)], ['kr2'])
            tt('pool', KRb2[bp][:, q4 * 128:(q4 + 1) * 128], kr1, kr2, ALU.add, ['kr1', 'kr2'], [('KRb2', bp)])
            if s > 0:
                tt('dve', ckt[b], dkk.rearrange("p (a h) -> p a h", a=2), coef[:, :, :, t], ALU.mult, ['dkk', 'coef'], [('ckt', b)])
                tt('pool', Kwf2[b], Kp2[b], ckt[b][:, 0, :].unsqueeze(2).broadcast_to([128, 4, 128]), ALU.mult,
                   [('Kp2', b), ('ckt', b)], [('Kwf2', b)])
                tt('pool', Kwb2[b], Kp2[b], ckt[b][:, 1, :].unsqueeze(2).broadcast_to([128, 4, 128]), ALU.mult,
                   [('Kp2', b), ('ckt', b)], [('Kwb2', b)])
            else:
                tt('pool', Kwb2[b], Kp2[b], dkbB, ALU.mult, [('Kp2', b), 'dkk'], [('Kwb2', b)])

        def secondA(t):
            b = t % 2
            s = t // NCH
            n = t % NCH
            q4 = t % 4
            bp = (t // 4) % 2
            for cj in range(2):
                mm(PB[4][:, 0:1], sqv2[b][:, cj, :], ones_bf[:, 0:1], cj == 0, cj == 1, [('sqv2', b), 'ones_bf'], [('pb', 4)])
            for cj in range(2):
                mm(PB[4][:, 128:256], ones_bf, sqv2[b][:, cj, :], cj == 0, cj == 1, ['ones_bf', ('sqv2', b)], [('pb', 4)])
            cp('dve', rstd_t, PB[4][:, 0:1], [('pb', 4)], ['rstd_t'])
            cp('dve', rB, PB[4][:, 128:256], [('pb', 4)], ['rB'])
            rsqrt_small(rstd_t, 'rstd_t', 1.0 / 256.0, 1e-6)
            rsqrt_small(rB, 'rB', 1.0 / 256.0, 1e-6)
            Pst = [PB[6][:, :].rearrange("p (h e) -> p h e", h=4), PB[7][:, :].rearrange("p (h e) -> p h e", h=4)]
            if s > 0:
                first_t = (t == NT - 1)
                last_t = (t == NCH)
                for h in range(4):
                    mm(Pst[1][:, h, :], Kwb2[b][:, h, :], Vr2[b][:, h * 128:(h + 1) * 128], first_t and h == 0, last_t,
                       [('Kwb2', b), ('Vr2', b)], [('pb', 7)])
                for h in range(4):
                    mm(Pst[0][:, h, :], Kwf2[b][:, h, :], Vr2[b][:, h * 128:(h + 1) * 128], first_t and h == 0, last_t,
                       [('Kwf2', b), ('Vr2', b)], [('pb', 6)])
            else:
                for h in range(4):
                    mm(Pst[1][:, h, :], Kwb2[b][:, h, :], Vr2[b][:, h * 128:(h + 1) * 128], True, True,
                       [('Kwb2', b), ('Vr2', b)], [('pb', 7)])
            knp = PB[5][:, :].rearrange("p (h d) -> p h d", h=4)
            for h in range(4):
                for cj in range(2):
                    mm(knp[:, h, :], Wukv[:, cj, 0, h, :], ckv2[b][:, cj, :], cj == 0, cj == 1, ['Wukv', ('ckv2', b)], [('pb', 5)])
            for cj in range(2):
                mm(PB[4][:, :], ckv2[b][:, cj, :], Wukv[:, cj, 1].rearrange("p h d -> p (h d)"), cj == 0, cj == 1,
                   [('ckv2', b), 'Wukv'], [('pb', 4)])
            if s > 0:
                if t == NCH:
                    cp('dve', Sf, Pst[0], [('pb', 6)], ['Sf'])
                    cp('dve', Tb, Pst[1], [('pb', 7)], ['Tb'])
            else:
                cp('act', Tsave[:, n], Tb, ['Tb'], ['Tsave'])
                for h in range(4):
                    stt('dve', Tb[:, h, :], Tb[:, h, :], dcc[:, 4 + h:5 + h], Pst[1][:, h, :], ALU.mult, ALU.add,
                        [('pb', 7), 'dcc', 'Tb'], ['Tb'])
            tt('dve', KTb2[bp][:, :, q4 * 128:(q4 + 1) * 128], knp, rB.unsqueeze(1).broadcast_to([128, 4, 128]), ALU.mult,
               [('pb', 5), 'rB'], [('KTb2', bp)])
            act(VMb2[bp][:, :, q4, :], PB[4][:, :].rearrange("p (h d) -> p h d", h=4), AF.Identity, [('pb', 4), 'rstd_t'],
                [('VMb2', bp)], scale=rstd_t[:, 0:1])
            if q4 == 0:
                t0 = t
                dma('sp', KTd[j][:, :, t0 * 128:(t0 + 4) * 128].rearrange("h p n -> p h n"), KTb2[bp], r=[('KTb2', bp)], w=[('KTd', j)])
                dma('sp', KRd[j][:, t0 * 128:(t0 + 4) * 128], KRb2[bp], r=[('KRb2', bp)], w=[('KRd', j)])
                dma('sp', VVd[j][:, :, t0:t0 + 4, :].rearrange("h p t e -> p h t e"), VMb2[bp], r=[('VMb2', bp)], w=[('VVd', j)])

        load(NT - 1, (NT - 1) % 2)
        load(NT - 2, (NT - 2) % 2)
        lnA(NT - 1)
        trA(NT - 1)
        projA(NT - 1)
        lnA(NT - 2)
        firstA(NT - 1)
        trA(NT - 2)
        for t in range(NT - 1, -1, -1):
            if t - 1 >= 0:
                projA(t - 1)
                if t - 2 >= 0:
                    load(t - 2, t % 2)
                    lnA(t - 2)
                firstA(t - 1)
                if t - 2 >= 0:
                    trA(t - 2)
            secondA(t)

        P.barrier()
        AR.release(aonly_mark)
        QT3 = AR.alloc([128, 3, 4, 128], BF16)
        KT = AR.alloc([128, 4, 128], BF16)
        AcT = AR.alloc([128, 4, 128], BF16)
        gnB = AR.alloc([128, 512], F32)
        dma('sp', gnB, gnw_in.partition_broadcast(128).rearrange("p a b -> p (a b)"), r=[], w=['gnB'])
        st4 = AR.alloc([128, 4, 6], F32)
        mv4 = AR.alloc([128, 4, 2], F32)
        rs4 = AR.alloc([128, 4], F32)
        retb = AR.alloc([128, 512], BF16)
        RTb = AR.alloc([128, 4, 512], BF16)
        QNb = AR.alloc([128, 4, 512], BF16)
        QRb = AR.alloc([64, 4, 512], BF16)
        qr1 = AR.alloc([64, 4, 128], F32)
        qr2 = AR.alloc([64, 4, 128], F32)
        xnB = [xn, AR.alloc([128, D], BF16)]
        xTB = [xT, AR.alloc([128, 8, 128], BF16)]
        ln_scratch('b0')
        ln_scratch('b1')
        tmt3 = [AR.alloc([64, 2, 128], F32) for _ in range(3)]
        Qp2 = [AR.alloc([128, 4, 128], BF16) for _ in range(2)]
        KpB = [AR.alloc([128, 4, 128], BF16) for _ in range(2)]
        VrB = [AR.alloc([128, 512], BF16) for _ in range(2)]
        gate2 = [AR.alloc([128, 512], BF16) for _ in range(2)]
        KwfB = [AR.alloc([128, 4, 128], BF16) for _ in range(2)]
        sqB = [AR.alloc([128, 3, 128], BF16) for _ in range(2)]
        cqB = [AR.alloc([128, 3, 128], BF16) for _ in range(2)]
        dkfB2 = dkk[:, 0:4].unsqueeze(2).broadcast_to([128, 4, 128])
        y0 = [AR.alloc([128, 4, 128], F32) for _ in range(2)]
        retb2 = [retb, AR.alloc([128, 512], BF16)]

        def loadB(c):
            b = c % 2
            dma('sp', xs[b], xj[c * 128:(c + 1) * 128, :], r=[], w=[('xs', b)])
            dma('sp', tbt[b], tbj[c * 128:(c + 1) * 128, :], r=[], w=[('tbt', b)])
            dma('sp', tmt3[c % 3], tmj[:, :, c * 128:(c + 1) * 128], r=[], w=[('tmt3', c % 3)])

        def lnB_(c):
            b = c % 2
            ln_apply(xs[b], xnB[b], 'b%d' % b, [('xs', b)], [('xnB', b)])

        def trB(c):
            b = c % 2
            for k in range(8):
                tr(pT[:, k, :], xnB[b][:, k * 128:(k + 1) * 128], [('xnB', b)], [('pb', 0)])
            cp('act', xTB[b], pT, [('pb', 0)], [('xTB', b)])

        def projB(c):
            b = c % 2
            xTb = xTB[b]
            xtok = ('xTB', b)
            for (pb_, c0) in ((1, 0), (2, 512), (3, 1024), (4, 1536)):
                for k in range(8):
                    mm(PB[pb_][:, :], xTb[:, k, :], Win[:, k, c0:c0 + 512], k == 0, False, [xtok, 'Win'], [('pb', pb_)])
                mm(PB[pb_][:, :], ones_bf[0:1, :], brow[0:1, c0:c0 + 512], False, True, ['ones_bf', 'brow'], [('pb', pb_)])
            for cj in range(3):
                for k in range(8):
                    mm(PB[5][:, cj * 128:(cj + 1) * 128], Win[:, k, 2048 + cj * 128:2048 + (cj + 1) * 128], xTb[:, k, :],
                       k == 0, k == 7, ['Win', xtok], [('pb', 5)])

        def evacB(c):
            b = c % 2
            rotary_tok(PB[1], ('pb', 1), b, Qp2[b], ('Qp2', b))
            rotary_tok(PB[2], ('pb', 2), b, KpB[b], ('KpB', b))
            cp('act', VrB[b], PB[3][:, :], [('pb', 3)], [('VrB', b)])
            for cj in range(3):
                act(sqB[b][:, cj, :], PB[5][:, cj * 128:(cj + 1) * 128], AF.Square, [('pb', 5), 'bcfm'], [('sqB', b)],
                    bias=bcfm[:, cj:cj + 1])
                act(cqB[b][:, cj, :], PB[5][:, cj * 128:(cj + 1) * 128], AF.Identity, [('pb', 5), 'bcfm'], [('cqB', b)],
                    bias=bcfm[:, cj:cj + 1])
            act(gate2[b], PB[4][:, :], AF.Silu, [('pb', 4)], [('gate2', b)])
            tt('pool', gate2[b], gate2[b], gnB, ALU.mult, [('gate2', b), 'gnB'], [('gate2', b)])
            tt('pool', KwfB[b], KpB[b], dkfB2, ALU.mult, [('KpB', b), 'dkk'], [('KwfB', b)])

        def secondB(c):
            b = c % 2
            q4 = c % 4
            tm_ = tmt3[c % 3]
            tmtok = ('tmt3', c % 3)
            pQK = PB[0][:, :].bitcast(BF16).rearrange("p (a b) -> p a b", a=8)[:, :, 0:128]
            for h in range(4):
                tr(pQK[:, h, :], Qp2[b][:, h, :], [('Qp2', b)], [('pb', 0)])
                tr(pQK[:, 4 + h, :], KpB[b][:, h, :], [('KpB', b)], [('pb', 0)])
            cp('act', QT3[:, 0], pQK[:, 0:4, :], [('pb', 0)], ['QT0'])
            cp('act', KT, pQK[:, 4:8, :], [('pb', 0)], ['KT'])
            tt('dve', QT3[:, 1], pQK[:, 0:4, :], dqfB, ALU.mult, [('pb', 0), 'dqfB'], ['QT1'])
            tt('dve', QT3[:, 2], pQK[:, 0:4, :], dqbB, ALU.mult, [('pb', 0), 'dqbB'], ['QT2'])
            for cj in range(3):
                mm(PB[5][:, 128:256], ones_bf, sqB[b][:, cj, :], cj == 0, cj == 2, ['ones_bf', ('sqB', b)], [('pb', 5)])
            Ap = PB[6][:, :].rearrange("p (h d) -> p h d", h=4)
            for h in range(4):
                mm(Ap[:, h, :], KT[:, h, :], QT3[:, 0, h, :], True, True, ['KT', 'QT0'], [('pb', 6)])
            cp('dve', rB, PB[5][:, 128:256], [('pb', 5)], ['rB'])
            rsqrt_small(rB, 'rB', 1.0 / 384.0, 1e-6)
            tt('dve', AcT, Ap, DcT, ALU.mult, [('pb', 6), 'DcT'], ['AcT'])
            cp('act', Sbf, Sf, ['Sf'], ['Sbf'])
            qnp = PB[5][:, :].rearrange("p (h d) -> p h d", h=4)
            for h in range(4):
                for cj in range(3):
                    mm(qnp[:, h, :], Wuq[:, cj, h, 0:128], cqB[b][:, cj, :], cj == 0, cj == 2, ['Wuq', ('cqB', b)], [('pb', 5)])
            qrp = PB[1][0:64, :].rearrange("p (h d) -> p h d", h=4)
            qrrp = PB[2][0:64, :].rearrange("p (h d) -> p h d", h=4)
            for h in range(4):
                for cj in range(3):
                    mm(qrp[:, h, :], Wuq[:, cj, h, 128:192], cqB[b][:, cj, :], cj == 0, cj == 2, ['Wuq', ('cqB', b)], [('pb', 1)])
                for cj in range(3):
                    mm(qrrp[:, h, :], Wuqr[:, cj, h, :], cqB[b][:, cj, :], cj == 0, cj == 2, ['Wuqr', ('cqB', b)], [('pb', 2)])
            Op = PB[7][:, :].rearrange("p (h d) -> p h d", h=4)
            for h in range(4):
                mm(Op[:, h, :], AcT[:, h, :], VrB[b][:, h * 128:(h + 1) * 128], True, False, ['AcT', ('VrB', b)], [('pb', 7)])
                mm(Op[:, h, :], QT3[:, 1, h, :], Sbf[:, h, :], False, False, ['QT1', 'Sbf'], [('pb', 7)])
                mm(Op[:, h, :], QT3[:, 2, h, :], Tsave[:, c, h, :], False, True, ['QT2', 'Tsave'], [('pb', 7)])
            Pf = PB[6][:, :].rearrange("p (h d) -> p h d", h=4)
            for h in range(4):
                mm(Pf[:, h, :], KwfB[b][:, h, :], VrB[b][:, h * 128:(h + 1) * 128], True, True, [('KwfB', b), ('VrB', b)], [('pb', 6)])
            tt('dve', QNb[:, :, q4 * 128:(q4 + 1) * 128], qnp, rB.unsqueeze(1).broadcast_to([128, 4, 128]), ALU.mult,
               [('pb', 5), 'rB'], ['QNb'])
            tt('dve', qr1, qrp, tm_[:, 0, :].unsqueeze(1).broadcast_to([64, 4, 128]), ALU.mult, [('pb', 1), tmtok], ['qr1'])
            tt('dve', qr2, qrrp, tm_[:, 1, :].unsqueeze(1).broadcast_to([64, 4, 128]), ALU.mult, [('pb', 2), tmtok], ['qr2'])
            tt('pool', qr1, qr1, qr2, ALU.add, ['qr1', 'qr2'], ['qr1'])
            tt('pool', QRb[:, :, q4 * 128:(q4 + 1) * 128], qr1, rB[0:64, :].unsqueeze(1).broadcast_to([64, 4, 128]),
               ALU.mult, ['qr1', 'rB'], ['QRb'])
            for h in range(4):
                stt('dve', Sf[:, h, :], Sf[:, h, :], dcc[:, h:h + 1], Pf[:, h, :], ALU.mult, ALU.add,
                    [('pb', 6), 'dcc', 'Sf', 'Sbf'], ['Sf'])
            cp('act', y0[b], Op, [('pb', 7)], [('y0', b)])
            if q4 == 3:
                t0 = c - 3
                dma('sp', QTd[j][:, :, t0 * 128:(t0 + 4) * 128].rearrange("h p n -> p h n"), QNb, r=['QNb'], w=[('QTd', j)])
                dma('sp', QRd[j][:, :, t0 * 128:(t0 + 4) * 128].rearrange("h p n -> p h n"), QRb, r=['QRb'], w=[('QRd', j)])

        def thirdB_vec(c):
            b = c % 2
            for h in range(4):
                bnstats(st4[:, h, :], y0[b][:, h, :], [('y0', b)], ['st4'])
            for h in range(4):
                bnaggr(mv4[:, h, :], st4[:, h, :], ['st4'], ['mv4'])
            cp('dve', rs4, mv4[:, :, 1], ['mv4'], ['rs4'])
            rsqrt_small(rs4, 'rs4', 1.0, 1e-5)
            tt('dve', y0[b], y0[b], mv4[:, :, 0:1].broadcast_to([128, 4, 128]), ALU.subtract, [('y0', b), 'mv4'], [('y0', b)])
            tt('pool', y0[b], y0[b], rs4.unsqueeze(2).broadcast_to([128, 4, 128]), ALU.mult, [('y0', b), 'rs4'], [('y0', b)])
            tt('pool', retb2[b], y0[b].rearrange("p h d -> p (h d)"), gate2[b], ALU.mult, [('y0', b), ('gate2', b)], [('retb2', b)])

        def thirdB_pe(c):
            b = c % 2
            q4 = c % 4
            pR = PB[0][:, :].bitcast(BF16).rearrange("p (a b) -> p a b", a=8)[:, :, 0:128]
            for h in range(4):
                tr(pR[:, h, :], retb2[b][:, h * 128:(h + 1) * 128], [('retb2', b)], [('pb', 0)])
            cp('act', RTb[:, :, q4 * 128:(q4 + 1) * 128], pR[:, 0:4, :], [('pb', 0)], ['RTb'])
            if q4 == 3:
                t0 = c - 3
                dma('sp', RTd[j][:, :, t0 * 128:(t0 + 4) * 128].rearrange("h p n -> p h n"), RTb, r=['RTb'], w=[('RTd', j)])

        loadB(0)
        loadB(1)
        lnB_(0)
        trB(0)
        projB(0)
        lnB_(1)
        evacB(0)
        trB(1)
        for c in range(NCH):
            if c >= 1:
                thirdB_vec(c - 1)
            if c + 1 < NCH:
                projB(c + 1)
                if c + 2 < NCH:
                    loadB(c + 2)
                    lnB_(c + 2)
                evacB(c + 1)
                if c + 2 < NCH:
                    trB(c + 2)
            secondB(c)
            if c >= 1:
                thirdB_pe(c - 1)
        thirdB_vec(NCH - 1)
        thirdB_pe(NCH - 1)

        P.barrier()
        AR.release(job_mark)
        KTh = AR.alloc([128, S], BF16)
        KRh = AR.alloc([128, S], BF16)
        Vh = AR.alloc([128, NT, 128], BF16)
        QTh = [AR.alloc([128, L], BF16) for _ in range(2)]
        QRh = [AR.alloc([128, L], BF16) for _ in range(2)]
        memset('pool', KRh, 0.0, ['KRh'])
        memset('pool', QRh[0], 0.0, [('QRh', 0)])
        memset('pool', QRh[1], 0.0, [('QRh', 1)])
        NPT = 4
        PT = [AR.alloc([128, 512], BF16) for _ in range(NPT)]
        dacc = [AR.alloc([128, 512], F32) for _ in range(6)]
        rden = AR.alloc([128, 512], F32)
        ato = AR.alloc([128, 512], BF16)
        dma('sp', KRh[0:64, :], KRd[j], r=[('KRd', j)], w=['KRh'])
        NQB = L // 512
        NH2 = NT // 2

        def load_head(h, hf):
            hp = h % 2
            dma('sp', KTh[:, hf * NH2 * 128:(hf + 1) * NH2 * 128], KTd[j][h][:, hf * NH2 * 128:(hf + 1) * NH2 * 128],
                r=[('KTd', j)], w=[('KTh', hf)])
            dma('sp', Vh[:, hf * NH2:(hf + 1) * NH2, :], VVd[j][h][:, hf * NH2:(hf + 1) * NH2, :],
                r=[('VVd', j)], w=[('Vh', hf)])
            if hf == 0:
                dma('sp', QTh[hp], QTd[j][h], r=[('QTd', j)], w=[('QTh', hp)])
                dma('sp', QRh[hp][0:64, :], QRd[j][h], r=[('QRd', j)], w=[('QRh', hp)])

        items = [(h, qb, kt) for h in range(4) for qb in range(NQB) for kt in range(NT)]

        def qk(item, gi_):
            h, qb, kt = item
            hp = h % 2
            if qb == 0 and kt == 0:
                load_head(h, 0)
                if h == 0:
                    load_head(h, 1)
            qs = slice(qb * 512, (qb + 1) * 512)
            sb = gi_ % NPT
            hf = kt // NH2
            mm(PB[sb][:, :], KTh[:, kt * 128:(kt + 1) * 128], QTh[hp][:, qs], True, False,
               [('KTh', hf), ('QTh', hp)], [('pb', sb)])
            mm(PB[sb][:, :], KRh[:, kt * 128:(kt + 1) * 128], QRh[hp][:, qs], False, True,
               ['KRh', ('QRh', hp)], [('pb', sb)])
            act(PT[sb], PB[sb][:, :], AF.Exp, [('pb', sb)], [('PT', sb)])

        def pv(item, gi_):
            h, qb, kt = item
            qs = slice(qb * 512, (qb + 1) * 512)
            blk = h * NQB + qb
            ob = 4 + (blk % 2)
            db = 6 + (blk % 2)
            sb = gi_ % NPT
            hf = kt // NH2
            mm(PB[ob][:, :], Vh[:, kt, :], PT[sb], kt == 0, kt == NT - 1, [('Vh', hf), ('PT', sb)], [('pb', ob)])
            ai, eng_ = kt % 3, 'dve'
            da = dacc[(blk % 2) * 3 + ai]
            dtok = ('dacc', blk % 2, ai)
            if kt in (0, 1, 2):
                cp(eng_, da, PT[sb], [('PT', sb)], [dtok])
            else:
                tt(eng_, da, da, PT[sb], ALU.add, [dtok, ('PT', sb)], [dtok], nosync=(eng_ == 'dve'))
            if kt == NT - 1:
                for ai2 in range(3):
                    mm(PB[db][:, :], ones_f, dacc[(blk % 2) * 3 + ai2], ai2 == 0, ai2 == 2,
                       ['ones_f', ('dacc', blk % 2, ai2)], [('pb', db)])
                recip(rden, PB[db][:, :], [('pb', db)], ['rden'])
                tt('dve', ato, PB[ob][:, :], rden, ALU.mult, [('pb', ob), 'rden'], ['ato'])
                dma('sp', ATd[j][h][:, qs], ato, r=['ato'], w=[('ATd', j)])
                if qb == NQB - 1 and h < 3:
                    load_head(h + 1, 1)

        LA = 3
        for i_ in range(min(LA, len(items))):
            qk(items[i_], i_)
        for i_ in range(len(items)):
            if i_ + LA < len(items):
                qk(items[i_ + LA], i_ + LA)
            pv(items[i_], i_)

        P.barrier()
        AR.release(job_mark)
        lnB = AR.alloc([128, 2, 1024], F32)
        dma('sp', lnB, ln_in[0:2, :].partition_broadcast(128), r=[], w=['lnB'])
        Wo = AR.alloc([128, 8, 1024], BF16)
        d_mark = AR.mark()
        wst = [AR.alloc([128, 8, 512], F32) for _ in range(2)]
        gi = 0
        for g in range(2):
            b = gi % 2
            gi += 1
            dma('sp', wst[b], wo_in[:, g * 512:(g + 1) * 512].rearrange("(k p) n -> p k n", p=128), r=[], w=[('wst', b)])
            for k in range(8):
                tt('dve', Wo[:, k, g * 512:(g + 1) * 512], wst[b][:, k, :], gB[:, 0, g * 512:(g + 1) * 512], ALU.mult,
                   [('wst', b), 'gB'], ['Wo'])
        cat2 = [AR.alloc([128, 8, 512], BF16) for _ in range(2)]
        xo = [AR.alloc([128, D], F32) for _ in range(2)]
        yv2 = [AR.alloc([128, D], F32) for _ in range(2)]
        x1 = [AR.alloc([128, D], F32) for _ in range(2)]
        ln_scratch('d1')
        NT1 = L // 128

        def load_d1(tix):
            m, u = tix // 4, tix % 4
            if u == 0:
                ms = slice(m * 512, (m + 1) * 512)
                cb = m % 2
                dma('sp', cat2[cb][:, 0:4, :], RTd[j][:, :, ms].rearrange("h p n -> p h n"), r=[('RTd', j)], w=[('cat', cb, 0)])
                dma('sp', cat2[cb][:, 4:8, :], ATd[j][:, :, ms].rearrange("h p n -> p h n"), r=[('ATd', j)], w=[('cat', cb, 1)])
            dma('sp', xo[tix % 2], xj[tix * 128:(tix + 1) * 128, :], r=[], w=[('xo', tix % 2)])

        load_d1(0)
        for tix in range(NT1):
            m, u = tix // 4, tix % 4
            b = tix % 2
            cb = m % 2
            if tix + 1 < NT1:
                load_d1(tix + 1)
            for half in range(2):
                for k in range(8):
                    mm(PB[half][:, :], cat2[cb][:, k, u * 128:(u + 1) * 128], Wo[:, k, half * 512:(half + 1) * 512],
                       k == 0, k == 7, [('cat', cb, 0), ('cat', cb, 1), 'Wo'], [('pb', half)])
                stt('dve', yv2[b][:, half * 512:(half + 1) * 512], xo[b][:, half * 512:(half + 1) * 512], ALPHA,
                    PB[half][:, :], ALU.mult, ALU.add, [('xo', b), ('pb', half)], [('yv', b)])
            ln_apply(yv2[b], yv2[b], 'd1', [('yv', b)], [('yv', b)])
            tt('pool', yv2[b], yv2[b], lnB[:, 0, :], ALU.mult, [('yv', b), 'lnB'], [('yv', b)])
            tt('pool', x1[b], yv2[b], lnB[:, 1, :], ALU.add, [('yv', b), 'lnB'], [('x1', b)])
            dma('sp', X1d[j][tix * 128:(tix + 1) * 128, :], x1[b], r=[('x1', b)], w=[('X1d', j)])

        P.barrier()
        AR.release(job_mark)
        lnB = AR.alloc([128, 2, 1024], F32)
        dma('sp', lnB, ln_in[2:4, :].partition_broadcast(128), r=[], w=['lnB'])
        Wup = AR.alloc([128, 8, DFF], BF16)
        Wdn = AR.alloc([128, 32, D], BF16)
        bup = AR.alloc([128, 32], F32)
        e_mark = AR.mark()
        wst = [AR.alloc([128, 8, 512], F32) for _ in range(2)]
        gi = 0
        for g in range(8):
            b = gi % 2
            gi += 1
            dma('sp', wst[b], wup_in[:, g * 512:(g + 1) * 512].rearrange("(k p) n -> p k n", p=128), r=[], w=[('wst', b)])
            for c in range(4):
                for k in range(8):
                    mm(PB[1][:, 0:1], wst[b][:, k, c * 128:(c + 1) * 128], modfm[:, 24 + k, j:j + 1], k == 0, k == 7,
                       [('wst', b), 'modfm'], [('pb', 1)])
                cp('dve', bup[:, g * 4 + c:g * 4 + c + 1], PB[1][:, 0:1], [('pb', 1)], ['bup'])
            for k in range(8):
                ts('dve', Wup[:, k, g * 512:(g + 1) * 512], wst[b][:, k, :], scp[:, 1, k, j:j + 1], None, ALU.mult, None,
                   [('wst', b), 'scp'], ['Wup'])
        for g in range(8):
            b = gi % 2
            gi += 1
            fg, half = g // 2, g % 2
            dma('sp', wst[b], wdn_in[fg * 1024:(fg + 1) * 1024, half * 512:(half + 1) * 512].rearrange("(k p) n -> p k n", p=128),
                r=[], w=[('wst', b)])
            for k in range(8):
                tt('dve', Wdn[:, fg * 8 + k, half * 512:(half + 1) * 512], wst[b][:, k, :], gB[:, 1, half * 512:(half + 1) * 512],
                   ALU.mult, [('wst', b), 'gB'], ['Wdn'])
        P.barrier()
        AR.release(e_mark)
        TBK = 256
        NU = TBK // 128
        NB = L // TBK
        x1t = [[AR.alloc([128, D], F32) for _ in range(NU)] for _ in range(2)]
        xn2 = AR.alloc([128, D], BF16)
        h2T = [AR.alloc([128, 8, TBK], BF16) for _ in range(2)]
        u2 = [AR.alloc([128, 8, TBK], BF16) for _ in range(2)]
        rt = [AR.alloc([128, TBK], F32) for _ in range(2)]
        oo = [gB[:, 0, :], gB[:, 1, :]]
        ln_scratch('d2a')
        ln_scratch('d2b')
        pT2 = PB[6][:, :].bitcast(BF16).rearrange("p (a b) -> p a b", a=8)[:, :, 0:128]

        def front2(m):
            mp = m % 2
            for u in range(NU):
                tix = m * NU + u
                dma('sp', x1t[mp][u], X1d[j][tix * 128:(tix + 1) * 128, :], r=[('X1d', j)], w=[('x1t', mp, u)])
                ln_apply(x1t[mp][u], xn2, 'd2a', [('x1t', mp, u)], ['xn2'])
                for k in range(8):
                    tr(pT2[:, k, :], xn2[:, k * 128:(k + 1) * 128], ['xn2'], [('pb', 6)])
                cp('act', h2T[mp][:, :, u * 128:(u + 1) * 128], pT2, [('pb', 6)], [('h2T', mp)])

        def up(m, g):
            mp = m % 2
            ub = (m * 4 + g) % 2
            for fl in range(8):
                f = g * 8 + fl
                pb = 4 + (f % 2)
                for k in range(8):
                    mm(PB[pb][:, 0:TBK], Wup[:, k, f * 128:(f + 1) * 128], h2T[mp][:, k, :], k == 0, k == 7,
                       ['Wup', ('h2T', mp)], [('pb', pb)])
                rb = f % 2
                act(rt[rb], PB[pb][:, 0:TBK], AF.Relu, [('pb', pb), 'bup'], [('rt', rb)], bias=bup[:, f:f + 1])
                tt('pool' if f % 2 else 'dve', u2[ub][:, fl, :], rt[rb], rt[rb], ALU.mult, [('rt', rb)], [('u2', ub)])

        def down(m, g):
            ub = (m * 4 + g) % 2
            for u in range(NU):
                for half in range(2):
                    pbd = u * 2 + half
                    for fl in range(8):
                        f = g * 8 + fl
                        mm(PB[pbd][:, :], u2[ub][:, fl, u * 128:(u + 1) * 128], Wdn[:, f, half * 512:(half + 1) * 512],
                           f == 0, f == 31, [('u2', ub), 'Wdn'], [('pb', pbd)])

        def tail(m):
            mp = m % 2
            for u in range(NU):
                tix = m * NU + u
                ob = tix % 2
                for half in range(2):
                    pbd = u * 2 + half
                    stt('dve', oo[ob][:, half * 512:(half + 1) * 512], x1t[mp][u][:, half * 512:(half + 1) * 512], ALPHA,
                        PB[pbd][:, :], ALU.mult, ALU.add, [('x1t', mp, u), ('pb', pbd)], [('oo', ob)])
                ln_apply(oo[ob], oo[ob], 'd2b', [('oo', ob)], [('oo', ob)])
                tt('pool', oo[ob], oo[ob], lnB[:, 0, :], ALU.mult, [('oo', ob), 'lnB'], [('oo', ob)])
                tt('pool', oo[ob], oo[ob], lnB[:, 1, :], ALU.add, [('oo', ob), 'lnB'], [('oo', ob)])
                dma('sp', y_out[j][tix * 128:(tix + 1) * 128, :], oo[ob], r=[('oo', ob)], w=[('yout', j, ob)], outp=True)

        G = [(m, g) for m in range(NB) for g in range(4)]
        front2(0)
        up(0, 0)
        for i_, (m, g) in enumerate(G):
            if i_ + 1 < len(G):
                m2, g2 = G[i_ + 1]
                if g2 == 0:
                    front2(m2)
                up(m2, g2)
            down(m, g)
            if g == 3:
                tail(m)

    P.emit()
    return nc, P


def _rope_tables(pos, d):
    inv = (10000.0 ** (-np.arange(0, d, 2, dtype=np.float32) / np.float32(d))).astype(np.float32)
    ang = pos.astype(np.float32)[:, None] * inv[None, :]
    return np.cos(ang).astype(np.float32), np.sin(ang).astype(np.float32)


def host_inputs(L, NSLOT, core_jobs, xs, cs, w):
    NJ = len(NSLOT)
    NCH = L // 128
    p = np.arange(128, dtype=np.float32)
    i = np.arange(128, dtype=np.float32)
    cst = np.zeros((128, 4 * 128 + 8), np.float32)
    cst[:, 0:128] = np.maximum(i[None, :] - p[:, None], 0)
    cst[:, 128:256] = np.maximum(p[:, None] - i[None, :], 0)
    cst[:, 256:384] = (p[:, None] <= i[None, :])
    cst[:, 384:512] = (p[:, None] > i[None, :])
    cst[:, 512] = 127 - p
    cst[:, 513] = p
    cst[:, 514] = 128.0
    shared = dict(
        cst=cst,
        w_ada=np.ascontiguousarray(w['w_ada'][0]), b_ada_fm=np.ascontiguousarray(w['b_ada'][0].reshape(48, 128).T),
        b_ada_row=np.ascontiguousarray(w['b_ada'][0].reshape(1, -1)),
        w_in=np.ascontiguousarray(w['w_in'][0]), w_uq=np.ascontiguousarray(w['w_uq'][0]),
        w_ukv=np.ascontiguousarray(w['w_ukv'][0]), w_o=np.ascontiguousarray(w['w_o'][0]),
        w_up=np.ascontiguousarray(w['w_up'][0]), w_down=np.ascontiguousarray(w['w_down'][0]),
        dec=np.concatenate([w['ret_decay_f'][0], w['ret_decay_b'][0]]).reshape(1, 8).astype(np.float32),
        gnw=np.ascontiguousarray(w['ret_gn_w'][0].reshape(1, 512)),
        qnw_fm=np.ascontiguousarray(w['q_norm_w'][0].reshape(3, 128).T),
        kvnw_fm=np.ascontiguousarray(w['kv_norm_w'][0].reshape(2, 128).T),
        ln=np.ascontiguousarray(np.stack([w['ln1_w'][0], w['ln1_b'][0], w['ln2_w'][0], w['ln2_b'][0]])),
    )
    in_maps = []
    for core, jobs in enumerate(core_jobs):
        m = dict(shared)
        cT = np.zeros((128, 8, NJ), np.float32)
        mk = np.zeros((128, NJ, 2, 128), np.float32)
        ex = np.zeros((128, NJ, 2, 128), np.float32)
        for j, (bi, own) in enumerate(jobs):
            NS = NSLOT[j]
            S = NS * L
            pos = (np.arange(S) + own * L) % S
            m["x%d" % j] = np.ascontiguousarray(np.roll(xs[j][bi], -own * L, axis=0))
            cR, sR = _rope_tables(pos, 128)
            m["tb%d" % j] = np.ascontiguousarray(np.concatenate([cR, cR, sR, sR], axis=1))
            cM, sM = _rope_tables(pos, 64)
            tm = np.stack([np.concatenate([cM, cM], 1).T, np.concatenate([sM, sM], 1).T], axis=1)
            m["tm%d" % j] = np.ascontiguousarray(tm.astype(np.float32))
            cT[:, :, j] = cs[j][bi].reshape(8, 128).T
            for s in range(1, NS):
                before = (own + s) >= NS
                for n in range(NCH):
                    t = s * NCH + n
                    ex[:, j, 0, t] = (NS - s - 1) * L + (L - 128 - 128 * n)
                    ex[:, j, 1, t] = (s - 1) * L + 128 * n
                    mk[:, j, 0, t] = 1.0 if before else 0.0
                    mk[:, j, 1, t] = 0.0 if before else 1.0
        m["cT"] = cT
        m["mk"] = mk
        m["ex"] = ex
        in_maps.append(m)
    return in_maps


_CACHE = {}


def kernel(x_prompt, x_sample, c_prompt, c_sample, **w):
    L = 4096
    NSLOT = [4, 2]
    x_prompt = np.asarray(x_prompt, np.float32)
    x_sample = np.asarray(x_sample, np.float32)
    c_prompt = np.asarray(c_prompt, np.float32)
    c_sample = np.asarray(c_sample, np.float32)
    w = {k: np.asarray(v, np.float32) for k, v in w.items()}
    core_jobs = [[(c // 4, c % 4), (c // 2, c % 2)] for c in range(NCORES)]
    in_maps = host_inputs(L, NSLOT, core_jobs, [x_prompt, x_sample], [c_prompt, c_sample], w)
    if 'nc' not in _CACHE:
        _CACHE['nc'] = build(L, NSLOT)[0]
    res = run_bass_kernel_spmd(_CACHE['nc'], in_maps, core_ids=list(range(NCORES)))
    yp = np.empty_like(x_prompt)
    ysm = np.empty_like(x_sample)
    for c in range(NCORES):
        r = res.results[c]
        yp[c // 4, (c % 4) * L:(c % 4 + 1) * L] = r["y0"]
        ysm[c // 2, (c % 2) * L:(c % 2 + 1) * L] = r["y1"]
    return (yp, ysm)
```

```python
import math
import numpy as np
import concourse.bass as bass
import concourse.mybir as mybir
from concourse.bass_utils import run_bass_kernel_spmd

F32 = mybir.dt.float32
BF16 = mybir.dt.bfloat16
AF = mybir.ActivationFunctionType
ALU = mybir.AluOpType

D = 1024
DIN = 2752
DFF = 4096
ALPHA = float(2.0 ** 0.25)
MLA_SCALE = float(192 ** -0.5)
NCORES = 8


class Prog:
    SEM_LIMIT = 24000

    def __init__(self, nc):
        self.nc = nc
        self.ops = []
        self.last_w = {}
        self.readers = {}
        self.out_dma_ops = []
        self.barriers = []

    def add(self, eng, fn, r=(), w=(), dma=False, key=None, out=False, nosync=False):
        i = len(self.ops)
        if dma and key is None:
            key = w[0]
        ps = [t for t in r if isinstance(t, tuple) and t[0] == 'pb']
        r = [t for t in r if not (isinstance(t, tuple) and t[0] == 'pb')]
        w = list(w) + [t for t in ps if t not in w]
        deps = set()
        for t in r:
            j = self.last_w.get(t)
            if j is not None:
                deps.add((j, 'raw'))
        for t in w:
            j = self.last_w.get(t)
            if j is not None:
                deps.add((j, 'waw'))
            for k in self.readers.get(t, ()):
                deps.add((k, 'war'))
        for t in r:
            self.readers.setdefault(t, []).append(i)
        for t in w:
            self.last_w[t] = i
            self.readers[t] = []
        self.ops.append(dict(eng=eng, fn=fn, deps=deps, dma=dma, key=key, bar=len(self.barriers), nosync=nosync))
        if out:
            self.out_dma_ops.append(i)
        return i

    def barrier(self):
        self.barriers.append(len(self.ops))

    def emit(self):
        nc = self.nc
        ops = self.ops
        bar_sets = []
        for b in self.barriers:
            last = {}
            for i in range(b):
                op = ops[i]
                k = ('dma', op['key']) if op['dma'] else ('eng', op['eng'])
                last[k] = i
            bar_sets.append(set(last.values()))
        for i, op in enumerate(ops):
            need = set()
            for (j, kind) in op['deps']:
                pj = ops[j]
                if j == i:
                    continue
                if not pj['dma'] and not op['dma'] and pj['eng'] == op['eng']:
                    if op['eng'] == 'pe':
                        continue
                    if kind != 'raw' or op['nosync']:
                        continue
                need.add(j)
            if op['bar'] > 0:
                for j in bar_sets[op['bar'] - 1]:
                    pj = ops[j]
                    if not pj['dma'] and pj['eng'] == op['eng']:
                        continue
                    need.add(j)
            op['need'] = need
        has_dep = [False] * len(ops)
        for op in ops:
            for j in op['need']:
                has_dep[j] = True
        for j in self.out_dma_ops:
            has_dep[j] = True
        cnt = {}
        for i, op in enumerate(ops):
            if op['dma']:
                k = ('dma', op['key'])
            elif has_dep[i]:
                k = ('eng', op['eng'])
            else:
                op['sig'] = None
                continue
            cnt[k] = cnt.get(k, 0) + 1
            op['sig'] = (k, cnt[k])
        sems = {}
        cms = []
        for k, n in cnt.items():
            inc = 16 if k[0] == 'dma' else 1
            per = self.SEM_LIMIT // inc
            ns = (n + per - 1) // per
            lst = []
            for s in range(ns):
                cm = nc.semaphore("s%d_%d" % (len(sems), s))
                lst.append(cm.__enter__())
                cms.append(cm)
            sems[k] = (lst, inc, per)
        self.n_sems = len(cms)

        def semval(sig):
            k, n = sig
            lst, inc, per = sems[k]
            return lst[(n - 1) // per], ((n - 1) % per + 1) * inc

        engs = {}
        for i, op in enumerate(ops):
            engs.setdefault(op['eng'], []).append(i)

        def run(ename, e):
            waited = {}
            for i in engs.get(ename, []):
                op = ops[i]
                tgt = {}
                for j in op['need']:
                    k, n = ops[j]['sig']
                    if waited.get(k, 0) >= n:
                        continue
                    if tgt.get(k, 0) < n:
                        tgt[k] = n
                for k, n in tgt.items():
                    s, v = semval((k, n))
                    e.wait_ge(s, v)
                    waited[k] = n
                ins = op['fn'](e)
                if op['sig'] is not None:
                    s, v = semval(op['sig'])
                    ins.then_inc(s, 16 if op['dma'] else 1)
            if ename == 'sp':
                fin = {}
                for j in self.out_dma_ops:
                    k, n = ops[j]['sig']
                    fin[k] = max(fin.get(k, 0), n)
                for k, n in fin.items():
                    if waited.get(k, 0) >= n:
                        continue
                    s, v = semval((k, n))
                    e.wait_ge(s, v)

        with nc.Block() as block:
            @block.tensor
            def _(e):
                run('pe', e)

            @block.scalar
            def _(e):
                run('act', e)

            @block.vector
            def _(e):
                run('dve', e)

            @block.gpsimd
            def _(e):
                run('pool', e)

            @block.sync
            def _(e):
                run('sp', e)
        for cm in cms:
            cm.__exit__(None, None, None)


class Arena:
    def __init__(self, nc, words):
        self.t = nc.alloc_sbuf_tensor("arena", [128, words], F32)
        self.words = words
        self.top = 0

    def mark(self):
        return self.top

    def release(self, m):
        self.top = m

    def alloc(self, shape, dtype=F32):
        n = 1
        for s in shape[1:]:
            n *= s
        nw = n if dtype == F32 else (n + 1) // 2
        nw = (nw + 7) // 8 * 8
        assert self.top + nw <= self.words, ("SBUF arena overflow", self.top, nw, self.words)
        v = self.t[:, self.top:self.top + nw]
        self.top += nw
        if dtype != F32:
            v = v.bitcast(dtype)
        v = v[:, 0:n]
        if len(shape) == 3:
            v = v.rearrange("p (a b) -> p a b", a=shape[1])
        elif len(shape) == 4:
            v = v.rearrange("p (a b c) -> p a b c", a=shape[1], b=shape[2])
        elif len(shape) == 5:
            v = v.rearrange("p (a b c d) -> p a b c d", a=shape[1], b=shape[2], c=shape[3])
        if shape[0] < 128:
            v = v[0:shape[0]]
        return v


def build(L, NSLOT, debug=False):
    NJ = len(NSLOT)
    NCH = L // 128
    SS = [ns * L for ns in NSLOT]
    nc = bass.Bass("TRN2", target_bir_lowering=False)
    P = Prog(nc)

    def din(name, shape, dt=F32):
        return nc.dram_tensor(name, list(shape), dt, kind="ExternalInput").ap()

    def dscr(name, shape, dt=BF16):
        return nc.dram_tensor(name, list(shape), dt, kind=("ExternalOutput" if debug else "Internal")).ap()

    x_in = [din("x%d" % j, [SS[j], D]) for j in range(NJ)]
    tb_in = [din("tb%d" % j, [SS[j], 256]) for j in range(NJ)]
    tm_in = [din("tm%d" % j, [64, 2, SS[j]]) for j in range(NJ)]
    cT_in = din("cT", [128, 8, NJ])
    mk_in = din("mk", [128, NJ, 2, 128])
    ex_in = din("ex", [128, NJ, 2, 128])
    cst_in = din("cst", [128, 4 * 128 + 8])
    wada_in = din("w_ada", [D, 6 * D])
    bfm_in = din("b_ada_fm", [128, 48])
    brow_in = din("b_ada_row", [1, 6 * D])
    win_in = din("w_in", [D, DIN])
    wuq_in = din("w_uq", [384, 768])
    wukv_in = din("w_ukv", [256, 1024])
    wo_in = din("w_o", [D, D])
    wup_in = din("w_up", [D, DFF])
    wdn_in = din("w_down", [DFF, D])
    dec_in = din("dec", [1, 8])
    gnw_in = din("gnw", [1, 512])
    qnw_in = din("qnw_fm", [128, 3])
    kvnw_in = din("kvnw_fm", [128, 2])
    ln_in = din("ln", [4, D])
    y_out = [nc.dram_tensor("y%d" % j, [L, D], F32, kind="ExternalOutput").ap() for j in range(NJ)]

    KTd = [dscr("KTd%d" % j, [4, 128, SS[j]]) for j in range(NJ)]
    KRd = [dscr("KRd%d" % j, [64, SS[j]]) for j in range(NJ)]
    VVd = [dscr("VVd%d" % j, [4, 128, SS[j] // 128, 128]) for j in range(NJ)]
    QTd = [dscr("QTd%d" % j, [4, 128, L]) for j in range(NJ)]
    QRd = [dscr("QRd%d" % j, [4, 64, L]) for j in range(NJ)]
    RTd = [dscr("RTd%d" % j, [4, 128, L]) for j in range(NJ)]
    ATd = [dscr("ATd%d" % j, [4, 128, L]) for j in range(NJ)]
    X1d = [dscr("X1d%d" % j, [L, D], F32) for j in range(NJ)]

    AR = Arena(nc, 49100)
    PB = [nc.alloc_psum_tensor("pb%d" % i, [128, 512], F32) for i in range(8)]

    def mm(out, lhsT, rhs, start, stop, r, w):
        P.add('pe', lambda e: e.matmul(out, lhsT=lhsT, rhs=rhs, start=start, stop=stop), r=r, w=w)

    def act(out, in_, func, r, w, bias=None, scale=None):
        kw = {}
        if bias is not None:
            kw['bias'] = bias
        if scale is not None:
            kw['scale'] = scale
        P.add('act', lambda e: e.activation(out=out, in_=in_, func=func, **kw), r=r, w=w)

    def tt(eng, out, in0, in1, op, r, w, nosync=False):
        P.add(eng, lambda e: e.tensor_tensor(out=out, in0=in0, in1=in1, op=op), r=r, w=w, nosync=nosync)

    def ts(eng, out, in0, s1, s2, op0, op1, r, w):
        if s2 is None:
            P.add(eng, lambda e: e.tensor_scalar(out=out, in0=in0, scalar1=s1, scalar2=None, op0=op0), r=r, w=w)
        else:
            P.add(eng, lambda e: e.tensor_scalar(out=out, in0=in0, scalar1=s1, scalar2=s2, op0=op0, op1=op1), r=r, w=w)

    def stt(eng, out, in0, scalar, in1, op0, op1, r, w):
        P.add(eng, lambda e: e.scalar_tensor_tensor(out=out, in0=in0, scalar=scalar, in1=in1, op0=op0, op1=op1),
              r=r, w=w)

    def cp(eng, out, in_, r, w):
        if eng == 'act':
            P.add('act', lambda e: e.copy(out=out, in_=in_), r=r, w=w)
        else:
            P.add(eng, lambda e: e.tensor_copy(out=out, in_=in_), r=r, w=w)

    def dma(q, out, in_, r, w, key=None, outp=False):
        P.add(q, lambda e: e.dma_start(out=out, in_=in_), r=r, w=w, dma=True, key=key, out=outp)

    def recip(out, in_, r, w):
        P.add('dve', lambda e: e.reciprocal(out=out, in_=in_), r=r, w=w)

    def bnstats(out, in_, r, w):
        P.add('dve', lambda e: e.bn_stats(out=out, in_=in_), r=r, w=w)

    def bnaggr(out, in_, r, w):
        P.add('dve', lambda e: e.bn_aggr(out=out, in_=in_), r=r, w=w)

    def actmul(out, in_, m, r, w):
        P.add('act', lambda e: e.mul(out=out, in_=in_, mul=m), r=r, w=w)

    def memset(eng, ap, v, w):
        P.add(eng, lambda e: e.memset(ap, v), w=w)

    ident_bf = AR.alloc([128, 128], BF16)
    ident_f = AR.alloc([128, 128], F32)
    ones_bf = AR.alloc([128, 128], BF16)
    ones_f = AR.alloc([128, 128], F32)
    cst = AR.alloc([128, 4 * 128 + 8], F32)
    dma('sp', cst, cst_in, r=[], w=['cst'])
    memset('pool', ident_f, 1.0, ['ident_f'])
    P.add('pool', lambda e: e.affine_select(out=ident_f, in_=ident_f, pattern=[[-1, 128]], compare_op=ALU.is_equal,
                                            fill=0.0, base=0, channel_multiplier=1), r=['ident_f'], w=['ident_f'])
    cp('pool', ident_bf, ident_f, ['ident_f'], ['ident_bf'])
    memset('pool', ones_f, 1.0, ['ones_f'])
    memset('pool', ones_bf, 1.0, ['ones_bf'])

    def tr(out, in_, r, w):
        k = in_.shape[0]
        P.add('pe', lambda e: e.transpose(out=out, in_=in_, identity=ident_bf[0:k, 0:k]), r=list(r) + ['ident_bf'], w=w)

    RF = cst[:, 0:128]
    RB = cst[:, 128:256]
    MF = cst[:, 256:384]
    MB = cst[:, 384:512]
    c_127mj = cst[:, 512:513]
    c_j = cst[:, 513:514]
    c_128 = cst[:, 514:515]

    dec = AR.alloc([128, 8], F32)
    dma('sp', dec, dec_in.partition_broadcast(128).rearrange("p a b -> p (a b)"), r=[], w=['dec'])
    lg = AR.alloc([128, 8], F32)
    act(lg, dec, AF.Exp, ['dec'], ['lg'], scale=-1.0)
    ts('dve', lg, lg, 1.0, None, ALU.add, None, ['lg'], ['lg'])
    act(lg, lg, AF.Ln, ['lg'], ['lg'])
    ts('dve', lg, lg, -1.0, None, ALU.mult, None, ['lg'], ['lg'])
    qnw = AR.alloc([128, 3], F32)
    kvnw = AR.alloc([128, 2], F32)
    dma('sp', qnw, qnw_in, r=[], w=['qnw'])
    dma('sp', kvnw, kvnw_in, r=[], w=['kvnw'])
    bfm = AR.alloc([128, 48], F32)
    dma('sp', bfm, bfm_in, r=[], w=['bfm'])
    cT = AR.alloc([128, 8, NJ], F32)
    dma('sp', cT, cT_in, r=[], w=['cT'])
    siluc = AR.alloc([128, 8, NJ], F32)
    act(siluc, cT, AF.Silu, ['cT'], ['siluc'])
    modfm = AR.alloc([128, 48, NJ], F32)
    scp = AR.alloc([128, 2, 8, NJ], F32)

    persist_mark = AR.mark()

    def ln_apply(src, dst, tag, rtok, wtok, eps=1e-5, eng='dve'):
        st, mv, rs = LNS[tag]
        for c in range(2):
            P.add('dve', (lambda e, c=c: e.bn_stats(out=st[:, c, :], in_=src[:, c * 512:(c + 1) * 512])),
                  r=rtok, w=[('st', tag, c)])
        P.add('dve', lambda e: e.bn_aggr(out=mv, in_=st), r=[('st', tag, 0), ('st', tag, 1)], w=[('mv', tag)])
        ts('dve', rs, mv[:, 1:2], eps, None, ALU.add, None, [('mv', tag)], [('rs', tag)])
        P.add('dve', lambda e: e.reciprocal(out=rs, in_=rs), r=[('rs', tag)], w=[('rs', tag)])
        P.add('act', lambda e: e.sqrt(out=rs, in_=rs), r=[('rs', tag)], w=[('rs', tag)])
        ts(eng, dst, src, mv[:, 0:1], rs[:, 0:1], ALU.subtract, ALU.mult, list(rtok) + [('mv', tag), ('rs', tag)], wtok)

    LNS = {}

    def ln_scratch(tag):
        LNS[tag] = (AR.alloc([128, 2, 6], F32), AR.alloc([128, 2], F32), AR.alloc([128, 1], F32))

    def rsqrt_small(x, tag, mul, eps):
        ts('dve', x, x, mul, eps, ALU.mult, ALU.add, [tag], [tag])
        P.add('dve', lambda e: e.reciprocal(out=x, in_=x), r=[tag], w=[tag])
        P.add('act', lambda e: e.sqrt(out=x, in_=x), r=[tag], w=[tag])

    for j in range(NJ):
        S = SS[j]
        NS = NSLOT[j]
        NT = S // 128
        xj, tbj, tmj = x_in[j], tb_in[j], tm_in[j]
        P.barrier()
        AR.release(persist_mark)
        gB = AR.alloc([128, 2, 1024], F32)
        job_mark = AR.mark()
        Win = AR.alloc([128, 8, DIN], BF16)
        Wkrr = AR.alloc([128, 8, 64], BF16)
        brow = AR.alloc([1, DIN], BF16)
        bcfm = AR.alloc([128, 5], F32)
        bkr = AR.alloc([64, 2], F32)
        coef = AR.alloc([128, 2, 4, 128], F32)
        Sf = AR.alloc([128, 4, 128], F32)
        Tb = AR.alloc([128, 4, 128], F32)
        Sbf = AR.alloc([128, 4, 128], BF16)
        Tsave = AR.alloc([128, NCH, 4, 128], BF16)
        DcT = AR.alloc([128, 4, 128], F32)
        dqfB = AR.alloc([128, 4, 128], F32)
        dqbB = AR.alloc([128, 4, 128], F32)
        dkk = AR.alloc([128, 8], F32)
        dcc = AR.alloc([128, 8], F32)
        tmpa = AR.alloc([128, 128], F32)
        tmpb = AR.alloc([128, 128], F32)
        rowi1 = AR.alloc([128, 128], F32)
        row128mi = AR.alloc([128, 128], F32)
        Wukv = AR.alloc([128, 2, 2, 4, 128], BF16)
        Wuq = AR.alloc([128, 3, 4, 192], BF16)
        Wuqr = AR.alloc([128, 3, 4, 64], BF16)
        prep_mark = AR.mark()
        mkt = AR.alloc([128, 2, 128], F32)
        ext = AR.alloc([128, 2, 128], F32)
        st_ukv = AR.alloc([128, 2, 1024], F32)
        st_uq = AR.alloc([128, 3, 768], F32)
        silucB = AR.alloc([128, 8, 128], F32)
        browada = AR.alloc([1, 2, 1024], F32)
        stkr = AR.alloc([128, 8, 64], F32)
        wst = [AR.alloc([128, 8, 512], F32) for _ in range(2)]
        tt('dve', rowi1, RF, RB, ALU.subtract, ['cst'], ['rowi1'])
        ts('dve', rowi1, rowi1, c_j, 1.0, ALU.add, ALU.add, ['rowi1', 'cst'], ['rowi1'])
        ts('dve', row128mi, rowi1, -1.0, 129.0, ALU.mult, ALU.add, ['rowi1'], ['row128mi'])
        for h in range(4):
            act(tmpa, RF, AF.Exp, ['cst', 'lg'], ['tmpa'], scale=lg[:, h:h + 1])
            tt('dve', tmpa, tmpa, MF, ALU.mult, ['tmpa', 'cst'], ['tmpa'])
            act(tmpb, RB, AF.Exp, ['cst', 'lg'], ['tmpb'], scale=lg[:, 4 + h:5 + h])
            tt('dve', tmpb, tmpb, MB, ALU.mult, ['tmpb', 'cst'], ['tmpb'])
            tt('dve', DcT[:, h, :], tmpa, tmpb, ALU.add, ['tmpa', 'tmpb'], ['DcT'])
            act(dqfB[:, h, :], rowi1, AF.Exp, ['rowi1', 'lg'], ['dqfB'], scale=lg[:, h:h + 1])
            act(dqbB[:, h, :], row128mi, AF.Exp, ['row128mi', 'lg'], ['dqbB'], scale=lg[:, 4 + h:5 + h])
            act(dkk[:, h:h + 1], c_127mj, AF.Exp, ['cst', 'lg'], ['dkk'], scale=lg[:, h:h + 1])
            act(dkk[:, 4 + h:5 + h], c_j, AF.Exp, ['cst', 'lg'], ['dkk'], scale=lg[:, 4 + h:5 + h])
            act(dcc[:, h:h + 1], c_128, AF.Exp, ['cst', 'lg'], ['dcc'], scale=lg[:, h:h + 1])
            act(dcc[:, 4 + h:5 + h], c_128, AF.Exp, ['cst', 'lg'], ['dcc'], scale=lg[:, 4 + h:5 + h])

        dma('sp', st_ukv, wukv_in.rearrange("(j p) n -> p j n", p=128), r=[], w=['st_ukv'])
        dma('sp', st_uq, wuq_in.rearrange("(j p) n -> p j n", p=128), r=[], w=['st_uq'])
        for cj in range(2):
            v = st_ukv[:, cj, :].rearrange("p (h t d) -> p t h d", h=4, t=2)
            ts('dve', Wukv[:, cj], v, kvnw[:, cj:cj + 1], None, ALU.mult, None, ['st_ukv', 'kvnw'], ['Wukv'])
        for cj in range(3):
            v = st_uq[:, cj, :].rearrange("p (h d) -> p h d", h=4)
            ts('dve', Wuq[:, cj], v, qnw[:, cj:cj + 1], MLA_SCALE, ALU.mult, ALU.mult, ['st_uq', 'qnw'], ['Wuq'])
            ts('dve', Wuqr[:, cj, :, 0:32], v[:, :, 160:192], qnw[:, cj:cj + 1], -MLA_SCALE, ALU.mult, ALU.mult,
               ['st_uq', 'qnw'], ['Wuqr'])
            ts('dve', Wuqr[:, cj, :, 32:64], v[:, :, 128:160], qnw[:, cj:cj + 1], MLA_SCALE, ALU.mult, ALU.mult,
               ['st_uq', 'qnw'], ['Wuqr'])

        for k in range(8):
            cp('dve', silucB[:, k, :], siluc[:, k, j:j + 1].broadcast_to([128, 128]), ['siluc'], ['silucB'])
        dma('sp', browada[:, 0, :], brow_in[:, 2048:3072], r=[], w=['browada'])
        dma('sp', browada[:, 1, :], brow_in[:, 5120:6144], r=[], w=['browada'])
        gi = 0
        for g in range(12):
            b = gi % 2
            gi += 1
            dma('sp', wst[b], wada_in[:, g * 512:(g + 1) * 512].rearrange("(k p) n -> p k n", p=128),
                r=[], w=[('wst', b)])
            if g in (4, 5, 10, 11):
                which = 0 if g < 6 else 1
                half = g % 2 if g < 6 else (g - 10)
                for k in range(8):
                    mm(PB[0][:, :], silucB[:, k, :], wst[b][:, k, :], k == 0, False, ['silucB', ('wst', b)], [('pb', 0)])
                mm(PB[0][:, :], ones_f[0:1, :], browada[0:1, which, half * 512:(half + 1) * 512], False, True,
                   ['ones_f', 'browada'], [('pb', 0)])
                cp('act', gB[:, which, half * 512:(half + 1) * 512], PB[0][:, :], [('pb', 0)], ['gB'])
            else:
                for c in range(4):
                    ch = g * 4 + c
                    for k in range(8):
                        mm(PB[1][:, 0:1], wst[b][:, k, c * 128:(c + 1) * 128], siluc[:, k, j:j + 1], k == 0, k == 7,
                           [('wst', b), 'siluc'], [('pb', 1)])
                    ts('dve', modfm[:, ch, j:j + 1], PB[1][:, 0:1], bfm[:, ch:ch + 1], None, ALU.add, None,
                       [('pb', 1), 'bfm'], ['modfm'])
        for k in range(8):
            ts('dve', scp[:, 0, k, j:j + 1], modfm[:, 8 + k, j:j + 1], 1.0, None, ALU.add, None, ['modfm'], ['scp'])
            ts('dve', scp[:, 1, k, j:j + 1], modfm[:, 32 + k, j:j + 1], 1.0, None, ALU.add, None, ['modfm'], ['scp'])

        groups = [(0, 512), (512, 512), (1024, 512), (1536, 512), (2048, 512), (2560, 192)]
        for (c0, cw) in groups:
            b = gi % 2
            gi += 1
            dma('sp', wst[b][:, :, 0:cw], win_in[:, c0:c0 + cw].rearrange("(k p) n -> p k n", p=128),
                r=[], w=[('wst', b)])
            for k in range(8):
                mm(PB[0][0:1, 0:cw], modfm[:, k, j:j + 1], wst[b][:, k, 0:cw], k == 0, k == 7,
                   ['modfm', ('wst', b)], [('pb', 0)])
            if c0 == 512:
                actmul(brow[0:1, c0:c0 + cw], PB[0][0:1, 0:cw], float(128 ** -0.5), [('pb', 0)], ['brow'])
            else:
                cp('act', brow[0:1, c0:c0 + cw], PB[0][0:1, 0:cw], [('pb', 0)], ['brow'])
            if c0 == 2048:
                for c in range(4):
                    for k in range(8):
                        mm(PB[1][:, 0:1], wst[b][:, k, c * 128:(c + 1) * 128], modfm[:, k, j:j + 1], k == 0, k == 7,
                           [('wst', b), 'modfm'], [('pb', 1)])
                    cp('dve', bcfm[:, c:c + 1], PB[1][:, 0:1], [('pb', 1)], ['bcfm'])
            if c0 == 2560:
                for k in range(8):
                    mm(PB[1][:, 0:1], wst[b][:, k, 0:128], modfm[:, k, j:j + 1], k == 0, k == 7,
                       [('wst', b), 'modfm'], [('pb', 1)])
                cp('dve', bcfm[:, 4:5], PB[1][:, 0:1], [('pb', 1)], ['bcfm'])
                ts('dve', stkr[:, :, 0:32], wst[b][:, :, 160:192], -1.0, None, ALU.mult, None, [('wst', b)], ['stkr'])
                cp('dve', stkr[:, :, 32:64], wst[b][:, :, 128:160], [('wst', b)], ['stkr'])
                for k in range(8):
                    mm(PB[1][0:64, 0:1], wst[b][:, k, 128:192], modfm[:, k, j:j + 1], k == 0, k == 7,
                       [('wst', b), 'modfm'], [('pb', 1)])
                cp('dve', bkr[:, 0:1], PB[1][0:64, 0:1], [('pb', 1)], ['bkr'])
                for k in range(8):
                    mm(PB[1][0:64, 0:1], stkr[:, k, :], modfm[:, k, j:j + 1], k == 0, k == 7,
                       ['stkr', 'modfm'], [('pb', 1)])
                cp('dve', bkr[:, 1:2], PB[1][0:64, 0:1], [('pb', 1)], ['bkr'])
                for k in range(8):
                    ts('dve', Wkrr[:, k, :], stkr[:, k, :], scp[:, 0, k, j:j + 1], None, ALU.mult, None,
                       ['stkr', 'scp'], ['Wkrr'])
            for k in range(8):
                if c0 == 512:
                    ts('dve', Win[:, k, c0:c0 + cw], wst[b][:, k, 0:cw], scp[:, 0, k, j:j + 1], float(128 ** -0.5),
                       ALU.mult, ALU.mult, [('wst', b), 'scp'], ['Win'])
                else:
                    ts('dve', Win[:, k, c0:c0 + cw], wst[b][:, k, 0:cw], scp[:, 0, k, j:j + 1], None, ALU.mult, None,
                       [('wst', b), 'scp'], ['Win'])
        dma('sp', mkt, mk_in[:, j], r=[], w=['mkt'])
        dma('sp', ext, ex_in[:, j], r=[], w=['ext'])
        for dr in range(2):
            for h in range(4):
                act(coef[:, dr, h, :], ext[:, dr, :], AF.Exp, ['ext', 'lg'], ['coef'], scale=lg[:, 4 * dr + h:4 * dr + h + 1])
                tt('dve', coef[:, dr, h, :], coef[:, dr, h, :], mkt[:, dr, :], ALU.mult, ['coef', 'mkt'], ['coef'])
        memset('pool', Sf, 0.0, ['Sf'])
        memset('pool', Tb, 0.0, ['Tb'])
        P.barrier()
        AR.release(prep_mark)
        ab_mark = AR.mark()

        xs = [AR.alloc([128, D], F32) for _ in range(2)]
        tbt = [AR.alloc([128, 256], F32) for _ in range(2)]
        tmt = [AR.alloc([64, 2, 128], F32) for _ in range(2)]
        xn = AR.alloc([128, D], BF16)
        xT = AR.alloc([128, 8, 128], BF16)
        rA = AR.alloc([128, 4, 128], F32)
        rBt = AR.alloc([128, 4, 128], F32)
        rstd_t = AR.alloc([128, 1], F32)
        rB = AR.alloc([128, 128], F32)
        aonly_mark = AR.mark()
        kr1 = AR.alloc([64, 128], F32)
        kr2 = AR.alloc([64, 128], F32)
        pT = PB[0][:, :].bitcast(BF16).rearrange("p (a b) -> p a b", a=8)[:, :, 0:128]

        def load(t, b):
            dma('sp', xs[b], xj[t * 128:(t + 1) * 128, :], r=[], w=[('xs', b)])
            dma('sp', tbt[b], tbj[t * 128:(t + 1) * 128, :], r=[], w=[('tbt', b)])
            dma('sp', tmt[b], tmj[:, :, t * 128:(t + 1) * 128], r=[], w=[('tmt', b)])

        def proj_tok(pb, tok, c0, cw):
            for k in range(8):
                mm(pb[:, 0:cw], xT[:, k, :], Win[:, k, c0:c0 + cw], k == 0, False, ['xT', 'Win'], [tok])
            mm(pb[:, 0:cw], ones_bf[0:1, :], brow[0:1, c0:c0 + cw], False, True, ['ones_bf', 'brow'], [tok])

        def rotary_tok(pb, tok, b, dst, dtok):
            v = pb[:, :].rearrange("p (h d) -> p h d", h=4)
            cosB = tbt[b][:, 0:128].unsqueeze(1).broadcast_to([128, 4, 128])
            sinB = tbt[b][:, 128:256].unsqueeze(1).broadcast_to([128, 4, 128])
            tt('dve', rA, v, cosB, ALU.mult, [tok, ('tbt', b)], ['rA'])
            tt('dve', rBt, v, sinB, ALU.mult, [tok, ('tbt', b)], ['rBt'])
            tt('pool', dst[:, :, 0:64], rA[:, :, 0:64], rBt[:, :, 64:128], ALU.subtract, ['rA', 'rBt'], [dtok])
            tt('pool', dst[:, :, 64:128], rBt[:, :, 0:64], rA[:, :, 64:128], ALU.add, ['rA', 'rBt'], [dtok])

        Kp2 = [AR.alloc([128, 4, 128], BF16) for _ in range(2)]
        Kwf2 = [AR.alloc([128, 4, 128], BF16) for _ in range(2)]
        Kwb2 = [AR.alloc([128, 4, 128], BF16) for _ in range(2)]
        Vr2 = [AR.alloc([128, 512], BF16) for _ in range(2)]
        sqv2 = [AR.alloc([128, 2, 128], BF16) for _ in range(2)]
        ckv2 = [AR.alloc([128, 2, 128], BF16) for _ in range(2)]
        KTb2 = [AR.alloc([128, 4, 512], BF16) for _ in range(2)]
        KRb2 = [AR.alloc([64, 512], BF16) for _ in range(2)]
        VMb2 = [AR.alloc([128, 4, 4, 128], BF16) for _ in range(2)]
        ckt = [AR.alloc([128, 2, 4], F32) for _ in range(2)]
        dkfB = dkk[:, 0:4].unsqueeze(2).broadcast_to([128, 4, 128])
        dkbB = dkk[:, 4:8].unsqueeze(2).broadcast_to([128, 4, 128])

        xnA = [xn, AR.alloc([128, D], BF16)]
        xTA = [xT, AR.alloc([128, 8, 128], BF16)]
        ln_scratch('a0')
        ln_scratch('a1')

        def lnA(t):
            b = t % 2
            ln_apply(xs[b], xnA[b], 'a%d' % b, [('xs', b)], [('xnA', b)])

        def trA(t):
            b = t % 2
            for k in range(8):
                tr(pT[:, k, :], xnA[b][:, k * 128:(k + 1) * 128], [('xnA', b)], [('pb', 0)])
            cp('act', xTA[b], pT, [('pb', 0)], [('xTA', b)])

        def projA(t):
            b = t % 2
            xTb = xTA[b]
            xtok = ('xTA', b)
            for (pb_, c0) in ((1, 512), (2, 1024)):
                for k in range(8):
                    mm(PB[pb_][:, :], xTb[:, k, :], Win[:, k, c0:c0 + 512], k == 0, False, [xtok, 'Win'], [('pb', pb_)])
                mm(PB[pb_][:, :], ones_bf[0:1, :], brow[0:1, c0:c0 + 512], False, True, ['ones_bf', 'brow'], [('pb', pb_)])
            for cj in range(2):
                for k in range(8):
                    mm(PB[3][:, cj * 128:(cj + 1) * 128], Win[:, k, 2432 + cj * 128:2432 + (cj + 1) * 128], xTb[:, k, :],
                       k == 0, k == 7, ['Win', xtok], [('pb', 3)])
            for k in range(8):
                mm(PB[3][0:64, 256:384], Win[:, k, 2688:2752], xTb[:, k, :], k == 0, k == 7, ['Win', xtok], [('pb', 3)])
            for k in range(8):
                mm(PB[3][0:64, 384:512], Wkrr[:, k, :], xTb[:, k, :], k == 0, k == 7, ['Wkrr', xtok], [('pb', 3)])

        def firstA(t):
            b = t % 2
            s = t // NCH
            q4 = t % 4
            bp = (t // 4) % 2
            rotary_tok(PB[1], ('pb', 1), b, Kp2[b], ('Kp2', b))
            cp('act', Vr2[b], PB[2][:, :], [('pb', 2)], [('Vr2', b)])
            ckvp = PB[3][:, 0:256].rearrange("p (a b) -> p a b", a=2)
            for cj in range(2):
                act(sqv2[b][:, cj, :], ckvp[:, cj, :], AF.Square, [('pb', 3), 'bcfm'], [('sqv2', b)], bias=bcfm[:, 3 + cj:4 + cj])
                act(ckv2[b][:, cj, :], ckvp[:, cj, :], AF.Identity, [('pb', 3), 'bcfm'], [('ckv2', b)], bias=bcfm[:, 3 + cj:4 + cj])
            stt('dve', kr1, PB[3][0:64, 256:384], bkr[:, 0:1], tmt[b][:, 0, :], ALU.add, ALU.mult,
                [('pb', 3), 'bkr', ('tmt', b)], ['kr1'])
            stt('dve', kr2, PB[3][0:64, 384:512], bkr[:, 1:2], tmt[b][:, 1, :], ALU.add, ALU.mult,
                [('pb', 3), 'bkr', ('tmt', b)], ['kr2'])
            tt('pool', KRb2[bp][:, q4 * 128:(q4 + 1) * 128], kr1, kr2, ALU.add, ['kr1', 'kr2'], [('KRb2', bp)])
            if s > 0:
                tt('dve', ckt[b], dkk.rearrange("p (a h) -> p a h", a=2), coef[:, :, :, t], ALU.mult, ['dkk', 'coef'], [('ckt', b)])
                tt('pool', Kwf2[b], Kp2[b], ckt[b][:, 0, :].unsqueeze(2).broadcast_to([128, 4, 128]), ALU.mult,
                   [('Kp2', b), ('ckt', b)], [('Kwf2', b)])
                tt('pool', Kwb2[b], Kp2[b], ckt[b][:, 1, :].unsqueeze(2).broadcast_to([128, 4, 128]), ALU.mult,
                   [('Kp2', b), ('ckt', b)], [('Kwb2', b)])
            else:
                tt('pool', Kwb2[b], Kp2[b], dkbB, ALU.mult, [('Kp2', b), 'dkk'], [('Kwb2', b)])

        def secondA(t):
            b = t % 2
            s = t // NCH
            n = t % NCH
            q4 = t % 4
            bp = (t // 4) % 2
            for cj in range(2):
                mm(PB[4][:, 0:1], sqv2[b][:, cj, :], ones_bf[:, 0:1], cj == 0, cj == 1, [('sqv2', b), 'ones_bf'], [('pb', 4)])
            for cj in range(2):
                mm(PB[4][:, 128:256], ones_bf, sqv2[b][:, cj, :], cj == 0, cj == 1, ['ones_bf', ('sqv2', b)], [('pb', 4)])
            cp('dve', rstd_t, PB[4][:, 0:1], [('pb', 4)], ['rstd_t'])
            cp('dve', rB, PB[4][:, 128:256], [('pb', 4)], ['rB'])
            rsqrt_small(rstd_t, 'rstd_t', 1.0 / 256.0, 1e-6)
            rsqrt_small(rB, 'rB', 1.0 / 256.0, 1e-6)
            Pst = [PB[6][:, :].rearrange("p (h e) -> p h e", h=4), PB[7][:, :].rearrange("p (h e) -> p h e", h=4)]
            if s > 0:
                first_t = (t == NT - 1)
                last_t = (t == NCH)
                for h in range(4):
                    mm(Pst[1][:, h, :], Kwb2[b][:, h, :], Vr2[b][:, h * 128:(h + 1) * 128], first_t and h == 0, last_t,
                       [('Kwb2', b), ('Vr2', b)], [('pb', 7)])
                for h in range(4):
                    mm(Pst[0][:, h, :], Kwf2[b][:, h, :], Vr2[b][:, h * 128:(h + 1) * 128], first_t and h == 0, last_t,
                       [('Kwf2', b), ('Vr2', b)], [('pb', 6)])
            else:
                for h in range(4):
                    mm(Pst[1][:, h, :], Kwb2[b][:, h, :], Vr2[b][:, h * 128:(h + 1) * 128], True, True,
                       [('Kwb2', b), ('Vr2', b)], [('pb', 7)])
            knp = PB[5][:, :].rearrange("p (h d) -> p h d", h=4)
            for h in range(4):
                for cj in range(2):
                    mm(knp[:, h, :], Wukv[:, cj, 0, h, :], ckv2[b][:, cj, :], cj == 0, cj == 1, ['Wukv', ('ckv2', b)], [('pb', 5)])
            for cj in range(2):
                mm(PB[4][:, :], ckv2[b][:, cj, :], Wukv[:, cj, 1].rearrange("p h d -> p (h d)"), cj == 0, cj == 1,
                   [('ckv2', b), 'Wukv'], [('pb', 4)])
            if s > 0:
                if t == NCH:
                    cp('dve', Sf, Pst[0], [('pb', 6)], ['Sf'])
                    cp('dve', Tb, Pst[1], [('pb', 7)], ['Tb'])
            else:
                cp('act', Tsave[:, n], Tb, ['Tb'], ['Tsave'])
                for h in range(4):
                    stt('dve', Tb[:, h, :], Tb[:, h, :], dcc[:, 4 + h:5 + h], Pst[1][:, h, :], ALU.mult, ALU.add,
                        [('pb', 7), 'dcc', 'Tb'], ['Tb'])
            tt('dve', KTb2[bp][:, :, q4 * 128:(q4 + 1) * 128], knp, rB.unsqueeze(1).broadcast_to([128, 4, 128]), ALU.mult,
               [('pb', 5), 'rB'], [('KTb2', bp)])
            act(VMb2[bp][:, :, q4, :], PB[4][:, :].rearrange("p (h d) -> p h d", h=4), AF.Identity, [('pb', 4), 'rstd_t'],
                [('VMb2', bp)], scale=rstd_t[:, 0:1])
            if q4 == 0:
                t0 = t
                dma('sp', KTd[j][:, :, t0 * 128:(t0 + 4) * 128].rearrange("h p n -> p h n"), KTb2[bp], r=[('KTb2', bp)], w=[('KTd', j)])
                dma('sp', KRd[j][:, t0 * 128:(t0 + 4) * 128], KRb2[bp], r=[('KRb2', bp)], w=[('KRd', j)])
                dma('sp', VVd[j][:, :, t0:t0 + 4, :].rearrange("h p t e -> p h t e"), VMb2[bp], r=[('VMb2', bp)], w=[('VVd', j)])

        load(NT - 1, (NT - 1) % 2)
        load(NT - 2, (NT - 2) % 2)
        lnA(NT - 1)
        trA(NT - 1)
        projA(NT - 1)
        lnA(NT - 2)
        firstA(NT - 1)
        trA(NT - 2)
        for t in range(NT - 1, -1, -1):
            if t - 1 >= 0:
                projA(t - 1)
                if t - 2 >= 0:
                    load(t - 2, t % 2)
                    lnA(t - 2)
                firstA(t - 1)
                if t - 2 >= 0:
                    trA(t - 2)
            secondA(t)

        P.barrier()
        AR.release(aonly_mark)
        QT3 = AR.alloc([128, 3, 4, 128], BF16)
        KT = AR.alloc([128, 4, 128], BF16)
        AcT = AR.alloc([128, 4, 128], BF16)
        gnB = AR.alloc([128, 512], F32)
        dma('sp', gnB, gnw_in.partition_broadcast(128).rearrange("p a b -> p (a b)"), r=[], w=['gnB'])
        st4 = AR.alloc([128, 4, 6], F32)
        mv4 = AR.alloc([128, 4, 2], F32)
        rs4 = AR.alloc([128, 4], F32)
        retb = AR.alloc([128, 512], BF16)
        RTb = AR.alloc([128, 4, 512], BF16)
        QNb = AR.alloc([128, 4, 512], BF16)
        QRb = AR.alloc([64, 4, 512], BF16)
        qr1 = AR.alloc([64, 4, 128], F32)
        qr2 = AR.alloc([64, 4, 128], F32)
        xnB = [xn, AR.alloc([128, D], BF16)]
        xTB = [xT, AR.alloc([128, 8, 128], BF16)]
        ln_scratch('b0')
        ln_scratch('b1')
        tmt3 = [AR.alloc([64, 2, 128], F32) for _ in range(3)]
        Qp2 = [AR.alloc([128, 4, 128], BF16) for _ in range(2)]
        KpB = [AR.alloc([128, 4, 128], BF16) for _ in range(2)]
        VrB = [AR.alloc([128, 512], BF16) for _ in range(2)]
        gate2 = [AR.alloc([128, 512], BF16) for _ in range(2)]
        KwfB = [AR.alloc([128, 4, 128], BF16) for _ in range(2)]
        sqB = [AR.alloc([128, 3, 128], BF16) for _ in range(2)]
        cqB = [AR.alloc([128, 3, 128], BF16) for _ in range(2)]
        dkfB2 = dkk[:, 0:4].unsqueeze(2).broadcast_to([128, 4, 128])
        y0 = [AR.alloc([128, 4, 128], F32) for _ in range(2)]
        retb2 = [retb, AR.alloc([128, 512], BF16)]

        def loadB(c):
            b = c % 2
            dma('sp', xs[b], xj[c * 128:(c + 1) * 128, :], r=[], w=[('xs', b)])
            dma('sp', tbt[b], tbj[c * 128:(c + 1) * 128, :], r=[], w=[('tbt', b)])
            dma('sp', tmt3[c % 3], tmj[:, :, c * 128:(c + 1) * 128], r=[], w=[('tmt3', c % 3)])

        def lnB_(c):
            b = c % 2
            ln_apply(xs[b], xnB[b], 'b%d' % b, [('xs', b)], [('xnB', b)])

        def trB(c):
            b = c % 2
            for k in range(8):
                tr(pT[:, k, :], xnB[b][:, k * 128:(k + 1) * 128], [('xnB', b)], [('pb', 0)])
            cp('act', xTB[b], pT, [('pb', 0)], [('xTB', b)])

        def projB(c):
            b = c % 2
            xTb = xTB[b]
            xtok = ('xTB', b)
            for (pb_, c0) in ((1, 0), (2, 512), (3, 1024), (4, 1536)):
                for k in range(8):
                    mm(PB[pb_][:, :], xTb[:, k, :], Win[:, k, c0:c0 + 512], k == 0, False, [xtok, 'Win'], [('pb', pb_)])
                mm(PB[pb_][:, :], ones_bf[0:1, :], brow[0:1, c0:c0 + 512], False, True, ['ones_bf', 'brow'], [('pb', pb_)])
            for cj in range(3):
                for k in range(8):
                    mm(PB[5][:, cj * 128:(cj + 1) * 128], Win[:, k, 2048 + cj * 128:2048 + (cj + 1) * 128], xTb[:, k, :],
                       k == 0, k == 7, ['Win', xtok], [('pb', 5)])

        def evacB(c):
            b = c % 2
            rotary_tok(PB[1], ('pb', 1), b, Qp2[b], ('Qp2', b))
            rotary_tok(PB[2], ('pb', 2), b, KpB[b], ('KpB', b))
            cp('act', VrB[b], PB[3][:, :], [('pb', 3)], [('VrB', b)])
            for cj in range(3):
                act(sqB[b][:, cj, :], PB[5][:, cj * 128:(cj + 1) * 128], AF.Square, [('pb', 5), 'bcfm'], [('sqB', b)],
                    bias=bcfm[:, cj:cj + 1])
                act(cqB[b][:, cj, :], PB[5][:, cj * 128:(cj + 1) * 128], AF.Identity, [('pb', 5), 'bcfm'], [('cqB', b)],
                    bias=bcfm[:, cj:cj + 1])
            act(gate2[b], PB[4][:, :], AF.Silu, [('pb', 4)], [('gate2', b)])
            tt('pool', gate2[b], gate2[b], gnB, ALU.mult, [('gate2', b), 'gnB'], [('gate2', b)])
            tt('pool', KwfB[b], KpB[b], dkfB2, ALU.mult, [('KpB', b), 'dkk'], [('KwfB', b)])

        def secondB(c):
            b = c % 2
            q4 = c % 4
            tm_ = tmt3[c % 3]
            tmtok = ('tmt3', c % 3)
            pQK = PB[0][:, :].bitcast(BF16).rearrange("p (a b) -> p a b", a=8)[:, :, 0:128]
            for h in range(4):
                tr(pQK[:, h, :], Qp2[b][:, h, :], [('Qp2', b)], [('pb', 0)])
                tr(pQK[:, 4 + h, :], KpB[b][:, h, :], [('KpB', b)], [('pb', 0)])
            cp('act', QT3[:, 0], pQK[:, 0:4, :], [('pb', 0)], ['QT0'])
            cp('act', KT, pQK[:, 4:8, :], [('pb', 0)], ['KT'])
            tt('dve', QT3[:, 1], pQK[:, 0:4, :], dqfB, ALU.mult, [('pb', 0), 'dqfB'], ['QT1'])
            tt('dve', QT3[:, 2], pQK[:, 0:4, :], dqbB, ALU.mult, [('pb', 0), 'dqbB'], ['QT2'])
            for cj in range(3):
                mm(PB[5][:, 128:256], ones_bf, sqB[b][:, cj, :], cj == 0, cj == 2, ['ones_bf', ('sqB', b)], [('pb', 5)])
            Ap = PB[6][:, :].rearrange("p (h d) -> p h d", h=4)
            for h in range(4):
                mm(Ap[:, h, :], KT[:, h, :], QT3[:, 0, h, :], True, True, ['KT', 'QT0'], [('pb', 6)])
            cp('dve', rB, PB[5][:, 128:256], [('pb', 5)], ['rB'])
            rsqrt_small(rB, 'rB', 1.0 / 384.0, 1e-6)
            tt('dve', AcT, Ap, DcT, ALU.mult, [('pb', 6), 'DcT'], ['AcT'])
            cp('act', Sbf, Sf, ['Sf'], ['Sbf'])
            qnp = PB[5][:, :].rearrange("p (h d) -> p h d", h=4)
            for h in range(4):
                for cj in range(3):
                    mm(qnp[:, h, :], Wuq[:, cj, h, 0:128], cqB[b][:, cj, :], cj == 0, cj == 2, ['Wuq', ('cqB', b)], [('pb', 5)])
            qrp = PB[1][0:64, :].rearrange("p (h d) -> p h d", h=4)
            qrrp = PB[2][0:64, :].rearrange("p (h d) -> p h d", h=4)
            for h in range(4):
                for cj in range(3):
                    mm(qrp[:, h, :], Wuq[:, cj, h, 128:192], cqB[b][:, cj, :], cj == 0, cj == 2, ['Wuq', ('cqB', b)], [('pb', 1)])
                for cj in range(3):
                    mm(qrrp[:, h, :], Wuqr[:, cj, h, :], cqB[b][:, cj, :], cj == 0, cj == 2, ['Wuqr', ('cqB', b)], [('pb', 2)])
            Op = PB[7][:, :].rearrange("p (h d) -> p h d", h=4)
            for h in range(4):
                mm(Op[:, h, :], AcT[:, h, :], VrB[b][:, h * 128:(h + 1) * 128], True, False, ['AcT', ('VrB', b)], [('pb', 7)])
                mm(Op[:, h, :], QT3[:, 1, h, :], Sbf[:, h, :], False, False, ['QT1', 'Sbf'], [('pb', 7)])
                mm(Op[:, h, :], QT3[:, 2, h, :], Tsave[:, c, h, :], False, True, ['QT2', 'Tsave'], [('pb', 7)])
            Pf = PB[6][:, :].rearrange("p (h d) -> p h d", h=4)
            for h in range(4):
                mm(Pf[:, h, :], KwfB[b][:, h, :], VrB[b][:, h * 128:(h + 1) * 128], True, True, [('KwfB', b), ('VrB', b)], [('pb', 6)])
            tt('dve', QNb[:, :, q4 * 128:(q4 + 1) * 128], qnp, rB.unsqueeze(1).broadcast_to([128, 4, 128]), ALU.mult,
               [('pb', 5), 'rB'], ['QNb'])
            tt('dve', qr1, qrp, tm_[:, 0, :].unsqueeze(1).broadcast_to([64, 4, 128]), ALU.mult, [('pb', 1), tmtok], ['qr1'])
            tt('dve', qr2, qrrp, tm_[:, 1, :].unsqueeze(1).broadcast_to([64, 4, 128]), ALU.mult, [('pb', 2), tmtok], ['qr2'])
            tt('pool', qr1, qr1, qr2, ALU.add, ['qr1', 'qr2'], ['qr1'])
            tt('pool', QRb[:, :, q4 * 128:(q4 + 1) * 128], qr1, rB[0:64, :].unsqueeze(1).broadcast_to([64, 4, 128]),
               ALU.mult, ['qr1', 'rB'], ['QRb'])
            for h in range(4):
                stt('dve', Sf[:, h, :], Sf[:, h, :], dcc[:, h:h + 1], Pf[:, h, :], ALU.mult, ALU.add,
                    [('pb', 6), 'dcc', 'Sf', 'Sbf'], ['Sf'])
            cp('act', y0[b], Op, [('pb', 7)], [('y0', b)])
            if q4 == 3:
                t0 = c - 3
                dma('sp', QTd[j][:, :, t0 * 128:(t0 + 4) * 128].rearrange("h p n -> p h n"), QNb, r=['QNb'], w=[('QTd', j)])
                dma('sp', QRd[j][:, :, t0 * 128:(t0 + 4) * 128].rearrange("h p n -> p h n"), QRb, r=['QRb'], w=[('QRd', j)])

        def thirdB_vec(c):
            b = c % 2
            for h in range(4):
                bnstats(st4[:, h, :], y0[b][:, h, :], [('y0', b)], ['st4'])
            for h in range(4):
                bnaggr(mv4[:, h, :], st4[:, h, :], ['st4'], ['mv4'])
            cp('dve', rs4, mv4[:, :, 1], ['mv4'], ['rs4'])
            rsqrt_small(rs4, 'rs4', 1.0, 1e-5)
            tt('dve', y0[b], y0[b], mv4[:, :, 0:1].broadcast_to([128, 4, 128]), ALU.subtract, [('y0', b), 'mv4'], [('y0', b)])
            tt('pool', y0[b], y0[b], rs4.unsqueeze(2).broadcast_to([128, 4, 128]), ALU.mult, [('y0', b), 'rs4'], [('y0', b)])
            tt('pool', retb2[b], y0[b].rearrange("p h d -> p (h d)"), gate2[b], ALU.mult, [('y0', b), ('gate2', b)], [('retb2', b)])

        def thirdB_pe(c):
            b = c % 2
            q4 = c % 4
            pR = PB[0][:, :].bitcast(BF16).rearrange("p (a b) -> p a b", a=8)[:, :, 0:128]
            for h in range(4):
                tr(pR[:, h, :], retb2[b][:, h * 128:(h + 1) * 128], [('retb2', b)], [('pb', 0)])
            cp('act', RTb[:, :, q4 * 128:(q4 + 1) * 128], pR[:, 0:4, :], [('pb', 0)], ['RTb'])
            if q4 == 3:
                t0 = c - 3
                dma('sp', RTd[j][:, :, t0 * 128:(t0 + 4) * 128].rearrange("h p n -> p h n"), RTb, r=['RTb'], w=[('RTd', j)])

        loadB(0)
        loadB(1)
        lnB_(0)
        trB(0)
        projB(0)
        lnB_(1)
        evacB(0)
        trB(1)
        for c in range(NCH):
            if c >= 1:
                thirdB_vec(c - 1)
            if c + 1 < NCH:
                projB(c + 1)
                if c + 2 < NCH:
                    loadB(c + 2)
                    lnB_(c + 2)
                evacB(c + 1)
                if c + 2 < NCH:
                    trB(c + 2)
            secondB(c)
            if c >= 1:
                thirdB_pe(c - 1)
        thirdB_vec(NCH - 1)
        thirdB_pe(NCH - 1)

        P.barrier()
        AR.release(job_mark)
        KTh = AR.alloc([128, S], BF16)
        KRh = AR.alloc([128, S], BF16)
        Vh = AR.alloc([128, NT, 128], BF16)
        QTh = [AR.alloc([128, L], BF16) for _ in range(2)]
        QRh = [AR.alloc([128, L], BF16) for _ in range(2)]
        memset('pool', KRh, 0.0, ['KRh'])
        memset('pool', QRh[0], 0.0, [('QRh', 0)])
        memset('pool', QRh[1], 0.0, [('QRh', 1)])
        NPT = 4
        PT = [AR.alloc([128, 512], BF16) for _ in range(NPT)]
        dacc = [AR.alloc([128, 512], F32) for _ in range(6)]
        rden = AR.alloc([128, 512], F32)
        ato = AR.alloc([128, 512], BF16)
        dma('sp', KRh[0:64, :], KRd[j], r=[('KRd', j)], w=['KRh'])
        NQB = L // 512
        NH2 = NT // 2

        def load_head(h, hf):
            hp = h % 2
            dma('sp', KTh[:, hf * NH2 * 128:(hf + 1) * NH2 * 128], KTd[j][h][:, hf * NH2 * 128:(hf + 1) * NH2 * 128],
                r=[('KTd', j)], w=[('KTh', hf)])
            dma('sp', Vh[:, hf * NH2:(hf + 1) * NH2, :], VVd[j][h][:, hf * NH2:(hf + 1) * NH2, :],
                r=[('VVd', j)], w=[('Vh', hf)])
            if hf == 0:
                dma('sp', QTh[hp], QTd[j][h], r=[('QTd', j)], w=[('QTh', hp)])
                dma('sp', QRh[hp][0:64, :], QRd[j][h], r=[('QRd', j)], w=[('QRh', hp)])

        items = [(h, qb, kt) for h in range(4) for qb in range(NQB) for kt in range(NT)]

        def qk(item, gi_):
            h, qb, kt = item
            hp = h % 2
            if qb == 0 and kt == 0:
                load_head(h, 0)
                if h == 0:
                    load_head(h, 1)
            qs = slice(qb * 512, (qb + 1) * 512)
            sb = gi_ % NPT
            hf = kt // NH2
            mm(PB[sb][:, :], KTh[:, kt * 128:(kt + 1) * 128], QTh[hp][:, qs], True, False,
               [('KTh', hf), ('QTh', hp)], [('pb', sb)])
            mm(PB[sb][:, :], KRh[:, kt * 128:(kt + 1) * 128], QRh[hp][:, qs], False, True,
               ['KRh', ('QRh', hp)], [('pb', sb)])
            act(PT[sb], PB[sb][:, :], AF.Exp, [('pb', sb)], [('PT', sb)])

        def pv(item, gi_):
            h, qb, kt = item
            qs = slice(qb * 512, (qb + 1) * 512)
            blk = h * NQB + qb
            ob = 4 + (blk % 2)
            db = 6 + (blk % 2)
            sb = gi_ % NPT
            hf = kt // NH2
            mm(PB[ob][:, :], Vh[:, kt, :], PT[sb], kt == 0, kt == NT - 1, [('Vh', hf), ('PT', sb)], [('pb', ob)])
            ai, eng_ = kt % 3, 'dve'
            da = dacc[(blk % 2) * 3 + ai]
            dtok = ('dacc', blk % 2, ai)
            if kt in (0, 1, 2):
                cp(eng_, da, PT[sb], [('PT', sb)], [dtok])
            else:
                tt(eng_, da, da, PT[sb], ALU.add, [dtok, ('PT', sb)], [dtok], nosync=(eng_ == 'dve'))
            if kt == NT - 1:
                for ai2 in range(3):
                    mm(PB[db][:, :], ones_f, dacc[(blk % 2) * 3 + ai2], ai2 == 0, ai2 == 2,
                       ['ones_f', ('dacc', blk % 2, ai2)], [('pb', db)])
                recip(rden, PB[db][:, :], [('pb', db)], ['rden'])
                tt('dve', ato, PB[ob][:, :], rden, ALU.mult, [('pb', ob), 'rden'], ['ato'])
                dma('sp', ATd[j][h][:, qs], ato, r=['ato'], w=[('ATd', j)])
                if qb == NQB - 1 and h < 3:
                    load_head(h + 1, 1)

        LA = 3
        for i_ in range(min(LA, len(items))):
            qk(items[i_], i_)
        for i_ in range(len(items)):
            if i_ + LA < len(items):
                qk(items[i_ + LA], i_ + LA)
            pv(items[i_], i_)

        P.barrier()
        AR.release(job_mark)
        lnB = AR.alloc([128, 2, 1024], F32)
        dma('sp', lnB, ln_in[0:2, :].partition_broadcast(128), r=[], w=['lnB'])
        Wo = AR.alloc([128, 8, 1024], BF16)
        d_mark = AR.mark()
        wst = [AR.alloc([128, 8, 512], F32) for _ in range(2)]
        gi = 0
        for g in range(2):
            b = gi % 2
            gi += 1
            dma('sp', wst[b], wo_in[:, g * 512:(g + 1) * 512].rearrange("(k p) n -> p k n", p=128), r=[], w=[('wst', b)])
            for k in range(8):
                tt('dve', Wo[:, k, g * 512:(g + 1) * 512], wst[b][:, k, :], gB[:, 0, g * 512:(g + 1) * 512], ALU.mult,
                   [('wst', b), 'gB'], ['Wo'])
        cat2 = [AR.alloc([128, 8, 512], BF16) for _ in range(2)]
        xo = [AR.alloc([128, D], F32) for _ in range(2)]
        yv2 = [AR.alloc([128, D], F32) for _ in range(2)]
        x1 = [AR.alloc([128, D], F32) for _ in range(2)]
        ln_scratch('d1')
        NT1 = L // 128

        def load_d1(tix):
            m, u = tix // 4, tix % 4
            if u == 0:
                ms = slice(m * 512, (m + 1) * 512)
                cb = m % 2
                dma('act', cat2[cb][:, 0:4, :], RTd[j][:, :, ms].rearrange("h p n -> p h n"), r=[('RTd', j)], w=[('cat', cb, 0)])
                dma('act', cat2[cb][:, 4:8, :], ATd[j][:, :, ms].rearrange("h p n -> p h n"), r=[('ATd', j)], w=[('cat', cb, 1)])
            dma('act', xo[tix % 2], xj[tix * 128:(tix + 1) * 128, :], r=[], w=[('xo', tix % 2)])

        load_d1(0)
        for tix in range(NT1):
            m, u = tix // 4, tix % 4
            b = tix % 2
            cb = m % 2
            if tix + 1 < NT1:
                load_d1(tix + 1)
            for half in range(2):
                for k in range(8):
                    mm(PB[half][:, :], cat2[cb][:, k, u * 128:(u + 1) * 128], Wo[:, k, half * 512:(half + 1) * 512],
                       k == 0, k == 7, [('cat', cb, 0), ('cat', cb, 1), 'Wo'], [('pb', half)])
                stt('dve', yv2[b][:, half * 512:(half + 1) * 512], xo[b][:, half * 512:(half + 1) * 512], ALPHA,
                    PB[half][:, :], ALU.mult, ALU.add, [('xo', b), ('pb', half)], [('yv', b)])
            ln_apply(yv2[b], yv2[b], 'd1', [('yv', b)], [('yv', b)])
            tt('pool', yv2[b], yv2[b], lnB[:, 0, :], ALU.mult, [('yv', b), 'lnB'], [('yv', b)])
            tt('pool', x1[b], yv2[b], lnB[:, 1, :], ALU.add, [('yv', b), 'lnB'], [('x1', b)])
            dma('sp', X1d[j][tix * 128:(tix + 1) * 128, :], x1[b], r=[('x1', b)], w=[('X1d', j)])

        P.barrier()
        AR.release(job_mark)
        lnB = AR.alloc([128, 2, 1024], F32)
        dma('sp', lnB, ln_in[2:4, :].partition_broadcast(128), r=[], w=['lnB'])
        Wup = AR.alloc([128, 8, DFF], BF16)
        Wdn = AR.alloc([128, 32, D], BF16)
        bup = AR.alloc([128, 32], F32)
        e_mark = AR.mark()
        wst = [AR.alloc([128, 8, 512], F32) for _ in range(2)]
        gi = 0
        for g in range(8):
            b = gi % 2
            gi += 1
            dma('sp', wst[b], wup_in[:, g * 512:(g + 1) * 512].rearrange("(k p) n -> p k n", p=128), r=[], w=[('wst', b)])
            for c in range(4):
                for k in range(8):
                    mm(PB[1][:, 0:1], wst[b][:, k, c * 128:(c + 1) * 128], modfm[:, 24 + k, j:j + 1], k == 0, k == 7,
                       [('wst', b), 'modfm'], [('pb', 1)])
                cp('dve', bup[:, g * 4 + c:g * 4 + c + 1], PB[1][:, 0:1], [('pb', 1)], ['bup'])
            for k in range(8):
                ts('dve', Wup[:, k, g * 512:(g + 1) * 512], wst[b][:, k, :], scp[:, 1, k, j:j + 1], None, ALU.mult, None,
                   [('wst', b), 'scp'], ['Wup'])
        for g in range(8):
            b = gi % 2
            gi += 1
            fg, half = g // 2, g % 2
            dma('sp', wst[b], wdn_in[fg * 1024:(fg + 1) * 1024, half * 512:(half + 1) * 512].rearrange("(k p) n -> p k n", p=128),
                r=[], w=[('wst', b)])
            for k in range(8):
                tt('dve', Wdn[:, fg * 8 + k, half * 512:(half + 1) * 512], wst[b][:, k, :], gB[:, 1, half * 512:(half + 1) * 512],
                   ALU.mult, [('wst', b), 'gB'], ['Wdn'])
        P.barrier()
        AR.release(e_mark)
        TBK = 256
        NU = TBK // 128
        NB = L // TBK
        x1t = [[AR.alloc([128, D], F32) for _ in range(NU)] for _ in range(2)]
        xn2 = AR.alloc([128, D], BF16)
        h2T = [AR.alloc([128, 8, TBK], BF16) for _ in range(2)]
        u2 = [AR.alloc([128, 8, TBK], BF16) for _ in range(2)]
        rt = [AR.alloc([128, TBK], F32) for _ in range(2)]
        oo = [gB[:, 0, :], gB[:, 1, :]]
        ln_scratch('d2a')
        ln_scratch('d2b')
        pT2 = PB[6][:, :].bitcast(BF16).rearrange("p (a b) -> p a b", a=8)[:, :, 0:128]

        def front2(m):
            mp = m % 2
            for u in range(NU):
                tix = m * NU + u
                dma('sp', x1t[mp][u], X1d[j][tix * 128:(tix + 1) * 128, :], r=[('X1d', j)], w=[('x1t', mp, u)])
                ln_apply(x1t[mp][u], xn2, 'd2a', [('x1t', mp, u)], ['xn2'])
                for k in range(8):
                    tr(pT2[:, k, :], xn2[:, k * 128:(k + 1) * 128], ['xn2'], [('pb', 6)])
                cp('act', h2T[mp][:, :, u * 128:(u + 1) * 128], pT2, [('pb', 6)], [('h2T', mp)])

        def up(m, g):
            mp = m % 2
            ub = (m * 4 + g) % 2
            for fl in range(8):
                f = g * 8 + fl
                pb = 4 + (f % 2)
                for k in range(8):
                    mm(PB[pb][:, 0:TBK], Wup[:, k, f * 128:(f + 1) * 128], h2T[mp][:, k, :], k == 0, k == 7,
                       ['Wup', ('h2T', mp)], [('pb', pb)])
                rb = f % 2
                act(rt[rb], PB[pb][:, 0:TBK], AF.Relu, [('pb', pb), 'bup'], [('rt', rb)], bias=bup[:, f:f + 1])
                tt('pool' if f % 2 else 'dve', u2[ub][:, fl, :], rt[rb], rt[rb], ALU.mult, [('rt', rb)], [('u2', ub)])

        def down(m, g):
            ub = (m * 4 + g) % 2
            for u in range(NU):
                for half in range(2):
                    pbd = u * 2 + half
                    for fl in range(8):
                        f = g * 8 + fl
                        mm(PB[pbd][:, :], u2[ub][:, fl, u * 128:(u + 1) * 128], Wdn[:, f, half * 512:(half + 1) * 512],
                           f == 0, f == 31, [('u2', ub), 'Wdn'], [('pb', pbd)])

        def tail(m):
            mp = m % 2
            for u in range(NU):
                tix = m * NU + u
                ob = tix % 2
                for half in range(2):
                    pbd = u * 2 + half
                    stt('dve', oo[ob][:, half * 512:(half + 1) * 512], x1t[mp][u][:, half * 512:(half + 1) * 512], ALPHA,
                        PB[pbd][:, :], ALU.mult, ALU.add, [('x1t', mp, u), ('pb', pbd)], [('oo', ob)])
                ln_apply(oo[ob], oo[ob], 'd2b', [('oo', ob)], [('oo', ob)])
                tt('pool', oo[ob], oo[ob], lnB[:, 0, :], ALU.mult, [('oo', ob), 'lnB'], [('oo', ob)])
                tt('pool', oo[ob], oo[ob], lnB[:, 1, :], ALU.add, [('oo', ob), 'lnB'], [('oo', ob)])
                dma('sp', y_out[j][tix * 128:(tix + 1) * 128, :], oo[ob], r=[('oo', ob)], w=[('yout', j, ob)], outp=True)

        G = [(m, g) for m in range(NB) for g in range(4)]
        front2(0)
        up(0, 0)
        for i_, (m, g) in enumerate(G):
            if i_ + 1 < len(G):
                m2, g2 = G[i_ + 1]
                if g2 == 0:
                    front2(m2)
                up(m2, g2)
            down(m, g)
            if g == 3:
                tail(m)

    P.emit()
    return nc, P


def _rope_tables(pos, d):
    inv = (10000.0 ** (-np.arange(0, d, 2, dtype=np.float32) / np.float32(d))).astype(np.float32)
    ang = pos.astype(np.float32)[:, None] * inv[None, :]
    return np.cos(ang).astype(np.float32), np.sin(ang).astype(np.float32)


def host_inputs(L, NSLOT, core_jobs, xs, cs, w):
    NJ = len(NSLOT)
    NCH = L // 128
    p = np.arange(128, dtype=np.float32)
    i = np.arange(128, dtype=np.float32)
    cst = np.zeros((128, 4 * 128 + 8), np.float32)
    cst[:, 0:128] = np.maximum(i[None, :] - p[:, None], 0)
    cst[:, 128:256] = np.maximum(p[:, None] - i[None, :], 0)
    cst[:, 256:384] = (p[:, None] <= i[None, :])
    cst[:, 384:512] = (p[:, None] > i[None, :])
    cst[:, 512] = 127 - p
    cst[:, 513] = p
    cst[:, 514] = 128.0
    shared = dict(
        cst=cst,
        w_ada=np.ascontiguousarray(w['w_ada'][0]), b_ada_fm=np.ascontiguousarray(w['b_ada'][0].reshape(48, 128).T),
        b_ada_row=np.ascontiguousarray(w['b_ada'][0].reshape(1, -1)),
        w_in=np.ascontiguousarray(w['w_in'][0]), w_uq=np.ascontiguousarray(w['w_uq'][0]),
        w_ukv=np.ascontiguousarray(w['w_ukv'][0]), w_o=np.ascontiguousarray(w['w_o'][0]),
        w_up=np.ascontiguousarray(w['w_up'][0]), w_down=np.ascontiguousarray(w['w_down'][0]),
        dec=np.concatenate([w['ret_decay_f'][0], w['ret_decay_b'][0]]).reshape(1, 8).astype(np.float32),
        gnw=np.ascontiguousarray(w['ret_gn_w'][0].reshape(1, 512)),
        qnw_fm=np.ascontiguousarray(w['q_norm_w'][0].reshape(3, 128).T),
        kvnw_fm=np.ascontiguousarray(w['kv_norm_w'][0].reshape(2, 128).T),
        ln=np.ascontiguousarray(np.stack([w['ln1_w'][0], w['ln1_b'][0], w['ln2_w'][0], w['ln2_b'][0]])),
    )
    in_maps = []
    for core, jobs in enumerate(core_jobs):
        m = dict(shared)
        cT = np.zeros((128, 8, NJ), np.float32)
        mk = np.zeros((128, NJ, 2, 128), np.float32)
        ex = np.zeros((128, NJ, 2, 128), np.float32)
        for j, (bi, own) in enumerate(jobs):
            NS = NSLOT[j]
            S = NS * L
            pos = (np.arange(S) + own * L) % S
            m["x%d" % j] = np.ascontiguousarray(np.roll(xs[j][bi], -own * L, axis=0))
            cR, sR = _rope_tables(pos, 128)
            m["tb%d" % j] = np.ascontiguousarray(np.concatenate([cR, cR, sR, sR], axis=1))
            cM, sM = _rope_tables(pos, 64)
            tm = np.stack([np.concatenate([cM, cM], 1).T, np.concatenate([sM, sM], 1).T], axis=1)
            m["tm%d" % j] = np.ascontiguousarray(tm.astype(np.float32))
            cT[:, :, j] = cs[j][bi].reshape(8, 128).T
            for s in range(1, NS):
                before = (own + s) >= NS
                for n in range(NCH):
                    t = s * NCH + n
                    ex[:, j, 0, t] = (NS - s - 1) * L + (L - 128 - 128 * n)
                    ex[:, j, 1, t] = (s - 1) * L + 128 * n
                    mk[:, j, 0, t] = 1.0 if before else 0.0
                    mk[:, j, 1, t] = 0.0 if before else 1.0
        m["cT"] = cT
        m["mk"] = mk
        m["ex"] = ex
        in_maps.append(m)
    return in_maps


_CACHE = {}


def kernel(x_prompt, x_sample, c_prompt, c_sample, **w):
    L = 4096
    NSLOT = [4, 2]
    x_prompt = np.asarray(x_prompt, np.float32)
    x_sample = np.asarray(x_sample, np.float32)
    c_prompt = np.asarray(c_prompt, np.float32)
    c_sample = np.asarray(c_sample, np.float32)
    w = {k: np.asarray(v, np.float32) for k, v in w.items()}
    core_jobs = [[(c // 4, c % 4), (c // 2, c % 2)] for c in range(NCORES)]
    in_maps = host_inputs(L, NSLOT, core_jobs, [x_prompt, x_sample], [c_prompt, c_sample], w)
    if 'nc' not in _CACHE:
        _CACHE['nc'] = build(L, NSLOT)[0]
    res = run_bass_kernel_spmd(_CACHE['nc'], in_maps, core_ids=list(range(NCORES)))
    yp = np.empty_like(x_prompt)
    ysm = np.empty_like(x_sample)
    for c in range(NCORES):
        r = res.results[c]
        yp[c // 4, (c % 4) * L:(c % 4 + 1) * L] = r["y0"]
        ysm[c // 2, (c % 2) * L:(c % 2 + 1) * L] = r["y1"]
    return (yp, ysm)
```

```python
import math
import numpy as np
import concourse.bass as bass
import concourse.mybir as mybir
from concourse.bass_utils import run_bass_kernel_spmd

F32 = mybir.dt.float32
BF16 = mybir.dt.bfloat16
AF = mybir.ActivationFunctionType
ALU = mybir.AluOpType

D = 1024
DIN = 2752
DFF = 4096
ALPHA = float(2.0 ** 0.25)
MLA_SCALE = float(192 ** -0.5)
NCORES = 8


class Prog:
    SEM_LIMIT = 24000

    def __init__(self, nc):
        self.nc = nc
        self.ops = []
        self.last_w = {}
        self.readers = {}
        self.out_dma_ops = []
        self.barriers = []

    def add(self, eng, fn, r=(), w=(), dma=False, key=None, out=False, nosync=False):
        i = len(self.ops)
        if dma and key is None:
            key = w[0]
        ps = [t for t in r if isinstance(t, tuple) and t[0] == 'pb']
        r = [t for t in r if not (isinstance(t, tuple) and t[0] == 'pb')]
        w = list(w) + [t for t in ps if t not in w]
        deps = set()
        for t in r:
            j = self.last_w.get(t)
            if j is not None:
                deps.add((j, 'raw'))
        for t in w:
            j = self.last_w.get(t)
            if j is not None:
                deps.add((j, 'waw'))
            for k in self.readers.get(t, ()):
                deps.add((k, 'war'))
        for t in r:
            self.readers.setdefault(t, []).append(i)
        for t in w:
            self.last_w[t] = i
            self.readers[t] = []
        self.ops.append(dict(eng=eng, fn=fn, deps=deps, dma=dma, key=key, bar=len(self.barriers), nosync=nosync))
        if out:
            self.out_dma_ops.append(i)
        return i

    def barrier(self):
        self.barriers.append(len(self.ops))

    def emit(self):
        nc = self.nc
        ops = self.ops
        bar_sets = []
        for b in self.barriers:
            last = {}
            for i in range(b):
                op = ops[i]
                k = ('dma', op['key']) if op['dma'] else ('eng', op['eng'])
                last[k] = i
            bar_sets.append(set(last.values()))
        for i, op in enumerate(ops):
            need = set()
            for (j, kind) in op['deps']:
                pj = ops[j]
                if j == i:
                    continue
                if not pj['dma'] and not op['dma'] and pj['eng'] == op['eng']:
                    if op['eng'] == 'pe':
                        continue
                    if kind != 'raw' or op['nosync']:
                        continue
                need.add(j)
            if op['bar'] > 0:
                for j in bar_sets[op['bar'] - 1]:
                    pj = ops[j]
                    if not pj['dma'] and pj['eng'] == op['eng']:
                        continue
                    need.add(j)
            op['need'] = need
        has_dep = [False] * len(ops)
        for op in ops:
            for j in op['need']:
                has_dep[j] = True
        for j in self.out_dma_ops:
            has_dep[j] = True
        cnt = {}
        for i, op in enumerate(ops):
            if op['dma']:
                k = ('dma', op['key'])
            elif has_dep[i]:
                k = ('eng', op['eng'])
            else:
                op['sig'] = None
                continue
            cnt[k] = cnt.get(k, 0) + 1
            op['sig'] = (k, cnt[k])
        sems = {}
        cms = []
        for k, n in cnt.items():
            inc = 16 if k[0] == 'dma' else 1
            per = self.SEM_LIMIT // inc
            ns = (n + per - 1) // per
            lst = []
            for s in range(ns):
                cm = nc.semaphore("s%d_%d" % (len(sems), s))
                lst.append(cm.__enter__())
                cms.append(cm)
            sems[k] = (lst, inc, per)
        self.n_sems = len(cms)

        def semval(sig):
            k, n = sig
            lst, inc, per = sems[k]
            return lst[(n - 1) // per], ((n - 1) % per + 1) * inc

        engs = {}
        for i, op in enumerate(ops):
            engs.setdefault(op['eng'], []).append(i)

        def run(ename, e):
            waited = {}
            for i in engs.get(ename, []):
                op = ops[i]
                tgt = {}
                for j in op['need']:
                    k, n = ops[j]['sig']
                    if waited.get(k, 0) >= n:
                        continue
                    if tgt.get(k, 0) < n:
                        tgt[k] = n
                for k, n in tgt.items():
                    s, v = semval((k, n))
                    e.wait_ge(s, v)
                    waited[k] = n
                ins = op['fn'](e)
                if op['sig'] is not None:
                    s, v = semval(op['sig'])
                    ins.then_inc(s, 16 if op['dma'] else 1)
            if ename == 'sp':
                fin = {}
                for j in self.out_dma_ops:
                    k, n = ops[j]['sig']
                    fin[k] = max(fin.get(k, 0), n)
                for k, n in fin.items():
                    if waited.get(k, 0) >= n:
                        continue
                    s, v = semval((k, n))
                    e.wait_ge(s, v)

        with nc.Block() as block:
            @block.tensor
            def _(e):
                run('pe', e)

            @block.scalar
            def _(e):
                run('act', e)

            @block.vector
            def _(e):
                run('dve', e)

            @block.gpsimd
            def _(e):
                run('pool', e)

            @block.sync
            def _(e):
                run('sp', e)
        for cm in cms:
            cm.__exit__(None, None, None)


class Arena:
    def __init__(self, nc, words):
        self.t = nc.alloc_sbuf_tensor("arena", [128, words], F32)
        self.words = words
        self.top = 0

    def mark(self):
        return self.top

    def release(self, m):
        self.top = m

    def alloc(self, shape, dtype=F32):
        n = 1
        for s in shape[1:]:
            n *= s
        nw = n if dtype == F32 else (n + 1) // 2
        nw = (nw + 7) // 8 * 8
        assert self.top + nw <= self.words, ("SBUF arena overflow", self.top, nw, self.words)
        v = self.t[:, self.top:self.top + nw]
        self.top += nw
        if dtype != F32:
            v = v.bitcast(dtype)
        v = v[:, 0:n]
        if len(shape) == 3:
            v = v.rearrange("p (a b) -> p a b", a=shape[1])
        elif len(shape) == 4:
            v = v.rearrange("p (a b c) -> p a b c", a=shape[1], b=shape[2])
        elif len(shape) == 5:
            v = v.rearrange("p (a b c d) -> p a b c d", a=shape[1], b=shape[2], c=shape[3])
        if shape[0] < 128:
            v = v[0:shape[0]]
        return v


def build(L, NSLOT, debug=False):
    NJ = len(NSLOT)
    NCH = L // 128
    SS = [ns * L for ns in NSLOT]
    nc = bass.Bass("TRN2", target_bir_lowering=False)
    P = Prog(nc)

    def din(name, shape, dt=F32):
        return nc.dram_tensor(name, list(shape), dt, kind="ExternalInput").ap()

    def dscr(name, shape, dt=BF16):
        return nc.dram_tensor(name, list(shape), dt, kind=("ExternalOutput" if debug else "Internal")).ap()

    x_in = [din("x%d" % j, [SS[j], D]) for j in range(NJ)]
    tb_in = [din("tb%d" % j, [SS[j], 256]) for j in range(NJ)]
    tm_in = [din("tm%d" % j, [64, 2, SS[j]]) for j in range(NJ)]
    cT_in = din("cT", [128, 8, NJ])
    mk_in = din("mk", [128, NJ, 2, 128])
    ex_in = din("ex", [128, NJ, 2, 128])
    cst_in = din("cst", [128, 4 * 128 + 8])
    wada_in = din("w_ada", [D, 6 * D])
    bfm_in = din("b_ada_fm", [128, 48])
    brow_in = din("b_ada_row", [1, 6 * D])
    win_in = din("w_in", [D, DIN])
    wuq_in = din("w_uq", [384, 768])
    wukv_in = din("w_ukv", [256, 1024])
    wo_in = din("w_o", [D, D])
    wup_in = din("w_up", [D, DFF])
    wdn_in = din("w_down", [DFF, D])
    dec_in = din("dec", [1, 8])
    gnw_in = din("gnw", [1, 512])
    qnw_in = din("qnw_fm", [128, 3])
    kvnw_in = din("kvnw_fm", [128, 2])
    ln_in = din("ln", [4, D])
    y_out = [nc.dram_tensor("y%d" % j, [L, D], F32, kind="ExternalOutput").ap() for j in range(NJ)]

    KTd = [dscr("KTd%d" % j, [4, 128, SS[j]]) for j in range(NJ)]
    KRd = [dscr("KRd%d" % j, [64, SS[j]]) for j in range(NJ)]
    VVd = [dscr("VVd%d" % j, [4, 128, SS[j] // 128, 128]) for j in range(NJ)]
    QTd = [dscr("QTd%d" % j, [4, 128, L]) for j in range(NJ)]
    QRd = [dscr("QRd%d" % j, [4, 64, L]) for j in range(NJ)]
    RTd = [dscr("RTd%d" % j, [4, 128, L]) for j in range(NJ)]
    ATd = [dscr("ATd%d" % j, [4, 128, L]) for j in range(NJ)]
    X1d = [dscr("X1d%d" % j, [L, D], F32) for j in range(NJ)]

    AR = Arena(nc, 49100)
    PB = [nc.alloc_psum_tensor("pb%d" % i, [128, 512], F32) for i in range(8)]

    def mm(out, lhsT, rhs, start, stop, r, w):
        P.add('pe', lambda e: e.matmul(out, lhsT=lhsT, rhs=rhs, start=start, stop=stop), r=r, w=w)

    def act(out, in_, func, r, w, bias=None, scale=None):
        kw = {}
        if bias is not None:
            kw['bias'] = bias
        if scale is not None:
            kw['scale'] = scale
        P.add('act', lambda e: e.activation(out=out, in_=in_, func=func, **kw), r=r, w=w)

    def tt(eng, out, in0, in1, op, r, w, nosync=False):
        P.add(eng, lambda e: e.tensor_tensor(out=out, in0=in0, in1=in1, op=op), r=r, w=w, nosync=nosync)

    def ts(eng, out, in0, s1, s2, op0, op1, r, w):
        if s2 is None:
            P.add(eng, lambda e: e.tensor_scalar(out=out, in0=in0, scalar1=s1, scalar2=None, op0=op0), r=r, w=w)
        else:
            P.add(eng, lambda e: e.tensor_scalar(out=out, in0=in0, scalar1=s1, scalar2=s2, op0=op0, op1=op1), r=r, w=w)

    def stt(eng, out, in0, scalar, in1, op0, op1, r, w):
        P.add(eng, lambda e: e.scalar_tensor_tensor(out=out, in0=in0, scalar=scalar, in1=in1, op0=op0, op1=op1),
              r=r, w=w)

    def cp(eng, out, in_, r, w):
        if eng == 'act':
            P.add('act', lambda e: e.copy(out=out, in_=in_), r=r, w=w)
        else:
            P.add(eng, lambda e: e.tensor_copy(out=out, in_=in_), r=r, w=w)

    def dma(q, out, in_, r, w, key=None, outp=False):
        P.add(q, lambda e: e.dma_start(out=out, in_=in_), r=r, w=w, dma=True, key=key, out=outp)

    def recip(out, in_, r, w):
        P.add('dve', lambda e: e.reciprocal(out=out, in_=in_), r=r, w=w)

    def bnstats(out, in_, r, w):
        P.add('dve', lambda e: e.bn_stats(out=out, in_=in_), r=r, w=w)

    def bnaggr(out, in_, r, w):
        P.add('dve', lambda e: e.bn_aggr(out=out, in_=in_), r=r, w=w)

    def actmul(out, in_, m, r, w):
        P.add('act', lambda e: e.mul(out=out, in_=in_, mul=m), r=r, w=w)

    def memset(eng, ap, v, w):
        P.add(eng, lambda e: e.memset(ap, v), w=w)

    ident_bf = AR.alloc([128, 128], BF16)
    ident_f = AR.alloc([128, 128], F32)
    ones_bf = AR.alloc([128, 128], BF16)
    ones_f = AR.alloc([128, 128], F32)
    cst = AR.alloc([128, 4 * 128 + 8], F32)
    dma('sp', cst, cst_in, r=[], w=['cst'])
    memset('pool', ident_f, 1.0, ['ident_f'])
    P.add('pool', lambda e: e.affine_select(out=ident_f, in_=ident_f, pattern=[[-1, 128]], compare_op=ALU.is_equal,
                                            fill=0.0, base=0, channel_multiplier=1), r=['ident_f'], w=['ident_f'])
    cp('pool', ident_bf, ident_f, ['ident_f'], ['ident_bf'])
    memset('pool', ones_f, 1.0, ['ones_f'])
    memset('pool', ones_bf, 1.0, ['ones_bf'])

    def tr(out, in_, r, w):
        k = in_.shape[0]
        P.add('pe', lambda e: e.transpose(out=out, in_=in_, identity=ident_bf[0:k, 0:k]), r=list(r) + ['ident_bf'], w=w)

    RF = cst[:, 0:128]
    RB = cst[:, 128:256]
    MF = cst[:, 256:384]
    MB = cst[:, 384:512]
    c_127mj = cst[:, 512:513]
    c_j = cst[:, 513:514]
    c_128 = cst[:, 514:515]

    dec = AR.alloc([128, 8], F32)
    dma('sp', dec, dec_in.partition_broadcast(128).rearrange("p a b -> p (a b)"), r=[], w=['dec'])
    lg = AR.alloc([128, 8], F32)
    act(lg, dec, AF.Exp, ['dec'], ['lg'], scale=-1.0)
    ts('dve', lg, lg, 1.0, None, ALU.add, None, ['lg'], ['lg'])
    act(lg, lg, AF.Ln, ['lg'], ['lg'])
    ts('dve', lg, lg, -1.0, None, ALU.mult, None, ['lg'], ['lg'])
    qnw = AR.alloc([128, 3], F32)
    kvnw = AR.alloc([128, 2], F32)
    dma('sp', qnw, qnw_in, r=[], w=['qnw'])
    dma('sp', kvnw, kvnw_in, r=[], w=['kvnw'])
    bfm = AR.alloc([128, 48], F32)
    dma('sp', bfm, bfm_in, r=[], w=['bfm'])
    cT = AR.alloc([128, 8, NJ], F32)
    dma('sp', cT, cT_in, r=[], w=['cT'])
    siluc = AR.alloc([128, 8, NJ], F32)
    act(siluc, cT, AF.Silu, ['cT'], ['siluc'])
    modfm = AR.alloc([128, 48, NJ], F32)
    scp = AR.alloc([128, 2, 8, NJ], F32)

    persist_mark = AR.mark()

    def ln_apply(src, dst, tag, rtok, wtok, eps=1e-5, eng='dve'):
        st, mv, rs = LNS[tag]
        for c in range(2):
            P.add('dve', (lambda e, c=c: e.bn_stats(out=st[:, c, :], in_=src[:, c * 512:(c + 1) * 512])),
                  r=rtok, w=[('st', tag, c)])
        P.add('dve', lambda e: e.bn_aggr(out=mv, in_=st), r=[('st', tag, 0), ('st', tag, 1)], w=[('mv', tag)])
        ts('dve', rs, mv[:, 1:2], eps, None, ALU.add, None, [('mv', tag)], [('rs', tag)])
        P.add('dve', lambda e: e.reciprocal(out=rs, in_=rs), r=[('rs', tag)], w=[('rs', tag)])
        P.add('act', lambda e: e.sqrt(out=rs, in_=rs), r=[('rs', tag)], w=[('rs', tag)])
        ts(eng, dst, src, mv[:, 0:1], rs[:, 0:1], ALU.subtract, ALU.mult, list(rtok) + [('mv', tag), ('rs', tag)], wtok)

    LNS = {}

    def ln_scratch(tag):
        LNS[tag] = (AR.alloc([128, 2, 6], F32), AR.alloc([128, 2], F32), AR.alloc([128, 1], F32))

    def rsqrt_small(x, tag, mul, eps):
        ts('dve', x, x, mul, eps, ALU.mult, ALU.add, [tag], [tag])
        P.add('dve', lambda e: e.reciprocal(out=x, in_=x), r=[tag], w=[tag])
        P.add('act', lambda e: e.sqrt(out=x, in_=x), r=[tag], w=[tag])

    for j in range(NJ):
        S = SS[j]
        NS = NSLOT[j]
        NT = S // 128
        xj, tbj, tmj = x_in[j], tb_in[j], tm_in[j]
        P.barrier()
        AR.release(persist_mark)
        gB = AR.alloc([128, 2, 1024], F32)
        job_mark = AR.mark()
        Win = AR.alloc([128, 8, DIN], BF16)
        Wkrr = AR.alloc([128, 8, 64], BF16)
        brow = AR.alloc([1, DIN], BF16)
        bcfm = AR.alloc([128, 5], F32)
        bkr = AR.alloc([64, 2], F32)
        coef = AR.alloc([128, 2, 4, 128], F32)
        Sf = AR.alloc([128, 4, 128], F32)
        Tb = AR.alloc([128, 4, 128], F32)
        Sbf = AR.alloc([128, 4, 128], BF16)
        Tsave = AR.alloc([128, NCH, 4, 128], BF16)
        DcT = AR.alloc([128, 4, 128], F32)
        dqfB = AR.alloc([128, 4, 128], F32)
        dqbB = AR.alloc([128, 4, 128], F32)
        dkk = AR.alloc([128, 8], F32)
        dcc = AR.alloc([128, 8], F32)
        tmpa = AR.alloc([128, 128], F32)
        tmpb = AR.alloc([128, 128], F32)
        rowi1 = AR.alloc([128, 128], F32)
        row128mi = AR.alloc([128, 128], F32)
        Wukv = AR.alloc([128, 2, 2, 4, 128], BF16)
        Wuq = AR.alloc([128, 3, 4, 192], BF16)
        Wuqr = AR.alloc([128, 3, 4, 64], BF16)
        prep_mark = AR.mark()
        mkt = AR.alloc([128, 2, 128], F32)
        ext = AR.alloc([128, 2, 128], F32)
        st_ukv = AR.alloc([128, 2, 1024], F32)
        st_uq = AR.alloc([128, 3, 768], F32)
        silucB = AR.alloc([128, 8, 128], F32)
        browada = AR.alloc([1, 2, 1024], F32)
        stkr = AR.alloc([128, 8, 64], F32)
        wst = [AR.alloc([128, 8, 512], F32) for _ in range(2)]
        tt('dve', rowi1, RF, RB, ALU.subtract, ['cst'], ['rowi1'])
        ts('dve', rowi1, rowi1, c_j, 1.0, ALU.add, ALU.add, ['rowi1', 'cst'], ['rowi1'])
        ts('dve', row128mi, rowi1, -1.0, 129.0, ALU.mult, ALU.add, ['rowi1'], ['row128mi'])
        for h in range(4):
            act(tmpa, RF, AF.Exp, ['cst', 'lg'], ['tmpa'], scale=lg[:, h:h + 1])
            tt('dve', tmpa, tmpa, MF, ALU.mult, ['tmpa', 'cst'], ['tmpa'])
            act(tmpb, RB, AF.Exp, ['cst', 'lg'], ['tmpb'], scale=lg[:, 4 + h:5 + h])
            tt('dve', tmpb, tmpb, MB, ALU.mult, ['tmpb', 'cst'], ['tmpb'])
            tt('dve', DcT[:, h, :], tmpa, tmpb, ALU.add, ['tmpa', 'tmpb'], ['DcT'])
            act(dqfB[:, h, :], rowi1, AF.Exp, ['rowi1', 'lg'], ['dqfB'], scale=lg[:, h:h + 1])
            act(dqbB[:, h, :], row128mi, AF.Exp, ['row128mi', 'lg'], ['dqbB'], scale=lg[:, 4 + h:5 + h])
            act(dkk[:, h:h + 1], c_127mj, AF.Exp, ['cst', 'lg'], ['dkk'], scale=lg[:, h:h + 1])
            act(dkk[:, 4 + h:5 + h], c_j, AF.Exp, ['cst', 'lg'], ['dkk'], scale=lg[:, 4 + h:5 + h])
            act(dcc[:, h:h + 1], c_128, AF.Exp, ['cst', 'lg'], ['dcc'], scale=lg[:, h:h + 1])
            act(dcc[:, 4 + h:5 + h], c_128, AF.Exp, ['cst', 'lg'], ['dcc'], scale=lg[:, 4 + h:5 + h])

        dma('sp', st_ukv, wukv_in.rearrange("(j p) n -> p j n", p=128), r=[], w=['st_ukv'])
        dma('sp', st_uq, wuq_in.rearrange("(j p) n -> p j n", p=128), r=[], w=['st_uq'])
        for cj in range(2):
            v = st_ukv[:, cj, :].rearrange("p (h t d) -> p t h d", h=4, t=2)
            ts('dve', Wukv[:, cj], v, kvnw[:, cj:cj + 1], None, ALU.mult, None, ['st_ukv', 'kvnw'], ['Wukv'])
        for cj in range(3):
            v = st_uq[:, cj, :].rearrange("p (h d) -> p h d", h=4)
            ts('dve', Wuq[:, cj], v, qnw[:, cj:cj + 1], MLA_SCALE, ALU.mult, ALU.mult, ['st_uq', 'qnw'], ['Wuq'])
            ts('dve', Wuqr[:, cj, :, 0:32], v[:, :, 160:192], qnw[:, cj:cj + 1], -MLA_SCALE, ALU.mult, ALU.mult,
               ['st_uq', 'qnw'], ['Wuqr'])
            ts('dve', Wuqr[:, cj, :, 32:64], v[:, :, 128:160], qnw[:, cj:cj + 1], MLA_SCALE, ALU.mult, ALU.mult,
               ['st_uq', 'qnw'], ['Wuqr'])

        for k in range(8):
            cp('dve', silucB[:, k, :], siluc[:, k, j:j + 1].broadcast_to([128, 128]), ['siluc'], ['silucB'])
        dma('sp', browada[:, 0, :], brow_in[:, 2048:3072], r=[], w=['browada'])
        dma('sp', browada[:, 1, :], brow_in[:, 5120:6144], r=[], w=['browada'])
        gi = 0
        for g in range(12):
            b = gi % 2
            gi += 1
            dma('sp', wst[b], wada_in[:, g * 512:(g + 1) * 512].rearrange("(k p) n -> p k n", p=128),
                r=[], w=[('wst', b)])
            if g in (4, 5, 10, 11):
                which = 0 if g < 6 else 1
                half = g % 2 if g < 6 else (g - 10)
                for k in range(8):
                    mm(PB[0][:, :], silucB[:, k, :], wst[b][:, k, :], k == 0, False, ['silucB', ('wst', b)], [('pb', 0)])
                mm(PB[0][:, :], ones_f[0:1, :], browada[0:1, which, half * 512:(half + 1) * 512], False, True,
                   ['ones_f', 'browada'], [('pb', 0)])
                cp('act', gB[:, which, half * 512:(half + 1) * 512], PB[0][:, :], [('pb', 0)], ['gB'])
            else:
                for c in range(4):
                    ch = g * 4 + c
                    for k in range(8):
                        mm(PB[1][:, 0:1], wst[b][:, k, c * 128:(c + 1) * 128], siluc[:, k, j:j + 1], k == 0, k == 7,
                           [('wst', b), 'siluc'], [('pb', 1)])
                    ts('dve', modfm[:, ch, j:j + 1], PB[1][:, 0:1], bfm[:, ch:ch + 1], None, ALU.add, None,
                       [('pb', 1), 'bfm'], ['modfm'])
        for k in range(8):
            ts('dve', scp[:, 0, k, j:j + 1], modfm[:, 8 + k, j:j + 1], 1.0, None, ALU.add, None, ['modfm'], ['scp'])
            ts('dve', scp[:, 1, k, j:j + 1], modfm[:, 32 + k, j:j + 1], 1.0, None, ALU.add, None, ['modfm'], ['scp'])

        groups = [(0, 512), (512, 512), (1024, 512), (1536, 512), (2048, 512), (2560, 192)]
        for (c0, cw) in groups:
            b = gi % 2
            gi += 1
            dma('sp', wst[b][:, :, 0:cw], win_in[:, c0:c0 + cw].rearrange("(k p) n -> p k n", p=128),
                r=[], w=[('wst', b)])
            for k in range(8):
                mm(PB[0][0:1, 0:cw], modfm[:, k, j:j + 1], wst[b][:, k, 0:cw], k == 0, k == 7,
                   ['modfm', ('wst', b)], [('pb', 0)])
            if c0 == 512:
                actmul(brow[0:1, c0:c0 + cw], PB[0][0:1, 0:cw], float(128 ** -0.5), [('pb', 0)], ['brow'])
            else:
                cp('act', brow[0:1, c0:c0 + cw], PB[0][0:1, 0:cw], [('pb', 0)], ['brow'])
            if c0 == 2048:
                for c in range(4):
                    for k in range(8):
                        mm(PB[1][:, 0:1], wst[b][:, k, c * 128:(c + 1) * 128], modfm[:, k, j:j + 1], k == 0, k == 7,
                           [('wst', b), 'modfm'], [('pb', 1)])
                    cp('dve', bcfm[:, c:c + 1], PB[1][:, 0:1], [('pb', 1)], ['bcfm'])
            if c0 == 2560:
                for k in range(8):
                    mm(PB[1][:, 0:1], wst[b][:, k, 0:128], modfm[:, k, j:j + 1], k == 0, k == 7,
                       [('wst', b), 'modfm'], [('pb', 1)])
                cp('dve', bcfm[:, 4:5], PB[1][:, 0:1], [('pb', 1)], ['bcfm'])
                ts('dve', stkr[:, :, 0:32], wst[b][:, :, 160:192], -1.0, None, ALU.mult, None, [('wst', b)], ['stkr'])
                cp('dve', stkr[:, :, 32:64], wst[b][:, :, 128:160], [('wst', b)], ['stkr'])
                for k in range(8):
                    mm(PB[1][0:64, 0:1], wst[b][:, k, 128:192], modfm[:, k, j:j + 1], k == 0, k == 7,
                       [('wst', b), 'modfm'], [('pb', 1)])
                cp('dve', bkr[:, 0:1], PB[1][0:64, 0:1], [('pb', 1)], ['bkr'])
                for k in range(8):
                    mm(PB[1][0:64, 0:1], stkr[:, k, :], modfm[:, k, j:j + 1], k == 0, k == 7,
                       ['stkr', 'modfm'], [('pb', 1)])
                cp('dve', bkr[:, 1:2], PB[1][0:64, 0:1], [('pb', 1)], ['bkr'])
                for k in range(8):
                    ts('dve', Wkrr[:, k, :], stkr[:, k, :], scp[:, 0, k, j:j + 1], None, ALU.mult, None,
                       ['stkr', 'scp'], ['Wkrr'])
            for k in range(8):
                if c0 == 512:
                    ts('dve', Win[:, k, c0:c0 + cw], wst[b][:, k, 0:cw], scp[:, 0, k, j:j + 1], float(128 ** -0.5),
                       ALU.mult, ALU.mult, [('wst', b), 'scp'], ['Win'])
                else:
                    ts('dve', Win[:, k, c0:c0 + cw], wst[b][:, k, 0:cw], scp[:, 0, k, j:j + 1], None, ALU.mult, None,
                       [('wst', b), 'scp'], ['Win'])
        dma('sp', mkt, mk_in[:, j], r=[], w=['mkt'])
        dma('sp', ext, ex_in[:, j], r=[], w=['ext'])
        for dr in range(2):
            for h in range(4):
                act(coef[:, dr, h, :], ext[:, dr, :], AF.Exp, ['ext', 'lg'], ['coef'], scale=lg[:, 4 * dr + h:4 * dr + h + 1])
                tt('dve', coef[:, dr, h, :], coef[:, dr, h, :], mkt[:, dr, :], ALU.mult, ['coef', 'mkt'], ['coef'])
        memset('pool', Sf, 0.0, ['Sf'])
        memset('pool', Tb, 0.0, ['Tb'])
        P.barrier()
        AR.release(prep_mark)
        ab_mark = AR.mark()

        xs = [AR.alloc([128, D], F32) for _ in range(2)]
        tbt = [AR.alloc([128, 256], F32) for _ in range(2)]
        tmt = [AR.alloc([64, 2, 128], F32) for _ in range(2)]
        xn = AR.alloc([128, D], BF16)
        xT = AR.alloc([128, 8, 128], BF16)
        rA = AR.alloc([128, 4, 128], F32)
        rBt = AR.alloc([128, 4, 128], F32)
        rstd_t = AR.alloc([128, 1], F32)
        rB = AR.alloc([128, 128], F32)
        aonly_mark = AR.mark()
        kr1 = AR.alloc([64, 128], F32)
        kr2 = AR.alloc([64, 128], F32)
        pT = PB[0][:, :].bitcast(BF16).rearrange("p (a b) -> p a b", a=8)[:, :, 0:128]

        def load(t, b):
            dma('act', xs[b], xj[t * 128:(t + 1) * 128, :], r=[], w=[('xs', b)])
            dma('act', tbt[b], tbj[t * 128:(t + 1) * 128, :], r=[], w=[('tbt', b)])
            dma('act', tmt[b], tmj[:, :, t * 128:(t + 1) * 128], r=[], w=[('tmt', b)])

        def proj_tok(pb, tok, c0, cw):
            for k in range(8):
                mm(pb[:, 0:cw], xT[:, k, :], Win[:, k, c0:c0 + cw], k == 0, False, ['xT', 'Win'], [tok])
            mm(pb[:, 0:cw], ones_bf[0:1, :], brow[0:1, c0:c0 + cw], False, True, ['ones_bf', 'brow'], [tok])

        def rotary_tok(pb, tok, b, dst, dtok):
            v = pb[:, :].rearrange("p (h d) -> p h d", h=4)
            cosB = tbt[b][:, 0:128].unsqueeze(1).broadcast_to([128, 4, 128])
            sinB = tbt[b][:, 128:256].unsqueeze(1).broadcast_to([128, 4, 128])
            tt('dve', rA, v, cosB, ALU.mult, [tok, ('tbt', b)], ['rA'])
            tt('dve', rBt, v, sinB, ALU.mult, [tok, ('tbt', b)], ['rBt'])
            tt('pool', dst[:, :, 0:64], rA[:, :, 0:64], rBt[:, :, 64:128], ALU.subtract, ['rA', 'rBt'], [dtok])
            tt('pool', dst[:, :, 64:128], rBt[:, :, 0:64], rA[:, :, 64:128], ALU.add, ['rA', 'rBt'], [dtok])

        Kp2 = [AR.alloc([128, 4, 128], BF16) for _ in range(2)]
        Kwf2 = [AR.alloc([128, 4, 128], BF16) for _ in range(2)]
        Kwb2 = [AR.alloc([128, 4, 128], BF16) for _ in range(2)]
        Vr2 = [AR.alloc([128, 512], BF16) for _ in range(2)]
        sqv2 = [AR.alloc([128, 2, 128], BF16) for _ in range(2)]
        ckv2 = [AR.alloc([128, 2, 128], BF16) for _ in range(2)]
        KTb2 = [AR.alloc([128, 4, 512], BF16) for _ in range(2)]
        KRb2 = [AR.alloc([64, 512], BF16) for _ in range(2)]
        VMb2 = [AR.alloc([128, 4, 4, 128], BF16) for _ in range(2)]
        ckt = [AR.alloc([128, 2, 4], F32) for _ in range(2)]
        dkfB = dkk[:, 0:4].unsqueeze(2).broadcast_to([128, 4, 128])
        dkbB = dkk[:, 4:8].unsqueeze(2).broadcast_to([128, 4, 128])

        xnA = [xn, AR.alloc([128, D], BF16)]
        xTA = [xT, AR.alloc([128, 8, 128], BF16)]
        ln_scratch('a0')
        ln_scratch('a1')

        def lnA(t):
            b = t % 2
            ln_apply(xs[b], xnA[b], 'a%d' % b, [('xs', b)], [('xnA', b)])

        def trA(t):
            b = t % 2
            for k in range(8):
                tr(pT[:, k, :], xnA[b][:, k * 128:(k + 1) * 128], [('xnA', b)], [('pb', 0)])
            cp('act', xTA[b], pT, [('pb', 0)], [('xTA', b)])

        def projA(t):
            b = t % 2
            xTb = xTA[b]
            xtok = ('xTA', b)
            for (pb_, c0) in ((1, 512), (2, 1024)):
                for k in range(8):
                    mm(PB[pb_][:, :], xTb[:, k, :], Win[:, k, c0:c0 + 512], k == 0, False, [xtok, 'Win'], [('pb', pb_)])
                mm(PB[pb_][:, :], ones_bf[0:1, :], brow[0:1, c0:c0 + 512], False, True, ['ones_bf', 'brow'], [('pb', pb_)])
            for cj in range(2):
                for k in range(8):
                    mm(PB[3][:, cj * 128:(cj + 1) * 128], Win[:, k, 2432 + cj * 128:2432 + (cj + 1) * 128], xTb[:, k, :],
                       k == 0, k == 7, ['Win', xtok], [('pb', 3)])
            for k in range(8):
                mm(PB[3][0:64, 256:384], Win[:, k, 2688:2752], xTb[:, k, :], k == 0, k == 7, ['Win', xtok], [('pb', 3)])
            for k in range(8):
                mm(PB[3][0:64, 384:512], Wkrr[:, k, :], xTb[:, k, :], k == 0, k == 7, ['Wkrr', xtok], [('pb', 3)])

        def firstA(t):
            b = t % 2
            s = t // NCH
            q4 = t % 4
            bp = (t // 4) % 2
            rotary_tok(PB[1], ('pb', 1), b, Kp2[b], ('Kp2', b))
            cp('act', Vr2[b], PB[2][:, :], [('pb', 2)], [('Vr2', b)])
            ckvp = PB[3][:, 0:256].rearrange("p (a b) -> p a b", a=2)
            for cj in range(2):
                act(sqv2[b][:, cj, :], ckvp[:, cj, :], AF.Square, [('pb', 3), 'bcfm'], [('sqv2', b)], bias=bcfm[:, 3 + cj:4 + cj])
                act(ckv2[b][:, cj, :], ckvp[:, cj, :], AF.Identity, [('pb', 3), 'bcfm'], [('ckv2', b)], bias=bcfm[:, 3 + cj:4 + cj])
            stt('dve', kr1, PB[3][0:64, 256:384], bkr[:, 0:1], tmt[b][:, 0, :], ALU.add, ALU.mult,
                [('pb', 3), 'bkr', ('tmt', b)], ['kr1'])
            stt('dve', kr2, PB[3][0:64, 384:512], bkr[:, 1:2], tmt[b][:, 1, :], ALU.add, ALU.mult,
                [('pb', 3), 'bkr', ('tmt', b)], ['kr2'])
            tt('pool', KRb2[bp][:, q4 * 128:(q4 + 1) * 128], kr1, kr2, ALU.add, ['kr1', 'kr2'], [('KRb2', bp)])
            if s > 0:
                tt('dve', ckt[b], dkk.rearrange("p (a h) -> p a h", a=2), coef[:, :, :, t], ALU.mult, ['dkk', 'coef'], [('ckt', b)])
                tt('pool', Kwf2[b], Kp2[b], ckt[b][:, 0, :].unsqueeze(2).broadcast_to([128, 4, 128]), ALU.mult,
                   [('Kp2', b), ('ckt', b)], [('Kwf2', b)])
                tt('pool', Kwb2[b], Kp2[b], ckt[b][:, 1, :].unsqueeze(2).broadcast_to([128, 4, 128]), ALU.mult,
                   [('Kp2', b), ('ckt', b)], [('Kwb2', b)])
            else:
                tt('pool', Kwb2[b], Kp2[b], dkbB, ALU.mult, [('Kp2', b), 'dkk'], [('Kwb2', b)])

        def secondA(t):
            b = t % 2
            s = t // NCH
            n = t % NCH
            q4 = t % 4
            bp = (t // 4) % 2
            for cj in range(2):
                mm(PB[4][:, 0:1], sqv2[b][:, cj, :], ones_bf[:, 0:1], cj == 0, cj == 1, [('sqv2', b), 'ones_bf'], [('pb', 4)])
            for cj in range(2):
                mm(PB[4][:, 128:256], ones_bf, sqv2[b][:, cj, :], cj == 0, cj == 1, ['ones_bf', ('sqv2', b)], [('pb', 4)])
            cp('dve', rstd_t, PB[4][:, 0:1], [('pb', 4)], ['rstd_t'])
            cp('dve', rB, PB[4][:, 128:256], [('pb', 4)], ['rB'])
            rsqrt_small(rstd_t, 'rstd_t', 1.0 / 256.0, 1e-6)
            rsqrt_small(rB, 'rB', 1.0 / 256.0, 1e-6)
            Pst = [PB[6][:, :].rearrange("p (h e) -> p h e", h=4), PB[7][:, :].rearrange("p (h e) -> p h e", h=4)]
            if s > 0:
                first_t = (t == NT - 1)
                last_t = (t == NCH)
                for h in range(4):
                    mm(Pst[1][:, h, :], Kwb2[b][:, h, :], Vr2[b][:, h * 128:(h + 1) * 128], first_t and h == 0, last_t,
                       [('Kwb2', b), ('Vr2', b)], [('pb', 7)])
                for h in range(4):
                    mm(Pst[0][:, h, :], Kwf2[b][:, h, :], Vr2[b][:, h * 128:(h + 1) * 128], first_t and h == 0, last_t,
                       [('Kwf2', b), ('Vr2', b)], [('pb', 6)])
            else:
                for h in range(4):
                    mm(Pst[1][:, h, :], Kwb2[b][:, h, :], Vr2[b][:, h * 128:(h + 1) * 128], True, True,
                       [('Kwb2', b), ('Vr2', b)], [('pb', 7)])
            knp = PB[5][:, :].rearrange("p (h d) -> p h d", h=4)
            for h in range(4):
                for cj in range(2):
                    mm(knp[:, h, :], Wukv[:, cj, 0, h, :], ckv2[b][:, cj, :], cj == 0, cj == 1, ['Wukv', ('ckv2', b)], [('pb', 5)])
            for cj in range(2):
                mm(PB[4][:, :], ckv2[b][:, cj, :], Wukv[:, cj, 1].rearrange("p h d -> p (h d)"), cj == 0, cj == 1,
                   [('ckv2', b), 'Wukv'], [('pb', 4)])
            if s > 0:
                if t == NCH:
                    cp('dve', Sf, Pst[0], [('pb', 6)], ['Sf'])
                    cp('dve', Tb, Pst[1], [('pb', 7)], ['Tb'])
            else:
                cp('act', Tsave[:, n], Tb, ['Tb'], ['Tsave'])
                for h in range(4):
                    stt('dve', Tb[:, h, :], Tb[:, h, :], dcc[:, 4 + h:5 + h], Pst[1][:, h, :], ALU.mult, ALU.add,
                        [('pb', 7), 'dcc', 'Tb'], ['Tb'])
            tt('dve', KTb2[bp][:, :, q4 * 128:(q4 + 1) * 128], knp, rB.unsqueeze(1).broadcast_to([128, 4, 128]), ALU.mult,
               [('pb', 5), 'rB'], [('KTb2', bp)])
            act(VMb2[bp][:, :, q4, :], PB[4][:, :].rearrange("p (h d) -> p h d", h=4), AF.Identity, [('pb', 4), 'rstd_t'],
                [('VMb2', bp)], scale=rstd_t[:, 0:1])
            if q4 == 0:
                t0 = t
                dma('sp', KTd[j][:, :, t0 * 128:(t0 + 4) * 128].rearrange("h p n -> p h n"), KTb2[bp], r=[('KTb2', bp)], w=[('KTd', j)])
                dma('sp', KRd[j][:, t0 * 128:(t0 + 4) * 128], KRb2[bp], r=[('KRb2', bp)], w=[('KRd', j)])
                dma('sp', VVd[j][:, :, t0:t0 + 4, :].rearrange("h p t e -> p h t e"), VMb2[bp], r=[('VMb2', bp)], w=[('VVd', j)])

        load(NT - 1, (NT - 1) % 2)
        load(NT - 2, (NT - 2) % 2)
        lnA(NT - 1)
        trA(NT - 1)
        projA(NT - 1)
        lnA(NT - 2)
        firstA(NT - 1)
        trA(NT - 2)
        for t in range(NT - 1, -1, -1):
            if t - 1 >= 0:
                projA(t - 1)
                if t - 2 >= 0:
                    load(t - 2, t % 2)
                    lnA(t - 2)
                firstA(t - 1)
                if t - 2 >= 0:
                    trA(t - 2)
            secondA(t)

        P.barrier()
        AR.release(aonly_mark)
        QT3 = AR.alloc([128, 3, 4, 128], BF16)
        KT = AR.alloc([128, 4, 128], BF16)
        AcT = AR.alloc([128, 4, 128], BF16)
        gnB = AR.alloc([128, 512], F32)
        dma('sp', gnB, gnw_in.partition_broadcast(128).rearrange("p a b -> p (a b)"), r=[], w=['gnB'])
        st4 = AR.alloc([128, 4, 6], F32)
        mv4 = AR.alloc([128, 4, 2], F32)
        rs4 = AR.alloc([128, 4], F32)
        retb = AR.alloc([128, 512], BF16)
        RTb = AR.alloc([128, 4, 512], BF16)
        QNb = AR.alloc([128, 4, 512], BF16)
        QRb = AR.alloc([64, 4, 512], BF16)
        qr1 = AR.alloc([64, 4, 128], F32)
        qr2 = AR.alloc([64, 4, 128], F32)
        xnB = [xn, AR.alloc([128, D], BF16)]
        xTB = [xT, AR.alloc([128, 8, 128], BF16)]
        ln_scratch('b0')
        ln_scratch('b1')
        tmt3 = [AR.alloc([64, 2, 128], F32) for _ in range(3)]
        Qp2 = [AR.alloc([128, 4, 128], BF16) for _ in range(2)]
        KpB = [AR.alloc([128, 4, 128], BF16) for _ in range(2)]
        VrB = [AR.alloc([128, 512], BF16) for _ in range(2)]
        gate2 = [AR.alloc([128, 512], BF16) for _ in range(2)]
        KwfB = [AR.alloc([128, 4, 128], BF16) for _ in range(2)]
        sqB = [AR.alloc([128, 3, 128], BF16) for _ in range(2)]
        cqB = [AR.alloc([128, 3, 128], BF16) for _ in range(2)]
        dkfB2 = dkk[:, 0:4].unsqueeze(2).broadcast_to([128, 4, 128])
        y0 = [AR.alloc([128, 4, 128], F32) for _ in range(2)]
        retb2 = [retb, AR.alloc([128, 512], BF16)]

        def loadB(c):
            b = c % 2
            dma('act', xs[b], xj[c * 128:(c + 1) * 128, :], r=[], w=[('xs', b)])
            dma('act', tbt[b], tbj[c * 128:(c + 1) * 128, :], r=[], w=[('tbt', b)])
            dma('act', tmt3[c % 3], tmj[:, :, c * 128:(c + 1) * 128], r=[], w=[('tmt3', c % 3)])

        def lnB_(c):
            b = c % 2
            ln_apply(xs[b], xnB[b], 'b%d' % b, [('xs', b)], [('xnB', b)])

        def trB(c):
            b = c % 2
            for k in range(8):
                tr(pT[:, k, :], xnB[b][:, k * 128:(k + 1) * 128], [('xnB', b)], [('pb', 0)])
            cp('act', xTB[b], pT, [('pb', 0)], [('xTB', b)])

        def projB(c):
            b = c % 2
            xTb = xTB[b]
            xtok = ('xTB', b)
            for (pb_, c0) in ((1, 0), (2, 512), (3, 1024), (4, 1536)):
                for k in range(8):
                    mm(PB[pb_][:, :], xTb[:, k, :], Win[:, k, c0:c0 + 512], k == 0, False, [xtok, 'Win'], [('pb', pb_)])
                mm(PB[pb_][:, :], ones_bf[0:1, :], brow[0:1, c0:c0 + 512], False, True, ['ones_bf', 'brow'], [('pb', pb_)])
            for cj in range(3):
                for k in range(8):
                    mm(PB[5][:, cj * 128:(cj + 1) * 128], Win[:, k, 2048 + cj * 128:2048 + (cj + 1) * 128], xTb[:, k, :],
                       k == 0, k == 7, ['Win', xtok], [('pb', 5)])

        def evacB(c):
            b = c % 2
            rotary_tok(PB[1], ('pb', 1), b, Qp2[b], ('Qp2', b))
            rotary_tok(PB[2], ('pb', 2), b, KpB[b], ('KpB', b))
            cp('act', VrB[b], PB[3][:, :], [('pb', 3)], [('VrB', b)])
            for cj in range(3):
                act(sqB[b][:, cj, :], PB[5][:, cj * 128:(cj + 1) * 128], AF.Square, [('pb', 5), 'bcfm'], [('sqB', b)],
                    bias=bcfm[:, cj:cj + 1])
                act(cqB[b][:, cj, :], PB[5][:, cj * 128:(cj + 1) * 128], AF.Identity, [('pb', 5), 'bcfm'], [('cqB', b)],
                    bias=bcfm[:, cj:cj + 1])
            act(gate2[b], PB[4][:, :], AF.Silu, [('pb', 4)], [('gate2', b)])
            tt('pool', gate2[b], gate2[b], gnB, ALU.mult, [('gate2', b), 'gnB'], [('gate2', b)])
            tt('pool', KwfB[b], KpB[b], dkfB2, ALU.mult, [('KpB', b), 'dkk'], [('KwfB', b)])

        def secondB(c):
            b = c % 2
            q4 = c % 4
            tm_ = tmt3[c % 3]
            tmtok = ('tmt3', c % 3)
            pQK = PB[0][:, :].bitcast(BF16).rearrange("p (a b) -> p a b", a=8)[:, :, 0:128]
            for h in range(4):
                tr(pQK[:, h, :], Qp2[b][:, h, :], [('Qp2', b)], [('pb', 0)])
                tr(pQK[:, 4 + h, :], KpB[b][:, h, :], [('KpB', b)], [('pb', 0)])
            cp('act', QT3[:, 0], pQK[:, 0:4, :], [('pb', 0)], ['QT0'])
            cp('act', KT, pQK[:, 4:8, :], [('pb', 0)], ['KT'])
            tt('dve', QT3[:, 1], pQK[:, 0:4, :], dqfB, ALU.mult, [('pb', 0), 'dqfB'], ['QT1'])
            tt('dve', QT3[:, 2], pQK[:, 0:4, :], dqbB, ALU.mult, [('pb', 0), 'dqbB'], ['QT2'])
            for cj in range(3):
                mm(PB[5][:, 128:256], ones_bf, sqB[b][:, cj, :], cj == 0, cj == 2, ['ones_bf', ('sqB', b)], [('pb', 5)])
            Ap = PB[6][:, :].rearrange("p (h d) -> p h d", h=4)
            for h in range(4):
                mm(Ap[:, h, :], KT[:, h, :], QT3[:, 0, h, :], True, True, ['KT', 'QT0'], [('pb', 6)])
            cp('dve', rB, PB[5][:, 128:256], [('pb', 5)], ['rB'])
            rsqrt_small(rB, 'rB', 1.0 / 384.0, 1e-6)
            tt('dve', AcT, Ap, DcT, ALU.mult, [('pb', 6), 'DcT'], ['AcT'])
            cp('act', Sbf, Sf, ['Sf'], ['Sbf'])
            qnp = PB[5][:, :].rearrange("p (h d) -> p h d", h=4)
            for h in range(4):
                for cj in range(3):
                    mm(qnp[:, h, :], Wuq[:, cj, h, 0:128], cqB[b][:, cj, :], cj == 0, cj == 2, ['Wuq', ('cqB', b)], [('pb', 5)])
            qrp = PB[1][0:64, :].rearrange("p (h d) -> p h d", h=4)
            qrrp = PB[2][0:64, :].rearrange("p (h d) -> p h d", h=4)
            for h in range(4):
                for cj in range(3):
                    mm(qrp[:, h, :], Wuq[:, cj, h, 128:192], cqB[b][:, cj, :], cj == 0, cj == 2, ['Wuq', ('cqB', b)], [('pb', 1)])
                for cj in range(3):
                    mm(qrrp[:, h, :], Wuqr[:, cj, h, :], cqB[b][:, cj, :], cj == 0, cj == 2, ['Wuqr', ('cqB', b)], [('pb', 2)])
            Op = PB[7][:, :].rearrange("p (h d) -> p h d", h=4)
            for h in range(4):
                mm(Op[:, h, :], AcT[:, h, :], VrB[b][:, h * 128:(h + 1) * 128], True, False, ['AcT', ('VrB', b)], [('pb', 7)])
                mm(Op[:, h, :], QT3[:, 1, h, :], Sbf[:, h, :], False, False, ['QT1', 'Sbf'], [('pb', 7)])
                mm(Op[:, h, :], QT3[:, 2, h, :], Tsave[:, c, h, :], False, True, ['QT2', 'Tsave'], [('pb', 7)])
            Pf = PB[6][:, :].rearrange("p (h d) -> p h d", h=4)
            for h in range(4):
                mm(Pf[:, h, :], KwfB[b][:, h, :], VrB[b][:, h * 128:(h + 1) * 128], True, True, [('KwfB', b), ('VrB', b)], [('pb', 6)])
            tt('dve', QNb[:, :, q4 * 128:(q4 + 1) * 128], qnp, rB.unsqueeze(1).broadcast_to([128, 4, 128]), ALU.mult,
               [('pb', 5), 'rB'], ['QNb'])
            tt('dve', qr1, qrp, tm_[:, 0, :].unsqueeze(1).broadcast_to([64, 4, 128]), ALU.mult, [('pb', 1), tmtok], ['qr1'])
            tt('dve', qr2, qrrp, tm_[:, 1, :].unsqueeze(1).broadcast_to([64, 4, 128]), ALU.mult, [('pb', 2), tmtok], ['qr2'])
            tt('pool', qr1, qr1, qr2, ALU.add, ['qr1', 'qr2'], ['qr1'])
            tt('pool', QRb[:, :, q4 * 128:(q4 + 1) * 128], qr1, rB[0:64, :].unsqueeze(1).broadcast_to([64, 4, 128]),
               ALU.mult, ['qr1', 'rB'], ['QRb'])
            for h in range(4):
                stt('dve', Sf[:, h, :], Sf[:, h, :], dcc[:, h:h + 1], Pf[:, h, :], ALU.mult, ALU.add,
                    [('pb', 6), 'dcc', 'Sf', 'Sbf'], ['Sf'])
            cp('act', y0[b], Op, [('pb', 7)], [('y0', b)])
            if q4 == 3:
                t0 = c - 3
                dma('sp', QTd[j][:, :, t0 * 128:(t0 + 4) * 128].rearrange("h p n -> p h n"), QNb, r=['QNb'], w=[('QTd', j)])
                dma('sp', QRd[j][:, :, t0 * 128:(t0 + 4) * 128].rearrange("h p n -> p h n"), QRb, r=['QRb'], w=[('QRd', j)])

        def thirdB_vec(c):
            b = c % 2
            for h in range(4):
                bnstats(st4[:, h, :], y0[b][:, h, :], [('y0', b)], ['st4'])
            for h in range(4):
                bnaggr(mv4[:, h, :], st4[:, h, :], ['st4'], ['mv4'])
            cp('dve', rs4, mv4[:, :, 1], ['mv4'], ['rs4'])
            rsqrt_small(rs4, 'rs4', 1.0, 1e-5)
            tt('dve', y0[b], y0[b], mv4[:, :, 0:1].broadcast_to([128, 4, 128]), ALU.subtract, [('y0', b), 'mv4'], [('y0', b)])
            tt('pool', y0[b], y0[b], rs4.unsqueeze(2).broadcast_to([128, 4, 128]), ALU.mult, [('y0', b), 'rs4'], [('y0', b)])
            tt('pool', retb2[b], y0[b].rearrange("p h d -> p (h d)"), gate2[b], ALU.mult, [('y0', b), ('gate2', b)], [('retb2', b)])

        def thirdB_pe(c):
            b = c % 2
            q4 = c % 4
            pR = PB[0][:, :].bitcast(BF16).rearrange("p (a b) -> p a b", a=8)[:, :, 0:128]
            for h in range(4):
                tr(pR[:, h, :], retb2[b][:, h * 128:(h + 1) * 128], [('retb2', b)], [('pb', 0)])
            cp('act', RTb[:, :, q4 * 128:(q4 + 1) * 128], pR[:, 0:4, :], [('pb', 0)], ['RTb'])
            if q4 == 3:
                t0 = c - 3
                dma('sp', RTd[j][:, :, t0 * 128:(t0 + 4) * 128].rearrange("h p n -> p h n"), RTb, r=['RTb'], w=[('RTd', j)])

        loadB(0)
        loadB(1)
        lnB_(0)
        trB(0)
        projB(0)
        lnB_(1)
        evacB(0)
        trB(1)
        for c in range(NCH):
            if c >= 1:
                thirdB_vec(c - 1)
            if c + 1 < NCH:
                projB(c + 1)
                if c + 2 < NCH:
                    loadB(c + 2)
                    lnB_(c + 2)
                evacB(c + 1)
                if c + 2 < NCH:
                    trB(c + 2)
            secondB(c)
            if c >= 1:
                thirdB_pe(c - 1)
        thirdB_vec(NCH - 1)
        thirdB_pe(NCH - 1)

        P.barrier()
        AR.release(job_mark)
        KTh = AR.alloc([128, S], BF16)
        KRh = AR.alloc([128, S], BF16)
        Vh = AR.alloc([128, NT, 128], BF16)
        QTh = [AR.alloc([128, L], BF16) for _ in range(2)]
        QRh = [AR.alloc([128, L], BF16) for _ in range(2)]
        memset('pool', KRh, 0.0, ['KRh'])
        memset('pool', QRh[0], 0.0, [('QRh', 0)])
        memset('pool', QRh[1], 0.0, [('QRh', 1)])
        NPT = 4
        PT = [AR.alloc([128, 512], BF16) for _ in range(NPT)]
        dacc = [AR.alloc([128, 512], F32) for _ in range(6)]
        rden = AR.alloc([128, 512], F32)
        ato = AR.alloc([128, 512], BF16)
        dma('sp', KRh[0:64, :], KRd[j], r=[('KRd', j)], w=['KRh'])
        NQB = L // 512
        NH2 = NT // 2

        def load_head(h, hf):
            hp = h % 2
            dma('sp', KTh[:, hf * NH2 * 128:(hf + 1) * NH2 * 128], KTd[j][h][:, hf * NH2 * 128:(hf + 1) * NH2 * 128],
                r=[('KTd', j)], w=[('KTh', hf)])
            dma('sp', Vh[:, hf * NH2:(hf + 1) * NH2, :], VVd[j][h][:, hf * NH2:(hf + 1) * NH2, :],
                r=[('VVd', j)], w=[('Vh', hf)])
            if hf == 0:
                dma('sp', QTh[hp], QTd[j][h], r=[('QTd', j)], w=[('QTh', hp)])
                dma('sp', QRh[hp][0:64, :], QRd[j][h], r=[('QRd', j)], w=[('QRh', hp)])

        items = [(h, qb, kt) for h in range(4) for qb in range(NQB) for kt in range(NT)]

        def qk(item, gi_):
            h, qb, kt = item
            hp = h % 2
            if qb == 0 and kt == 0:
                load_head(h, 0)
                if h == 0:
                    load_head(h, 1)
            qs = slice(qb * 512, (qb + 1) * 512)
            sb = gi_ % NPT
            hf = kt // NH2
            mm(PB[sb][:, :], KTh[:, kt * 128:(kt + 1) * 128], QTh[hp][:, qs], True, False,
               [('KTh', hf), ('QTh', hp)], [('pb', sb)])
            mm(PB[sb][:, :], KRh[:, kt * 128:(kt + 1) * 128], QRh[hp][:, qs], False, True,
               ['KRh', ('QRh', hp)], [('pb', sb)])
            act(PT[sb], PB[sb][:, :], AF.Exp, [('pb', sb)], [('PT', sb)])

        def pv(item, gi_):
            h, qb, kt = item
            qs = slice(qb * 512, (qb + 1) * 512)
            blk = h * NQB + qb
            ob = 4 + (blk % 2)
            db = 6 + (blk % 2)
            sb = gi_ % NPT
            hf = kt // NH2
            mm(PB[ob][:, :], Vh[:, kt, :], PT[sb], kt == 0, kt == NT - 1, [('Vh', hf), ('PT', sb)], [('pb', ob)])
            ai, eng_ = kt % 3, 'dve'
            da = dacc[(blk % 2) * 3 + ai]
            dtok = ('dacc', blk % 2, ai)
            if kt in (0, 1, 2):
                cp(eng_, da, PT[sb], [('PT', sb)], [dtok])
            else:
                tt(eng_, da, da, PT[sb], ALU.add, [dtok, ('PT', sb)], [dtok], nosync=(eng_ == 'dve'))
            if kt == NT - 1:
                for ai2 in range(3):
                    mm(PB[db][:, :], ones_f, dacc[(blk % 2) * 3 + ai2], ai2 == 0, ai2 == 2,
                       ['ones_f', ('dacc', blk % 2, ai2)], [('pb', db)])
                recip(rden, PB[db][:, :], [('pb', db)], ['rden'])
                tt('dve', ato, PB[ob][:, :], rden, ALU.mult, [('pb', ob), 'rden'], ['ato'])
                dma('sp', ATd[j][h][:, qs], ato, r=['ato'], w=[('ATd', j)])
                if qb == NQB - 1 and h < 3:
                    load_head(h + 1, 1)

        LA = 3
        for i_ in range(min(LA, len(items))):
            qk(items[i_], i_)
        for i_ in range(len(items)):
            if i_ + LA < len(items):
                qk(items[i_ + LA], i_ + LA)
            pv(items[i_], i_)

        P.barrier()
        AR.release(job_mark)
        lnB = AR.alloc([128, 2, 1024], F32)
        dma('sp', lnB, ln_in[0:2, :].partition_broadcast(128), r=[], w=['lnB'])
        Wo = AR.alloc([128, 8, 1024], BF16)
        d_mark = AR.mark()
        wst = [AR.alloc([128, 8, 512], F32) for _ in range(2)]
        gi = 0
        for g in range(2):
            b = gi % 2
            gi += 1
            dma('sp', wst[b], wo_in[:, g * 512:(g + 1) * 512].rearrange("(k p) n -> p k n", p=128), r=[], w=[('wst', b)])
            for k in range(8):
                tt('dve', Wo[:, k, g * 512:(g + 1) * 512], wst[b][:, k, :], gB[:, 0, g * 512:(g + 1) * 512], ALU.mult,
                   [('wst', b), 'gB'], ['Wo'])
        cat2 = [AR.alloc([128, 8, 512], BF16) for _ in range(2)]
        xo = [AR.alloc([128, D], F32) for _ in range(2)]
        yv2 = [AR.alloc([128, D], F32) for _ in range(2)]
        x1 = [AR.alloc([128, D], F32) for _ in range(2)]
        ln_scratch('d1')
        NT1 = L // 128

        def load_d1(tix):
            m, u = tix // 4, tix % 4
            if u == 0:
                ms = slice(m * 512, (m + 1) * 512)
                cb = m % 2
                dma('act', cat2[cb][:, 0:4, :], RTd[j][:, :, ms].rearrange("h p n -> p h n"), r=[('RTd', j)], w=[('cat', cb, 0)])
                dma('act', cat2[cb][:, 4:8, :], ATd[j][:, :, ms].rearrange("h p n -> p h n"), r=[('ATd', j)], w=[('cat', cb, 1)])
            dma('act', xo[tix % 2], xj[tix * 128:(tix + 1) * 128, :], r=[], w=[('xo', tix % 2)])

        load_d1(0)
        for tix in range(NT1):
            m, u = tix // 4, tix % 4
            b = tix % 2
            cb = m % 2
            if tix + 1 < NT1:
                load_d1(tix + 1)
            for half in range(2):
                for k in range(8):
                    mm(PB[half][:, :], cat2[cb][:, k, u * 128:(u + 1) * 128], Wo[:, k, half * 512:(half + 1) * 512],
                       k == 0, k == 7, [('cat', cb, 0), ('cat', cb, 1), 'Wo'], [('pb', half)])
                stt('dve', yv2[b][:, half * 512:(half + 1) * 512], xo[b][:, half * 512:(half + 1) * 512], ALPHA,
                    PB[half][:, :], ALU.mult, ALU.add, [('xo', b), ('pb', half)], [('yv', b)])
            ln_apply(yv2[b], yv2[b], 'd1', [('yv', b)], [('yv', b)])
            tt('pool', yv2[b], yv2[b], lnB[:, 0, :], ALU.mult, [('yv', b), 'lnB'], [('yv', b)])
            tt('pool', x1[b], yv2[b], lnB[:, 1, :], ALU.add, [('yv', b), 'lnB'], [('x1', b)])
            dma('sp', X1d[j][tix * 128:(tix + 1) * 128, :], x1[b], r=[('x1', b)], w=[('X1d', j)])

        P.barrier()
        AR.release(job_mark)
        lnB = AR.alloc([128, 2, 1024], F32)
        dma('sp', lnB, ln_in[2:4, :].partition_broadcast(128), r=[], w=['lnB'])
        Wup = AR.alloc([128, 8, DFF], BF16)
        Wdn = AR.alloc([128, 32, D], BF16)
        bup = AR.alloc([128, 32], F32)
        e_mark = AR.mark()
        wst = [AR.alloc([128, 8, 512], F32) for _ in range(2)]
        gi = 0
        for g in range(8):
            b = gi % 2
            gi += 1
            dma('sp', wst[b], wup_in[:, g * 512:(g + 1) * 512].rearrange("(k p) n -> p k n", p=128), r=[], w=[('wst', b)])
            for c in range(4):
                for k in range(8):
                    mm(PB[1][:, 0:1], wst[b][:, k, c * 128:(c + 1) * 128], modfm[:, 24 + k, j:j + 1], k == 0, k == 7,
                       [('wst', b), 'modfm'], [('pb', 1)])
                cp('dve', bup[:, g * 4 + c:g * 4 + c + 1], PB[1][:, 0:1], [('pb', 1)], ['bup'])
            for k in range(8):
                ts('dve', Wup[:, k, g * 512:(g + 1) * 512], wst[b][:, k, :], scp[:, 1, k, j:j + 1], None, ALU.mult, None,
                   [('wst', b), 'scp'], ['Wup'])
        for g in range(8):
            b = gi % 2
            gi += 1
            fg, half = g // 2, g % 2
            dma('sp', wst[b], wdn_in[fg * 1024:(fg + 1) * 1024, half * 512:(half + 1) * 512].rearrange("(k p) n -> p k n", p=128),
                r=[], w=[('wst', b)])
            for k in range(8):
                tt('dve', Wdn[:, fg * 8 + k, half * 512:(half + 1) * 512], wst[b][:, k, :], gB[:, 1, half * 512:(half + 1) * 512],
                   ALU.mult, [('wst', b), 'gB'], ['Wdn'])
        P.barrier()
        AR.release(e_mark)
        TBK = 256
        NU = TBK // 128
        NB = L // TBK
        x1t = [[AR.alloc([128, D], F32) for _ in range(NU)] for _ in range(2)]
        xn2 = AR.alloc([128, D], BF16)
        h2T = [AR.alloc([128, 8, TBK], BF16) for _ in range(2)]
        u2 = [AR.alloc([128, 8, TBK], BF16) for _ in range(2)]
        rt = [AR.alloc([128, TBK], F32) for _ in range(2)]
        oo = [gB[:, 0, :], gB[:, 1, :]]
        ln_scratch('d2a')
        ln_scratch('d2b')
        pT2 = PB[6][:, :].bitcast(BF16).rearrange("p (a b) -> p a b", a=8)[:, :, 0:128]

        def front2(m):
            mp = m % 2
            for u in range(NU):
                tix = m * NU + u
                dma('act', x1t[mp][u], X1d[j][tix * 128:(tix + 1) * 128, :], r=[('X1d', j)], w=[('x1t', mp, u)])
                ln_apply(x1t[mp][u], xn2, 'd2a', [('x1t', mp, u)], ['xn2'])
                for k in range(8):
                    tr(pT2[:, k, :], xn2[:, k * 128:(k + 1) * 128], ['xn2'], [('pb', 6)])
                cp('act', h2T[mp][:, :, u * 128:(u + 1) * 128], pT2, [('pb', 6)], [('h2T', mp)])

        def up(m, g):
            mp = m % 2
            ub = (m * 4 + g) % 2
            for fl in range(8):
                f = g * 8 + fl
                pb = 4 + (f % 2)
                for k in range(8):
                    mm(PB[pb][:, 0:TBK], Wup[:, k, f * 128:(f + 1) * 128], h2T[mp][:, k, :], k == 0, k == 7,
                       ['Wup', ('h2T', mp)], [('pb', pb)])
                rb = f % 2
                act(rt[rb], PB[pb][:, 0:TBK], AF.Relu, [('pb', pb), 'bup'], [('rt', rb)], bias=bup[:, f:f + 1])
                tt('pool' if f % 2 else 'dve', u2[ub][:, fl, :], rt[rb], rt[rb], ALU.mult, [('rt', rb)], [('u2', ub)])

        def down(m, g):
            ub = (m * 4 + g) % 2
            for u in range(NU):
                for half in range(2):
                    pbd = u * 2 + half
                    for fl in range(8):
                        f = g * 8 + fl
                        mm(PB[pbd][:, :], u2[ub][:, fl, u * 128:(u + 1) * 128], Wdn[:, f, half * 512:(half + 1) * 512],
                           f == 0, f == 31, [('u2', ub), 'Wdn'], [('pb', pbd)])

        def tail(m):
            mp = m % 2
            for u in range(NU):
                tix = m * NU + u
                ob = tix % 2
                for half in range(2):
                    pbd = u * 2 + half
                    stt('dve', oo[ob][:, half * 512:(half + 1) * 512], x1t[mp][u][:, half * 512:(half + 1) * 512], ALPHA,
                        PB[pbd][:, :], ALU.mult, ALU.add, [('x1t', mp, u), ('pb', pbd)], [('oo', ob)])
                ln_apply(oo[ob], oo[ob], 'd2b', [('oo', ob)], [('oo', ob)])
                tt('pool', oo[ob], oo[ob], lnB[:, 0, :], ALU.mult, [('oo', ob), 'lnB'], [('oo', ob)])
                tt('pool', oo[ob], oo[ob], lnB[:, 1, :], ALU.add, [('oo', ob), 'lnB'], [('oo', ob)])
                dma('sp', y_out[j][tix * 128:(tix + 1) * 128, :], oo[ob], r=[('oo', ob)], w=[('yout', j, ob)], outp=True)

        G = [(m, g) for m in range(NB) for g in range(4)]
        front2(0)
        up(0, 0)
        for i_, (m, g) in enumerate(G):
            if i_ + 1 < len(G):
                m2, g2 = G[i_ + 1]
                if g2 == 0:
                    front2(m2)
                up(m2, g2)
            down(m, g)
            if g == 3:
                tail(m)

    P.emit()
    return nc, P


def _rope_tables(pos, d):
    inv = (10000.0 ** (-np.arange(0, d, 2, dtype=np.float32) / np.float32(d))).astype(np.float32)
    ang = pos.astype(np.float32)[:, None] * inv[None, :]
    return np.cos(ang).astype(np.float32), np.sin(ang).astype(np.float32)


def host_inputs(L, NSLOT, core_jobs, xs, cs, w):
    NJ = len(NSLOT)
    NCH = L // 128
    p = np.arange(128, dtype=np.float32)
    i = np.arange(128, dtype=np.float32)
    cst = np.zeros((128, 4 * 128 + 8), np.float32)
    cst[:, 0:128] = np.maximum(i[None, :] - p[:, None], 0)
    cst[:, 128:256] = np.maximum(p[:, None] - i[None, :], 0)
    cst[:, 256:384] = (p[:, None] <= i[None, :])
    cst[:, 384:512] = (p[:, None] > i[None, :])
    cst[:, 512] = 127 - p
    cst[:, 513] = p
    cst[:, 514] = 128.0
    shared = dict(
        cst=cst,
        w_ada=np.ascontiguousarray(w['w_ada'][0]), b_ada_fm=np.ascontiguousarray(w['b_ada'][0].reshape(48, 128).T),
        b_ada_row=np.ascontiguousarray(w['b_ada'][0].reshape(1, -1)),
        w_in=np.ascontiguousarray(w['w_in'][0]), w_uq=np.ascontiguousarray(w['w_uq'][0]),
        w_ukv=np.ascontiguousarray(w['w_ukv'][0]), w_o=np.ascontiguousarray(w['w_o'][0]),
        w_up=np.ascontiguousarray(w['w_up'][0]), w_down=np.ascontiguousarray(w['w_down'][0]),
        dec=np.concatenate([w['ret_decay_f'][0], w['ret_decay_b'][0]]).reshape(1, 8).astype(np.float32),
        gnw=np.ascontiguousarray(w['ret_gn_w'][0].reshape(1, 512)),
        qnw_fm=np.ascontiguousarray(w['q_norm_w'][0].reshape(3, 128).T),
        kvnw_fm=np.ascontiguousarray(w['kv_norm_w'][0].reshape(2, 128).T),
        ln=np.ascontiguousarray(np.stack([w['ln1_w'][0], w['ln1_b'][0], w['ln2_w'][0], w['ln2_b'][0]])),
    )
    in_maps = []
    for core, jobs in enumerate(core_jobs):
        m = dict(shared)
        cT = np.zeros((128, 8, NJ), np.float32)
        mk = np.zeros((128, NJ, 2, 128), np.float32)
        ex = np.zeros((128, NJ, 2, 128), np.float32)
        for j, (bi, own) in enumerate(jobs):
            NS = NSLOT[j]
            S = NS * L
            pos = (np.arange(S) + own * L) % S
            m["x%d" % j] = np.ascontiguousarray(np.roll(xs[j][bi], -own * L, axis=0))
            cR, sR = _rope_tables(pos, 128)
            m["tb%d" % j] = np.ascontiguousarray(np.concatenate([cR, cR, sR, sR], axis=1))
            cM, sM = _rope_tables(pos, 64)
            tm = np.stack([np.concatenate([cM, cM], 1).T, np.concatenate([sM, sM], 1).T], axis=1)
            m["tm%d" % j] = np.ascontiguousarray(tm.astype(np.float32))
            cT[:, :, j] = cs[j][bi].reshape(8, 128).T
            for s in range(1, NS):
                before = (own + s) >= NS
                for n in range(NCH):
                    t = s * NCH + n
                    ex[:, j, 0, t] = (NS - s - 1) * L + (L - 128 - 128 * n)
                    ex[:, j, 1, t] = (s - 1) * L + 128 * n
                    mk[:, j, 0, t] = 1.0 if before else 0.0
                    mk[:, j, 1, t] = 0.0 if before else 1.0
        m["cT"] = cT
        m["mk"] = mk
        m["ex"] = ex
        in_maps.append(m)
    return in_maps


_CACHE = {}


def kernel(x_prompt, x_sample, c_prompt, c_sample, **w):
    L = 4096
    NSLOT = [4, 2]
    x_prompt = np.asarray(x_prompt, np.float32)
    x_sample = np.asarray(x_sample, np.float32)
    c_prompt = np.asarray(c_prompt, np.float32)
    c_sample = np.asarray(c_sample, np.float32)
    w = {k: np.asarray(v, np.float32) for k, v in w.items()}
    core_jobs = [[(c // 4, c % 4), (c // 2, c % 2)] for c in range(NCORES)]
    in_maps = host_inputs(L, NSLOT, core_jobs, [x_prompt, x_sample], [c_prompt, c_sample], w)
    if 'nc' not in _CACHE:
        _CACHE['nc'] = build(L, NSLOT)[0]
    res = run_bass_kernel_spmd(_CACHE['nc'], in_maps, core_ids=list(range(NCORES)))
    yp = np.empty_like(x_prompt)
    ysm = np.empty_like(x_sample)
    for c in range(NCORES):
        r = res.results[c]
        yp[c // 4, (c % 4) * L:(c % 4 + 1) * L] = r["y0"]
        ysm[c // 2, (c % 2) * L:(c % 2 + 1) * L] = r["y1"]
    return (yp, ysm)
```

```python
import math
import numpy as np
import concourse.bass as bass
import concourse.mybir as mybir
from concourse.bass_utils import run_bass_kernel_spmd

F32 = mybir.dt.float32
BF16 = mybir.dt.bfloat16
AF = mybir.ActivationFunctionType
ALU = mybir.AluOpType

D = 1024
DIN = 2752
DFF = 4096
ALPHA = float(2.0 ** 0.25)
MLA_SCALE = float(192 ** -0.5)
NCORES = 8


class Prog:
    SEM_LIMIT = 24000

    def __init__(self, nc):
        self.nc = nc
        self.ops = []
        self.last_w = {}
        self.readers = {}
        self.out_dma_ops = []
        self.barriers = []

    def add(self, eng, fn, r=(), w=(), dma=False, key=None, out=False, nosync=False):
        i = len(self.ops)
        if dma and key is None:
            key = w[0]
        ps = [t for t in r if isinstance(t, tuple) and t[0] == 'pb']
        r = [t for t in r if not (isinstance(t, tuple) and t[0] == 'pb')]
        w = list(w) + [t for t in ps if t not in w]
        deps = set()
        for t in r:
            j = self.last_w.get(t)
            if j is not None:
                deps.add((j, 'raw'))
        for t in w:
            j = self.last_w.get(t)
            if j is not None:
                deps.add((j, 'waw'))
            for k in self.readers.get(t, ()):
                deps.add((k, 'war'))
        for t in r:
            self.readers.setdefault(t, []).append(i)
        for t in w:
            self.last_w[t] = i
            self.readers[t] = []
        self.ops.append(dict(eng=eng, fn=fn, deps=deps, dma=dma, key=key, bar=len(self.barriers), nosync=nosync))
        if out:
            self.out_dma_ops.append(i)
        return i

    def barrier(self):
        self.barriers.append(len(self.ops))

    def emit(self):
        nc = self.nc
        ops = self.ops
        bar_sets = []
        for b in self.barriers:
            last = {}
            for i in range(b):
                op = ops[i]
                k = ('dma', op['key']) if op['dma'] else ('eng', op['eng'])
                last[k] = i
            bar_sets.append(set(last.values()))
        for i, op in enumerate(ops):
            need = set()
            for (j, kind) in op['deps']:
                pj = ops[j]
                if j == i:
                    continue
                if not pj['dma'] and not op['dma'] and pj['eng'] == op['eng']:
                    if op['eng'] == 'pe':
                        continue
                    if kind != 'raw' or op['nosync']:
                        continue
                need.add(j)
            if op['bar'] > 0:
                for j in bar_sets[op['bar'] - 1]:
                    pj = ops[j]
                    if not pj['dma'] and pj['eng'] == op['eng']:
                        continue
                    need.add(j)
            op['need'] = need
        has_dep = [False] * len(ops)
        for op in ops:
            for j in op['need']:
                has_dep[j] = True
        for j in self.out_dma_ops:
            has_dep[j] = True
        cnt = {}
        for i, op in enumerate(ops):
            if op['dma']:
                k = ('dma', op['key'])
            elif has_dep[i]:
                k = ('eng', op['eng'])
            else:
                op['sig'] = None
                continue
            cnt[k] = cnt.get(k, 0) + 1
            op['sig'] = (k, cnt[k])
        sems = {}
        cms = []
        for k, n in cnt.items():
            inc = 16 if k[0] == 'dma' else 1
            per = self.SEM_LIMIT // inc
            ns = (n + per - 1) // per
            lst = []
            for s in range(ns):
                cm = nc.semaphore("s%d_%d" % (len(sems), s))
                lst.append(cm.__enter__())
                cms.append(cm)
            sems[k] = (lst, inc, per)
        self.n_sems = len(cms)

        def semval(sig):
            k, n = sig
            lst, inc, per = sems[k]
            return lst[(n - 1) // per], ((n - 1) % per + 1) * inc

        engs = {}
        for i, op in enumerate(ops):
            engs.setdefault(op['eng'], []).append(i)

        def run(ename, e):
            waited = {}
            for i in engs.get(ename, []):
                op = ops[i]
                tgt = {}
                for j in op['need']:
                    k, n = ops[j]['sig']
                    if waited.get(k, 0) >= n:
                        continue
                    if tgt.get(k, 0) < n:
                        tgt[k] = n
                for k, n in tgt.items():
                    s, v = semval((k, n))
                    e.wait_ge(s, v)
                    waited[k] = n
                ins = op['fn'](e)
                if op['sig'] is not None:
                    s, v = semval(op['sig'])
                    ins.then_inc(s, 16 if op['dma'] else 1)
            if ename == 'sp':
                fin = {}
                for j in self.out_dma_ops:
                    k, n = ops[j]['sig']
                    fin[k] = max(fin.get(k, 0), n)
                for k, n in fin.items():
                    if waited.get(k, 0) >= n:
                        continue
                    s, v = semval((k, n))
                    e.wait_ge(s, v)

        with nc.Block() as block:
            @block.tensor
            def _(e):
                run('pe', e)

            @block.scalar
            def _(e):
                run('act', e)

            @block.vector
            def _(e):
                run('dve', e)

            @block.gpsimd
            def _(e):
                run('pool', e)

            @block.sync
            def _(e):
                run('sp', e)
        for cm in cms:
            cm.__exit__(None, None, None)


class Arena:
    def __init__(self, nc, words):
        self.t = nc.alloc_sbuf_tensor("arena", [128, words], F32)
        self.words = words
        self.top = 0

    def mark(self):
        return self.top

    def release(self, m):
        self.top = m

    def alloc(self, shape, dtype=F32):
        n = 1
        for s in shape[1:]:
            n *= s
        nw = n if dtype == F32 else (n + 1) // 2
        nw = (nw + 7) // 8 * 8
        assert self.top + nw <= self.words, ("SBUF arena overflow", self.top, nw, self.words)
        v = self.t[:, self.top:self.top + nw]
        self.top += nw
        if dtype != F32:
            v = v.bitcast(dtype)
        v = v[:, 0:n]
        if len(shape) == 3:
            v = v.rearrange("p (a b) -> p a b", a=shape[1])
        elif len(shape) == 4:
            v = v.rearrange("p (a b c) -> p a b c", a=shape[1], b=shape[2])
        elif len(shape) == 5:
            v = v.rearrange("p (a b c d) -> p a b c d", a=shape[1], b=shape[2], c=shape[3])
        if shape[0] < 128:
            v = v[0:shape[0]]
        return v


def build(L, NSLOT, debug=False):
    NJ = len(NSLOT)
    NCH = L // 128
    SS = [ns * L for ns in NSLOT]
    nc = bass.Bass("TRN2", target_bir_lowering=False)
    P = Prog(nc)

    def din(name, shape, dt=F32):
        return nc.dram_tensor(name, list(shape), dt, kind="ExternalInput").ap()

    def dscr(name, shape, dt=BF16):
        return nc.dram_tensor(name, list(shape), dt, kind=("ExternalOutput" if debug else "Internal")).ap()

    x_in = [din("x%d" % j, [SS[j], D]) for j in range(NJ)]
    tb_in = [din("tb%d" % j, [SS[j], 256]) for j in range(NJ)]
    tm_in = [din("tm%d" % j, [64, 2, SS[j]]) for j in range(NJ)]
    cT_in = din("cT", [128, 8, NJ])
    mk_in = din("mk", [128, NJ, 2, 128])
    ex_in = din("ex", [128, NJ, 2, 128])
    cst_in = din("cst", [128, 4 * 128 + 8])
    wada_in = din("w_ada", [D, 6 * D])
    bfm_in = din("b_ada_fm", [128, 48])
    brow_in = din("b_ada_row", [1, 6 * D])
    win_in = din("w_in", [D, DIN])
    wuq_in = din("w_uq", [384, 768])
    wukv_in = din("w_ukv", [256, 1024])
    wo_in = din("w_o", [D, D])
    wup_in = din("w_up", [D, DFF])
    wdn_in = din("w_down", [DFF, D])
    dec_in = din("dec", [1, 8])
    gnw_in = din("gnw", [1, 512])
    qnw_in = din("qnw_fm", [128, 3])
    kvnw_in = din("kvnw_fm", [128, 2])
    ln_in = din("ln", [4, D])
    y_out = [nc.dram_tensor("y%d" % j, [L, D], F32, kind="ExternalOutput").ap() for j in range(NJ)]

    KTd = [dscr("KTd%d" % j, [4, 128, SS[j]]) for j in range(NJ)]
    KRd = [dscr("KRd%d" % j, [64, SS[j]]) for j in range(NJ)]
    VVd = [dscr("VVd%d" % j, [4, 128, SS[j] // 128, 128]) for j in range(NJ)]
    QTd = [dscr("QTd%d" % j, [4, 128, L]) for j in range(NJ)]
    QRd = [dscr("QRd%d" % j, [4, 64, L]) for j in range(NJ)]
    RTd = [dscr("RTd%d" % j, [4, 128, L]) for j in range(NJ)]
    ATd = [dscr("ATd%d" % j, [4, 128, L]) for j in range(NJ)]
    X1d = [dscr("X1d%d" % j, [L, D], F32) for j in range(NJ)]

    AR = Arena(nc, 49100)
    PB = [nc.alloc_psum_tensor("pb%d" % i, [128, 512], F32) for i in range(8)]

    def mm(out, lhsT, rhs, start, stop, r, w):
        P.add('pe', lambda e: e.matmul(out, lhsT=lhsT, rhs=rhs, start=start, stop=stop), r=r, w=w)

    def act(out, in_, func, r, w, bias=None, scale=None):
        kw = {}
        if bias is not None:
            kw['bias'] = bias
        if scale is not None:
            kw['scale'] = scale
        P.add('act', lambda e: e.activation(out=out, in_=in_, func=func, **kw), r=r, w=w)

    def tt(eng, out, in0, in1, op, r, w, nosync=False):
        P.add(eng, lambda e: e.tensor_tensor(out=out, in0=in0, in1=in1, op=op), r=r, w=w, nosync=nosync)

    def ts(eng, out, in0, s1, s2, op0, op1, r, w):
        if s2 is None:
            P.add(eng, lambda e: e.tensor_scalar(out=out, in0=in0, scalar1=s1, scalar2=None, op0=op0), r=r, w=w)
        else:
            P.add(eng, lambda e: e.tensor_scalar(out=out, in0=in0, scalar1=s1, scalar2=s2, op0=op0, op1=op1), r=r, w=w)

    def stt(eng, out, in0, scalar, in1, op0, op1, r, w):
        P.add(eng, lambda e: e.scalar_tensor_tensor(out=out, in0=in0, scalar=scalar, in1=in1, op0=op0, op1=op1),
              r=r, w=w)

    def cp(eng, out, in_, r, w):
        if eng == 'act':
            P.add('act', lambda e: e.copy(out=out, in_=in_), r=r, w=w)
        else:
            P.add(eng, lambda e: e.tensor_copy(out=out, in_=in_), r=r, w=w)

    def dma(q, out, in_, r, w, key=None, outp=False):
        P.add(q, lambda e: e.dma_start(out=out, in_=in_), r=r, w=w, dma=True, key=key, out=outp)

    def recip(out, in_, r, w):
        P.add('dve', lambda e: e.reciprocal(out=out, in_=in_), r=r, w=w)

    def bnstats(out, in_, r, w):
        P.add('dve', lambda e: e.bn_stats(out=out, in_=in_), r=r, w=w)

    def bnaggr(out, in_, r, w):
        P.add('dve', lambda e: e.bn_aggr(out=out, in_=in_), r=r, w=w)

    def actmul(out, in_, m, r, w):
        P.add('act', lambda e: e.mul(out=out, in_=in_, mul=m), r=r, w=w)

    def memset(eng, ap, v, w):
        P.add(eng, lambda e: e.memset(ap, v), w=w)

    ident_bf = AR.alloc([128, 128], BF16)
    ident_f = AR.alloc([128, 128], F32)
    ones_bf = AR.alloc([128, 128], BF16)
    ones_f = AR.alloc([128, 128], F32)
    cst = AR.alloc([128, 4 * 128 + 8], F32)
    dma('sp', cst, cst_in, r=[], w=['cst'])
    memset('pool', ident_f, 1.0, ['ident_f'])
    P.add('pool', lambda e: e.affine_select(out=ident_f, in_=ident_f, pattern=[[-1, 128]], compare_op=ALU.is_equal,
                                            fill=0.0, base=0, channel_multiplier=1), r=['ident_f'], w=['ident_f'])
    cp('pool', ident_bf, ident_f, ['ident_f'], ['ident_bf'])
    memset('pool', ones_f, 1.0, ['ones_f'])
    memset('pool', ones_bf, 1.0, ['ones_bf'])

    def tr(out, in_, r, w):
        k = in_.shape[0]
        P.add('pe', lambda e: e.transpose(out=out, in_=in_, identity=ident_bf[0:k, 0:k]), r=list(r) + ['ident_bf'], w=w)

    RF = cst[:, 0:128]
    RB = cst[:, 128:256]
    MF = cst[:, 256:384]
    MB = cst[:, 384:512]
    c_127mj = cst[:, 512:513]
    c_j = cst[:, 513:514]
    c_128 = cst[:, 514:515]

    dec = AR.alloc([128, 8], F32)
    dma('sp', dec, dec_in.partition_broadcast(128).rearrange("p a b -> p (a b)"), r=[], w=['dec'])
    lg = AR.alloc([128, 8], F32)
    act(lg, dec, AF.Exp, ['dec'], ['lg'], scale=-1.0)
    ts('dve', lg, lg, 1.0, None, ALU.add, None, ['lg'], ['lg'])
    act(lg, lg, AF.Ln, ['lg'], ['lg'])
    ts('dve', lg, lg, -1.0, None, ALU.mult, None, ['lg'], ['lg'])
    qnw = AR.alloc([128, 3], F32)
    kvnw = AR.alloc([128, 2], F32)
    dma('sp', qnw, qnw_in, r=[], w=['qnw'])
    dma('sp', kvnw, kvnw_in, r=[], w=['kvnw'])
    bfm = AR.alloc([128, 48], F32)
    dma('sp', bfm, bfm_in, r=[], w=['bfm'])
    cT = AR.alloc([128, 8, NJ], F32)
    dma('sp', cT, cT_in, r=[], w=['cT'])
    siluc = AR.alloc([128, 8, NJ], F32)
    act(siluc, cT, AF.Silu, ['cT'], ['siluc'])
    modfm = AR.alloc([128, 48, NJ], F32)
    scp = AR.alloc([128, 2, 8, NJ], F32)

    persist_mark = AR.mark()

    def ln_apply(src, dst, tag, rtok, wtok, eps=1e-5, eng='dve'):
        st, mv, rs = LNS[tag]
        for c in range(2):
            P.add('dve', (lambda e, c=c: e.bn_stats(out=st[:, c, :], in_=src[:, c * 512:(c + 1) * 512])),
                  r=rtok, w=[('st', tag, c)])
        P.add('dve', lambda e: e.bn_aggr(out=mv, in_=st), r=[('st', tag, 0), ('st', tag, 1)], w=[('mv', tag)])
        ts('dve', rs, mv[:, 1:2], eps, None, ALU.add, None, [('mv', tag)], [('rs', tag)])
        P.add('dve', lambda e: e.reciprocal(out=rs, in_=rs), r=[('rs', tag)], w=[('rs', tag)])
        P.add('act', lambda e: e.sqrt(out=rs, in_=rs), r=[('rs', tag)], w=[('rs', tag)])
        ts(eng, dst, src, mv[:, 0:1], rs[:, 0:1], ALU.subtract, ALU.mult, list(rtok) + [('mv', tag), ('rs', tag)], wtok)

    LNS = {}

    def ln_scratch(tag):
        LNS[tag] = (AR.alloc([128, 2, 6], F32), AR.alloc([128, 2], F32), AR.alloc([128, 1], F32))

    def rsqrt_small(x, tag, mul, eps):
        ts('dve', x, x, mul, eps, ALU.mult, ALU.add, [tag], [tag])
        P.add('dve', lambda e: e.reciprocal(out=x, in_=x), r=[tag], w=[tag])
        P.add('act', lambda e: e.sqrt(out=x, in_=x), r=[tag], w=[tag])

    for j in range(NJ):
        S = SS[j]
        NS = NSLOT[j]
        NT = S // 128
        xj, tbj, tmj = x_in[j], tb_in[j], tm_in[j]
        P.barrier()
        AR.release(persist_mark)
        gB = AR.alloc([128, 2, 1024], F32)
        job_mark = AR.mark()
        Win = AR.alloc([128, 8, DIN], BF16)
        Wkrr = AR.alloc([128, 8, 64], BF16)
        brow = AR.alloc([1, DIN], BF16)
        bcfm = AR.alloc([128, 5], F32)
        bkr = AR.alloc([64, 2], F32)
        coef = AR.alloc([128, 2, 4, 128], F32)
        Sf = AR.alloc([128, 4, 128], F32)
        Tb = AR.alloc([128, 4, 128], F32)
        Sbf = AR.alloc([128, 4, 128], BF16)
        Tsave = AR.alloc([128, NCH, 4, 128], BF16)
        DcT = AR.alloc([128, 4, 128], F32)
        dqfB = AR.alloc([128, 4, 128], F32)
        dqbB = AR.alloc([128, 4, 128], F32)
        dkk = AR.alloc([128, 8], F32)
        dcc = AR.alloc([128, 8], F32)
        tmpa = AR.alloc([128, 128], F32)
        tmpb = AR.alloc([128, 128], F32)
        rowi1 = AR.alloc([128, 128], F32)
        row128mi = AR.alloc([128, 128], F32)
        Wukv = AR.alloc([128, 2, 2, 4, 128], BF16)
        Wuq = AR.alloc([128, 3, 4, 192], BF16)
        Wuqr = AR.alloc([128, 3, 4, 64], BF16)
        prep_mark = AR.mark()
        mkt = AR.alloc([128, 2, 128], F32)
        ext = AR.alloc([128, 2, 128], F32)
        st_ukv = AR.alloc([128, 2, 1024], F32)
        st_uq = AR.alloc([128, 3, 768], F32)
        silucB = AR.alloc([128, 8, 128], F32)
        browada = AR.alloc([1, 2, 1024], F32)
        stkr = AR.alloc([128, 8, 64], F32)
        wst = [AR.alloc([128, 8, 512], F32) for _ in range(2)]
        tt('dve', rowi1, RF, RB, ALU.subtract, ['cst'], ['rowi1'])
        ts('dve', rowi1, rowi1, c_j, 1.0, ALU.add, ALU.add, ['rowi1', 'cst'], ['rowi1'])
        ts('dve', row128mi, rowi1, -1.0, 129.0, ALU.mult, ALU.add, ['rowi1'], ['row128mi'])
        for h in range(4):
            act(tmpa, RF, AF.Exp, ['cst', 'lg'], ['tmpa'], scale=lg[:, h:h + 1])
            tt('dve', tmpa, tmpa, MF, ALU.mult, ['tmpa', 'cst'], ['tmpa'])
            act(tmpb, RB, AF.Exp, ['cst', 'lg'], ['tmpb'], scale=lg[:, 4 + h:5 + h])
            tt('dve', tmpb, tmpb, MB, ALU.mult, ['tmpb', 'cst'], ['tmpb'])
            tt('dve', DcT[:, h, :], tmpa, tmpb, ALU.add, ['tmpa', 'tmpb'], ['DcT'])
            act(dqfB[:, h, :], rowi1, AF.Exp, ['rowi1', 'lg'], ['dqfB'], scale=lg[:, h:h + 1])
            act(dqbB[:, h, :], row128mi, AF.Exp, ['row128mi', 'lg'], ['dqbB'], scale=lg[:, 4 + h:5 + h])
            act(dkk[:, h:h + 1], c_127mj, AF.Exp, ['cst', 'lg'], ['dkk'], scale=lg[:, h:h + 1])
            act(dkk[:, 4 + h:5 + h], c_j, AF.Exp, ['cst', 'lg'], ['dkk'], scale=lg[:, 4 + h:5 + h])
            act(dcc[:, h:h + 1], c_128, AF.Exp, ['cst', 'lg'], ['dcc'], scale=lg[:, h:h + 1])
            act(dcc[:, 4 + h:5 + h], c_128, AF.Exp, ['cst', 'lg'], ['dcc'], scale=lg[:, 4 + h:5 + h])

        dma('sp', st_ukv, wukv_in.rearrange("(j p) n -> p j n", p=128), r=[], w=['st_ukv'])
        dma('sp', st_uq, wuq_in.rearrange("(j p) n -> p j n", p=128), r=[], w=['st_uq'])
        for cj in range(2):
            v = st_ukv[:, cj, :].rearrange("p (h t d) -> p t h d", h=4, t=2)
            ts('dve', Wukv[:, cj], v, kvnw[:, cj:cj + 1], None, ALU.mult, None, ['st_ukv', 'kvnw'], ['Wukv'])
        for cj in range(3):
            v = st_uq[:, cj, :].rearrange("p (h d) -> p h d", h=4)
            ts('dve', Wuq[:, cj], v, qnw[:, cj:cj + 1], MLA_SCALE, ALU.mult, ALU.mult, ['st_uq', 'qnw'], ['Wuq'])
            ts('dve', Wuqr[:, cj, :, 0:32], v[:, :, 160:192], qnw[:, cj:cj + 1], -MLA_SCALE, ALU.mult, ALU.mult,
               ['st_uq', 'qnw'], ['Wuqr'])
            ts('dve', Wuqr[:, cj, :, 32:64], v[:, :, 128:160], qnw[:, cj:cj + 1], MLA_SCALE, ALU.mult, ALU.mult,
               ['st_uq', 'qnw'], ['Wuqr'])

        for k in range(8):
            cp('dve', silucB[:, k, :], siluc[:, k, j:j + 1].broadcast_to([128, 128]), ['siluc'], ['silucB'])
        dma('sp', browada[:, 0, :], brow_in[:, 2048:3072], r=[], w=['browada'])
        dma('sp', browada[:, 1, :], brow_in[:, 5120:6144], r=[], w=['browada'])
        gi = 0
        for g in range(12):
            b = gi % 2
            gi += 1
            dma('sp', wst[b], wada_in[:, g * 512:(g + 1) * 512].rearrange("(k p) n -> p k n", p=128),
                r=[], w=[('wst', b)])
            if g in (4, 5, 10, 11):
                which = 0 if g < 6 else 1
                half = g % 2 if g < 6 else (g - 10)
                for k in range(8):
                    mm(PB[0][:, :], silucB[:, k, :], wst[b][:, k, :], k == 0, False, ['silucB', ('wst', b)], [('pb', 0)])
                mm(PB[0][:, :], ones_f[0:1, :], browada[0:1, which, half * 512:(half + 1) * 512], False, True,
                   ['ones_f', 'browada'], [('pb', 0)])
                cp('act', gB[:, which, half * 512:(half + 1) * 512], PB[0][:, :], [('pb', 0)], ['gB'])
            else:
                for c in range(4):
                    ch = g * 4 + c
                    for k in range(8):
                        mm(PB[1][:, 0:1], wst[b][:, k, c * 128:(c + 1) * 128], siluc[:, k, j:j + 1], k == 0, k == 7,
                           [('wst', b), 'siluc'], [('pb', 1)])
                    ts('dve', modfm[:, ch, j:j + 1], PB[1][:, 0:1], bfm[:, ch:ch + 1], None, ALU.add, None,
                       [('pb', 1), 'bfm'], ['modfm'])
        for k in range(8):
            ts('dve', scp[:, 0, k, j:j + 1], modfm[:, 8 + k, j:j + 1], 1.0, None, ALU.add, None, ['modfm'], ['scp'])
            ts('dve', scp[:, 1, k, j:j + 1], modfm[:, 32 + k, j:j + 1], 1.0, None, ALU.add, None, ['modfm'], ['scp'])

        groups = [(0, 512), (512, 512), (1024, 512), (1536, 512), (2048, 512), (2560, 192)]
        for (c0, cw) in groups:
            b = gi % 2
            gi += 1
            dma('sp', wst[b][:, :, 0:cw], win_in[:, c0:c0 + cw].rearrange("(k p) n -> p k n", p=128),
                r=[], w=[('wst', b)])
            for k in range(8):
                mm(PB[0][0:1, 0:cw], modfm[:, k, j:j + 1], wst[b][:, k, 0:cw], k == 0, k == 7,
                   ['modfm', ('wst', b)], [('pb', 0)])
            if c0 == 512:
                actmul(brow[0:1, c0:c0 + cw], PB[0][0:1, 0:cw], float(128 ** -0.5), [('pb', 0)], ['brow'])
            else:
                cp('act', brow[0:1, c0:c0 + cw], PB[0][0:1, 0:cw], [('pb', 0)], ['brow'])
            if c0 == 2048:
                for c in range(4):
                    for k in range(8):
                        mm(PB[1][:, 0:1], wst[b][:, k, c * 128:(c + 1) * 128], modfm[:, k, j:j + 1], k == 0, k == 7,
                           [('wst', b), 'modfm'], [('pb', 1)])
                    cp('dve', bcfm[:, c:c + 1], PB[1][:, 0:1], [('pb', 1)], ['bcfm'])
            if c0 == 2560:
                for k in range(8):
                    mm(PB[1][:, 0:1], wst[b][:, k, 0:128], modfm[:, k, j:j + 1], k == 0, k == 7,
                       [('wst', b), 'modfm'], [('pb', 1)])
                cp('dve', bcfm[:, 4:5], PB[1][:, 0:1], [('pb', 1)], ['bcfm'])
                ts('dve', stkr[:, :, 0:32], wst[b][:, :, 160:192], -1.0, None, ALU.mult, None, [('wst', b)], ['stkr'])
                cp('dve', stkr[:, :, 32:64], wst[b][:, :, 128:160], [('wst', b)], ['stkr'])
                for k in range(8):
                    mm(PB[1][0:64, 0:1], wst[b][:, k, 128:192], modfm[:, k, j:j + 1], k == 0, k == 7,
                       [('wst', b), 'modfm'], [('pb', 1)])
                cp('dve', bkr[:, 0:1], PB[1][0:64, 0:1], [('pb', 1)], ['bkr'])
                for k in range(8):
                    mm(PB[1][0:64, 0:1], stkr[:, k, :], modfm[:, k, j:j + 1], k == 0, k == 7,
                       ['stkr', 'modfm'], [('pb', 1)])
                cp('dve', bkr[:, 1:2], PB[1][0:64, 0:1], [('pb', 1)], ['bkr'])
                for k in range(8):
                    ts('dve', Wkrr[:, k, :], stkr[:, k, :], scp[:, 0, k, j:j + 1], None, ALU.mult, None,
                       ['stkr', 'scp'], ['Wkrr'])
            for k in range(8):
                if c0 == 512:
                    ts('dve', Win[:, k, c0:c0 + cw], wst[b][:, k, 0:cw], scp[:, 0, k, j:j + 1], float(128 ** -0.5),
                       ALU.mult, ALU.mult, [('wst', b), 'scp'], ['Win'])
                else:
                    ts('dve', Win[:, k, c0:c0 + cw], wst[b][:, k, 0:cw], scp[:, 0, k, j:j + 1], None, ALU.mult, None,
                       [('wst', b), 'scp'], ['Win'])
        dma('sp', mkt, mk_in[:, j], r=[], w=['mkt'])
        dma('sp', ext, ex_in[:, j], r=[], w=['ext'])
        for dr in range(2):
            for h in range(4):
                act(coef[:, dr, h, :], ext[:, dr, :], AF.Exp, ['ext', 'lg'], ['coef'], scale=lg[:, 4 * dr + h:4 * dr + h + 1])
                tt('dve', coef[:, dr, h, :], coef[:, dr, h, :], mkt[:, dr, :], ALU.mult, ['coef', 'mkt'], ['coef'])
        memset('pool', Sf, 0.0, ['Sf'])
        memset('pool', Tb, 0.0, ['Tb'])
        P.barrier()
        AR.release(prep_mark)
        ab_mark = AR.mark()

        xs = [AR.alloc([128, D], F32) for _ in range(2)]
        tbt = [AR.alloc([128, 256], F32) for _ in range(2)]
        tmt = [AR.alloc([64, 2, 128], F32) for _ in range(2)]
        xn = AR.alloc([128, D], BF16)
        xT = AR.alloc([128, 8, 128], BF16)
        rA = AR.alloc([128, 4, 128], F32)
        rBt = AR.alloc([128, 4, 128], F32)
        rstd_t = AR.alloc([128, 1], F32)
        rB = AR.alloc([128, 128], F32)
        aonly_mark = AR.mark()
        kr1 = AR.alloc([64, 128], F32)
        kr2 = AR.alloc([64, 128], F32)
        pT = PB[0][:, :].bitcast(BF16).rearrange("p (a b) -> p a b", a=8)[:, :, 0:128]

        def load(t, b):
            dma('act', xs[b], xj[t * 128:(t + 1) * 128, :], r=[], w=[('xs', b)])
            dma('act', tbt[b], tbj[t * 128:(t + 1) * 128, :], r=[], w=[('tbt', b)])
            dma('act', tmt[b], tmj[:, :, t * 128:(t + 1) * 128], r=[], w=[('tmt', b)])

        def proj_tok(pb, tok, c0, cw):
            for k in range(8):
                mm(pb[:, 0:cw], xT[:, k, :], Win[:, k, c0:c0 + cw], k == 0, False, ['xT', 'Win'], [tok])
            mm(pb[:, 0:cw], ones_bf[0:1, :], brow[0:1, c0:c0 + cw], False, True, ['ones_bf', 'brow'], [tok])

        def rotary_tok(pb, tok, b, dst, dtok):
            v = pb[:, :].rearrange("p (h d) -> p h d", h=4)
            cosB = tbt[b][:, 0:128].unsqueeze(1).broadcast_to([128, 4, 128])
            sinB = tbt[b][:, 128:256].unsqueeze(1).broadcast_to([128, 4, 128])
            tt('dve', rA, v, cosB, ALU.mult, [tok, ('tbt', b)], ['rA'])
            tt('dve', rBt, v, sinB, ALU.mult, [tok, ('tbt', b)], ['rBt'])
            tt('pool', dst[:, :, 0:64], rA[:, :, 0:64], rBt[:, :, 64:128], ALU.subtract, ['rA', 'rBt'], [dtok])
            tt('pool', dst[:, :, 64:128], rBt[:, :, 0:64], rA[:, :, 64:128], ALU.add, ['rA', 'rBt'], [dtok])

        Kp2 = [AR.alloc([128, 4, 128], BF16) for _ in range(2)]
        Kwf2 = [AR.alloc([128, 4, 128], BF16) for _ in range(2)]
        Kwb2 = [AR.alloc([128, 4, 128], BF16) for _ in range(2)]
        Vr2 = [AR.alloc([128, 512], BF16) for _ in range(2)]
        sqv2 = [AR.alloc([128, 2, 128], BF16) for _ in range(2)]
        ckv2 = [AR.alloc([128, 2, 128], BF16) for _ in range(2)]
        KTb2 = [AR.alloc([128, 4, 512], BF16) for _ in range(2)]
        KRb2 = [AR.alloc([64, 512], BF16) for _ in range(2)]
        VMb2 = [AR.alloc([128, 4, 4, 128], BF16) for _ in range(2)]
        ckt = [AR.alloc([128, 2, 4], F32) for _ in range(2)]
        dkfB = dkk[:, 0:4].unsqueeze(2).broadcast_to([128, 4, 128])
        dkbB = dkk[:, 4:8].unsqueeze(2).broadcast_to([128, 4, 128])

        xnA = [xn, AR.alloc([128, D], BF16)]
        xTA = [xT, AR.alloc([128, 8, 128], BF16)]
        ln_scratch('a0')
        ln_scratch('a1')

        def lnA(t):
            b = t % 2
            ln_apply(xs[b], xnA[b], 'a%d' % b, [('xs', b)], [('xnA', b)])

        def trA(t):
            b = t % 2
            for k in range(8):
                tr(pT[:, k, :], xnA[b][:, k * 128:(k + 1) * 128], [('xnA', b)], [('pb', 0)])
            cp('act', xTA[b], pT, [('pb', 0)], [('xTA', b)])

        def projA(t):
            b = t % 2
            xTb = xTA[b]
            xtok = ('xTA', b)
            for (pb_, c0) in ((1, 512), (2, 1024)):
                for k in range(8):
                    mm(PB[pb_][:, :], xTb[:, k, :], Win[:, k, c0:c0 + 512], k == 0, False, [xtok, 'Win'], [('pb', pb_)])
                mm(PB[pb_][:, :], ones_bf[0:1, :], brow[0:1, c0:c0 + 512], False, True, ['ones_bf', 'brow'], [('pb', pb_)])
            for cj in range(2):
                for k in range(8):
                    mm(PB[3][:, cj * 128:(cj + 1) * 128], Win[:, k, 2432 + cj * 128:2432 + (cj + 1) * 128], xTb[:, k, :],
                       k == 0, k == 7, ['Win', xtok], [('pb', 3)])
            for k in range(8):
                mm(PB[3][0:64, 256:384], Win[:, k, 2688:2752], xTb[:, k, :], k == 0, k == 7, ['Win', xtok], [('pb', 3)])
            for k in range(8):
                mm(PB[3][0:64, 384:512], Wkrr[:, k, :], xTb[:, k, :], k == 0, k == 7, ['Wkrr', xtok], [('pb', 3)])

        def firstA(t):
            b = t % 2
            s = t // NCH
            q4 = t % 4
            bp = (t // 4) % 2
            rotary_tok(PB[1], ('pb', 1), b, Kp2[b], ('Kp2', b))
            cp('act', Vr2[b], PB[2][:, :], [('pb', 2)], [('Vr2', b)])
            ckvp = PB[3][:, 0:256].rearrange("p (a b) -> p a b", a=2)
            for cj in range(2):
                act(sqv2[b][:, cj, :], ckvp[:, cj, :], AF.Square, [('pb', 3), 'bcfm'], [('sqv2', b)], bias=bcfm[:, 3 + cj:4 + cj])
                act(ckv2[b][:, cj, :], ckvp[:, cj, :], AF.Identity, [('pb', 3), 'bcfm'], [('ckv2', b)], bias=bcfm[:, 3 + cj:4 + cj])
            stt('dve', kr1, PB[3][0:64, 256:384], bkr[:, 0:1], tmt[b][:, 0, :], ALU.add, ALU.mult,
                [('pb', 3), 'bkr', ('tmt', b)], ['kr1'])
            stt('dve', kr2, PB[3][0:64, 384:512], bkr[:, 1:2], tmt[b][:, 1, :], ALU.add, ALU.mult,
                [('pb', 3), 'bkr', ('tmt', b)], ['kr2'])
            tt('pool', KRb2[bp][:, q4 * 128:(q4 + 1) * 128], kr1, kr2, ALU.add, ['kr1', 'kr2'], [('KRb2', bp)])
            if s > 0:
                tt('dve', ckt[b], dkk.rearrange("p (a h) -> p a h", a=2), coef[:, :, :, t], ALU.mult, ['dkk', 'coef'], [('ckt', b)])
                tt('pool', Kwf2[b], Kp2[b], ckt[b][:, 0, :].unsqueeze(2).broadcast_to([128, 4, 128]), ALU.mult,
                   [('Kp2', b), ('ckt', b)], [('Kwf2', b)])
                tt('pool', Kwb2[b], Kp2[b], ckt[b][:, 1, :].unsqueeze(2).broadcast_to([128, 4, 128]), ALU.mult,
                   [('Kp2', b), ('ckt', b)], [('Kwb2', b)])
            else:
                tt('pool', Kwb2[b], Kp2[b], dkbB, ALU.mult, [('Kp2', b), 'dkk'], [('Kwb2', b)])

        def secondA(t):
            b = t % 2
            s = t // NCH
            n = t % NCH
            q4 = t % 4
            bp = (t // 4) % 2
            for cj in range(2):
                mm(PB[4][:, 0:1], sqv2[b][:, cj, :], ones_bf[:, 0:1], cj == 0, cj == 1, [('sqv2', b), 'ones_bf'], [('pb', 4)])
            for cj in range(2):
                mm(PB[4][:, 128:256], ones_bf, sqv2[b][:, cj, :], cj == 0, cj == 1, ['ones_bf', ('sqv2', b)], [('pb', 4)])
            cp('dve', rstd_t, PB[4][:, 0:1], [('pb', 4)], ['rstd_t'])
            cp('dve', rB, PB[4][:, 128:256], [('pb', 4)], ['rB'])
            rsqrt_small(rstd_t, 'rstd_t', 1.0 / 256.0, 1e-6)
            rsqrt_small(rB, 'rB', 1.0 / 256.0, 1e-6)
            Pst = [PB[6][:, :].rearrange("p (h e) -> p h e", h=4), PB[7][:, :].rearrange("p (h e) -> p h e", h=4)]
            if s > 0:
                first_t = (t == NT - 1)
                last_t = (t == NCH)
                for h in range(4):
                    mm(Pst[1][:, h, :], Kwb2[b][:, h, :], Vr2[b][:, h * 128:(h + 1) * 128], first_t and h == 0, last_t,
                       [('Kwb2', b), ('Vr2', b)], [('pb', 7)])
                for h in range(4):
                    mm(Pst[0][:, h, :], Kwf2[b][:, h, :], Vr2[b][:, h * 128:(h + 1) * 128], first_t and h == 0, last_t,
                       [('Kwf2', b), ('Vr2', b)], [('pb', 6)])
            else:
                for h in range(4):
                    mm(Pst[1][:, h, :], Kwb2[b][:, h, :], Vr2[b][:, h * 128:(h + 1) * 128], True, True,
                       [('Kwb2', b), ('Vr2', b)], [('pb', 7)])
            knp = PB[5][:, :].rearrange("p (h d) -> p h d", h=4)
            for h in range(4):
                for cj in range(2):
                    mm(knp[:, h, :], Wukv[:, cj, 0, h, :], ckv2[b][:, cj, :], cj == 0, cj == 1, ['Wukv', ('ckv2', b)], [('pb', 5)])
            for cj in range(2):
                mm(PB[4][:, :], ckv2[b][:, cj, :], Wukv[:, cj, 1].rearrange("p h d -> p (h d)"), cj == 0, cj == 1,
                   [('ckv2', b), 'Wukv'], [('pb', 4)])
            if s > 0:
                if t == NCH:
                    cp('dve', Sf, Pst[0], [('pb', 6)], ['Sf'])
                    cp('dve', Tb, Pst[1], [('pb', 7)], ['Tb'])
            else:
                cp('act', Tsave[:, n], Tb, ['Tb'], ['Tsave'])
                for h in range(4):
                    stt('dve', Tb[:, h, :], Tb[:, h, :], dcc[:, 4 + h:5 + h], Pst[1][:, h, :], ALU.mult, ALU.add,
                        [('pb', 7), 'dcc', 'Tb'], ['Tb'])
            tt('dve', KTb2[bp][:, :, q4 * 128:(q4 + 1) * 128], knp, rB.unsqueeze(1).broadcast_to([128, 4, 128]), ALU.mult,
               [('pb', 5), 'rB'], [('KTb2', bp)])
            act(VMb2[bp][:, :, q4, :], PB[4][:, :].rearrange("p (h d) -> p h d", h=4), AF.Identity, [('pb', 4), 'rstd_t'],
                [('VMb2', bp)], scale=rstd_t[:, 0:1])
            if q4 == 0:
                t0 = t
                dma('sp', KTd[j][:, :, t0 * 128:(t0 + 4) * 128].rearrange("h p n -> p h n"), KTb2[bp], r=[('KTb2', bp)], w=[('KTd', j)])
                dma('sp', KRd[j][:, t0 * 128:(t0 + 4) * 128], KRb2[bp], r=[('KRb2', bp)], w=[('KRd', j)])
                dma('sp', VVd[j][:, :, t0:t0 + 4, :].rearrange("h p t e -> p h t e"), VMb2[bp], r=[('VMb2', bp)], w=[('VVd', j)])

        load(NT - 1, (NT - 1) % 2)
        load(NT - 2, (NT - 2) % 2)
        lnA(NT - 1)
        trA(NT - 1)
        projA(NT - 1)
        lnA(NT - 2)
        firstA(NT - 1)
        trA(NT - 2)
        for t in range(NT - 1, -1, -1):
            if t - 1 >= 0:
                projA(t - 1)
                if t - 2 >= 0:
                    load(t - 2, t % 2)
                    lnA(t - 2)
                firstA(t - 1)
                if t - 2 >= 0:
                    trA(t - 2)
            secondA(t)

        P.barrier()
        AR.release(aonly_mark)
        QT3 = AR.alloc([128, 3, 4, 128], BF16)
        KT = AR.alloc([128, 4, 128], BF16)
        AcT = AR.alloc([128, 4, 128], BF16)
        gnB = AR.alloc([128, 512], F32)
        dma('sp', gnB, gnw_in.partition_broadcast(128).rearrange("p a b -> p (a b)"), r=[], w=['gnB'])
        st4 = AR.alloc([128, 4, 6], F32)
        mv4 = AR.alloc([128, 4, 2], F32)
        rs4 = AR.alloc([128, 4], F32)
        retb = AR.alloc([128, 512], BF16)
        RTb = AR.alloc([128, 4, 512], BF16)
        QNb = AR.alloc([128, 4, 512], BF16)
        QRb = AR.alloc([64, 4, 512], BF16)
        qr1 = AR.alloc([64, 4, 128], F32)
        qr2 = AR.alloc([64, 4, 128], F32)
        xnB = [xn, AR.alloc([128, D], BF16)]
        xTB = [xT, AR.alloc([128, 8, 128], BF16)]
        ln_scratch('b0')
        ln_scratch('b1')
        tmt3 = [AR.alloc([64, 2, 128], F32) for _ in range(3)]
        Qp2 = [AR.alloc([128, 4, 128], BF16) for _ in range(2)]
        KpB = [AR.alloc([128, 4, 128], BF16) for _ in range(2)]
        VrB = [AR.alloc([128, 512], BF16) for _ in range(2)]
        gate2 = [AR.alloc([128, 512], BF16) for _ in range(2)]
        KwfB = [AR.alloc([128, 4, 128], BF16) for _ in range(2)]
        sqB = [AR.alloc([128, 3, 128], BF16) for _ in range(2)]
        cqB = [AR.alloc([128, 3, 128], BF16) for _ in range(2)]
        dkfB2 = dkk[:, 0:4].unsqueeze(2).broadcast_to([128, 4, 128])
        y0 = [AR.alloc([128, 4, 128], F32) for _ in range(2)]
        retb2 = [retb, AR.alloc([128, 512], BF16)]

        def loadB(c):
            b = c % 2
            dma('act', xs[b], xj[c * 128:(c + 1) * 128, :], r=[], w=[('xs', b)])
            dma('act', tbt[b], tbj[c * 128:(c + 1) * 128, :], r=[], w=[('tbt', b)])
            dma('act', tmt3[c % 3], tmj[:, :, c * 128:(c + 1) * 128], r=[], w=[('tmt3', c % 3)])

        def lnB_(c):
            b = c % 2
            ln_apply(xs[b], xnB[b], 'b%d' % b, [('xs', b)], [('xnB', b)])

        def trB(c):
            b = c % 2
            for k in range(8):
                tr(pT[:, k, :], xnB[b][:, k * 128:(k + 1) * 128], [('xnB', b)], [('pb', 0)])
            cp('act', xTB[b], pT, [('pb', 0)], [('xTB', b)])

        def projB(c):
            b = c % 2
            xTb = xTB[b]
            xtok = ('xTB', b)
            for (pb_, c0) in ((1, 0), (2, 512), (3, 1024), (4, 1536)):
                for k in range(8):
                    mm(PB[pb_][:, :], xTb[:, k, :], Win[:, k, c0:c0 + 512], k == 0, False, [xtok, 'Win'], [('pb', pb_)])
                mm(PB[pb_][:, :], ones_bf[0:1, :], brow[0:1, c0:c0 + 512], False, True, ['ones_bf', 'brow'], [('pb', pb_)])
            for cj in range(3):
                for k in range(8):
                    mm(PB[5][:, cj * 128:(cj + 1) * 128], Win[:, k, 2048 + cj * 128:2048 + (cj + 1) * 128], xTb[:, k, :],
                       k == 0, k == 7, ['Win', xtok], [('pb', 5)])

        def evacB(c):
            b = c % 2
            rotary_tok(PB[1], ('pb', 1), b, Qp2[b], ('Qp2', b))
            rotary_tok(PB[2], ('pb', 2), b, KpB[b], ('KpB', b))
            cp('act', VrB[b], PB[3][:, :], [('pb', 3)], [('VrB', b)])
            for cj in range(3):
                act(sqB[b][:, cj, :], PB[5][:, cj * 128:(cj + 1) * 128], AF.Square, [('pb', 5), 'bcfm'], [('sqB', b)],
                    bias=bcfm[:, cj:cj + 1])
                act(cqB[b][:, cj, :], PB[5][:, cj * 128:(cj + 1) * 128], AF.Identity, [('pb', 5), 'bcfm'], [('cqB', b)],
                    bias=bcfm[:, cj:cj + 1])
            act(gate2[b], PB[4][:, :], AF.Silu, [('pb', 4)], [('gate2', b)])
            tt('pool', gate2[b], gate2[b], gnB, ALU.mult, [('gate2', b), 'gnB'], [('gate2', b)])
            tt('pool', KwfB[b], KpB[b], dkfB2, ALU.mult, [('KpB', b), 'dkk'], [('KwfB', b)])

        def secondB(c):
            b = c % 2
            q4 = c % 4
            tm_ = tmt3[c % 3]
            tmtok = ('tmt3', c % 3)
            pQK = PB[0][:, :].bitcast(BF16).rearrange("p (a b) -> p a b", a=8)[:, :, 0:128]
            for h in range(4):
                tr(pQK[:, h, :], Qp2[b][:, h, :], [('Qp2', b)], [('pb', 0)])
                tr(pQK[:, 4 + h, :], KpB[b][:, h, :], [('KpB', b)], [('pb', 0)])
            cp('act', QT3[:, 0], pQK[:, 0:4, :], [('pb', 0)], ['QT0'])
            cp('act', KT, pQK[:, 4:8, :], [('pb', 0)], ['KT'])
            tt('dve', QT3[:, 1], pQK[:, 0:4, :], dqfB, ALU.mult, [('pb', 0), 'dqfB'], ['QT1'])
            tt('dve', QT3[:, 2], pQK[:, 0:4, :], dqbB, ALU.mult, [('pb', 0), 'dqbB'], ['QT2'])
            for cj in range(3):
                mm(PB[5][:, 128:256], ones_bf, sqB[b][:, cj, :], cj == 0, cj == 2, ['ones_bf', ('sqB', b)], [('pb', 5)])
            Ap = PB[6][:, :].rearrange("p (h d) -> p h d", h=4)
            for h in range(4):
                mm(Ap[:, h, :], KT[:, h, :], QT3[:, 0, h, :], True, True, ['KT', 'QT0'], [('pb', 6)])
            cp('dve', rB, PB[5][:, 128:256], [('pb', 5)], ['rB'])
            rsqrt_small(rB, 'rB', 1.0 / 384.0, 1e-6)
            tt('dve', AcT, Ap, DcT, ALU.mult, [('pb', 6), 'DcT'], ['AcT'])
            cp('act', Sbf, Sf, ['Sf'], ['Sbf'])
            qnp = PB[5][:, :].rearrange("p (h d) -> p h d", h=4)
            for h in range(4):
                for cj in range(3):
                    mm(qnp[:, h, :], Wuq[:, cj, h, 0:128], cqB[b][:, cj, :], cj == 0, cj == 2, ['Wuq', ('cqB', b)], [('pb', 5)])
            qrp = PB[1][0:64, :].rearrange("p (h d) -> p h d", h=4)
            qrrp = PB[2][0:64, :].rearrange("p (h d) -> p h d", h=4)
            for h in range(4):
                for cj in range(3):
                    mm(qrp[:, h, :], Wuq[:, cj, h, 128:192], cqB[b][:, cj, :], cj == 0, cj == 2, ['Wuq', ('cqB', b)], [('pb', 1)])
                for cj in range(3):
                    mm(qrrp[:, h, :], Wuqr[:, cj, h, :], cqB[b][:, cj, :], cj == 0, cj == 2, ['Wuqr', ('cqB', b)], [('pb', 2)])
            Op = PB[7][:, :].rearrange("p (h d) -> p h d", h=4)
            for h in range(4):
                mm(Op[:, h, :], AcT[:, h, :], VrB[b][:, h * 128:(h + 1) * 128], True, False, ['AcT', ('VrB', b)], [('pb', 7)])
                mm(Op[:, h, :], QT3[:, 1, h, :], Sbf[:, h, :], False, False, ['QT1', 'Sbf'], [('pb', 7)])
                mm(Op[:, h, :], QT3[:, 2, h, :], Tsave[:, c, h, :], False, True, ['QT2', 'Tsave'], [('pb', 7)])
            Pf = PB[6][:, :].rearrange("p (h d) -> p h d", h=4)
            for h in range(4):
                mm(Pf[:, h, :], KwfB[b][:, h, :], VrB[b][:, h * 128:(h + 1) * 128], True, True, [('KwfB', b), ('VrB', b)], [('pb', 6)])
            tt('dve', QNb[:, :, q4 * 128:(q4 + 1) * 128], qnp, rB.unsqueeze(1).broadcast_to([128, 4, 128]), ALU.mult,
               [('pb', 5), 'rB'], ['QNb'])
            tt('dve', qr1, qrp, tm_[:, 0, :].unsqueeze(1).broadcast_to([64, 4, 128]), ALU.mult, [('pb', 1), tmtok], ['qr1'])
            tt('dve', qr2, qrrp, tm_[:, 1, :].unsqueeze(1).broadcast_to([64, 4, 128]), ALU.mult, [('pb', 2), tmtok], ['qr2'])
            tt('pool', qr1, qr1, qr2, ALU.add, ['qr1', 'qr2'], ['qr1'])
            tt('pool', QRb[:, :, q4 * 128:(q4 + 1) * 128], qr1, rB[0:64, :].unsqueeze(1).broadcast_to([64, 4, 128]),
               ALU.mult, ['qr1', 'rB'], ['QRb'])
            for h in range(4):
                stt('dve', Sf[:, h, :], Sf[:, h, :], dcc[:, h:h + 1], Pf[:, h, :], ALU.mult, ALU.add,
                    [('pb', 6), 'dcc', 'Sf', 'Sbf'], ['Sf'])
            cp('act', y0[b], Op, [('pb', 7)], [('y0', b)])
            if q4 == 3:
                t0 = c - 3
                dma('sp', QTd[j][:, :, t0 * 128:(t0 + 4) * 128].rearrange("h p n -> p h n"), QNb, r=['QNb'], w=[('QTd', j)])
                dma('sp', QRd[j][:, :, t0 * 128:(t0 + 4) * 128].rearrange("h p n -> p h n"), QRb, r=['QRb'], w=[('QRd', j)])

        def thirdB_vec(c):
            b = c % 2
            for h in range(4):
                bnstats(st4[:, h, :], y0[b][:, h, :], [('y0', b)], ['st4'])
            for h in range(4):
                bnaggr(mv4[:, h, :], st4[:, h, :], ['st4'], ['mv4'])
            cp('dve', rs4, mv4[:, :, 1], ['mv4'], ['rs4'])
            rsqrt_small(rs4, 'rs4', 1.0, 1e-5)
            tt('dve', y0[b], y0[b], mv4[:, :, 0:1].broadcast_to([128, 4, 128]), ALU.subtract, [('y0', b), 'mv4'], [('y0', b)])
            tt('pool', y0[b], y0[b], rs4.unsqueeze(2).broadcast_to([128, 4, 128]), ALU.mult, [('y0', b), 'rs4'], [('y0', b)])
            tt('pool', retb2[b], y0[b].rearrange("p h d -> p (h d)"), gate2[b], ALU.mult, [('y0', b), ('gate2', b)], [('retb2', b)])

        def thirdB_pe(c):
            b = c % 2
            q4 = c % 4
            pR = PB[0][:, :].bitcast(BF16).rearrange("p (a b) -> p a b", a=8)[:, :, 0:128]
            for h in range(4):
                tr(pR[:, h, :], retb2[b][:, h * 128:(h + 1) * 128], [('retb2', b)], [('pb', 0)])
            cp('act', RTb[:, :, q4 * 128:(q4 + 1) * 128], pR[:, 0:4, :], [('pb', 0)], ['RTb'])
            if q4 == 3:
                t0 = c - 3
                dma('sp', RTd[j][:, :, t0 * 128:(t0 + 4) * 128].rearrange("h p n -> p h n"), RTb, r=['RTb'], w=[('RTd', j)])

        loadB(0)
        loadB(1)
        lnB_(0)
        trB(0)
        projB(0)
        lnB_(1)
        evacB(0)
        trB(1)
        for c in range(NCH):
            if c >= 1:
                thirdB_vec(c - 1)
            if c + 1 < NCH:
                projB(c + 1)
                if c + 2 < NCH:
                    loadB(c + 2)
                    lnB_(c + 2)
                evacB(c + 1)
                if c + 2 < NCH:
                    trB(c + 2)
            secondB(c)
            if c >= 1:
                thirdB_pe(c - 1)
        thirdB_vec(NCH - 1)
        thirdB_pe(NCH - 1)

        P.barrier()
        AR.release(job_mark)
        KTh = AR.alloc([128, S], BF16)
        KRh = AR.alloc([128, S], BF16)
        Vh = AR.alloc([128, NT, 128], BF16)
        QTh = [AR.alloc([128, L], BF16) for _ in range(2)]
        QRh = [AR.alloc([128, L], BF16) for _ in range(2)]
        memset('pool', KRh, 0.0, ['KRh'])
        memset('pool', QRh[0], 0.0, [('QRh', 0)])
        memset('pool', QRh[1], 0.0, [('QRh', 1)])
        NPT = 4
        PT = [AR.alloc([128, 512], BF16) for _ in range(NPT)]
        dacc = [AR.alloc([128, 512], F32) for _ in range(6)]
        rden = AR.alloc([128, 512], F32)
        ato = AR.alloc([128, 512], BF16)
        dma('sp', KRh[0:64, :], KRd[j], r=[('KRd', j)], w=['KRh'])
        NQB = L // 512
        NH2 = NT // 2

        def load_head(h, hf):
            hp = h % 2
            dma('sp', KTh[:, hf * NH2 * 128:(hf + 1) * NH2 * 128], KTd[j][h][:, hf * NH2 * 128:(hf + 1) * NH2 * 128],
                r=[('KTd', j)], w=[('KTh', hf)])
            dma('sp', Vh[:, hf * NH2:(hf + 1) * NH2, :], VVd[j][h][:, hf * NH2:(hf + 1) * NH2, :],
                r=[('VVd', j)], w=[('Vh', hf)])
            if hf == 0:
                dma('sp', QTh[hp], QTd[j][h], r=[('QTd', j)], w=[('QTh', hp)])
                dma('sp', QRh[hp][0:64, :], QRd[j][h], r=[('QRd', j)], w=[('QRh', hp)])

        items = [(h, qb, kt) for h in range(4) for qb in range(NQB) for kt in range(NT)]

        def qk(item, gi_):
            h, qb, kt = item
            hp = h % 2
            if qb == 0 and kt == 0:
                load_head(h, 0)
                if h == 0:
                    load_head(h, 1)
            qs = slice(qb * 512, (qb + 1) * 512)
            sb = gi_ % NPT
            hf = kt // NH2
            mm(PB[sb][:, :], KTh[:, kt * 128:(kt + 1) * 128], QTh[hp][:, qs], True, False,
               [('KTh', hf), ('QTh', hp)], [('pb', sb)])
            mm(PB[sb][:, :], KRh[:, kt * 128:(kt + 1) * 128], QRh[hp][:, qs], False, True,
               ['KRh', ('QRh', hp)], [('pb', sb)])
            act(PT[sb], PB[sb][:, :], AF.Exp, [('pb', sb)], [('PT', sb)])

        def pv(item, gi_):
            h, qb, kt = item
            qs = slice(qb * 512, (qb + 1) * 512)
            blk = h * NQB + qb
            ob = 4 + (blk % 2)
            db = 6 + (blk % 2)
            sb = gi_ % NPT
            hf = kt // NH2
            mm(PB[ob][:, :], Vh[:, kt, :], PT[sb], kt == 0, kt == NT - 1, [('Vh', hf), ('PT', sb)], [('pb', ob)])
            ai, eng_ = kt % 3, 'dve'
            da = dacc[(blk % 2) * 3 + ai]
            dtok = ('dacc', blk % 2, ai)
            if kt in (0, 1, 2):
                cp(eng_, da, PT[sb], [('PT', sb)], [dtok])
            else:
                tt(eng_, da, da, PT[sb], ALU.add, [dtok, ('PT', sb)], [dtok], nosync=(eng_ == 'dve'))
            if kt == NT - 1:
                for ai2 in range(3):
                    mm(PB[db][:, :], ones_f, dacc[(blk % 2) * 3 + ai2], ai2 == 0, ai2 == 2,
                       ['ones_f', ('dacc', blk % 2, ai2)], [('pb', db)])
                recip(rden, PB[db][:, :], [('pb', db)], ['rden'])
                tt('dve', ato, PB[ob][:, :], rden, ALU.mult, [('pb', ob), 'rden'], ['ato'])
                dma('sp', ATd[j][h][:, qs], ato, r=['ato'], w=[('ATd', j)])
                if qb == NQB - 1 and h < 3:
                    load_head(h + 1, 1)

        LA = 3
        for i_ in range(min(LA, len(items))):
            qk(items[i_], i_)
        for i_ in range(len(items)):
            if i_ + LA < len(items):
                qk(items[i_ + LA], i_ + LA)
            pv(items[i_], i_)

        P.barrier()
        AR.release(job_mark)
        lnB = AR.alloc([128, 2, 1024], F32)
        dma('sp', lnB, ln_in[0:2, :].partition_broadcast(128), r=[], w=['lnB'])
        Wo = AR.alloc([128, 8, 1024], BF16)
        d_mark = AR.mark()
        wst = [AR.alloc([128, 8, 512], F32) for _ in range(2)]
        gi = 0
        for g in range(2):
            b = gi % 2
            gi += 1
            dma('sp', wst[b], wo_in[:, g * 512:(g + 1) * 512].rearrange("(k p) n -> p k n", p=128), r=[], w=[('wst', b)])
            for k in range(8):
                tt('dve', Wo[:, k, g * 512:(g + 1) * 512], wst[b][:, k, :], gB[:, 0, g * 512:(g + 1) * 512], ALU.mult,
                   [('wst', b), 'gB'], ['Wo'])
        cat2 = [AR.alloc([128, 8, 512], BF16) for _ in range(2)]
        xo = [AR.alloc([128, D], F32) for _ in range(2)]
        yv2 = [AR.alloc([128, D], F32) for _ in range(2)]
        x1 = [AR.alloc([128, D], F32) for _ in range(2)]
        ln_scratch('d1')
        NT1 = L // 128

        def load_d1(tix):
            m, u = tix // 4, tix % 4
            if u == 0:
                ms = slice(m * 512, (m + 1) * 512)
                cb = m % 2
                dma('act', cat2[cb][:, 0:4, :], RTd[j][:, :, ms].rearrange("h p n -> p h n"), r=[('RTd', j)], w=[('cat', cb, 0)])
                dma('act', cat2[cb][:, 4:8, :], ATd[j][:, :, ms].rearrange("h p n -> p h n"), r=[('ATd', j)], w=[('cat', cb, 1)])
            dma('act', xo[tix % 2], xj[tix * 128:(tix + 1) * 128, :], r=[], w=[('xo', tix % 2)])

        load_d1(0)
        for tix in range(NT1):
            m, u = tix // 4, tix % 4
            b = tix % 2
            cb = m % 2
            if tix + 1 < NT1:
                load_d1(tix + 1)
            for half in range(2):
                for k in range(8):
                    mm(PB[half][:, :], cat2[cb][:, k, u * 128:(u + 1) * 128], Wo[:, k, half * 512:(half + 1) * 512],
                       k == 0, k == 7, [('cat', cb, 0), ('cat', cb, 1), 'Wo'], [('pb', half)])
                stt('dve', yv2[b][:, half * 512:(half + 1) * 512], xo[b][:, half * 512:(half + 1) * 512], ALPHA,
                    PB[half][:, :], ALU.mult, ALU.add, [('xo', b), ('pb', half)], [('yv', b)])
            ln_apply(yv2[b], yv2[b], 'd1', [('yv', b)], [('yv', b)])
            tt('pool', yv2[b], yv2[b], lnB[:, 0, :], ALU.mult, [('yv', b), 'lnB'], [('yv', b)])
            tt('pool', x1[b], yv2[b], lnB[:, 1, :], ALU.add, [('yv', b), 'lnB'], [('x1', b)])
            dma('sp', X1d[j][tix * 128:(tix + 1) * 128, :], x1[b], r=[('x1', b)], w=[('X1d', j)])

        P.barrier()
        AR.release(job_mark)
        lnB = AR.alloc([128, 2, 1024], F32)
        dma('sp', lnB, ln_in[2:4, :].partition_broadcast(128), r=[], w=['lnB'])
        Wup = AR.alloc([128, 8, DFF], BF16)
        Wdn = AR.alloc([128, 32, D], BF16)
        bup = AR.alloc([128, 32], F32)
        e_mark = AR.mark()
        wst = [AR.alloc([128, 8, 512], F32) for _ in range(2)]
        gi = 0
        for g in range(8):
            b = gi % 2
            gi += 1
            dma('sp', wst[b], wup_in[:, g * 512:(g + 1) * 512].rearrange("(k p) n -> p k n", p=128), r=[], w=[('wst', b)])
            for c in range(4):
                for k in range(8):
                    mm(PB[1][:, 0:1], wst[b][:, k, c * 128:(c + 1) * 128], modfm[:, 24 + k, j:j + 1], k == 0, k == 7,
                       [('wst', b), 'modfm'], [('pb', 1)])
                cp('dve', bup[:, g * 4 + c:g * 4 + c + 1], PB[1][:, 0:1], [('pb', 1)], ['bup'])
            for k in range(8):
                ts('dve', Wup[:, k, g * 512:(g + 1) * 512], wst[b][:, k, :], scp[:, 1, k, j:j + 1], None, ALU.mult, None,
                   [('wst', b), 'scp'], ['Wup'])
        for g in range(8):
            b = gi % 2
            gi += 1
            fg, half = g // 2, g % 2
            dma('sp', wst[b], wdn_in[fg * 1024:(fg + 1) * 1024, half * 512:(half + 1) * 512].rearrange("(k p) n -> p k n", p=128),
                r=[], w=[('wst', b)])
            for k in range(8):
                tt('dve', Wdn[:, fg * 8 + k, half * 512:(half + 1) * 512], wst[b][:, k, :], gB[:, 1, half * 512:(half + 1) * 512],
                   ALU.mult, [('wst', b), 'gB'], ['Wdn'])
        P.barrier()
        AR.release(e_mark)
        TBK = 256
        NU = TBK // 128
        NB = L // TBK
        x1t = [[AR.alloc([128, D], F32) for _ in range(NU)] for _ in range(2)]
        xn2 = AR.alloc([128, D], BF16)
        h2T = [AR.alloc([128, 8, TBK], BF16) for _ in range(2)]
        u2 = [AR.alloc([128, 8, TBK], BF16) for _ in range(2)]
        rt = [AR.alloc([128, TBK], F32) for _ in range(2)]
        oo = [gB[:, 0, :], gB[:, 1, :]]
        ln_scratch('d2a')
        ln_scratch('d2b')
        pT2 = PB[6][:, :].bitcast(BF16).rearrange("p (a b) -> p a b", a=8)[:, :, 0:128]

        def front2(m):
            mp = m % 2
            for u in range(NU):
                tix = m * NU + u
                dma('sp', x1t[mp][u], X1d[j][tix * 128:(tix + 1) * 128, :], r=[('X1d', j)], w=[('x1t', mp, u)])
                ln_apply(x1t[mp][u], xn2, 'd2a', [('x1t', mp, u)], ['xn2'])
                for k in range(8):
                    tr(pT2[:, k, :], xn2[:, k * 128:(k + 1) * 128], ['xn2'], [('pb', 6)])
                cp('act', h2T[mp][:, :, u * 128:(u + 1) * 128], pT2, [('pb', 6)], [('h2T', mp)])

        def up(m, g):
            mp = m % 2
            ub = (m * 4 + g) % 2
            for fl in range(8):
                f = g * 8 + fl
                pb = 4 + (f % 2)
                for k in range(8):
                    mm(PB[pb][:, 0:TBK], Wup[:, k, f * 128:(f + 1) * 128], h2T[mp][:, k, :], k == 0, k == 7,
                       ['Wup', ('h2T', mp)], [('pb', pb)])
                rb = f % 2
                act(rt[rb], PB[pb][:, 0:TBK], AF.Relu, [('pb', pb), 'bup'], [('rt', rb)], bias=bup[:, f:f + 1])
                tt('pool' if f % 2 else 'dve', u2[ub][:, fl, :], rt[rb], rt[rb], ALU.mult, [('rt', rb)], [('u2', ub)])

        def down(m, g):
            ub = (m * 4 + g) % 2
            for u in range(NU):
                for half in range(2):
                    pbd = u * 2 + half
                    for fl in range(8):
                        f = g * 8 + fl
                        mm(PB[pbd][:, :], u2[ub][:, fl, u * 128:(u + 1) * 128], Wdn[:, f, half * 512:(half + 1) * 512],
                           f == 0, f == 31, [('u2', ub), 'Wdn'], [('pb', pbd)])

        def tail(m):
            mp = m % 2
            for u in range(NU):
                tix = m * NU + u
                ob = tix % 2
                for half in range(2):
                    pbd = u * 2 + half
                    stt('dve', oo[ob][:, half * 512:(half + 1) * 512], x1t[mp][u][:, half * 512:(half + 1) * 512], ALPHA,
                        PB[pbd][:, :], ALU.mult, ALU.add, [('x1t', mp, u), ('pb', pbd)], [('oo', ob)])
                ln_apply(oo[ob], oo[ob], 'd2b', [('oo', ob)], [('oo', ob)])
                tt('pool', oo[ob], oo[ob], lnB[:, 0, :], ALU.mult, [('oo', ob), 'lnB'], [('oo', ob)])
                tt('pool', oo[ob], oo[ob], lnB[:, 1, :], ALU.add, [('oo', ob), 'lnB'], [('oo', ob)])
                dma('sp', y_out[j][tix * 128:(tix + 1) * 128, :], oo[ob], r=[('oo', ob)], w=[('yout', j, ob)], outp=True)

        G = [(m, g) for m in range(NB) for g in range(4)]
        front2(0)
        up(0, 0)
        for i_, (m, g) in enumerate(G):
            if i_ + 1 < len(G):
                m2, g2 = G[i_ + 1]
                if g2 == 0:
                    front2(m2)
                up(m2, g2)
            down(m, g)
            if g == 3:
                tail(m)

    P.emit()
    return nc, P


def _rope_tables(pos, d):
    inv = (10000.0 ** (-np.arange(0, d, 2, dtype=np.float32) / np.float32(d))).astype(np.float32)
    ang = pos.astype(np.float32)[:, None] * inv[None, :]
    return np.cos(ang).astype(np.float32), np.sin(ang).astype(np.float32)


def host_inputs(L, NSLOT, core_jobs, xs, cs, w):
    NJ = len(NSLOT)
    NCH = L // 128
    p = np.arange(128, dtype=np.float32)
    i = np.arange(128, dtype=np.float32)
    cst = np.zeros((128, 4 * 128 + 8), np.float32)
    cst[:, 0:128] = np.maximum(i[None, :] - p[:, None], 0)
    cst[:, 128:256] = np.maximum(p[:, None] - i[None, :], 0)
    cst[:, 256:384] = (p[:, None] <= i[None, :])
    cst[:, 384:512] = (p[:, None] > i[None, :])
    cst[:, 512] = 127 - p
    cst[:, 513] = p
    cst[:, 514] = 128.0
    shared = dict(
        cst=cst,
        w_ada=np.ascontiguousarray(w['w_ada'][0]), b_ada_fm=np.ascontiguousarray(w['b_ada'][0].reshape(48, 128).T),
        b_ada_row=np.ascontiguousarray(w['b_ada'][0].reshape(1, -1)),
        w_in=np.ascontiguousarray(w['w_in'][0]), w_uq=np.ascontiguousarray(w['w_uq'][0]),
        w_ukv=np.ascontiguousarray(w['w_ukv'][0]), w_o=np.ascontiguousarray(w['w_o'][0]),
        w_up=np.ascontiguousarray(w['w_up'][0]), w_down=np.ascontiguousarray(w['w_down'][0]),
        dec=np.concatenate([w['ret_decay_f'][0], w['ret_decay_b'][0]]).reshape(1, 8).astype(np.float32),
        gnw=np.ascontiguousarray(w['ret_gn_w'][0].reshape(1, 512)),
        qnw_fm=np.ascontiguousarray(w['q_norm_w'][0].reshape(3, 128).T),
        kvnw_fm=np.ascontiguousarray(w['kv_norm_w'][0].reshape(2, 128).T),
        ln=np.ascontiguousarray(np.stack([w['ln1_w'][0], w['ln1_b'][0], w['ln2_w'][0], w['ln2_b'][0]])),
    )
    in_maps = []
    for core, jobs in enumerate(core_jobs):
        m = dict(shared)
        cT = np.zeros((128, 8, NJ), np.float32)
        mk = np.zeros((128, NJ, 2, 128), np.float32)
        ex = np.zeros((128, NJ, 2, 128), np.float32)
        for j, (bi, own) in enumerate(jobs):
            NS = NSLOT[j]
            S = NS * L
            pos = (np.arange(S) + own * L) % S
            m["x%d" % j] = np.ascontiguousarray(np.roll(xs[j][bi], -own * L, axis=0))
            cR, sR = _rope_tables(pos, 128)
            m["tb%d" % j] = np.ascontiguousarray(np.concatenate([cR, cR, sR, sR], axis=1))
            cM, sM = _rope_tables(pos, 64)
            tm = np.stack([np.concatenate([cM, cM], 1).T, np.concatenate([sM, sM], 1).T], axis=1)
            m["tm%d" % j] = np.ascontiguousarray(tm.astype(np.float32))
            cT[:, :, j] = cs[j][bi].reshape(8, 128).T
            for s in range(1, NS):
                before = (own + s) >= NS
                for n in range(NCH):
                    t = s * NCH + n
                    ex[:, j, 0, t] = (NS - s - 1) * L + (L - 128 - 128 * n)
                    ex[:, j, 1, t] = (s - 1) * L + 128 * n
                    mk[:, j, 0, t] = 1.0 if before else 0.0
                    mk[:, j, 1, t] = 0.0 if before else 1.0
        m["cT"] = cT
        m["mk"] = mk
        m["ex"] = ex
        in_maps.append(m)
    return in_maps


_CACHE = {}


def kernel(x_prompt, x_sample, c_prompt, c_sample, **w):
    L = 4096
    NSLOT = [4, 2]
    x_prompt = np.asarray(x_prompt, np.float32)
    x_sample = np.asarray(x_sample, np.float32)
    c_prompt = np.asarray(c_prompt, np.float32)
    c_sample = np.asarray(c_sample, np.float32)
    w = {k: np.asarray(v, np.float32) for k, v in w.items()}
    core_jobs = [[(c // 4, c % 4), (c // 2, c % 2)] for c in range(NCORES)]
    in_maps = host_inputs(L, NSLOT, core_jobs, [x_prompt, x_sample], [c_prompt, c_sample], w)
    if 'nc' not in _CACHE:
        _CACHE['nc'] = build(L, NSLOT)[0]
    res = run_bass_kernel_spmd(_CACHE['nc'], in_maps, core_ids=list(range(NCORES)))
    yp = np.empty_like(x_prompt)
    ysm = np.empty_like(x_sample)
    for c in range(NCORES):
        r = res.results[c]
        yp[c // 4, (c % 4) * L:(c % 4 + 1) * L] = r["y0"]
        ysm[c // 2, (c % 2) * L:(c % 2 + 1) * L] = r["y1"]
    return (yp, ysm)
```
